# Optimizing a Trainium2 kernel written in Bass

```python
import math
import jax, jax.numpy as jnp
from jax import lax
import numpy as np

D_MODEL = 2048
BATCH = 16
SEQ = 2048
DEPTH = 2

GRID_W = 64
CTX_LEN = 256
CHUNK = 64
NORM_EPS = 1e-6

GLA_HEADS = 4
GLA_DK = D_MODEL // 2
GLA_DV = D_MODEL
GLA_HK = GLA_DK // GLA_HEADS
GLA_HV = GLA_DV // GLA_HEADS
GLA_RANK = 16
GLA_LOGIT_NORM = 16.0

SSM_DI = 2 * D_MODEL
SSM_HEADDIM = 64
SSM_HEADS = SSM_DI // SSM_HEADDIM
SSM_GROUPS = 8
SSM_STATE = 128
SSM_CONV = 3
SSM_XBC = SSM_DI + 2 * SSM_GROUPS * SSM_STATE

RW_DIM = D_MODEL
RW_HEADSIZE = 64
RW_HEADS = RW_DIM // RW_HEADSIZE
RW_DECAY_LORA = max(32, int(round(1.8 * D_MODEL ** 0.5 / 32)) * 32)
RW_AAA_LORA = max(32, int(round(1.8 * D_MODEL ** 0.5 / 32)) * 32)
RW_GATE_LORA = max(32, int(round(0.6 * D_MODEL ** 0.8 / 32)) * 32)
RW_LN_EPS = 64e-5
RW_SPLITS = (RW_DIM, RW_DIM, RW_DIM, 2 * RW_DECAY_LORA, 2 * RW_AAA_LORA, RW_GATE_LORA)
RW_COLS = 3 * RW_DIM + 2 * RW_DECAY_LORA + 2 * RW_AAA_LORA + RW_GATE_LORA

FFN_HIDDEN = ((8 * D_MODEL // 3 + 255) // 256) * 256
FFN_CONV = 3

IN_SPLITS = (GLA_DK, GLA_DK, GLA_DV, GLA_DV, 2 * GLA_RANK, SSM_DI, SSM_XBC, 2 * SSM_HEADS, RW_COLS, 3 * D_MODEL)
N_IN = 2 * GLA_DK + 2 * GLA_DV + 2 * GLA_RANK + SSM_DI + SSM_XBC + 2 * SSM_HEADS + RW_COLS + 3 * D_MODEL

kernel_name = 'hybrid_gla_ssd_rwkv7_prefix_dit'

F32 = jnp.float32


def split_cols(z, sizes):
    idx, acc = [], 0
    for s in sizes[:-1]:
        acc += s
        idx.append(acc)
    return jnp.split(z, idx, axis=-1)


def rms_norm(x, g):
    xf = x.astype(F32)
    y = xf * lax.rsqrt(jnp.mean(xf * xf, axis=-1, keepdims=True) + NORM_EPS)
    return (y * g.astype(F32)).astype(x.dtype)


def modulate(h, shift, scale):
    return h * (1 + scale[:, None, :]) + shift[:, None, :]


def shift_centre(x):
    xp = jnp.pad(x, ((0, 0), (1, 1), (0, 0)))
    return 0.5 * (xp[:, :-2] + xp[:, 2:])


def dwconv_seq(x, w, b):
    k = w.shape[0]
    y = lax.conv_general_dilated(x, w[:, None, :].astype(x.dtype), (1,), [(k // 2, k // 2)],
                                 dimension_numbers=('NWC', 'WIO', 'NWC'), feature_group_count=x.shape[-1])
    return y + b


def dwconv_grid(x, w, b, rows, cols):
    bsz, t, ch = x.shape
    y = lax.conv_general_dilated(x.reshape(bsz, rows, cols, ch), w[:, :, None, :].astype(x.dtype), (1, 1), 'SAME',
                                 dimension_numbers=('NHWC', 'HWIO', 'NHWC'), feature_group_count=ch)
    return y.reshape(bsz, t, ch) + b


def gla_chunk_scan(q, k, v, log_a, s0):
    bsz, t, h, _ = q.shape
    n = t // CHUNK

    def chunks(z):
        return z.astype(F32).reshape(bsz, n, CHUNK, h, z.shape[-1]).transpose(1, 0, 3, 2, 4)

    tri = jnp.tril(jnp.ones((CHUNK, CHUNK), dtype=bool))

    def step(s, inp):
        qi, ki, vi, gi = inp
        b = jnp.cumsum(gi, axis=2)
        q_dec = qi * jnp.exp(b)
        k_inv = ki * jnp.exp(-b)
        att = jnp.where(tri, jnp.einsum('bhqd,bhsd->bhqs', q_dec, k_inv), 0.0)
        o = jnp.einsum('bhqd,bhdv->bhqv', q_dec, s) + jnp.einsum('bhqs,bhsv->bhqv', att, vi)
        b_end = b[:, :, -1, :]
        s = s * jnp.exp(b_end)[..., None] + jnp.einsum('bhsd,bhsv->bhdv', ki * jnp.exp(b_end[:, :, None, :] - b), vi)
        return s, o

    s_fin, o = lax.scan(step, s0, (chunks(q), chunks(k), chunks(v), chunks(log_a)))
    return o.transpose(1, 0, 3, 2, 4).reshape(bsz, t, h, v.shape[-1]), s_fin


def ssd_chunk_scan(x, dt, la, bm, cm, h0):
    bsz, t, nh, p = x.shape
    g = bm.shape[2]
    e = nh // g
    n = t // CHUNK
    xc = x.astype(F32).reshape(bsz, n, CHUNK, g, e, p).transpose(1, 0, 3, 4, 2, 5)
    dtc = dt.astype(F32).reshape(bsz, n, CHUNK, g, e).transpose(1, 0, 3, 4, 2)
    lac = la.astype(F32).reshape(bsz, n, CHUNK, g, e).transpose(1, 0, 3, 4, 2)
    bc = bm.astype(F32).reshape(bsz, n, CHUNK, g, -1).transpose(1, 0, 3, 2, 4)
    cc = cm.astype(F32).reshape(bsz, n, CHUNK, g, -1).transpose(1, 0, 3, 2, 4)
    tri = jnp.tril(jnp.ones((CHUNK, CHUNK), dtype=bool))

    def step(hs, inp):
        xi, dti, lai, bi, ci = inp
        cum = jnp.cumsum(lai, axis=-1)
        seg = jnp.where(tri, cum[..., :, None] - cum[..., None, :], -jnp.inf)
        cb = jnp.einsum('bgqn,bgsn->bgqs', ci, bi)
        w = cb[:, :, None] * jnp.exp(seg) * dti[..., None, :]
        y = jnp.einsum('bgeqs,bgesp->bgeqp', w, xi)
        y = y + jnp.einsum('bgqn,bgepn->bgeqp', ci, hs) * jnp.exp(cum)[..., None]
        dec_end = jnp.exp(cum[..., -1:] - cum) * dti
        hs = hs * jnp.exp(cum[..., -1])[..., None, None] + jnp.einsum('bgqn,bgeqp->bgepn', bi, xi * dec_end[..., None])
        return hs, y

    h_fin, y = lax.scan(step, h0, (xc, dtc, lac, bc, cc))
    return y.transpose(1, 0, 4, 2, 3, 5).reshape(bsz, t, nh, p), h_fin


def rwkv7_scan(r, w, k, v, kk, a, s0):
    def tm(z):
        return jnp.moveaxis(z.astype(F32), 1, 0)

    def step(s, inp):
        ri, wi, ki, vi, kki, ai = inp
        sa = jnp.einsum('bhvk,bhk->bhv', s, kki)
        s = s * wi[:, :, None, :] - sa[..., None] * (kki * ai)[:, :, None, :] + vi[..., None] * ki[:, :, None, :]
        return s, jnp.einsum('bhvk,bhk->bhv', s, ri)

    s_fin, y = lax.scan(step, s0, (tm(r), tm(w), tm(k), tm(v), tm(kk), tm(a)))
    return jnp.moveaxis(y, 0, 1), s_fin


def bidir_scan(scan_fn, ctx_fwd, lat_fwd, ctx_bwd, lat_bwd, s0):
    def run(ctx_in, lat_in, reverse):
        flip = (lambda z: jnp.flip(z, axis=1)) if reverse else (lambda z: z)
        yc, s_ctx = scan_fn(*[flip(z) for z in ctx_in], s0)
        yl, _ = scan_fn(*[flip(z) for z in lat_in], s_ctx)
        return flip(yc), flip(yl)

    ycf, ylf = run(ctx_fwd, lat_fwd, False)
    ycb, ylb = run(ctx_bwd, lat_bwd, True)
    return ycf + ycb, ylf + ylb


def gla_prep(q, k, v, a_down, p):
    bsz, t, _ = q.shape
    qh = q.reshape(bsz, t, GLA_HEADS, GLA_HK) * GLA_HK ** -0.5
    kh = k.reshape(bsz, t, GLA_HEADS, GLA_HK)
    vh = v.reshape(bsz, t, GLA_HEADS, GLA_HV)
    dirs = []
    for i, ad in enumerate(jnp.split(a_down, 2, axis=-1)):
        logit = (ad @ p['gla_wa_up'][i] + p['gla_ba'][i]).astype(F32)
        log_a = (jax.nn.log_sigmoid(logit) / GLA_LOGIT_NORM).reshape(bsz, t, GLA_HEADS, GLA_HK)
        dirs.append((qh, kh, vh, log_a))
    return dirs[0], dirs[1]


def gla_post(o, g, norm_g):
    bsz, t = g.shape[:2]
    o = o * lax.rsqrt(jnp.mean(o * o, axis=-1, keepdims=True) + NORM_EPS) * norm_g.astype(F32)
    return (o.reshape(bsz, t, GLA_DV) * jax.nn.silu(g.astype(F32))).astype(g.dtype)


def gla_branch(sc, sl, p, need_ctx):
    cf, cb = gla_prep(sc[0], sc[1], sc[2], sc[4], p)
    lf, lb = gla_prep(sl[0], sl[1], sl[2], sl[4], p)
    s0 = jnp.zeros((sl[0].shape[0], GLA_HEADS, GLA_HK, GLA_HV), F32)
    oc, ol = bidir_scan(gla_chunk_scan, cf, lf, cb, lb, s0)
    out_l = gla_post(ol, sl[3], p['gla_norm_g'])
    out_c = gla_post(oc, sc[3], p['gla_norm_g']) if need_ctx else None
    return out_c, out_l


def ssm_prep(xbc, dt_raw, p):
    bsz, t, _ = xbc.shape
    xbc = jax.nn.silu(dwconv_seq(xbc, p['ssm_conv_w'], p['ssm_conv_b']))
    xs, bm, cm = jnp.split(xbc, [SSM_DI, SSM_DI + SSM_GROUPS * SSM_STATE], axis=-1)
    xh = xs.reshape(bsz, t, SSM_HEADS, SSM_HEADDIM)
    bm = bm.reshape(bsz, t, SSM_GROUPS, SSM_STATE)
    cm = cm.reshape(bsz, t, SSM_GROUPS, SSM_STATE)
    dirs = []
    for i, dr in enumerate(jnp.split(dt_raw.astype(F32), 2, axis=-1)):
        dt = jax.nn.softplus(dr + p['ssm_dt_bias'][i])
        la = dt * (-jnp.exp(p['ssm_a_log'][i].astype(F32)))
        dirs.append((xh, dt, la, bm, cm))
    return xh, dirs[0], dirs[1]


def ssm_post(y, xh, z, p):
    bsz, t = z.shape[:2]
    y = y + p['ssm_d'].astype(F32)[:, None] * xh.astype(F32)
    y = y.reshape(bsz, t, SSM_DI) * jax.nn.silu(z.astype(F32))
    yg = y.reshape(bsz, t, SSM_GROUPS, SSM_DI // SSM_GROUPS)
    yg = yg * lax.rsqrt(jnp.mean(yg * yg, axis=-1, keepdims=True) + NORM_EPS)
    return (yg.reshape(bsz, t, SSM_DI) * p['ssm_norm_g'].astype(F32)).astype(z.dtype)


def ssm_branch(sc, sl, p, need_ctx):
    xh_c, cf, cb = ssm_prep(sc[6], sc[7], p)
    xh_l, lf, lb = ssm_prep(sl[6], sl[7], p)
    h0 = jnp.zeros((sl[6].shape[0], SSM_GROUPS, SSM_HEADS // SSM_GROUPS, SSM_HEADDIM, SSM_STATE), F32)
    yc, yl = bidir_scan(ssd_chunk_scan, cf, lf, cb, lb, h0)
    out_l = ssm_post(yl, xh_l, sl[5], p)
    out_c = ssm_post(yc, xh_c, sc[5], p) if need_ctx else None
    return out_c, out_l


def rwkv_prep(cols, p):
    bsz, t, _ = cols.shape
    cols = cols + (shift_centre(cols) - cols) * p['rwkv_mu']
    r, k, v, wd, ad, gd = split_cols(cols.astype(F32), RW_SPLITS)

    def heads(z):
        return z.reshape(bsz, t, RW_HEADS, RW_HEADSIZE)

    kk = heads(k * p['rwkv_k_k'])
    kk = kk * lax.rsqrt(jnp.maximum(jnp.sum(kk * kk, axis=-1, keepdims=True), 1e-24))
    g = jax.nn.sigmoid(gd) @ p['rwkv_g_up']
    wds = jnp.split(wd, 2, axis=-1)
    ads = jnp.split(ad, 2, axis=-1)
    dirs, bonuses = [], []
    for i in range(2):
        w_log = -jax.nn.softplus(-(p['rwkv_w0'][i] + jnp.tanh(wds[i]) @ p['rwkv_w_up'][i])) - 0.5
        decay = jnp.exp(-jnp.exp(w_log))
        a = jax.nn.sigmoid(p['rwkv_a0'][i] + ads[i] @ p['rwkv_a_up'][i])
        ki = k * (1 + (a - 1) * p['rwkv_k_a'])
        rh, kh, vh = heads(r), heads(ki), heads(v)
        dirs.append((rh, heads(decay), kh, vh, kk, heads(a)))
        bonuses.append(jnp.sum(rh * kh * p['rwkv_r_k'], axis=-1, keepdims=True) * vh)
    return g, bonuses[0] + bonuses[1], dirs[0], dirs[1]


def rwkv_post(y, g, bonus, p, dtype):
    bsz, t = y.shape[:2]
    mu = jnp.mean(y, axis=-1, keepdims=True)
    var = jnp.mean(jnp.square(y - mu), axis=-1, keepdims=True)
    yn = ((y - mu) * lax.rsqrt(var + RW_LN_EPS)).reshape(bsz, t, RW_DIM) * p['rwkv_ln_w'] + p['rwkv_ln_b']
    return ((yn + bonus.reshape(bsz, t, RW_DIM)) * g).astype(dtype)


def rwkv_branch(sc, sl, p, need_ctx):
    g_c, bonus_c, cf, cb = rwkv_prep(sc[8], p)
    g_l, bonus_l, lf, lb = rwkv_prep(sl[8], p)
    s0 = jnp.zeros((sl[8].shape[0], RW_HEADS, RW_HEADSIZE, RW_HEADSIZE), F32)
    yc, yl = bidir_scan(rwkv7_scan, cf, lf, cb, lb, s0)
    out_l = rwkv_post(yl, g_l, bonus_l, p, sl[8].dtype)
    out_c = rwkv_post(yc, g_c, bonus_c, p, sc[8].dtype) if need_ctx else None
    return out_c, out_l


def merge_branches(gate_cols, o_gla, o_ssm, o_rw, p):
    g_a, g_b, g_c = jnp.split(jax.nn.sigmoid(gate_cols), 3, axis=-1)
    m = g_a * (o_gla @ p['w_br_gla']) + g_b * (o_ssm @ p['w_br_ssm']) + g_c * (o_rw @ p['w_br_rwkv'])
    return m @ p['w_out']


def conv_ffn(h, w_up, conv_w, conv_b, w_down, rows, cols):
    u = dwconv_grid(h @ w_up, conv_w, conv_b, rows, cols)
    gate, val = jnp.split(u, 2, axis=-1)
    return (jax.nn.silu(gate) * val) @ w_down


def setup_inputs(seed: int = 0) -> dict:
    key = jax.random.key(seed)
    ks = iter(jax.random.split(key, 48))
    L, D = DEPTH, D_MODEL

    def nrm(shape, scale):
        return jax.random.normal(next(ks), shape, F32) * scale

    def uni(shape, lo, hi):
        return jax.random.uniform(next(ks), shape, F32, lo, hi)

    dt0 = jnp.exp(uni((L, 2, SSM_HEADS), math.log(1e-3), math.log(1e-1)))
    return {
        'x': nrm((BATCH, SEQ, D), 1.0),
        'c': nrm((BATCH, D), 1.0),
        'ctx': nrm((BATCH, CTX_LEN, D), 1.0),
        'c_ctx': nrm((D,), 1.0),
        'w_mod': nrm((L, D, 6 * D), 0.5 * D ** -0.5),
        'b_mod': nrm((L, 6 * D), 0.02),
        'norm1_g': 1.0 + nrm((L, D), 0.02),
        'norm2_g': 1.0 + nrm((L, D), 0.02),
        'w_in': nrm((L, D, N_IN), D ** -0.5),
        'gla_wa_up': nrm((L, 2, GLA_RANK, GLA_DK), GLA_RANK ** -0.5),
        'gla_ba': nrm((L, 2, GLA_DK), 0.1),
        'gla_norm_g': 1.0 + nrm((L, GLA_HV), 0.02),
        'ssm_conv_w': nrm((L, SSM_CONV, SSM_XBC), SSM_CONV ** -0.5),
        'ssm_conv_b': nrm((L, SSM_XBC), 0.02),
        'ssm_a_log': jnp.log(uni((L, 2, SSM_HEADS), 1.0, 16.0)),
        'ssm_dt_bias': dt0 + jnp.log(-jnp.expm1(-dt0)),
        'ssm_d': 1.0 + nrm((L, SSM_HEADS), 0.1),
        'ssm_norm_g': 1.0 + nrm((L, SSM_DI), 0.02),
        'rwkv_mu': uni((L, RW_COLS), 0.0, 1.0),
        'rwkv_w0': uni((L, 2, RW_DIM), -6.0, 0.0),
        'rwkv_w_up': nrm((L, 2, RW_DECAY_LORA, RW_DIM), 0.1 * RW_DECAY_LORA ** -0.5),
        'rwkv_a0': nrm((L, 2, RW_DIM), 0.1),
        'rwkv_a_up': nrm((L, 2, RW_AAA_LORA, RW_DIM), RW_AAA_LORA ** -0.5),
        'rwkv_g_up': nrm((L, RW_GATE_LORA, RW_DIM), RW_GATE_LORA ** -0.5),
        'rwkv_k_k': 0.85 + nrm((L, RW_DIM), 0.02),
        'rwkv_k_a': 1.0 + nrm((L, RW_DIM), 0.02),
        'rwkv_r_k': nrm((L, RW_HEADS, RW_HEADSIZE), 0.1),
        'rwkv_ln_w': 1.0 + nrm((L, RW_DIM), 0.02),
        'rwkv_ln_b': nrm((L, RW_DIM), 0.02),
        'w_br_gla': nrm((L, GLA_DV, D), GLA_DV ** -0.5),
        'w_br_ssm': nrm((L, SSM_DI, D), SSM_DI ** -0.5),
        'w_br_rwkv': nrm((L, RW_DIM, D), RW_DIM ** -0.5),
        'w_out': nrm((L, D, D), D ** -0.5),
        'ffn_w_up': nrm((L, D, 2 * FFN_HIDDEN), D ** -0.5),
        'ffn_conv_w': nrm((L, FFN_CONV, FFN_CONV, 2 * FFN_HIDDEN), 1.0 / FFN_CONV),
        'ffn_conv_b': nrm((L, 2 * FFN_HIDDEN), 0.02),
        'ffn_w_down': nrm((L, FFN_HIDDEN, D), FFN_HIDDEN ** -0.5),
        'final_norm_g': 1.0 + nrm((D,), 0.02),
    }


def reference(x, c, ctx, c_ctx, w_mod, b_mod, norm1_g, norm2_g, w_in, gla_wa_up, gla_ba, gla_norm_g,
              ssm_conv_w, ssm_conv_b, ssm_a_log, ssm_dt_bias, ssm_d, ssm_norm_g,
              rwkv_mu, rwkv_w0, rwkv_w_up, rwkv_a0, rwkv_a_up, rwkv_g_up, rwkv_k_k, rwkv_k_a, rwkv_r_k,
              rwkv_ln_w, rwkv_ln_b, w_br_gla, w_br_ssm, w_br_rwkv, w_out,
              ffn_w_up, ffn_conv_w, ffn_conv_b, ffn_w_down, final_norm_g):
    rows = x.shape[1] // GRID_W
    t_ctx = ctx.shape[1]
    xl, xc = x, ctx
    c_act = jax.nn.silu(c)
    cc_act = jax.nn.silu(c_ctx)[None]
    for l in range(DEPTH):
        need_ctx = l < DEPTH - 1
        p = {
            'gla_wa_up': gla_wa_up[l], 'gla_ba': gla_ba[l], 'gla_norm_g': gla_norm_g[l],
            'ssm_conv_w': ssm_conv_w[l], 'ssm_conv_b': ssm_conv_b[l], 'ssm_a_log': ssm_a_log[l],
            'ssm_dt_bias': ssm_dt_bias[l], 'ssm_d': ssm_d[l], 'ssm_norm_g': ssm_norm_g[l],
            'rwkv_mu': rwkv_mu[l], 'rwkv_w0': rwkv_w0[l], 'rwkv_w_up': rwkv_w_up[l], 'rwkv_a0': rwkv_a0[l],
            'rwkv_a_up': rwkv_a_up[l], 'rwkv_g_up': rwkv_g_up[l], 'rwkv_k_k': rwkv_k_k[l], 'rwkv_k_a': rwkv_k_a[l],
            'rwkv_r_k': rwkv_r_k[l], 'rwkv_ln_w': rwkv_ln_w[l], 'rwkv_ln_b': rwkv_ln_b[l],
            'w_br_gla': w_br_gla[l], 'w_br_ssm': w_br_ssm[l], 'w_br_rwkv': w_br_rwkv[l], 'w_out': w_out[l],
        }
        mod_l = jnp.split(c_act @ w_mod[l] + b_mod[l], 6, axis=-1)
        mod_c = jnp.split(cc_act @ w_mod[l] + b_mod[l], 6, axis=-1)
        hl = modulate(rms_norm(xl, norm1_g[l]), mod_l[0], mod_l[1])
        hc = modulate(rms_norm(xc, norm1_g[l]), mod_c[0], mod_c[1])
        sl = split_cols(hl @ w_in[l], IN_SPLITS)
        sc = split_cols(hc @ w_in[l], IN_SPLITS)
        gla_c, gla_l = gla_branch(sc, sl, p, need_ctx)
        ssm_c, ssm_l = ssm_branch(sc, sl, p, need_ctx)
        rw_c, rw_l = rwkv_branch(sc, sl, p, need_ctx)
        xl = xl + mod_l[2][:, None, :] * merge_branches(sl[9], gla_l, ssm_l, rw_l, p)
        hl = modulate(rms_norm(xl, norm2_g[l]), mod_l[3], mod_l[4])
        xl = xl + mod_l[5][:, None, :] * conv_ffn(hl, ffn_w_up[l], ffn_conv_w[l], ffn_conv_b[l], ffn_w_down[l], rows, GRID_W)
        if need_ctx:
            xc = xc + mod_c[2][:, None, :] * merge_branches(sc[9], gla_c, ssm_c, rw_c, p)
            hc = modulate(rms_norm(xc, norm2_g[l]), mod_c[3], mod_c[4])
            xc = xc + mod_c[5][:, None, :] * conv_ffn(hc, ffn_w_up[l], ffn_conv_w[l], ffn_conv_b[l], ffn_w_down[l], 1, t_ctx)
    return rms_norm(xl, final_norm_g)
```

```python
import numpy as np
import concourse.bass as bass
import concourse.mybir as mybir
from concourse.bass_utils import run_bass_kernel_spmd
from contextlib import ExitStack

F32 = mybir.dt.float32
BF16 = mybir.dt.bfloat16
ALU = mybir.AluOpType
AF = mybir.ActivationFunctionType
AX = mybir.AxisListType
NDS = 12

NT = 2304
TC = 256
DM = 2048
NIN = 29472
NPAD = 29568
TT = [(0, 512), (512, 512), (1024, 512), (1536, 512), (2048, 256)]
SEGS = [(0, 256), (256, 2304)]
FH = 5632

O_Q, O_K, O_V, O_G, O_AD, O_Z, O_XBC, O_DT, O_RW, O_GT = 0, 1024, 2048, 4096, 6144, 6176, 10272, 16416, 16544, 23328

WSHAPES = {
    "w_mod": [2, 2048, 12288], "b_mod": [2, 12288], "norm1_g": [2, 2048], "norm2_g": [2, 2048],
    "w_in": [2, 2048, 29472], "gla_wa_up": [2, 2, 16, 1024], "gla_ba": [2, 2, 1024], "gla_norm_g": [2, 512],
    "ssm_conv_w": [2, 3, 6144], "ssm_conv_b": [2, 6144], "ssm_a_log": [2, 2, 64], "ssm_dt_bias": [2, 2, 64],
    "ssm_d": [2, 64], "ssm_norm_g": [2, 4096], "rwkv_mu": [2, 6784], "rwkv_w0": [2, 2, 2048],
    "rwkv_w_up": [2, 2, 96, 2048], "rwkv_a0": [2, 2, 2048], "rwkv_a_up": [2, 2, 96, 2048],
    "rwkv_g_up": [2, 256, 2048], "rwkv_k_k": [2, 2048], "rwkv_k_a": [2, 2048], "rwkv_r_k": [2, 32, 64],
    "rwkv_ln_w": [2, 2048], "rwkv_ln_b": [2, 2048], "w_br_gla": [2, 2048, 2048], "w_br_ssm": [2, 4096, 2048],
    "w_br_rwkv": [2, 2048, 2048], "w_out": [2, 2048, 2048], "ffn_w_up": [2, 2048, 11264],
    "ffn_conv_w": [2, 3, 3, 11264], "ffn_conv_b": [2, 11264], "ffn_w_down": [2, 5632, 2048],
    "final_norm_g": [2048],
}


class Buf:
    __slots__ = ("w", "r")

    def __init__(self):
        self.w = None
        self.r = {}


class Rec:
    def __getattr__(self, name):
        def f(*a, **k):
            self.call = (name, a, k)
            return self
        return f


class Eng:
    def __init__(self, name, sem):
        self.name, self.sem, self.n, self.ops, self.waited = name, sem, 0, [], {}


class Prog:
    def __init__(self, nc, es):
        self.nc = nc
        self.E = {}
        for name in ("pe", "act", "dve", "pool", "sp"):
            self.E[name] = Eng(name, es.enter_context(nc.semaphore(name + "_sem")))
        self.dq = {}
        self.dqi = {}
        for q in ("sp", "pool", "act"):
            self.dq[q] = [[es.enter_context(nc.semaphore(f"d{q}{i}")), 0] for i in range(NDS)]
            self.dqi[q] = 0
        self.bufs = {}

    def B(self, key):
        b = self.bufs.get(key)
        if b is None:
            b = self.bufs[key] = Buf()
        return b

    def _wait(self, eng, tok):
        sem, val = tok
        k = id(sem)
        if eng.waited.get(k, 0) >= val:
            return
        eng.waited[k] = val
        eng.ops.append(("w", sem, val))

    def _deps(self, eng, reads, writes):
        own = id(eng.sem)
        skip_own = eng.name in ("pe", "sp")
        for b in reads:
            if b.w is not None and not (skip_own and id(b.w[0]) == own):
                self._wait(eng, b.w)
        for b in writes:
            if b.w is not None and not (skip_own and id(b.w[0]) == own):
                self._wait(eng, b.w)
            for t in b.r.values():
                if not (skip_own and id(t[0]) == own):
                    self._wait(eng, t)

    def _commit(self, tok, reads, writes):
        k = id(tok[0])
        for b in reads:
            b.r[k] = tok
        for b in writes:
            b.w = tok
            b.r = {}

    def op(self, en, fn, reads=(), writes=()):
        eng = self.E[en]
        writes = [getattr(r, "b", r) for r in writes] + [r.b for r in reads if getattr(r, "psum", False)]
        reads = [getattr(r, "b", r) for r in reads if not getattr(r, "psum", False)]
        self._deps(eng, reads, writes)
        eng.n += 1
        tok = (eng.sem, eng.n)
        rec = Rec()
        fn(rec)
        eng.ops.append(("i", rec.call, eng.sem, 1))
        self._commit(tok, reads, writes)

    def dma(self, q, out, in_, reads=(), writes=(), **kw):
        eng = self.E[q]
        reads = [getattr(r, "b", r) for r in reads]
        writes = [getattr(r, "b", r) for r in writes]
        self._deps(eng, reads, writes)
        slot = self.dq[q][self.dqi[q] % NDS]
        self.dqi[q] += 1
        if slot[1] > 0:
            self._wait(eng, (slot[0], slot[1]))
        slot[1] += 16
        tok = (slot[0], slot[1])
        eng.ops.append(("i", ("dma_start", (), dict(out=out, in_=in_, **kw)), slot[0], 16))
        self._commit(tok, reads, writes)

    def barrier(self):
        toks = [(e.sem, e.n) for e in self.E.values() if e.n > 0]
        for q in self.dq:
            for s in self.dq[q]:
                if s[1] > 0:
                    toks.append((s[0], s[1]))
        for e in self.E.values():
            for t in toks:
                if id(t[0]) == id(e.sem):
                    continue
                self._wait(e, t)
        self.bufs = {}

    def emit(self):
        nc = self.nc
        self.barrier()
        handles = {"pe": "tensor", "act": "scalar", "dve": "vector", "pool": "gpsimd", "sp": "sync"}
        with nc.Block() as block:
            for en, attr in handles.items():
                eng = self.E[en]

                def body(h, eng=eng):
                    for o in eng.ops:
                        if o[0] == "w":
                            h.wait_ge(o[1], o[2])
                        else:
                            getattr(h, o[1][0])(*o[1][1], **o[1][2]).then_inc(o[2], o[3])

                getattr(block, attr)(body)


class Tl:
    _n = 0

    def __init__(self, nc, es, shape, dtype, psum=False, name=None):
        Tl._n += 1
        name = (name or "t") + str(Tl._n)
        self.psum = psum
        self.b = Buf()
        if not psum:
            self.t = es.enter_context(nc.sbuf_tensor(name, list(shape), dtype))
            return
        esz = 2 if dtype == BF16 else 4
        n = 1
        for v in shape[1:]:
            n *= v
        assert n * esz <= 2048
        raw = es.enter_context(nc.psum_tensor(name, [shape[0], 2048 // esz], dtype))
        v = raw[:, :n]
        if len(shape) == 3:
            v = v.rearrange("p (a b) -> p a b", a=shape[1])
        self.t = v

    def __getitem__(self, k):
        return self.t[k]


class K:
    pass


def bc(ap, shape):
    return ap.to_broadcast(list(shape))


def build(NB=2, L=2, stop_after=None, dbg=(), skip=()):
    nc = bass.Bass("TRN2", target_bir_lowering=False)

    def din(name, shape, dt=F32):
        return nc.dram_tensor(name, list(shape), dt, kind="ExternalInput").ap()

    def dscr(name, shape, dt=F32):
        return nc.dram_tensor(name, list(shape), dt, kind="Internal").ap()

    x_in = din("x", [NB, 2048, DM])
    ctx_in = din("ctx", [NB, TC, DM])
    c_in = din("c", [NB, DM])
    cctx_in = din("c_ctx", [1, DM])
    W = {n: din(n, ([L] + s[1:]) if len(s) > 1 else s) for n, s in WSHAPES.items()}
    ident_in = din("ident", [128, 128])
    segmask_in = din("segmask", [128, NT])
    tri_in = din("tri", [2, 64, 64])
    out = nc.dram_tensor("out", [NB, 2048, DM], F32, kind="ExternalOutput").ap()
    dbg_out = {n: nc.dram_tensor("dbg_" + n, list(s), F32, kind="ExternalOutput").ap() for n, s in dbg}

    xT = dscr("xT", [NB, DM, NT])
    PF = {"g": dscr("PFg", [6272, NT]), "s": dscr("PFs", [10368, NT]), "r": dscr("PFr", [6784, NT]), "t": dscr("PFt", [6144, NT])}

    with ExitStack() as es0:
        P = Prog(nc, es0)
        S = K()
        S.nc, S.P, S.W, S.NB = nc, P, W, NB
        S.xT, S.PF = xT, PF
        S.dbg = dbg_out
        S.GOF = dscr("GOF", [NT, 2048])
        S.OG = dscr("OG", [2048, NT], BF16)
        S.CUM = dscr("CUM", [128, NT])
        S.XT = dscr("XT", [NT, 4096], BF16)
        S.BT = dscr("BT", [NT, 1024], BF16)
        S.BCF = dscr("BCF", [2048, NT], BF16)
        S.YF = dscr("YF", [NT, 4096])
        S.YT = dscr("YT", [NT, 4096])
        S.OS = dscr("OS", [4096, NT], BF16)
        CK = [36, 64, 32, 64]
        S.RW = {n_: [dscr(f"RW{n_}{d_}", CK, BF16) for d_ in range(2)] for n_ in ("A", "R", "K", "B", "KT", "BT")}
        S.RW["V"] = dscr("RWV", CK, BF16)
        S.RW["G"] = dscr("RWG", CK)
        S.RW["BON"] = dscr("RWBON", CK)
        S.RW["GAM"] = [dscr(f"RWGAM{d_}", [32, 64, 36]) for d_ in range(2)]
        S.RW["YF"] = dscr("RWYF", [NT, 2048])
        S.ORW = dscr("ORW", [2048, NT], BF16)
        mk_in = din("masks", [6, 64, 64])
        mks = [Tl(nc, es0, [64, 64], F32, name="mk") for _ in range(6)]
        for i_ in range(6):
            P.dma("sp", mks[i_][:], mk_in[i_], writes=[mks[i_]])
        S.str_, S.nstr, S.ntri = mks[0:2], mks[2:4], mks[4:6]
        S.ACTG = dscr("ACTG", [FH, NT], BF16)
        S.ACT = dscr("ACT", [FH, NT], BF16)
        S.segmask = Tl(nc, es0, [128, NT], F32, name="segmask")
        P.dma("sp", S.segmask[:], segmask_in, writes=[S.segmask])
        S.tri = [Tl(nc, es0, [64, 64], F32, name="tri") for _ in range(2)]
        for d_ in range(2):
            P.dma("sp", S.tri[d_][:], tri_in[d_], writes=[S.tri[d_]])
        S.ident = Tl(nc, es0, [128, 128], F32, name="ident")
        S.identb = Tl(nc, es0, [128, 128], BF16, name="identb")
        S.ones = Tl(nc, es0, [128, 128], F32, name="ones")
        P.dma("sp", S.ident[:], ident_in, writes=[S.ident])
        P.op("dve", lambda e: e.tensor_copy(out=S.identb[:], in_=S.ident[:]), [S.ident], [S.identb])
        P.op("dve", lambda e: e.memset(S.ones[:], 1.0), [], [S.ones])
        S.mod = Tl(nc, es0, [128, 96, 3], F32, name="mod")
        S.gs = [Tl(nc, es0, [128, 16, 3], F32, name="gs") for _ in range(2)]
        S.cact = Tl(nc, es0, [128, 16, 3], F32, name="cact")

        phase_init(S, x_in, ctx_in, c_in, cctx_in)
        P.barrier()
        done = False
        for l in range(L):
            if done:
                break
            phase_mod(S, l)
            P.barrier()
            for b in range(NB):
                with ExitStack() as es:
                    hT = Tl(nc, es, [128, 16, NT], BF16, name="hT")
                    phase_norm(S, l, b, 0, hT)
                    P.barrier()
                    if stop_after == "norm1":
                        dump_bf(S, es, hT, "hT")
                        done = True
                        break
                    phase_inproj(S, l, hT)
                    P.barrier()
                if stop_after == "ip":
                    done = True
                    break
                if "gla" not in skip:
                    phase_gla(S, l, b)
                    P.barrier()
                if stop_after == "gla":
                    done = True
                    break
                if "ssd" not in skip:
                    phase_ssd(S, l, b)
                    P.barrier()
                if stop_after == "ssd":
                    done = True
                    break
                if "rwkv" not in skip:
                    phase_rwkv(S, l, b)
                    P.barrier()
                if stop_after == "rwkv":
                    done = True
                    break
                phase_merge(S, l, b)
                P.barrier()
                if stop_after == "merge":
                    done = True
                    break
                with ExitStack() as es:
                    hT = Tl(nc, es, [128, 16, NT], BF16, name="hT2")
                    phase_norm(S, l, b, 1, hT)
                    P.barrier()
                    phase_ffn_up(S, l, hT)
                    P.barrier()
                phase_ffn_down(S, l, b)
                P.barrier()
        if not done:
            for b in range(NB):
                phase_final(S, b, out)
            P.barrier()
        if "xT" in dbg_out:
            P.dma("sp", dbg_out["xT"], S.xT[0], reads=[], writes=[])
        if "RWYF" in dbg_out:
            P.dma("sp", dbg_out["RWYF"], S.RW["YF"], reads=[], writes=[])
        if "ORW" in dbg_out:
            with ExitStack() as es:
                dump_bf2(S, es, S.ORW, "ORW", 16)
        if "YT" in dbg_out:
            P.dma("sp", dbg_out["YT"], S.YT, reads=[], writes=[])
        if "YF" in dbg_out:
            P.dma("sp", dbg_out["YF"], S.YF, reads=[], writes=[])
        if "OS" in dbg_out:
            with ExitStack() as es:
                dump_bf2(S, es, S.OS, "OS", 32)
        if "GOF" in dbg_out:
            P.dma("sp", dbg_out["GOF"], S.GOF, reads=[], writes=[])
        if "OG" in dbg_out:
            with ExitStack() as es:
                dump_bf2(S, es, S.OG, "OG", 16)
        for kk_ in ("g", "s", "r", "t"):
            if "PF" + kk_ in dbg_out:
                P.dma("sp", dbg_out["PF" + kk_], PF[kk_], reads=[], writes=[])
        P.emit()
    return nc


def dump_bf(S, es, t, name):
    nc, P = S.nc, S.P
    st = Tl(nc, es, [128, NT], F32, name="dump")
    for c in range(16):
        P.op("dve", lambda e, c=c: e.tensor_copy(out=st[:], in_=t[:, c, :]), [t], [st])
        P.dma("sp", S.dbg[name][c * 128:(c + 1) * 128, :], st[:], reads=[st], writes=[])


def dump_bf2(S, es, src, name, nch):
    nc, P = S.nc, S.P
    P.barrier()
    sb = Tl(nc, es, [128, NT], BF16, name="dumpb")
    st = Tl(nc, es, [128, NT], F32, name="dump")
    for c in range(nch):
        P.dma("sp", sb[:], src[c * 128:(c + 1) * 128, :], writes=[sb])
        P.op("dve", lambda e: e.tensor_copy(out=st[:], in_=sb[:]), [sb], [st])
        P.dma("sp", S.dbg[name][c * 128:(c + 1) * 128, :], st[:], reads=[st], writes=[])


def consts():
    seg = np.ones((128, NT), np.float32)
    seg[:, ::64] = 0.0
    s = np.arange(64)[:, None]
    q = np.arange(64)[None, :]
    tri = np.stack([(s <= q), (s >= q)]).astype(np.float32)
    eye = np.eye(64, dtype=np.float32)
    strict = tri - eye
    masks = np.concatenate([strict, -strict, -tri], 0)
    return {"ident": np.eye(128, dtype=np.float32), "segmask": seg, "tri": tri, "masks": masks}


def phase_init(S, x_in, ctx_in, c_in, cctx_in):
    nc, P = S.nc, S.P
    with ExitStack() as es:
        xt = [Tl(nc, es, [128, DM], F32, name="xt") for _ in range(2)]
        st = [Tl(nc, es, [128, 16, 128], F32, name="st") for _ in range(2)]
        ps = [Tl(nc, es, [128, 512], F32, psum=True, name="ps") for _ in range(4)]
        k = 0
        n = 0
        for b in range(S.NB):
            for tt in range(18):
                src = ctx_in[b, tt * 128:(tt + 1) * 128, :] if tt < 2 else x_in[b, (tt - 2) * 128:(tt - 1) * 128, :]
                xi = xt[n % 2]
                so = st[n % 2]
                n += 1
                P.dma("sp", xi[:], src, writes=[xi])
                for g in range(4):
                    pt = ps[k % 4]
                    k += 1
                    for j in range(4):
                        c = g * 4 + j
                        P.op("pe", lambda e, pt=pt, xi=xi, c=c, j=j: e.transpose(
                            out=pt[:, j * 128:(j + 1) * 128], in_=xi[:, c * 128:(c + 1) * 128], identity=S.ident[:]),
                            [xi, S.ident], [pt])
                    en = "dve" if g % 2 == 0 else "act"
                    if en == "dve":
                        P.op("dve", lambda e, pt=pt, so=so, g=g: e.tensor_copy(
                            out=so[:, g * 4:(g + 1) * 4, :], in_=pt[:].rearrange("p (a b) -> p a b", a=4)), [pt], [so])
                    else:
                        P.op("act", lambda e, pt=pt, so=so, g=g: e.activation(
                            out=so[:, g * 4:(g + 1) * 4, :], in_=pt[:].rearrange("p (a b) -> p a b", a=4), func=AF.Copy), [pt], [so])
                P.dma("act", S.xT[b, :, tt * 128:(tt + 1) * 128].rearrange("(c p) t -> p c t", p=128), so[:], reads=[so], writes=[])
        craw = Tl(nc, es, [128, 16, 3], F32, name="craw")
        P.op("dve", lambda e: e.memset(craw[:], 0.0), [], [craw])
        for b in range(S.NB):
            P.dma("sp", craw[:, :, b], c_in[b, :].rearrange("(c p) -> p c", p=128), reads=[], writes=[craw], allow_slow_non_contiguous=True)
        P.dma("sp", craw[:, :, 2], cctx_in[0, :].rearrange("(c p) -> p c", p=128), reads=[], writes=[craw], allow_slow_non_contiguous=True)
        P.op("act", lambda e: e.activation(out=S.cact[:], in_=craw[:], func=AF.Silu), [craw], [S.cact])


def phase_mod(S, l):
    nc, P, W = S.nc, S.P, S.W
    with ExitStack() as es:
        wt = [Tl(nc, es, [128, 16, 512], F32, name="wm") for _ in range(2)]
        ps = [Tl(nc, es, [128, 4, 3], F32, psum=True, name="psm") for _ in range(2)]
        bm = Tl(nc, es, [128, 96], F32, name="bm")
        prm = Tl(nc, es, [128, 2, 16], F32, name="prm")
        P.dma("sp", bm[:], W["b_mod"][l, :].rearrange("(c p) -> p c", p=128), writes=[bm], allow_slow_non_contiguous=True)
        P.dma("sp", prm[:, 0, :], W["norm1_g"][l, :].rearrange("(c p) -> p c", p=128), writes=[prm], allow_slow_non_contiguous=True)
        P.dma("sp", prm[:, 1, :], W["norm2_g"][l, :].rearrange("(c p) -> p c", p=128), writes=[prm], allow_slow_non_contiguous=True)
        for blk in range(24):
            w = wt[blk % 2]
            pt = ps[blk % 2]
            P.dma("sp" if blk % 2 == 0 else "act", w[:], W["w_mod"][l, :, blk * 512:(blk + 1) * 512].rearrange("(kc p) n -> p kc n", p=128), writes=[w])
            for j in range(4):
                for kc in range(16):
                    P.op("pe", lambda e, w=w, pt=pt, j=j, kc=kc: e.matmul(
                        out=pt[:, j, :], lhsT=w[:, kc, j * 128:(j + 1) * 128], rhs=S.cact[:, kc, :], start=(kc == 0), stop=(kc == 15)),
                        [w, S.cact], [pt])
            P.op("dve", lambda e, pt=pt, blk=blk: e.tensor_tensor(
                out=S.mod[:, blk * 4:(blk + 1) * 4, :], in0=pt[:], in1=bc(bm[:, blk * 4:(blk + 1) * 4].unsqueeze(2), [128, 4, 3]), op=ALU.add),
                [pt, bm], [S.mod])
        for i in range(2):
            sc = S.mod[:, (3 * i + 1) * 16:(3 * i + 2) * 16, :]
            P.op("dve", lambda e, i=i, sc=sc: e.scalar_tensor_tensor(
                out=S.gs[i][:], in0=sc, scalar=1.0, in1=bc(prm[:, i, :].unsqueeze(2), [128, 16, 3]), op0=ALU.add, op1=ALU.mult),
                [S.mod, prm], [S.gs[i]])


def phase_norm(S, l, b, which, hT):
    nc, P = S.nc, S.P
    with ExitStack() as es:
        xc = [Tl(nc, es, [128, NT], F32, name="xc") for _ in range(3)]
        sq = [Tl(nc, es, [128, NT], F32, name="sq") for _ in range(2)]
        ps = [Tl(nc, es, [128, 512], F32, psum=True, name="psn") for _ in range(5)]
        rstd = Tl(nc, es, [128, NT], F32, name="rstd")
        epsb = Tl(nc, es, [128, 1], F32, name="epsb")
        P.op("dve", lambda e: e.memset(epsb[:], 1e-6), [], [epsb])
        for c in range(16):
            xi, si = xc[c % 3], sq[c % 2]
            P.dma("sp" if c % 2 == 0 else "act", xi[:], S.xT[b, c * 128:(c + 1) * 128, :], writes=[xi])
            if c % 2 == 0:
                P.op("act", lambda e, xi=xi, si=si: e.activation(out=si[:], in_=xi[:], func=AF.Square), [xi], [si])
            else:
                P.op("dve", lambda e, xi=xi, si=si: e.tensor_tensor(out=si[:], in0=xi[:], in1=xi[:], op=ALU.mult), [xi], [si])
            for ti, (t0, tn) in enumerate(TT):
                P.op("pe", lambda e, si=si, ti=ti, t0=t0, tn=tn, c=c: e.matmul(
                    out=ps[ti][:, :tn], lhsT=S.ones[:], rhs=si[:, t0:t0 + tn], start=(c == 0), stop=(c == 15)),
                    [si, S.ones], [ps[ti]])
        for ti, (t0, tn) in enumerate(TT):
            P.op("act", lambda e, ti=ti, t0=t0, tn=tn: e.activation(
                out=rstd[:, t0:t0 + tn], in_=ps[ti][:, :tn], func=AF.Sqrt, bias=epsb[:], scale=1.0 / DM), [ps[ti], epsb], [rstd])
        P.op("dve", lambda e: e.reciprocal(out=rstd[:], in_=rstd[:]), [rstd], [rstd])
        shi = 3 * which
        for c in range(16):
            xi, si = xc[c % 3], sq[c % 2]
            P.dma("sp" if c % 2 == 0 else "act", xi[:], S.xT[b, c * 128:(c + 1) * 128, :], writes=[xi])
            for (s0, s1), v in zip(SEGS, (2, b)):
                P.op("dve", lambda e, xi=xi, si=si, s0=s0, s1=s1, v=v, c=c: e.scalar_tensor_tensor(
                    out=si[:, s0:s1], in0=xi[:, s0:s1], scalar=S.gs[which][:, c, v:v + 1], in1=rstd[:, s0:s1], op0=ALU.mult, op1=ALU.mult),
                    [xi, rstd, S.gs[which]], [si])
                P.op("act", lambda e, si=si, s0=s0, s1=s1, v=v, c=c: e.activation(
                    out=hT[:, c, s0:s1], in_=si[:, s0:s1], func=AF.Identity, bias=S.mod[:, shi * 16 + c, v:v + 1], scale=1.0),
                    [si, S.mod], [hT])


def proj(S, es, w_ap, ncols, rhs, KC, evac, wq="pool", blk=512, nps=7):
    nc, P = S.nc, S.P
    wt = [Tl(nc, es, [128, KC, blk], BF16, name="wp") for _ in range(2)]
    ps = [Tl(nc, es, [128, 512], F32, psum=True, name="psp") for _ in range(nps)]
    k = 0
    nblk = (ncols + blk - 1) // blk
    for bi in range(nblk):
        w = wt[bi % 2]
        c0 = bi * blk
        cn = min(blk, ncols - c0)
        P.dma(wq, w[:, :, :cn], w_ap[:, c0:c0 + cn].rearrange("(kc p) n -> p kc n", p=128), writes=[w])
        for j in range((cn + 127) // 128):
            n = min(128, cn - j * 128)
            tiles = []
            for (t0, tn) in TT:
                pt = ps[k % nps]
                k += 1
                for kc in range(KC):
                    P.op("pe", lambda e, pt=pt, w=w, j=j, n=n, kc=kc, t0=t0, tn=tn: e.matmul(
                        out=pt[:n, :tn], lhsT=w[:, kc, j * 128:j * 128 + n], rhs=rhs[:, kc, t0:t0 + tn], start=(kc == 0), stop=(kc == KC - 1)),
                        [w, rhs], [pt])
                tiles.append((pt, t0, tn))
            evac(bi * (blk // 128) + j, n, tiles)


def phase_inproj(S, l, hT):
    nc, P = S.nc, S.P
    with ExitStack() as es:
        st = [Tl(nc, es, [128, NT], F32, name="ipst") for _ in range(3)]
        cnt = [0]

        def evac(ci, n, tiles, key):
            so = st[cnt[0] % 3]
            cnt[0] += 1
            for i, (pt, t0, tn) in enumerate(tiles):
                if i % 2 == 0:
                    P.op("dve", lambda e, pt=pt, t0=t0, tn=tn, so=so, n=n: e.tensor_copy(out=so[:n, t0:t0 + tn], in_=pt[:n, :tn]), [pt], [so])
                else:
                    P.op("act", lambda e, pt=pt, t0=t0, tn=tn, so=so, n=n: e.activation(out=so[:n, t0:t0 + tn], in_=pt[:n, :tn], func=AF.Copy), [pt], [so])
            P.dma("sp" if ci % 2 == 0 else "act", S.PF[key][ci * 128:ci * 128 + n, :], so[:n, :], reads=[so], writes=[])

        for key, c0, cn in (("g", 0, 6176), ("s", 6176, 10368), ("r", 16544, 6784), ("t", 23328, 6144)):
            with ExitStack() as es2:
                proj(S, es2, S.W["w_in"][l][:, c0:c0 + cn], cn, hT, 16, lambda ci, n, tiles, key=key: evac(ci, n, tiles, key))
            P.barrier()


def chunk_order(rev):
    return list(range(36)) if not rev else [3, 2, 1, 0] + list(range(35, 3, -1))


def seg_cumsum(S, out, src, tot, tmp, rev):
    P = S.P
    P.op("dve", lambda e: e.tensor_tensor_scan(out=out[:], data0=S.segmask[:out.t.shape[0], :], data1=src[:], initial=0.0, op0=ALU.mult, op1=ALU.add),
         [src, S.segmask], [out])
    P.op("dve", lambda e: e.tensor_copy(out=tot[:], in_=out[:].rearrange("p (c q) -> p c q", q=64)[:, :, 63]), [out], [tot])
    if rev:
        P.op("dve", lambda e: e.tensor_tensor(out=tmp[:], in0=src[:], in1=out[:], op=ALU.subtract), [src, out], [tmp])
        P.op("dve", lambda e: e.tensor_tensor(out=out[:].rearrange("p (c q) -> p c q", q=64), in0=tmp[:].rearrange("p (c q) -> p c q", q=64),
                                              in1=bc(tot[:].unsqueeze(2), [out.t.shape[0], 36, 64]), op=ALU.add), [tmp, tot], [out])


def phase_gla(S, l, b):
    nc, P, W = S.nc, S.P, S.W
    PFg = S.PF["g"]
    with ExitStack() as es:
        adT = [Tl(nc, es, [17, NT], F32, name="adT") for _ in range(2)]
        waA = [Tl(nc, es, [17, 1024], F32, name="waA") for _ in range(2)]
        ng = Tl(nc, es, [128, 4], F32, name="ng")
        vT = Tl(nc, es, [64, 36, 512], BF16, name="vT")
        sg = Tl(nc, es, [128, 4, NT], BF16, name="sg")
        ogh = Tl(nc, es, [128, 4, NT], BF16, name="ogh")
        qd = Tl(nc, es, [128, 2, NT], BF16, name="qd")
        ki = Tl(nc, es, [128, 2, NT], BF16, name="ki")
        kdT = Tl(nc, es, [128, 2, NT], BF16, name="kdT")
        etot = Tl(nc, es, [128, 2, 36], F32, name="etot")
        Sf = Tl(nc, es, [128, 2, 512], F32, name="Sf")
        Sb = Tl(nc, es, [128, 2, 512], BF16, name="Sb")
        eps = Tl(nc, es, [128, 1], F32, name="epsg")
        P.op("dve", lambda e: e.memset(eps[:], 1e-6), [], [eps])
        P.dma("sp", ng[:], W["gla_norm_g"][l, :].rearrange("(c p) -> p c", p=128), writes=[ng], allow_slow_non_contiguous=True)
        for d in range(2):
            P.op("dve", lambda e, d=d: e.memset(adT[d][:], 1.0), [], [adT[d]])
            P.dma("sp", adT[d][0:16, :], PFg[O_AD + d * 16:O_AD + d * 16 + 16, :], writes=[adT[d]])
            P.dma("sp", waA[d][0:16, :], W["gla_wa_up"][l, d], writes=[waA[d]])
            P.dma("sp", waA[d][16:17, :], W["gla_ba"][l, d:d + 1, :], writes=[waA[d]])
        for h in range(4):
            with ExitStack() as es1:
                ld = [Tl(nc, es1, [128, NT], F32, name="gld") for _ in range(2)]
                pv = [Tl(nc, es1, [64, 4, 128], F32, psum=True, name="pv") for _ in range(2)]
                for j in range(4):
                    P.dma("sp", ld[j % 2][:], PFg[O_G + h * 512 + j * 128:O_G + h * 512 + (j + 1) * 128, :], writes=[ld[j % 2]])
                    P.op("act", lambda e, j=j: e.activation(out=sg[:, j, :], in_=ld[j % 2][:], func=AF.Silu), [ld[j % 2]], [sg])
                for j in range(4):
                    vl = ld[j % 2]
                    P.dma("act", vl[:], PFg[O_V + h * 512 + j * 128:O_V + h * 512 + (j + 1) * 128, :], writes=[vl])
                    for cg in range(9):
                        pt = pv[cg % 2]
                        for a in range(4):
                            ck = cg * 4 + a
                            P.op("pe", lambda e, pt=pt, a=a, ck=ck, vl=vl: e.transpose(out=pt[:, a, :], in_=vl[:, ck * 64:(ck + 1) * 64], identity=S.ident[:]),
                                 [vl, S.ident], [pt])
                        if cg % 2 == 0:
                            P.op("dve", lambda e, pt=pt, cg=cg, j=j: e.tensor_copy(out=vT[:, cg * 4:cg * 4 + 4, j * 128:(j + 1) * 128], in_=pt[:]), [pt], [vT])
                        else:
                            P.op("act", lambda e, pt=pt, cg=cg, j=j: e.activation(out=vT[:, cg * 4:cg * 4 + 4, j * 128:(j + 1) * 128], in_=pt[:], func=AF.Copy), [pt], [vT])
            P.barrier()
            for d in range(2):
                rev = d == 1
                with ExitStack() as es1:
                    pl = [Tl(nc, es1, [128, 512], F32, psum=True, name="pl") for _ in range(5)]
                    nl = Tl(nc, es1, [128, NT], F32, name="nl")
                    Bc = Tl(nc, es1, [128, NT], F32, name="Bc")
                    tmp = Tl(nc, es1, [128, NT], F32, name="tmp")
                    ex = Tl(nc, es1, [128, NT], F32, name="ex")
                    qf = Tl(nc, es1, [128, NT], F32, name="qf")
                    kf = Tl(nc, es1, [128, NT], F32, name="kf")
                    tot = Tl(nc, es1, [128, 36], F32, name="tot")
                    for dc in range(2):
                        dg = h * 256 + dc * 128
                        P.dma("sp", qf[:], PFg[O_Q + dg:O_Q + dg + 128, :], writes=[qf])
                        P.dma("act", kf[:], PFg[O_K + dg:O_K + dg + 128, :], writes=[kf])
                        for ti, (t0, tn) in enumerate(TT):
                            P.op("pe", lambda e, ti=ti, t0=t0, tn=tn, dg=dg: e.matmul(
                                out=pl[ti][:, :tn], lhsT=waA[d][:, dg:dg + 128], rhs=adT[d][:, t0:t0 + tn], start=True, stop=True),
                                [waA[d], adT[d]], [pl[ti]])
                            P.op("act", lambda e, ti=ti, t0=t0, tn=tn: e.activation(out=nl[:, t0:t0 + tn], in_=pl[ti][:, :tn], func=AF.Exp, scale=-1.0),
                                 [pl[ti]], [nl])
                        P.op("act", lambda e: e.activation(out=nl[:], in_=nl[:], func=AF.Ln, bias=S.ones[:, 0:1], scale=1.0), [nl, S.ones], [nl])
                        seg_cumsum(S, Bc, nl, tot, tmp, rev)
                        P.op("act", lambda e, dc=dc: e.activation(out=etot[:, dc, :], in_=tot[:], func=AF.Exp, scale=-1.0 / 16), [tot], [etot])
                        P.op("act", lambda e: e.activation(out=ex[:], in_=Bc[:], func=AF.Exp, scale=-1.0 / 16), [Bc], [ex])
                        P.op("dve", lambda e, dc=dc: e.scalar_tensor_tensor(out=qd[:, dc, :], in0=qf[:], scalar=1.0 / 16, in1=ex[:], op0=ALU.mult, op1=ALU.mult),
                             [qf, ex], [qd])
                        P.op("act", lambda e: e.activation(out=ex[:], in_=Bc[:], func=AF.Exp, scale=1.0 / 16), [Bc], [ex])
                        P.op("dve", lambda e, dc=dc: e.tensor_tensor(out=ki[:, dc, :], in0=kf[:], in1=ex[:], op=ALU.mult), [kf, ex], [ki])
                        P.op("dve", lambda e: e.tensor_tensor(out=tmp[:].rearrange("p (c q) -> p c q", q=64), in0=Bc[:].rearrange("p (c q) -> p c q", q=64),
                                                              in1=bc(tot[:].unsqueeze(2), [128, 36, 64]), op=ALU.subtract), [Bc, tot], [tmp])
                        P.op("act", lambda e: e.activation(out=ex[:], in_=tmp[:], func=AF.Exp, scale=1.0 / 16), [tmp], [ex])
                        P.op("dve", lambda e, dc=dc: e.tensor_tensor(out=kdT[:, dc, :], in0=kf[:], in1=ex[:], op=ALU.mult), [kf, ex], [kdT])
                P.barrier()
                with ExitStack() as es1:
                    pa = Tl(nc, es1, [64, 64], F32, psum=True, name="pa")
                    pk = Tl(nc, es1, [64, 2, 128], BF16, psum=True, name="pk")
                    po = [Tl(nc, es1, [64, 512], F32, psum=True, name="po") for _ in range(2)]
                    pu = [Tl(nc, es1, [128, 512], F32, psum=True, name="pu") for _ in range(2)]
                    ptr = Tl(nc, es1, [128, 4, 64], F32, psum=True, name="ptr")
                    att = [Tl(nc, es1, [64, 64], BF16, name="att") for _ in range(2)]
                    kd = [Tl(nc, es1, [64, 256], BF16, name="kd") for _ in range(2)]
                    ost = [Tl(nc, es1, [64, 512], F32, name="ost") for _ in range(2)]
                    oft = [Tl(nc, es1, [64, 512], F32, name="oft") for _ in range(2)]
                    sqt = Tl(nc, es1, [64, 512], F32, name="sqt")
                    ss = Tl(nc, es1, [64, 1], F32, name="ss")
                    P.op("dve", lambda e: e.memset(Sf[:], 0.0), [], [Sf])
                    P.op("dve", lambda e: e.memset(Sb[:], 0.0), [], [Sb])
                    order = chunk_order(rev)

                    def GA(i):
                        ck = order[i]
                        ts = slice(ck * 64, ck * 64 + 64)
                        a_, k_, f_ = att[i % 2], kd[i % 2], oft[i % 2]
                        for dc in range(2):
                            P.op("pe", lambda e: e.matmul(out=pa[:], lhsT=ki[:, dc, ts], rhs=qd[:, dc, ts], start=(dc == 0), stop=(dc == 1)), [ki, qd], [pa])
                        P.op("dve", lambda e: e.tensor_tensor(out=a_[:], in0=pa[:], in1=S.tri[d][:], op=ALU.mult), [pa, S.tri[d]], [a_])
                        for dc in range(2):
                            P.op("pe", lambda e: e.transpose(out=pk[:, dc, :], in_=kdT[:, dc, ts], identity=S.identb[:]), [kdT, S.identb], [pk])
                        P.op("act", lambda e: e.activation(out=k_[:], in_=pk[:].rearrange("p a b -> p (a b)"), func=AF.Copy), [pk], [k_])
                        if rev:
                            P.dma("sp", f_[:], S.GOF[ts, h * 512:(h + 1) * 512], writes=[f_])

                    def GB(i):
                        ck = order[i]
                        ts = slice(ck * 64, ck * 64 + 64)
                        a_, k_, o_, f_, p_ = att[i % 2], kd[i % 2], ost[i % 2], oft[i % 2], po[i % 2]
                        for dc in range(2):
                            P.op("pe", lambda e: e.matmul(out=p_[:], lhsT=qd[:, dc, ts], rhs=Sb[:, dc, :], start=(dc == 0), stop=False), [qd, Sb], [p_])
                        P.op("pe", lambda e: e.matmul(out=p_[:], lhsT=a_[:], rhs=vT[:, ck, :], start=False, stop=True), [a_, vT], [p_])
                        for dc in range(2):
                            P.op("pe", lambda e: e.matmul(out=pu[dc][:], lhsT=k_[:, dc * 128:(dc + 1) * 128], rhs=vT[:, ck, :], start=True, stop=True), [k_, vT], [pu[dc]])
                            P.op("dve", lambda e: e.scalar_tensor_tensor(out=Sf[:, dc, :], in0=Sf[:, dc, :], scalar=etot[:, dc, ck:ck + 1], in1=pu[dc][:],
                                                                         op0=ALU.mult, op1=ALU.add), [Sf, etot, pu[dc]], [Sf])
                            P.op("act", lambda e: e.activation(out=Sb[:, dc, :], in_=Sf[:, dc, :], func=AF.Copy), [Sf], [Sb])
                        gof = S.GOF[ts, h * 512:(h + 1) * 512]
                        if not rev:
                            P.op("act", lambda e: e.activation(out=o_[:], in_=p_[:], func=AF.Copy), [p_], [o_])
                            P.dma("sp", gof, o_[:], reads=[o_], writes=[])
                        else:
                            P.op("dve", lambda e: e.tensor_tensor(out=o_[:], in0=p_[:], in1=f_[:], op=ALU.add), [p_, f_], [o_])
                            P.op("act", lambda e: e.activation(out=sqt[:], in_=o_[:], func=AF.Square), [o_], [sqt])
                            P.op("dve", lambda e: e.reduce_sum(out=ss[:], in_=sqt[:], axis=AX.X), [sqt], [ss])
                            P.op("act", lambda e: e.activation(out=ss[:], in_=ss[:], func=AF.Sqrt, bias=eps[:64, :], scale=1.0 / 512), [ss, eps], [ss])
                            P.op("dve", lambda e: e.reciprocal(out=ss[:], in_=ss[:]), [ss], [ss])
                            P.op("pool", lambda e: e.tensor_scalar(out=o_[:], in0=o_[:], scalar1=ss[:, 0:1], scalar2=None, op0=ALU.mult), [o_, ss], [o_])
                            for j in range(4):
                                P.op("pe", lambda e: e.transpose(out=ptr[:, j, :], in_=o_[:, j * 128:(j + 1) * 128], identity=S.ident[:64, :64]), [o_, S.ident], [ptr])
                            for j in range(4):
                                P.op("dve", lambda e: e.scalar_tensor_tensor(out=ogh[:, j, ts], in0=ptr[:, j, :], scalar=ng[:, j:j + 1], in1=sg[:, j, ts],
                                                                             op0=ALU.mult, op1=ALU.mult), [ptr, ng, sg], [ogh])

                    GA(0)
                    for i in range(36):
                        if i + 1 < 36:
                            GA(i + 1)
                        GB(i)
                P.barrier()
            for j in range(4):
                P.dma("sp", S.OG[h * 512 + j * 128:h * 512 + (j + 1) * 128, :], ogh[:, j, :], reads=[ogh], writes=[])
            P.barrier()


def vec_pp(S, es, src_ap, nchunk, name):
    t = Tl(S.nc, es, [128, nchunk], F32, name=name)
    S.P.dma("sp", t[:], src_ap.rearrange("(c p) -> p c", p=128), writes=[t], allow_slow_non_contiguous=True)
    return t


def phase_ssd(S, l, b):
    nc, P, W = S.nc, S.P, S.W
    PFs = S.PF["s"]
    OX, OB, OC, ODT = 4096, 8192, 9216, 10240
    with ExitStack() as es:
        dtT = Tl(nc, es, [64, 36, 128], F32, name="dtT")
        cumT = Tl(nc, es, [64, 36, 128], F32, name="cumT")
        laT = Tl(nc, es, [64, 36, 128], F32, name="laT")
        Dbc = Tl(nc, es, [64, 64], F32, name="Dbc")
        P.dma("sp", Dbc[:], bc(W["ssm_d"][l:l + 1, :], [64, 64]), writes=[Dbc])
        with ExitStack() as es1:
            raw = Tl(nc, es1, [128, NT], F32, name="raw")
            la = Tl(nc, es1, [128, NT], F32, name="la")
            cum = Tl(nc, es1, [128, NT], F32, name="cum")
            tmp = Tl(nc, es1, [128, NT], F32, name="tmp")
            tot = Tl(nc, es1, [128, 36], F32, name="tot")
            pp = Tl(nc, es1, [128, 2], F32, name="pp")
            pt = [Tl(nc, es1, [64, 4, 128], F32, psum=True, name="pts") for _ in range(2)]
            P.dma("sp", raw[:], PFs[ODT:ODT + 128, :], writes=[raw])
            P.dma("sp", pp[:, 0:1], W["ssm_dt_bias"][l].rearrange("d (h o) -> (d h) o", o=1), writes=[pp])
            P.dma("sp", pp[:, 1:2], W["ssm_a_log"][l].rearrange("d (h o) -> (d h) o", o=1), writes=[pp])
            P.op("act", lambda e: e.activation(out=pp[:, 1:2], in_=pp[:, 1:2], func=AF.Exp), [pp], [pp])
            P.op("dve", lambda e: e.tensor_scalar(out=pp[:, 1:2], in0=pp[:, 1:2], scalar1=-1.0, scalar2=None, op0=ALU.mult), [pp], [pp])
            P.op("act", lambda e: e.activation(out=raw[:], in_=raw[:], func=AF.Exp, bias=pp[:, 0:1], scale=1.0), [raw, pp], [raw])
            P.op("act", lambda e: e.activation(out=raw[:], in_=raw[:], func=AF.Ln, bias=S.ones[:, 0:1], scale=1.0), [raw, S.ones], [raw])
            P.op("dve", lambda e: e.tensor_scalar(out=la[:], in0=raw[:], scalar1=pp[:, 1:2], scalar2=None, op0=ALU.mult), [raw, pp], [la])
            P.op("dve", lambda e: e.tensor_tensor_scan(out=cum[:], data0=S.segmask[:], data1=la[:], initial=0.0, op0=ALU.mult, op1=ALU.add), [la, S.segmask], [cum])
            P.op("dve", lambda e: e.tensor_copy(out=tot[:], in_=cum[:].rearrange("p (c q) -> p c q", q=64)[:, :, 63]), [cum], [tot])
            P.op("dve", lambda e: e.tensor_tensor(out=tmp[64:128, :], in0=la[64:128, :], in1=cum[64:128, :], op=ALU.subtract), [la, cum], [tmp])
            P.op("dve", lambda e: e.tensor_tensor(out=cum[64:128, :].rearrange("p (c q) -> p c q", q=64), in0=tmp[64:128, :].rearrange("p (c q) -> p c q", q=64),
                                                  in1=bc(tot[64:128, :].unsqueeze(2), [64, 36, 64]), op=ALU.add), [tmp, tot], [cum])
            P.dma("sp", S.CUM, cum[:], reads=[cum], writes=[])
            k = 0
            for src, dst in ((raw, dtT), (cum, cumT), (la, laT)):
                for cg in range(9):
                    p_ = pt[k % 2]
                    k += 1
                    for a in range(4):
                        ck = cg * 4 + a
                        P.op("pe", lambda e: e.transpose(out=p_[:, a, :], in_=src[:, ck * 64:(ck + 1) * 64], identity=S.ident[:]), [src, S.ident], [p_])
                    P.op("act", lambda e: e.activation(out=dst[:, cg * 4:cg * 4 + 4, :], in_=p_[:], func=AF.Copy), [p_], [dst])
        P.barrier()
        with ExitStack() as es1:
            cw = [vec_pp(S, es1, W["ssm_conv_w"][l, j, :], 48, "cw") for j in range(3)]
            cb = vec_pp(S, es1, W["ssm_conv_b"][l, :], 48, "cb")
            xin = [Tl(nc, es1, [128, NT], F32, name="xin") for _ in range(2)]
            yc = [Tl(nc, es1, [128, NT], F32, name="yc") for _ in range(2)]
            yb = [Tl(nc, es1, [128, NT], BF16, name="yb") for _ in range(2)]
            stg = [Tl(nc, es1, [128, 18, 128], BF16, name="stg") for _ in range(2)]
            pt = [Tl(nc, es1, [128, 4, 128], BF16, psum=True, name="ptc") for _ in range(2)]
            k = 0
            for c in range(48):
                xi, yi, ybi, sg_ = xin[c % 2], yc[c % 2], yb[c % 2], stg[c % 2]
                P.dma("sp" if c % 2 == 0 else "act", xi[:], PFs[OX + c * 128:OX + (c + 1) * 128, :], writes=[xi])
                P.op("dve", lambda e: e.tensor_scalar(out=yi[:], in0=xi[:], scalar1=cw[1][:, c:c + 1], scalar2=None, op0=ALU.mult), [xi, cw[1]], [yi])
                for (s0, s1) in SEGS:
                    P.op("dve", lambda e: e.scalar_tensor_tensor(out=yi[:, s0 + 1:s1], in0=xi[:, s0:s1 - 1], scalar=cw[0][:, c:c + 1], in1=yi[:, s0 + 1:s1],
                                                                 op0=ALU.mult, op1=ALU.add), [xi, cw[0], yi], [yi])
                    P.op("dve", lambda e: e.scalar_tensor_tensor(out=yi[:, s0:s1 - 1], in0=xi[:, s0 + 1:s1], scalar=cw[2][:, c:c + 1], in1=yi[:, s0:s1 - 1],
                                                                 op0=ALU.mult, op1=ALU.add), [xi, cw[2], yi], [yi])
                P.op("act", lambda e: e.activation(out=ybi[:], in_=yi[:], func=AF.Silu, bias=cb[:, c:c + 1], scale=1.0), [yi, cb], [ybi])
                if c >= 32:
                    P.dma("sp", S.BCF[(c - 32) * 128:(c - 31) * 128, :], ybi[:], reads=[ybi], writes=[])
                if c < 40:
                    for tg in range(5):
                        p_ = pt[k % 2]
                        k += 1
                        na = 4 if tg < 4 else 2
                        for a in range(na):
                            tt = tg * 4 + a
                            P.op("pe", lambda e: e.transpose(out=p_[:, a, :], in_=ybi[:, tt * 128:(tt + 1) * 128], identity=S.identb[:]), [ybi, S.identb], [p_])
                        if tg % 2 == 0:
                            P.op("dve", lambda e: e.tensor_copy(out=sg_[:, tg * 4:tg * 4 + na, :], in_=p_[:, :na, :]), [p_], [sg_])
                        else:
                            P.op("act", lambda e: e.activation(out=sg_[:, tg * 4:tg * 4 + na, :], in_=p_[:, :na, :], func=AF.Copy), [p_], [sg_])
                    dst = S.XT[:, c * 128:(c + 1) * 128] if c < 32 else S.BT[:, (c - 32) * 128:(c - 31) * 128]
                    P.dma("act", dst.rearrange("(tt p) c -> p tt c", p=128), sg_[:], reads=[sg_], writes=[])
        P.barrier()
        for g in range(8):
            with ExitStack() as es1:
                BF = Tl(nc, es1, [128, NT], BF16, name="BF")
                CF = Tl(nc, es1, [128, NT], BF16, name="CF")
                P.dma("sp", BF[:], S.BCF[g * 128:(g + 1) * 128, :], writes=[BF])
                P.dma("sp", CF[:], S.BCF[1024 + g * 128:1024 + (g + 1) * 128, :], writes=[CF])
                def two(shape, dt, name, psum=False):
                    return [[Tl(nc, es1, shape, dt, psum=psum, name=name) for _ in range(2)] for _ in range(2)]
                hf = [Tl(nc, es1, [128, 512], F32, name="hf") for _ in range(2)]
                hb = [Tl(nc, es1, [128, 512], BF16, name="hb") for _ in range(2)]
                xc, bt, cbc, wd = two([64, 512], BF16, "xc"), two([64, 128], BF16, "bt"), two([64, 8, 64], F32, "cbc"), two([64, 8, 64], BF16, "wd")
                cbm, xd, ys, yf = two([64, 64], F32, "cbm"), two([64, 512], BF16, "xd"), two([64, 512], F32, "ys"), two([64, 512], F32, "yf")
                ec, sd, etb = two([64, 8], F32, "ec"), two([64, 8], F32, "sd"), two([128, 8], F32, "etb")
                pcb = Tl(nc, es1, [64, 64], F32, psum=True, name="pcb")
                py1 = [Tl(nc, es1, [64, 512], F32, psum=True, name="py1") for _ in range(2)]
                py2 = [Tl(nc, es1, [64, 512], F32, psum=True, name="py2") for _ in range(2)]
                pst = Tl(nc, es1, [128, 512], F32, psum=True, name="pst")
                ptb = Tl(nc, es1, [128, 8], F32, psum=True, name="ptb")
                orders = [chunk_order(False), chunk_order(True)]
                pos = [{c: i for i, c in enumerate(o)} for o in orders]
                for d in range(2):
                    P.op("dve", lambda e: e.memset(hf[d][:], 0.0), [], [hf[d]])
                    P.op("dve", lambda e: e.memset(hb[d][:], 0.0), [], [hb[d]])

                def A1(d, i):
                    ck, r = orders[d][i], i % 2
                    ts = slice(ck * 64, ck * 64 + 64)
                    hs = slice(d * 64 + g * 8, d * 64 + g * 8 + 8)
                    P.dma("sp", xc[d][r][:], S.XT[ts, g * 512:(g + 1) * 512], writes=[xc[d][r]])
                    P.dma("sp", bt[d][r][:], S.BT[ts, g * 128:(g + 1) * 128], writes=[bt[d][r]])
                    P.dma("act", cbc[d][r][:], bc(S.CUM[d * 64 + g * 8:d * 64 + g * 8 + 8, ts].unsqueeze(0), [64, 8, 64]), writes=[cbc[d][r]])
                    P.op("pe", lambda e: e.matmul(out=pcb[:], lhsT=BF[:, ts], rhs=CF[:, ts], start=True, stop=True), [BF, CF], [pcb])
                    P.op("dve", lambda e: e.tensor_tensor(out=cbm[d][r][:], in0=pcb[:], in1=S.tri[d][:], op=ALU.mult), [pcb, S.tri[d]], [cbm[d][r]])
                    P.op("dve", lambda e: e.tensor_tensor(out=cbc[d][r][:], in0=cbc[d][r][:], in1=bc(cumT[:, ck, hs].unsqueeze(2), [64, 8, 64]), op=ALU.subtract),
                         [cbc[d][r], cumT], [cbc[d][r]])
                    P.op("pool", lambda e: e.tensor_scalar_min(out=cbc[d][r][:], in0=cbc[d][r][:], scalar1=0.0), [cbc[d][r]], [cbc[d][r]])
                    P.op("act", lambda e: e.activation(out=cbc[d][r][:], in_=cbc[d][r][:], func=AF.Exp), [cbc[d][r]], [cbc[d][r]])
                    P.op("pool", lambda e: e.tensor_tensor(out=cbc[d][r][:], in0=cbc[d][r][:], in1=bc(dtT[:, ck, hs].unsqueeze(2), [64, 8, 64]), op=ALU.mult),
                         [cbc[d][r], dtT], [cbc[d][r]])
                    P.op("act", lambda e: e.activation(out=ec[d][r][:], in_=cumT[:, ck, hs], func=AF.Exp), [cumT], [ec[d][r]])

                def A2(d, i):
                    ck, r = orders[d][i], i % 2
                    hs = slice(d * 64 + g * 8, d * 64 + g * 8 + 8)
                    P.op("dve", lambda e: e.tensor_tensor(out=wd[d][r][:], in0=cbc[d][r][:], in1=bc(cbm[d][r][:].unsqueeze(1), [64, 8, 64]), op=ALU.mult),
                         [cbc[d][r], cbm[d][r]], [wd[d][r]])
                    P.op("pe", lambda e: e.matmul(out=ptb[:], lhsT=S.ones[:64, :], rhs=laT[:, ck, hs], start=True, stop=True), [S.ones, laT], [ptb])
                    P.op("dve", lambda e: e.tensor_tensor(out=sd[d][r][:], in0=ptb[:64, :], in1=cumT[:, ck, hs], op=ALU.subtract), [ptb, cumT], [sd[d][r]])
                    P.op("act", lambda e: e.activation(out=etb[d][r][:], in_=ptb[:], func=AF.Exp), [ptb], [etb[d][r]])
                    P.op("act", lambda e: e.activation(out=sd[d][r][:], in_=sd[d][r][:], func=AF.Exp), [sd[d][r]], [sd[d][r]])
                    P.op("pool", lambda e: e.tensor_tensor(out=sd[d][r][:], in0=sd[d][r][:], in1=dtT[:, ck, hs], op=ALU.mult), [sd[d][r], dtT], [sd[d][r]])
                    P.op("pool", lambda e: e.tensor_tensor(out=xd[d][r][:].rearrange("p (a b) -> p a b", a=8), in0=xc[d][r][:].rearrange("p (a b) -> p a b", a=8),
                                                           in1=bc(sd[d][r][:].unsqueeze(2), [64, 8, 64]), op=ALU.mult), [xc[d][r], sd[d][r]], [xd[d][r]])

                def Bst(d, i):
                    ck, r = orders[d][i], i % 2
                    ts = slice(ck * 64, ck * 64 + 64)
                    for e_ in range(8):
                        P.op("pe", lambda e: e.matmul(out=py1[d][:, e_ * 64:(e_ + 1) * 64], lhsT=wd[d][r][:, e_, :], rhs=xc[d][r][:, e_ * 64:(e_ + 1) * 64], start=True, stop=True),
                             [wd[d][r], xc[d][r]], [py1[d]])
                    P.op("pe", lambda e: e.matmul(out=py2[d][:], lhsT=CF[:, ts], rhs=hb[d][:], start=True, stop=True), [CF, hb[d]], [py2[d]])
                    P.op("pe", lambda e: e.matmul(out=pst[:], lhsT=bt[d][r][:], rhs=xd[d][r][:], start=True, stop=True), [bt[d][r], xd[d][r]], [pst])
                    y_, f_ = ys[d][r], yf[d][r]
                    P.op("dve", lambda e: e.tensor_tensor(out=y_[:].rearrange("p (a b) -> p a b", a=8), in0=py2[d][:].rearrange("p (a b) -> p a b", a=8),
                                                          in1=bc(ec[d][r][:].unsqueeze(2), [64, 8, 64]), op=ALU.mult), [py2[d], ec[d][r]], [y_])
                    P.op("dve", lambda e: e.tensor_tensor(out=y_[:], in0=y_[:], in1=py1[d][:], op=ALU.add), [y_, py1[d]], [y_])
                    P.op("dve", lambda e: e.tensor_tensor(out=hf[d][:].rearrange("p (a b) -> p a b", a=8), in0=hf[d][:].rearrange("p (a b) -> p a b", a=8),
                                                          in1=bc(etb[d][r][:].unsqueeze(2), [128, 8, 64]), op=ALU.mult), [hf[d], etb[d][r]], [hf[d]])
                    P.op("dve", lambda e: e.tensor_tensor(out=hf[d][:], in0=hf[d][:], in1=pst[:], op=ALU.add), [hf[d], pst], [hf[d]])
                    P.op("act", lambda e: e.activation(out=hb[d][:], in_=hf[d][:], func=AF.Copy), [hf[d]], [hb[d]])
                    second = pos[d][ck] > pos[1 - d][ck]
                    ydst = S.YF[ts, g * 512:(g + 1) * 512]
                    key = P.B(("ssdyf", g, ck))
                    if not second:
                        P.dma("sp", ydst, y_[:], reads=[y_], writes=[key])
                    else:
                        P.dma("sp", f_[:], ydst, reads=[key], writes=[f_])
                        P.op("pool", lambda e: e.tensor_tensor(out=y_[:], in0=y_[:], in1=f_[:], op=ALU.add), [y_, f_], [y_])
                        P.op("pool", lambda e: e.tensor_tensor(out=f_[:].rearrange("p (a b) -> p a b", a=8), in0=xc[d][r][:].rearrange("p (a b) -> p a b", a=8),
                                                               in1=bc(Dbc[:, g * 8:g * 8 + 8].unsqueeze(2), [64, 8, 64]), op=ALU.mult), [xc[d][r], Dbc], [f_])
                        P.op("pool", lambda e: e.tensor_tensor(out=y_[:], in0=y_[:], in1=f_[:], op=ALU.add), [y_, f_], [y_])
                        P.dma("sp", S.YT[ts, g * 512:(g + 1) * 512], y_[:], reads=[y_], writes=[])

                for d in range(2):
                    A1(d, 0)
                    A2(d, 0)
                for i in range(36):
                    for d in range(2):
                        if i + 1 < 36:
                            A1(d, i + 1)
                        Bst(d, i)
                        if i + 1 < 36:
                            A2(d, i + 1)
                P.barrier()
        with ExitStack() as es1:
            ngv = vec_pp(S, es1, W["ssm_norm_g"][l, :], 32, "ngv")
            eps = Tl(nc, es1, [128, 1], F32, name="epss")
            P.op("dve", lambda e: e.memset(eps[:], 1e-6), [], [eps])
            yt = [Tl(nc, es1, [128, 4096], F32, name="yt") for _ in range(2)]
            zt = [Tl(nc, es1, [128, 32, 128], F32, name="zt") for _ in range(2)]
            yg = Tl(nc, es1, [128, 32, 128], F32, name="yg")
            sq = Tl(nc, es1, [128, 32, 128], F32, name="sq")
            rs = Tl(nc, es1, [128, 8, 128], F32, name="rs")
            ob = [Tl(nc, es1, [128, 32, 128], BF16, name="ob") for _ in range(2)]
            ptr = [Tl(nc, es1, [128, 4, 128], F32, psum=True, name="ptr4") for _ in range(2)]
            pss = [Tl(nc, es1, [128, 4, 128], F32, psum=True, name="pss") for _ in range(2)]
            k = 0
            for tt in range(18):
                r = tt % 2
                tsl = slice(tt * 128, (tt + 1) * 128)
                P.dma("sp", yt[r][:], S.YT[tsl, :], writes=[yt[r]])
                P.dma("act", zt[r][:], PFs[0:4096, tsl].rearrange("(c p) t -> p c t", p=128), writes=[zt[r]])
                P.op("act", lambda e: e.activation(out=zt[r][:], in_=zt[r][:], func=AF.Silu), [zt[r]], [zt[r]])
                for cg in range(8):
                    p_ = ptr[k % 2]
                    k += 1
                    for a in range(4):
                        c = cg * 4 + a
                        P.op("pe", lambda e: e.transpose(out=p_[:, a, :], in_=yt[r][:, c * 128:(c + 1) * 128], identity=S.ident[:]), [yt[r], S.ident], [p_])
                    P.op("dve", lambda e: e.tensor_tensor(out=yg[:, cg * 4:cg * 4 + 4, :], in0=p_[:], in1=zt[r][:, cg * 4:cg * 4 + 4, :], op=ALU.mult), [p_, zt[r]], [yg])
                P.op("pool", lambda e: e.tensor_tensor(out=sq[:], in0=yg[:], in1=yg[:], op=ALU.mult), [yg], [sq])
                for gg in range(8):
                    p_ = pss[gg // 4]
                    for a in range(4):
                        P.op("pe", lambda e: e.matmul(out=p_[:, gg % 4, :], lhsT=S.ones[:], rhs=sq[:, gg * 4 + a, :], start=(a == 0), stop=(a == 3)), [sq, S.ones], [p_])
                for hh in range(2):
                    P.op("act", lambda e: e.activation(out=rs[:, hh * 4:hh * 4 + 4, :], in_=pss[hh][:], func=AF.Sqrt, bias=eps[:], scale=1.0 / 512), [pss[hh], eps], [rs])
                P.op("dve", lambda e: e.reciprocal(out=rs[:], in_=rs[:]), [rs], [rs])
                P.op("dve", lambda e: e.tensor_tensor(out=yg[:].rearrange("p (g a) t -> p g a t", a=4), in0=yg[:].rearrange("p (g a) t -> p g a t", a=4),
                                                      in1=bc(rs[:].unsqueeze(2), [128, 8, 4, 128]), op=ALU.mult), [yg, rs], [yg])
                P.op("pool", lambda e: e.tensor_tensor(out=ob[r][:], in0=yg[:], in1=bc(ngv[:].unsqueeze(2), [128, 32, 128]), op=ALU.mult), [yg, ngv], [ob[r]])
                P.dma("sp", S.OS[:, tsl].rearrange("(c p) t -> p c t", p=128), ob[r][:], reads=[ob[r]], writes=[])


CEXP = 0.6065306597126334


def mix_shift(S, out, x, omu, hmu, n):
    P = S.P
    P.op("dve", lambda e: e.tensor_scalar(out=out[:n, :], in0=x[:n, :], scalar1=omu, scalar2=None, op0=ALU.mult), [x], [out])
    for (s0, s1) in SEGS:
        P.op("dve", lambda e: e.scalar_tensor_tensor(out=out[:n, s0 + 1:s1], in0=x[:n, s0:s1 - 1], scalar=hmu, in1=out[:n, s0 + 1:s1], op0=ALU.mult, op1=ALU.add), [x, out], [out])
        P.op("dve", lambda e: e.scalar_tensor_tensor(out=out[:n, s0:s1 - 1], in0=x[:n, s0 + 1:s1], scalar=hmu, in1=out[:n, s0:s1 - 1], op0=ALU.mult, op1=ALU.add), [x, out], [out])


def ckview(d, h):
    return d[:, :, h, :].rearrange("c k t -> k c t")


def phase_rwkv(S, l, b):
    nc, P, W = S.nc, S.P, S.W
    PFr = S.PF["r"]
    OWD, OAD, OGD = 6144, 6336, 6528
    mu = W["rwkv_mu"][l]
    with ExitStack() as es:
        with ExitStack() as es1:
            def pp64(src, name):
                t = Tl(nc, es1, [128, 16], F32, name=name)
                P.dma("sp", t[:], src.rearrange("(c p) -> p c", p=128), writes=[t], allow_slow_non_contiguous=True)
                return t
            mu3 = Tl(nc, es1, [128, 3, 16], F32, name="mu3")
            P.dma("sp", mu3[:], mu[0:6144].rearrange("(c j p) -> p c j", c=3, p=128), writes=[mu3], allow_slow_non_contiguous=True)
            omu3 = Tl(nc, es1, [128, 3, 16], F32, name="omu3")
            P.op("dve", lambda e: e.tensor_scalar(out=omu3[:], in0=mu3[:], scalar1=-1.0, scalar2=1.0, op0=ALU.mult, op1=ALU.add), [mu3], [omu3])
            P.op("dve", lambda e: e.tensor_scalar(out=mu3[:], in0=mu3[:], scalar1=0.5, scalar2=None, op0=ALU.mult), [mu3], [mu3])
            kkk = pp64(W["rwkv_k_k"][l], "kkk")
            kka_ = pp64(W["rwkv_k_a"][l], "kka")
            rk = pp64(W["rwkv_r_k"][l].rearrange("h k -> (h k)"), "rk")
            w0 = [pp64(W["rwkv_w0"][l, d], "w0") for d in range(2)]
            a0 = [pp64(W["rwkv_a0"][l, d], "a0") for d in range(2)]
            blk = Tl(nc, es1, [128, 128], F32, name="blk1")
            P.op("dve", lambda e: e.memset(blk[:], 0.0), [], [blk])
            P.op("dve", lambda e: e.memset(blk[0:64, 0:64], 1.0), [], [blk])
            P.op("dve", lambda e: e.memset(blk[64:128, 64:128], 1.0), [], [blk])
            mus = Tl(nc, es1, [128, 3], F32, name="mus")
            twd = [Tl(nc, es1, [96, NT], F32, name="twd") for _ in range(2)]
            adm = [Tl(nc, es1, [96, NT], F32, name="adm") for _ in range(2)]
            sgd = Tl(nc, es1, [128, 2, NT], F32, name="sgd")
            es_l = ExitStack()
            lin = Tl(nc, es_l, [128, NT], F32, name="lin")
            mixt = Tl(nc, es_l, [128, NT], F32, name="mixt")
            jobs = [(OWD, 96, twd[0], AF.Tanh), (OWD + 96, 96, twd[1], AF.Tanh), (OAD, 96, adm[0], None), (OAD + 96, 96, adm[1], None),
                    (OGD, 128, None, AF.Sigmoid), (OGD + 128, 128, None, AF.Sigmoid)]
            for ji, (r0, n, dst, fn) in enumerate(jobs):
                P.dma("sp", lin[:n, :], PFr[r0:r0 + n, :], writes=[lin])
                P.dma("sp", mus[:n, 0:1], mu[r0:r0 + n].rearrange("(p o) -> p o", o=1), writes=[mus])
                P.op("dve", lambda e: e.tensor_scalar(out=mus[:n, 1:2], in0=mus[:n, 0:1], scalar1=-1.0, scalar2=1.0, op0=ALU.mult, op1=ALU.add), [mus], [mus])
                P.op("dve", lambda e: e.tensor_scalar(out=mus[:n, 2:3], in0=mus[:n, 0:1], scalar1=0.5, scalar2=None, op0=ALU.mult), [mus], [mus])
                if dst is None:
                    dsl = sgd[:, ji - 4, :]
                    mix_shift(S, mixt, lin, mus[:n, 1:2], mus[:n, 2:3], n)
                    P.op("act", lambda e: e.activation(out=dsl, in_=mixt[:], func=fn), [mixt], [sgd])
                else:
                    mix_shift(S, dst, lin, mus[:n, 1:2], mus[:n, 2:3], n)
                    if fn is not None:
                        P.op("act", lambda e: e.activation(out=dst[:], in_=dst[:], func=fn), [dst], [dst])
            P.barrier()
            es_l.close()
            def T64(name, dt=F32):
                return Tl(nc, es1, [128, NT], dt, name=name)
            xin = [T64("rxin") for _ in range(1)]
            rm, km, vm, kkn, tA, tB, tC, ki, kkat, bon, Cc, av = (T64(n_) for n_ in ("rm", "km", "vm", "kkn", "tA", "tB", "tC", "ki", "kkat", "bon", "Cc", "av"))
            ob = [T64("rob", BF16) for _ in range(3)]
            tot = Tl(nc, es1, [128, 36], F32, name="rtot")
            gam = Tl(nc, es1, [128, 36], F32, name="rgam")
            wl = [Tl(nc, es1, [96, 128], F32, name="wl") for _ in range(4)]
            gl = Tl(nc, es1, [128, 2, 128], F32, name="gl")
            ps = [Tl(nc, es1, [128, 512], F32, psum=True, name="rps") for _ in range(7)]
            pk = [0]
            obk = [0]

            def mm_tiles(fn_lhs_rhs, evac):
                for (t0, tn) in TT:
                    pt = ps[pk[0] % 7]
                    pk[0] += 1
                    lst = fn_lhs_rhs(t0, tn)
                    for i, (lh, rh, deps) in enumerate(lst):
                        P.op("pe", lambda e: e.matmul(out=pt[:, :tn], lhsT=lh, rhs=rh, start=(i == 0), stop=(i == len(lst) - 1)), deps, [pt])
                    evac(pt, t0, tn)

            def dma_ck(dst, j, t_, q="act"):
                for hh in range(2):
                    P.dma(q, ckview(dst, 2 * j + hh), t_[hh * 64:(hh + 1) * 64, :].rearrange("k (c t) -> k c t", t=64), reads=[t_], writes=[])

            def store_ck(dst, j, fn):
                o_ = ob[obk[0] % 3]
                obk[0] += 1
                fn(o_)
                dma_ck(dst, j, o_)

            for j in range(16):
                hs = slice(j * 128, (j + 1) * 128)
                for ci, dstt in enumerate((rm, km, vm)):
                    xi = xin[0]
                    P.dma("sp", xi[:], PFr[ci * 2048 + j * 128:ci * 2048 + (j + 1) * 128, :], writes=[xi])
                    mix_shift(S, dstt, xi, omu3[:, ci, j:j + 1], mu3[:, ci, j:j + 1], 128)
                store_ck(S.RW["V"], j, lambda o_: P.op("act", lambda e: e.activation(out=o_[:], in_=vm[:], func=AF.Copy), [vm], [o_]))
                P.op("dve", lambda e: e.tensor_scalar(out=kkn[:], in0=km[:], scalar1=kkk[:, j:j + 1], scalar2=None, op0=ALU.mult), [km, kkk], [kkn])
                P.op("pool", lambda e: e.tensor_tensor(out=tA[:], in0=kkn[:], in1=kkn[:], op=ALU.mult), [kkn], [tA])
                mm_tiles(lambda t0, tn: [(blk[:], tA[:, t0:t0 + tn], [blk, tA])],
                         lambda pt, t0, tn: P.op("dve", lambda e: e.tensor_scalar_max(out=tB[:, t0:t0 + tn], in0=pt[:, :tn], scalar1=1e-24), [pt], [tB]))
                P.op("act", lambda e: e.activation(out=tB[:], in_=tB[:], func=AF.Sqrt), [tB], [tB])
                P.op("dve", lambda e: e.reciprocal(out=tB[:], in_=tB[:]), [tB], [tB])
                P.op("pool", lambda e: e.tensor_tensor(out=kkn[:], in0=kkn[:], in1=tB[:], op=ALU.mult), [kkn, tB], [kkn])
                P.dma("sp", gl[:], W["rwkv_g_up"][l][:, hs].rearrange("(kc p) n -> p kc n", p=128), writes=[gl])
                mm_tiles(lambda t0, tn: [(gl[:, kc, :], sgd[:, kc, t0:t0 + tn], [gl, sgd]) for kc in range(2)],
                         lambda pt, t0, tn: P.op("act", lambda e: e.activation(out=tC[:, t0:t0 + tn], in_=pt[:, :tn], func=AF.Copy), [pt], [tC]))
                dma_ck(S.RW["G"], j, tC)
                for d in range(2):
                    rev = d == 1
                    P.dma("sp", wl[d][:], W["rwkv_w_up"][l, d][:, hs], writes=[wl[d]])
                    P.dma("sp", wl[2 + d][:], W["rwkv_a_up"][l, d][:, hs], writes=[wl[2 + d]])
                    mm_tiles(lambda t0, tn: [(wl[d][:], twd[d][:, t0:t0 + tn], [wl[d], twd[d]])],
                             lambda pt, t0, tn: P.op("act", lambda e: e.activation(out=tA[:, t0:t0 + tn], in_=pt[:, :tn], func=AF.Sigmoid, bias=w0[d][:, j:j + 1], scale=1.0), [pt, w0[d]], [tA]))
                    mm_tiles(lambda t0, tn: [(wl[2 + d][:], adm[d][:, t0:t0 + tn], [wl[2 + d], adm[d]])],
                             lambda pt, t0, tn: P.op("act", lambda e: e.activation(out=av[:, t0:t0 + tn], in_=pt[:, :tn], func=AF.Sigmoid, bias=a0[d][:, j:j + 1], scale=1.0), [pt, a0[d]], [av]))
                    P.op("dve", lambda e: e.tensor_scalar(out=ki[:], in0=av[:], scalar1=-1.0, scalar2=kka_[:, j:j + 1], op0=ALU.add, op1=ALU.mult), [av, kka_], [ki])
                    P.op("dve", lambda e: e.scalar_tensor_tensor(out=ki[:], in0=ki[:], scalar=1.0, in1=km[:], op0=ALU.add, op1=ALU.mult), [ki, km], [ki])
                    P.op("pool", lambda e: e.tensor_tensor(out=kkat[:], in0=kkn[:], in1=av[:], op=ALU.mult), [kkn, av], [kkat])
                    P.op("dve", lambda e: e.scalar_tensor_tensor(out=tB[:], in0=rm[:], scalar=rk[:, j:j + 1], in1=ki[:], op0=ALU.mult, op1=ALU.mult), [rm, rk, ki], [tB])
                    if d == 0:
                        mm_tiles(lambda t0, tn: [(blk[:], tB[:, t0:t0 + tn], [blk, tB])],
                                 lambda pt, t0, tn: P.op("dve", lambda e: e.tensor_tensor(out=bon[:, t0:t0 + tn], in0=pt[:, :tn], in1=vm[:, t0:t0 + tn], op=ALU.mult), [pt, vm], [bon]))
                    else:
                        mm_tiles(lambda t0, tn: [(blk[:], tB[:, t0:t0 + tn], [blk, tB])],
                                 lambda pt, t0, tn: P.op("dve", lambda e: e.tensor_tensor(out=tC[:, t0:t0 + tn], in0=pt[:, :tn], in1=vm[:, t0:t0 + tn], op=ALU.mult), [pt, vm], [tC]))
                        P.op("pool", lambda e: e.tensor_tensor(out=bon[:], in0=bon[:], in1=tC[:], op=ALU.add), [bon, tC], [bon])
                        dma_ck(S.RW["BON"], j, bon)
                    seg_cumsum(S, Cc, tA, tot, tB, rev)
                    P.op("act", lambda e: e.activation(out=gam[:], in_=tot[:], func=AF.Exp, scale=-CEXP), [tot], [gam])
                    for hh in range(2):
                        P.dma("sp", S.RW["GAM"][d][2 * j + hh], gam[hh * 64:(hh + 1) * 64, :], reads=[gam], writes=[])
                    P.op("pool", lambda e: e.tensor_tensor(out=tB[:], in0=Cc[:], in1=tA[:], op=ALU.subtract), [Cc, tA], [tB])
                    P.op("act", lambda e: e.activation(out=tB[:], in_=tB[:], func=AF.Exp, scale=-CEXP), [tB], [tB])
                    store_ck(S.RW["A"][d], j, lambda o_: P.op("dve", lambda e: e.tensor_tensor(out=o_[:], in0=kkn[:], in1=tB[:], op=ALU.mult), [kkn, tB], [o_]))
                    P.op("act", lambda e: e.activation(out=tB[:], in_=Cc[:], func=AF.Exp, scale=-CEXP), [Cc], [tB])
                    store_ck(S.RW["R"][d], j, lambda o_: P.op("pool", lambda e: e.tensor_tensor(out=o_[:], in0=rm[:], in1=tB[:], op=ALU.mult), [rm, tB], [o_]))
                    P.op("act", lambda e: e.activation(out=tB[:], in_=Cc[:], func=AF.Exp, scale=CEXP), [Cc], [tB])
                    store_ck(S.RW["K"][d], j, lambda o_: P.op("dve", lambda e: e.tensor_tensor(out=o_[:], in0=ki[:], in1=tB[:], op=ALU.mult), [ki, tB], [o_]))
                    store_ck(S.RW["B"][d], j, lambda o_: P.op("pool", lambda e: e.tensor_tensor(out=o_[:], in0=kkat[:], in1=tB[:], op=ALU.mult), [kkat, tB], [o_]))
                    P.op("dve", lambda e: e.tensor_tensor(out=tB[:].rearrange("p (c q) -> p c q", q=64), in0=Cc[:].rearrange("p (c q) -> p c q", q=64),
                                                          in1=bc(tot[:].unsqueeze(2), [128, 36, 64]), op=ALU.subtract), [Cc, tot], [tB])
                    P.op("act", lambda e: e.activation(out=tB[:], in_=tB[:], func=AF.Exp, scale=CEXP), [tB], [tB])
                    store_ck(S.RW["KT"][d], j, lambda o_: P.op("dve", lambda e: e.tensor_tensor(out=o_[:], in0=ki[:], in1=tB[:], op=ALU.mult), [ki, tB], [o_]))
                    store_ck(S.RW["BT"][d], j, lambda o_: P.op("pool", lambda e: e.tensor_tensor(out=o_[:], in0=kkat[:], in1=tB[:], op=ALU.mult), [kkat, tB], [o_]))
        P.barrier()
        with ExitStack() as es1:
            def pp64b(src, name):
                t = Tl(nc, es1, [64, 32], F32, name=name)
                P.dma("sp", t[:], src.rearrange("(h k) -> k h", k=64), writes=[t], allow_slow_non_contiguous=True)
                return t
            lnw = pp64b(W["rwkv_ln_w"][l], "lnw")
            lnb = pp64b(W["rwkv_ln_b"][l], "lnb")
            eps = Tl(nc, es1, [64, 1], F32, name="repsl")
            P.op("dve", lambda e: e.memset(eps[:], 64e-5), [], [eps])
            SH = [64, 8, 64]
            pf = [Tl(nc, es1, SH, F32, psum=True, name="rpf") for _ in range(7)]
            pb = Tl(nc, es1, SH, BF16, psum=True, name="rpb")
            pfk = [0]

            def nps():
                t = pf[pfk[0] % 7]
                pfk[0] += 1
                return t
            def make_chain():
                C = K()
                C.ldb = {n_: [Tl(nc, es1, SH, BF16, name="l" + n_) for _ in range(2)] for n_ in ("A", "R", "K", "B", "KT", "BT", "V")}
                C.gamt = Tl(nc, es1, [64, 8, 36], F32, name="gamt")
                C.Tf = Tl(nc, es1, SH, F32, name="Tf")
                C.Tb = Tl(nc, es1, SH, BF16, name="Tb")
                C.PTf = [[Tl(nc, es1, SH, F32, name="PTf") for _ in range(6)] for _ in range(2)]
                C.Pb = [Tl(nc, es1, SH, BF16, name="Pb") for _ in range(2)]
                C.PTb = [Tl(nc, es1, SH, BF16, name="PTb") for _ in range(2)]
                dbl = lambda n_: [Tl(nc, es1, SH, BF16, name=n_) for _ in range(2)]
                C.LkT, C.MKT, C.MBT, C.KtT, C.BtT, C.VT = dbl("LkT"), dbl("MKT"), dbl("MBT"), dbl("KtT"), dbl("BtT"), dbl("VT")
                C.X = Tl(nc, es1, SH, F32, name="X")
                C.Ub = Tl(nc, es1, SH, BF16, name="Ub")
                C.ys = [Tl(nc, es1, SH, F32, name="rys") for _ in range(2)]
                C.yfl = [Tl(nc, es1, SH, F32, name="ryf") for _ in range(2)]
                C.bnl = [Tl(nc, es1, SH, F32, name="rbn") for _ in range(2)]
                C.ggl = [Tl(nc, es1, SH, F32, name="rgg") for _ in range(2)]
                C.sqv = Tl(nc, es1, SH, F32, name="rsq")
                C.st8 = Tl(nc, es1, [64, 8], F32, name="st8")
                C.st9 = Tl(nc, es1, [64, 8], F32, name="st9")
                C.obf = [Tl(nc, es1, SH, BF16, name="robf") for _ in range(2)]
                return C
            chains = [make_chain() for _ in range(2)]

            def mm8(pt, *pairs):
                for e_ in range(8):
                    for i_, (lh, rh) in enumerate(pairs):
                        P.op("pe", lambda e: e.matmul(out=pt[:, e_, :], lhsT=lh[:, e_, :], rhs=rh[:, e_, :], start=(i_ == 0), stop=(i_ == len(pairs) - 1)), [lh, rh], [pt])

            def masked(pt, dst, mask):
                P.op("dve", lambda e: e.tensor_tensor(out=dst[:], in0=pt[:], in1=bc(mask[:].unsqueeze(1), SH), op=ALU.mult), [pt, mask], [dst])

            for hgp in range(2):
                for d in range(2):
                    rev = d == 1
                    order = chunk_order(rev)
                    for c_ in range(2):
                        C = chains[c_]
                        hg = hgp * 2 + c_
                        P.dma("sp", C.gamt[:], S.RW["GAM"][d][hg * 8:hg * 8 + 8].rearrange("h k c -> k h c"), writes=[C.gamt])
                        P.op("dve", lambda e: e.memset(C.Tf[:], 0.0), [], [C.Tf])
                        P.op("dve", lambda e: e.memset(C.Tb[:], 0.0), [], [C.Tb])

                    def A_slices(C, hg, i):
                        ck, r_ = order[i], i % 2
                        h8 = slice(hg * 8, hg * 8 + 8)
                        L = {n_: C.ldb[n_][r_] for n_ in C.ldb}

                        def s0():
                            for qi, n_ in enumerate(("A", "R", "K", "B", "KT", "BT", "V")):
                                src = S.RW[n_] if n_ == "V" else S.RW[n_][d]
                                P.dma("sp" if qi % 2 == 0 else "act", L[n_][:], src[ck, :, h8, :], writes=[L[n_]])
                            p1 = nps(); mm8(p1, (L["B"], L["A"])); masked(p1, C.PTf[r_][0], S.nstr[d])
                            P.op("pool", lambda e: e.tensor_copy(out=C.PTb[0][:], in_=C.PTf[r_][0][:]), [C.PTf[r_][0]], [C.PTb[0]])
                            p2 = nps(); mm8(p2, (L["A"], L["B"])); masked(p2, C.Pb[0], S.nstr[1 - d])

                        def s1():
                            p3 = nps(); mm8(p3, (L["K"], L["A"])); masked(p3, C.LkT[r_], S.str_[d])
                            p4 = nps(); mm8(p4, (L["K"], L["R"])); masked(p4, C.MKT[r_], S.tri[d])
                            p5 = nps(); mm8(p5, (L["B"], L["R"])); masked(p5, C.MBT[r_], S.ntri[d])

                        def s2():
                            for src_, dst_, sc_ in ((L["KT"], C.KtT[r_], 1.0), (L["BT"], C.BtT[r_], -1.0), (L["V"], C.VT[r_], 1.0)):
                                for e_ in range(8):
                                    P.op("pe", lambda e: e.transpose(out=pb[:, e_, :], in_=src_[:, e_, :], identity=S.identb[:64, :64]), [src_, S.identb], [pb])
                                P.op("act", lambda e: e.activation(out=dst_[:], in_=pb[:], func=AF.Copy, scale=sc_), [pb], [dst_])

                        def sq(j):
                            def f():
                                cur, nxt = j % 2, (j + 1) % 2
                                pr = nps(); mm8(pr, (C.Pb[cur], C.PTb[cur]))
                                P.op("act", lambda e: e.activation(out=C.PTf[r_][j + 1][:], in_=pr[:], func=AF.Copy), [pr], [C.PTf[r_][j + 1]])
                                if j < 4:
                                    pq = nps(); mm8(pq, (C.PTb[cur], C.Pb[cur]))
                                    P.op("act", lambda e: e.activation(out=C.Pb[nxt][:], in_=pq[:], func=AF.Copy), [pq], [C.Pb[nxt]])
                                    P.op("pool", lambda e: e.tensor_copy(out=C.PTb[nxt][:], in_=C.PTf[r_][j + 1][:]), [C.PTf[r_][j + 1]], [C.PTb[nxt]])
                            return f
                        return [s0, s1, s2, sq(0), sq(1), sq(2), sq(3), sq(4)]

                    def B_steps(C, hg, i):
                        ck, r_ = order[i], i % 2
                        h8 = slice(hg * 8, hg * 8 + 8)
                        L = {n_: C.ldb[n_][r_] for n_ in C.ldb}
                        tsl = slice(ck * 64, ck * 64 + 64)

                        def b0():
                            px = nps()
                            mm8(px, (L["A"], C.Tb), (C.LkT[r_], C.VT[r_]))
                            P.op("act", lambda e: e.activation(out=C.X[:], in_=px[:], func=AF.Copy), [px], [C.X])

                        def ap(j):
                            def f():
                                pa = nps()
                                mm8(pa, (C.PTf[r_][j], C.X))
                                P.op("dve", lambda e: e.tensor_tensor(out=C.X[:], in0=C.X[:], in1=pa[:], op=ALU.add), [C.X, pa], [C.X])
                            return f

                        def fin():
                            P.op("act", lambda e: e.activation(out=C.Ub[:], in_=C.X[:], func=AF.Copy), [C.X], [C.Ub])
                            py = nps()
                            mm8(py, (L["R"], C.Tb), (C.MKT[r_], C.VT[r_]), (C.MBT[r_], C.Ub))
                            pS = nps()
                            mm8(pS, (C.KtT[r_], C.VT[r_]), (C.BtT[r_], C.Ub))
                            P.op("dve", lambda e: e.tensor_tensor(out=C.Tf[:], in0=C.Tf[:], in1=bc(C.gamt[:, :, ck].unsqueeze(2), SH), op=ALU.mult), [C.Tf, C.gamt], [C.Tf])
                            P.op("dve", lambda e: e.tensor_tensor(out=C.Tf[:], in0=C.Tf[:], in1=pS[:], op=ALU.add), [C.Tf, pS], [C.Tf])
                            P.op("act", lambda e: e.activation(out=C.Tb[:], in_=C.Tf[:], func=AF.Copy), [C.Tf], [C.Tb])
                            ydst = S.RW["YF"][tsl, hg * 512:(hg + 1) * 512]
                            y_ = C.ys[r_]
                            if not rev:
                                P.op("act", lambda e: e.activation(out=y_[:], in_=py[:], func=AF.Copy), [py], [y_])
                                P.dma("sp", ydst, y_[:].rearrange("p a b -> p (a b)"), reads=[y_], writes=[])
                            else:
                                f_, bn_, gg_, o_ = C.yfl[r_], C.bnl[r_], C.ggl[r_], C.obf[r_]
                                P.dma("sp", f_[:].rearrange("p a b -> p (a b)"), ydst, writes=[f_])
                                P.dma("act", bn_[:], S.RW["BON"][ck, :, h8, :], writes=[bn_])
                                P.dma("act", gg_[:], S.RW["G"][ck, :, h8, :], writes=[gg_])
                                P.op("dve", lambda e: e.tensor_tensor(out=y_[:], in0=py[:], in1=f_[:], op=ALU.add), [py, f_], [y_])
                                P.op("dve", lambda e: e.reduce_sum(out=C.st8[:], in_=y_[:], axis=AX.X), [y_], [C.st8])
                                P.op("pool", lambda e: e.tensor_scalar(out=C.st8[:], in0=C.st8[:], scalar1=1.0 / 64, scalar2=None, op0=ALU.mult), [C.st8], [C.st8])
                                P.op("pool", lambda e: e.tensor_tensor(out=y_[:], in0=y_[:], in1=bc(C.st8[:].unsqueeze(2), SH), op=ALU.subtract), [y_, C.st8], [y_])
                                P.op("pool", lambda e: e.tensor_tensor(out=C.sqv[:], in0=y_[:], in1=y_[:], op=ALU.mult), [y_], [C.sqv])
                                P.op("dve", lambda e: e.reduce_sum(out=C.st9[:], in_=C.sqv[:], axis=AX.X), [C.sqv], [C.st9])
                                P.op("act", lambda e: e.activation(out=C.st9[:], in_=C.st9[:], func=AF.Sqrt, bias=eps[:], scale=1.0 / 64), [C.st9, eps], [C.st9])
                                P.op("dve", lambda e: e.reciprocal(out=C.st9[:], in_=C.st9[:]), [C.st9], [C.st9])
                                P.op("pool", lambda e: e.tensor_tensor(out=y_[:], in0=y_[:], in1=bc(C.st9[:].unsqueeze(2), SH), op=ALU.mult), [y_, C.st9], [y_])
                                pt_ = nps()
                                for e_ in range(8):
                                    P.op("pe", lambda e: e.transpose(out=pt_[:, e_, :], in_=y_[:, e_, :], identity=S.ident[:64, :64]), [y_, S.ident], [pt_])
                                P.op("dve", lambda e: e.tensor_tensor(out=C.sqv[:], in0=pt_[:], in1=bc(lnw[:, h8].unsqueeze(2), SH), op=ALU.mult), [pt_, lnw], [C.sqv])
                                P.op("pool", lambda e: e.tensor_tensor(out=C.sqv[:], in0=C.sqv[:], in1=bc(lnb[:, h8].unsqueeze(2), SH), op=ALU.add), [C.sqv, lnb], [C.sqv])
                                P.op("pool", lambda e: e.tensor_tensor(out=C.sqv[:], in0=C.sqv[:], in1=bn_[:], op=ALU.add), [C.sqv, bn_], [C.sqv])
                                P.op("pool", lambda e: e.tensor_tensor(out=o_[:], in0=C.sqv[:], in1=gg_[:], op=ALU.mult), [C.sqv, gg_], [o_])
                                P.dma("sp", S.ORW[hg * 512:(hg + 1) * 512, tsl].rearrange("(h v) t -> v h t", v=64), o_[:], reads=[o_], writes=[])
                        return [b0, ap(0), ap(1), ap(2), ap(3), ap(4), ap(5), fin]

                    for c_ in range(2):
                        for f in A_slices(chains[c_], hgp * 2 + c_, 0):
                            f()
                    for i in range(36):
                        a_ = [A_slices(chains[c_], hgp * 2 + c_, i + 1) if i + 1 < 36 else [] for c_ in range(2)]
                        b_ = [B_steps(chains[c_], hgp * 2 + c_, i) for c_ in range(2)]
                        for k_ in range(8):
                            for c_ in range(2):
                                b_[c_][k_]()
                                if k_ < len(a_[c_]):
                                    a_[c_][k_]()
                    P.barrier()


TT2 = [(0, 256), (256, 512), (768, 512), (1280, 512), (1792, 512)]


def resid_update(S, b, ci, pt_ap, t0, tn, gidx, xo, k, pt):
    nc, P = S.nc, S.P
    v = 2 if t0 < 256 else b
    xi = xo[k % len(xo)]
    dst = S.xT[b, ci * 128:(ci + 1) * 128, t0:t0 + tn]
    P.dma("sp", xi[:, :tn], dst, writes=[xi])
    P.op("dve", lambda e: e.scalar_tensor_tensor(out=xi[:, :tn], in0=pt_ap, scalar=S.mod[:, gidx * 16 + ci, v:v + 1], in1=xi[:, :tn], op0=ALU.mult, op1=ALU.add),
         [pt, S.mod, xi], [xi])
    P.dma("act", dst, xi[:, :tn], reads=[xi], writes=[])


def resid_evac(S, b, gidx, xo, cnt):
    def evac(ci, n, tiles):
        for (pt, t0, tn) in tiles:
            if t0 == 0:
                for (a0, a1) in ((0, 256), (256, tn)):
                    resid_update(S, b, ci, pt[:, a0:a1], a0, a1 - a0, gidx, xo, cnt[0], pt)
                    cnt[0] += 1
            else:
                resid_update(S, b, ci, pt[:, :tn], t0, tn, gidx, xo, cnt[0], pt)
                cnt[0] += 1
    return evac


def phase_merge(S, l, b):
    nc, P, W = S.nc, S.P, S.W
    passes = [(S.OG, 0, W["w_br_gla"][l], 0), (S.OS, 0, W["w_br_ssm"][l][0:2048, :], 2048), (S.OS, 16, W["w_br_ssm"][l][2048:4096, :], 2048),
              (S.ORW, 0, W["w_br_rwkv"][l], 4096)]
    with ExitStack() as es:
        mT = Tl(nc, es, [128, 16, NT], BF16, name="mT")
        with ExitStack() as es1:
            rhs = Tl(nc, es1, [128, 16, NT], BF16, name="mrhs")
            gt = [Tl(nc, es1, [128, NT], F32, name="mgt") for _ in range(2)]
            tmp = [Tl(nc, es1, [128, 512], F32, name="mtmp") for _ in range(3)]
            tk = [0]
            for pi, (src, k0, w_ap, go) in enumerate(passes):
                P.dma("sp", rhs[:], src[k0 * 128:(k0 + 16) * 128, :].rearrange("(kc p) t -> p kc t", p=128), writes=[rhs])

                def evac(ci, n, tiles, pi=pi, go=go):
                    g_ = gt[ci % 2]
                    P.dma("act", g_[:], S.PF["t"][go + ci * 128:go + (ci + 1) * 128, :], writes=[g_])
                    P.op("act", lambda e: e.activation(out=g_[:], in_=g_[:], func=AF.Sigmoid), [g_], [g_])
                    for (pt, t0, tn) in tiles:
                        if pi == 0:
                            P.op("dve", lambda e: e.tensor_tensor(out=mT[:, ci, t0:t0 + tn], in0=pt[:, :tn], in1=g_[:, t0:t0 + tn], op=ALU.mult), [pt, g_], [mT])
                        else:
                            t_ = tmp[tk[0] % 3]
                            tk[0] += 1
                            P.op("dve", lambda e: e.tensor_tensor(out=t_[:, :tn], in0=pt[:, :tn], in1=g_[:, t0:t0 + tn], op=ALU.mult), [pt, g_], [t_])
                            P.op("pool", lambda e: e.tensor_tensor(out=mT[:, ci, t0:t0 + tn], in0=mT[:, ci, t0:t0 + tn], in1=t_[:, :tn], op=ALU.add), [mT, t_], [mT])

                with ExitStack() as es2:
                    proj(S, es2, w_ap, 2048, rhs, 16, evac, blk=256)
                P.barrier()
        with ExitStack() as es1:
            xo = [Tl(nc, es1, [128, 512], F32, name="mxo") for _ in range(4)]
            proj(S, es1, W["w_out"][l], 2048, mT, 16, resid_evac(S, b, 2, xo, [0]), blk=256)
        P.barrier()


def phase_ffn_up(S, l, hT):
    nc, P, W = S.nc, S.P, S.W
    with ExitStack() as es:
        cw = [[vec_pp(S, es, W["ffn_conv_w"][l, i, j, :], 88, "fcw") for j in range(3)] for i in range(3)]
        cb = vec_pp(S, es, W["ffn_conv_b"][l, :], 88, "fcb")
        u = [Tl(nc, es, [128, NT], F32, name="fu") for _ in range(2)]
        y = [Tl(nc, es, [128, NT], F32, name="fy") for _ in range(2)]
        gb = [Tl(nc, es, [128, NT], BF16, name="fgb") for _ in range(2)]
        ab = [Tl(nc, es, [128, NT], BF16, name="fab") for _ in range(2)]
        cnt = [0]

        def evac(ci, n, tiles):
            k = cnt[0]
            cnt[0] += 1
            ui, yi = u[k % 2], y[k % 2]
            eng = "dve"
            for i, (pt, t0, tn) in enumerate(tiles):
                if i % 2 == 0 or True:
                    P.op("act", lambda e: e.activation(out=ui[:, t0:t0 + tn], in_=pt[:, :tn], func=AF.Copy), [pt], [ui])
                else:
                    P.op("dve", lambda e: e.tensor_copy(out=ui[:, t0:t0 + tn], in_=pt[:, :tn]), [pt], [ui])
            P.op(eng, lambda e: e.tensor_scalar(out=yi[:], in0=ui[:], scalar1=cw[1][1][:, ci:ci + 1], scalar2=cb[:, ci:ci + 1], op0=ALU.mult, op1=ALU.add), [ui, cw[1][1], cb], [yi])
            for dc in (-1, 1):
                o0, o1 = max(0, -dc), 256 - max(0, dc)
                P.op(eng, lambda e: e.scalar_tensor_tensor(out=yi[:, o0:o1], in0=ui[:, o0 + dc:o1 + dc], scalar=cw[1][1 + dc][:, ci:ci + 1], in1=yi[:, o0:o1], op0=ALU.mult, op1=ALU.add),
                     [ui, yi, cw[1][1 + dc]], [yi])
            uv = ui[:, 256:].rearrange("p (r c) -> p r c", c=64)
            yv = yi[:, 256:].rearrange("p (r c) -> p r c", c=64)
            ti = 0
            for dr in (-1, 0, 1):
                for dc in (-1, 0, 1):
                    if dr == 0 and dc == 0:
                        continue
                    r0, r1 = max(0, -dr), 32 - max(0, dr)
                    c0, c1 = max(0, -dc), 64 - max(0, dc)
                    ti += 1
                    if ti % 3 == 0 and False:
                        pass
                    P.op(eng, lambda e: e.scalar_tensor_tensor(out=yv[:, r0:r1, c0:c1], in0=uv[:, r0 + dr:r1 + dr, c0 + dc:c1 + dc], scalar=cw[1 + dr][1 + dc][:, ci:ci + 1],
                                                               in1=yv[:, r0:r1, c0:c1], op0=ALU.mult, op1=ALU.add), [ui, yi, cw[1 + dr][1 + dc]], [yi])
            if ci < 44:
                g_ = gb[k % 2]
                P.op("act", lambda e: e.activation(out=g_[:], in_=yi[:], func=AF.Silu), [yi], [g_])
                P.dma("sp", S.ACTG[ci * 128:(ci + 1) * 128, :], g_[:], reads=[g_], writes=[S.P.B(("actg", ci))])
            else:
                g_, a_ = gb[k % 2], ab[k % 2]
                P.dma("sp", g_[:], S.ACTG[(ci - 44) * 128:(ci - 43) * 128, :], reads=[S.P.B(("actg", ci - 44))], writes=[g_])
                P.op("pool", lambda e: e.tensor_tensor(out=a_[:], in0=yi[:], in1=g_[:], op=ALU.mult), [yi, g_], [a_])
                P.dma("act", S.ACT[(ci - 44) * 128:(ci - 43) * 128, :], a_[:], reads=[a_], writes=[])

        proj(S, es, W["ffn_w_up"][l], 2 * FH, hT, 16, evac, blk=512, nps=7)


def phase_ffn_down(S, l, b):
    nc, P, W = S.nc, S.P, S.W
    with ExitStack() as es:
        rhs = Tl(nc, es, [128, 22, NT], BF16, name="drhs")
        xo = [Tl(nc, es, [128, 512], F32, name="dxo") for _ in range(4)]
        cnt = [0]
        for half in range(2):
            r0 = half * 22 * 128
            P.dma("sp", rhs[:], S.ACT[r0:r0 + 22 * 128, :].rearrange("(kc p) t -> p kc t", p=128), writes=[rhs])
            with ExitStack() as es2:
                proj(S, es2, W["ffn_w_down"][l][r0:r0 + 22 * 128, :], 2048, rhs, 22, resid_evac(S, b, 5, xo, cnt), blk=256)
            P.barrier()


def phase_final(S, b, out):
    nc, P, W = S.nc, S.P, S.W
    with ExitStack() as es:
        gbc = Tl(nc, es, [128, DM], F32, name="gbc")
        P.dma("sp", gbc[:], bc(W["final_norm_g"].unsqueeze(0), [128, DM]), writes=[gbc])
        eps = Tl(nc, es, [128, 1], F32, name="feps")
        P.op("dve", lambda e: e.memset(eps[:], 1e-6), [], [eps])
        xt = [Tl(nc, es, [128, 16, 128], F32, name="fxt") for _ in range(2)]
        sq = [Tl(nc, es, [128, 16, 128], F32, name="fsq") for _ in range(2)]
        ot = [Tl(nc, es, [128, DM], F32, name="fot") for _ in range(2)]
        rs = [Tl(nc, es, [128, 1], F32, name="frs") for _ in range(2)]
        pss = [Tl(nc, es, [128, 1], F32, psum=True, name="fpss") for _ in range(2)]
        ptr = [Tl(nc, es, [128, 512], F32, psum=True, name="fptr") for _ in range(4)]
        k = 0
        for tt in range(16):
            r = tt % 2
            t0 = 256 + tt * 128
            P.dma("sp" if r == 0 else "act", xt[r][:], S.xT[b, :, t0:t0 + 128].rearrange("(c p) t -> p c t", p=128), writes=[xt[r]])
            P.op("act", lambda e: e.activation(out=sq[r][:], in_=xt[r][:], func=AF.Square), [xt[r]], [sq[r]])
            for c in range(16):
                P.op("pe", lambda e: e.matmul(out=pss[r][:], lhsT=sq[r][:, c, :], rhs=S.ones[:, 0:1], start=(c == 0), stop=(c == 15)), [sq[r], S.ones], [pss[r]])
            P.op("act", lambda e: e.activation(out=rs[r][:], in_=pss[r][:], func=AF.Sqrt, bias=eps[:], scale=1.0 / DM), [pss[r], eps], [rs[r]])
            P.op("dve", lambda e: e.reciprocal(out=rs[r][:], in_=rs[r][:]), [rs[r]], [rs[r]])
            for cg in range(4):
                p_ = ptr[k % 4]
                k += 1
                for a in range(4):
                    c = cg * 4 + a
                    P.op("pe", lambda e: e.transpose(out=p_[:, a * 128:(a + 1) * 128], in_=xt[r][:, c, :], identity=S.ident[:]), [xt[r], S.ident], [p_])
                P.op("dve", lambda e: e.scalar_tensor_tensor(out=ot[r][:, cg * 512:(cg + 1) * 512], in0=p_[:], scalar=rs[r][:, 0:1], in1=gbc[:, cg * 512:(cg + 1) * 512],
                                                             op0=ALU.mult, op1=ALU.mult), [p_, rs[r], gbc], [ot[r]])
            P.dma("sp", out[b, tt * 128:(tt + 1) * 128, :], ot[r][:], reads=[ot[r]], writes=[])


_NC_CACHE = {}


def kernel(**inputs):
    n = 8
    NB = 2
    if "nc" not in _NC_CACHE:
        _NC_CACHE["nc"] = build(NB=NB, L=2)
    nc = _NC_CACHE["nc"]
    cst = consts()
    shared = {k: np.ascontiguousarray(np.asarray(inputs[k], dtype=np.float32)) for k in WSHAPES}
    x = np.asarray(inputs["x"], dtype=np.float32)
    ctx = np.asarray(inputs["ctx"], dtype=np.float32)
    c = np.asarray(inputs["c"], dtype=np.float32)
    c_ctx = np.ascontiguousarray(np.asarray(inputs["c_ctx"], dtype=np.float32)[None, :])
    in_maps = []
    for i in range(n):
        m = {"x": np.ascontiguousarray(x[i * NB:(i + 1) * NB]), "ctx": np.ascontiguousarray(ctx[i * NB:(i + 1) * NB]),
             "c": np.ascontiguousarray(c[i * NB:(i + 1) * NB]), "c_ctx": c_ctx}
        m.update(shared)
        m.update(cst)
        in_maps.append(m)
    res = run_bass_kernel_spmd(nc, in_maps, core_ids=list(range(n)))
    return np.concatenate([r["out"] for r in res.results], axis=0)
```

```python
import numpy as np
import concourse.bass as bass
import concourse.mybir as mybir
from concourse.bass_utils import run_bass_kernel_spmd
from contextlib import ExitStack

F32 = mybir.dt.float32
BF16 = mybir.dt.bfloat16
ALU = mybir.AluOpType
AF = mybir.ActivationFunctionType
AX = mybir.AxisListType
NDS = 12

NT = 2304
TC = 256
DM = 2048
NIN = 29472
NPAD = 29568
TT = [(0, 512), (512, 512), (1024, 512), (1536, 512), (2048, 256)]
SEGS = [(0, 256), (256, 2304)]
FH = 5632

O_Q, O_K, O_V, O_G, O_AD, O_Z, O_XBC, O_DT, O_RW, O_GT = 0, 1024, 2048, 4096, 6144, 6176, 10272, 16416, 16544, 23328

WSHAPES = {
    "w_mod": [2, 2048, 12288], "b_mod": [2, 12288], "norm1_g": [2, 2048], "norm2_g": [2, 2048],
    "w_in": [2, 2048, 29472], "gla_wa_up": [2, 2, 16, 1024], "gla_ba": [2, 2, 1024], "gla_norm_g": [2, 512],
    "ssm_conv_w": [2, 3, 6144], "ssm_conv_b": [2, 6144], "ssm_a_log": [2, 2, 64], "ssm_dt_bias": [2, 2, 64],
    "ssm_d": [2, 64], "ssm_norm_g": [2, 4096], "rwkv_mu": [2, 6784], "rwkv_w0": [2, 2, 2048],
    "rwkv_w_up": [2, 2, 96, 2048], "rwkv_a0": [2, 2, 2048], "rwkv_a_up": [2, 2, 96, 2048],
    "rwkv_g_up": [2, 256, 2048], "rwkv_k_k": [2, 2048], "rwkv_k_a": [2, 2048], "rwkv_r_k": [2, 32, 64],
    "rwkv_ln_w": [2, 2048], "rwkv_ln_b": [2, 2048], "w_br_gla": [2, 2048, 2048], "w_br_ssm": [2, 4096, 2048],
    "w_br_rwkv": [2, 2048, 2048], "w_out": [2, 2048, 2048], "ffn_w_up": [2, 2048, 11264],
    "ffn_conv_w": [2, 3, 3, 11264], "ffn_conv_b": [2, 11264], "ffn_w_down": [2, 5632, 2048],
    "final_norm_g": [2048],
}


class Buf:
    __slots__ = ("w", "r")

    def __init__(self):
        self.w = None
        self.r = {}


class Rec:
    def __getattr__(self, name):
        def f(*a, **k):
            self.call = (name, a, k)
            return self
        return f


class Eng:
    def __init__(self, name, sem):
        self.name, self.sem, self.n, self.ops, self.waited = name, sem, 0, [], {}


class Prog:
    def __init__(self, nc, es):
        self.nc = nc
        self.E = {}
        for name in ("pe", "act", "dve", "pool", "sp"):
            self.E[name] = Eng(name, es.enter_context(nc.semaphore(name + "_sem")))
        self.dq = {}
        self.dqi = {}
        for q in ("sp", "pool", "act"):
            self.dq[q] = [[es.enter_context(nc.semaphore(f"d{q}{i}")), 0] for i in range(NDS)]
            self.dqi[q] = 0
        self.bufs = {}

    def B(self, key):
        b = self.bufs.get(key)
        if b is None:
            b = self.bufs[key] = Buf()
        return b

    def _wait(self, eng, tok):
        sem, val = tok
        k = id(sem)
        if eng.waited.get(k, 0) >= val:
            return
        eng.waited[k] = val
        eng.ops.append(("w", sem, val))

    def _deps(self, eng, reads, writes):
        own = id(eng.sem)
        skip_own = eng.name in ("pe", "sp")
        for b in reads:
            if b.w is not None and not (skip_own and id(b.w[0]) == own):
                self._wait(eng, b.w)
        for b in writes:
            if b.w is not None and not (skip_own and id(b.w[0]) == own):
                self._wait(eng, b.w)
            for t in b.r.values():
                if not (skip_own and id(t[0]) == own):
                    self._wait(eng, t)

    def _commit(self, tok, reads, writes):
        k = id(tok[0])
        for b in reads:
            b.r[k] = tok
        for b in writes:
            b.w = tok
            b.r = {}

    def op(self, en, fn, reads=(), writes=()):
        eng = self.E[en]
        writes = [getattr(r, "b", r) for r in writes] + [r.b for r in reads if getattr(r, "psum", False)]
        reads = [getattr(r, "b", r) for r in reads if not getattr(r, "psum", False)]
        self._deps(eng, reads, writes)
        eng.n += 1
        tok = (eng.sem, eng.n)
        rec = Rec()
        fn(rec)
        eng.ops.append(("i", rec.call, eng.sem, 1))
        self._commit(tok, reads, writes)

    def dma(self, q, out, in_, reads=(), writes=(), **kw):
        eng = self.E[q]
        reads = [getattr(r, "b", r) for r in reads]
        writes = [getattr(r, "b", r) for r in writes]
        self._deps(eng, reads, writes)
        slot = self.dq[q][self.dqi[q] % NDS]
        self.dqi[q] += 1
        if slot[1] > 0:
            self._wait(eng, (slot[0], slot[1]))
        slot[1] += 16
        tok = (slot[0], slot[1])
        eng.ops.append(("i", ("dma_start", (), dict(out=out, in_=in_, **kw)), slot[0], 16))
        self._commit(tok, reads, writes)

    def barrier(self):
        toks = [(e.sem, e.n) for e in self.E.values() if e.n > 0]
        for q in self.dq:
            for s in self.dq[q]:
                if s[1] > 0:
                    toks.append((s[0], s[1]))
        for e in self.E.values():
            for t in toks:
                if id(t[0]) == id(e.sem):
                    continue
                self._wait(e, t)
        self.bufs = {}

    def emit(self):
        nc = self.nc
        self.barrier()
        handles = {"pe": "tensor", "act": "scalar", "dve": "vector", "pool": "gpsimd", "sp": "sync"}
        with nc.Block() as block:
            for en, attr in handles.items():
                eng = self.E[en]

                def body(h, eng=eng):
                    for o in eng.ops:
                        if o[0] == "w":
                            h.wait_ge(o[1], o[2])
                        else:
                            getattr(h, o[1][0])(*o[1][1], **o[1][2]).then_inc(o[2], o[3])

                getattr(block, attr)(body)


class Tl:
    _n = 0

    def __init__(self, nc, es, shape, dtype, psum=False, name=None):
        Tl._n += 1
        name = (name or "t") + str(Tl._n)
        self.psum = psum
        self.b = Buf()
        if not psum:
            self.t = es.enter_context(nc.sbuf_tensor(name, list(shape), dtype))
            return
        esz = 2 if dtype == BF16 else 4
        n = 1
        for v in shape[1:]:
            n *= v
        assert n * esz <= 2048
        raw = es.enter_context(nc.psum_tensor(name, [shape[0], 2048 // esz], dtype))
        v = raw[:, :n]
        if len(shape) == 3:
            v = v.rearrange("p (a b) -> p a b", a=shape[1])
        self.t = v

    def __getitem__(self, k):
        return self.t[k]


class K:
    pass


def bc(ap, shape):
    return ap.to_broadcast(list(shape))


def build(NB=2, L=2, stop_after=None, dbg=(), skip=()):
    nc = bass.Bass("TRN2", target_bir_lowering=False)

    def din(name, shape, dt=F32):
        return nc.dram_tensor(name, list(shape), dt, kind="ExternalInput").ap()

    def dscr(name, shape, dt=F32):
        return nc.dram_tensor(name, list(shape), dt, kind="Internal").ap()

    x_in = din("x", [NB, 2048, DM])
    ctx_in = din("ctx", [NB, TC, DM])
    c_in = din("c", [NB, DM])
    cctx_in = din("c_ctx", [1, DM])
    W = {n: din(n, ([L] + s[1:]) if len(s) > 1 else s) for n, s in WSHAPES.items()}
    ident_in = din("ident", [128, 128])
    segmask_in = din("segmask", [128, NT])
    tri_in = din("tri", [2, 64, 64])
    out = nc.dram_tensor("out", [NB, 2048, DM], F32, kind="ExternalOutput").ap()
    dbg_out = {n: nc.dram_tensor("dbg_" + n, list(s), F32, kind="ExternalOutput").ap() for n, s in dbg}

    xT = dscr("xT", [NB, DM, NT])
    PF = {"g": dscr("PFg", [6272, NT]), "s": dscr("PFs", [10368, NT]), "r": dscr("PFr", [6784, NT]), "t": dscr("PFt", [6144, NT])}

    with ExitStack() as es0:
        P = Prog(nc, es0)
        S = K()
        S.nc, S.P, S.W, S.NB = nc, P, W, NB
        S.xT, S.PF = xT, PF
        S.dbg = dbg_out
        S.GOF = dscr("GOF", [NT, 2048])
        S.OG = dscr("OG", [2048, NT], BF16)
        S.CUM = dscr("CUM", [128, NT])
        S.XT = dscr("XT", [NT, 4096], BF16)
        S.BT = dscr("BT", [NT, 1024], BF16)
        S.BCF = dscr("BCF", [2048, NT], BF16)
        S.YF = dscr("YF", [NT, 4096])
        S.YT = dscr("YT", [NT, 4096])
        S.OS = dscr("OS", [4096, NT], BF16)
        CK = [36, 64, 32, 64]
        S.RW = {n_: [dscr(f"RW{n_}{d_}", CK, BF16) for d_ in range(2)] for n_ in ("A", "R", "K", "B", "KT", "BT")}
        S.RW["V"] = dscr("RWV", CK, BF16)
        S.RW["G"] = dscr("RWG", CK)
        S.RW["BON"] = dscr("RWBON", CK)
        S.RW["GAM"] = [dscr(f"RWGAM{d_}", [32, 64, 36]) for d_ in range(2)]
        S.RW["YF"] = dscr("RWYF", [NT, 2048])
        S.ORW = dscr("ORW", [2048, NT], BF16)
        mk_in = din("masks", [6, 64, 64])
        mks = [Tl(nc, es0, [64, 64], F32, name="mk") for _ in range(6)]
        for i_ in range(6):
            P.dma("sp", mks[i_][:], mk_in[i_], writes=[mks[i_]])
        S.str_, S.nstr, S.ntri = mks[0:2], mks[2:4], mks[4:6]
        S.ACTG = dscr("ACTG", [FH, NT], BF16)
        S.ACT = dscr("ACT", [FH, NT], BF16)
        S.segmask = Tl(nc, es0, [128, NT], F32, name="segmask")
        P.dma("sp", S.segmask[:], segmask_in, writes=[S.segmask])
        S.tri = [Tl(nc, es0, [64, 64], F32, name="tri") for _ in range(2)]
        for d_ in range(2):
            P.dma("sp", S.tri[d_][:], tri_in[d_], writes=[S.tri[d_]])
        S.ident = Tl(nc, es0, [128, 128], F32, name="ident")
        S.identb = Tl(nc, es0, [128, 128], BF16, name="identb")
        S.ones = Tl(nc, es0, [128, 128], F32, name="ones")
        P.dma("sp", S.ident[:], ident_in, writes=[S.ident])
        P.op("dve", lambda e: e.tensor_copy(out=S.identb[:], in_=S.ident[:]), [S.ident], [S.identb])
        P.op("dve", lambda e: e.memset(S.ones[:], 1.0), [], [S.ones])
        S.mod = Tl(nc, es0, [128, 96, 3], F32, name="mod")
        S.gs = [Tl(nc, es0, [128, 16, 3], F32, name="gs") for _ in range(2)]
        S.cact = Tl(nc, es0, [128, 16, 3], F32, name="cact")

        phase_init(S, x_in, ctx_in, c_in, cctx_in)
        P.barrier()
        done = False
        for l in range(L):
            if done:
                break
            phase_mod(S, l)
            P.barrier()
            for b in range(NB):
                with ExitStack() as es:
                    hT = Tl(nc, es, [128, 16, NT], BF16, name="hT")
                    phase_norm(S, l, b, 0, hT)
                    P.barrier()
                    if stop_after == "norm1":
                        dump_bf(S, es, hT, "hT")
                        done = True
                        break
                    phase_inproj(S, l, hT)
                    P.barrier()
                if stop_after == "ip":
                    done = True
                    break
                if "gla" not in skip:
                    phase_gla(S, l, b)
                    P.barrier()
                if stop_after == "gla":
                    done = True
                    break
                if "ssd" not in skip:
                    phase_ssd(S, l, b)
                    P.barrier()
                if stop_after == "ssd":
                    done = True
                    break
                if "rwkv" not in skip:
                    phase_rwkv(S, l, b)
                    P.barrier()
                if stop_after == "rwkv":
                    done = True
                    break
                phase_merge(S, l, b)
                P.barrier()
                if stop_after == "merge":
                    done = True
                    break
                with ExitStack() as es:
                    hT = Tl(nc, es, [128, 16, NT], BF16, name="hT2")
                    phase_norm(S, l, b, 1, hT)
                    P.barrier()
                    phase_ffn_up(S, l, hT)
                    P.barrier()
                phase_ffn_down(S, l, b)
                P.barrier()
        if not done:
            for b in range(NB):
                phase_final(S, b, out)
            P.barrier()
        if "xT" in dbg_out:
            P.dma("sp", dbg_out["xT"], S.xT[0], reads=[], writes=[])
        if "RWYF" in dbg_out:
            P.dma("sp", dbg_out["RWYF"], S.RW["YF"], reads=[], writes=[])
        if "ORW" in dbg_out:
            with ExitStack() as es:
                dump_bf2(S, es, S.ORW, "ORW", 16)
        if "YT" in dbg_out:
            P.dma("sp", dbg_out["YT"], S.YT, reads=[], writes=[])
        if "YF" in dbg_out:
            P.dma("sp", dbg_out["YF"], S.YF, reads=[], writes=[])
        if "OS" in dbg_out:
            with ExitStack() as es:
                dump_bf2(S, es, S.OS, "OS", 32)
        if "GOF" in dbg_out:
            P.dma("sp", dbg_out["GOF"], S.GOF, reads=[], writes=[])
        if "OG" in dbg_out:
            with ExitStack() as es:
                dump_bf2(S, es, S.OG, "OG", 16)
        for kk_ in ("g", "s", "r", "t"):
            if "PF" + kk_ in dbg_out:
                P.dma("sp", dbg_out["PF" + kk_], PF[kk_], reads=[], writes=[])
        P.emit()
    return nc


def dump_bf(S, es, t, name):
    nc, P = S.nc, S.P
    st = Tl(nc, es, [128, NT], F32, name="dump")
    for c in range(16):
        P.op("dve", lambda e, c=c: e.tensor_copy(out=st[:], in_=t[:, c, :]), [t], [st])
        P.dma("sp", S.dbg[name][c * 128:(c + 1) * 128, :], st[:], reads=[st], writes=[])


def dump_bf2(S, es, src, name, nch):
    nc, P = S.nc, S.P
    P.barrier()
    sb = Tl(nc, es, [128, NT], BF16, name="dumpb")
    st = Tl(nc, es, [128, NT], F32, name="dump")
    for c in range(nch):
        P.dma("sp", sb[:], src[c * 128:(c + 1) * 128, :], writes=[sb])
        P.op("dve", lambda e: e.tensor_copy(out=st[:], in_=sb[:]), [sb], [st])
        P.dma("sp", S.dbg[name][c * 128:(c + 1) * 128, :], st[:], reads=[st], writes=[])


def consts():
    seg = np.ones((128, NT), np.float32)
    seg[:, ::64] = 0.0
    s = np.arange(64)[:, None]
    q = np.arange(64)[None, :]
    tri = np.stack([(s <= q), (s >= q)]).astype(np.float32)
    eye = np.eye(64, dtype=np.float32)
    strict = tri - eye
    masks = np.concatenate([strict, -strict, -tri], 0)
    return {"ident": np.eye(128, dtype=np.float32), "segmask": seg, "tri": tri, "masks": masks}


def phase_init(S, x_in, ctx_in, c_in, cctx_in):
    nc, P = S.nc, S.P
    with ExitStack() as es:
        xt = [Tl(nc, es, [128, DM], F32, name="xt") for _ in range(2)]
        st = [Tl(nc, es, [128, 16, 128], F32, name="st") for _ in range(2)]
        ps = [Tl(nc, es, [128, 512], F32, psum=True, name="ps") for _ in range(4)]
        k = 0
        n = 0
        for b in range(S.NB):
            for tt in range(18):
                src = ctx_in[b, tt * 128:(tt + 1) * 128, :] if tt < 2 else x_in[b, (tt - 2) * 128:(tt - 1) * 128, :]
                xi = xt[n % 2]
                so = st[n % 2]
                n += 1
                P.dma("sp", xi[:], src, writes=[xi])
                for g in range(4):
                    pt = ps[k % 4]
                    k += 1
                    for j in range(4):
                        c = g * 4 + j
                        P.op("pe", lambda e, pt=pt, xi=xi, c=c, j=j: e.transpose(
                            out=pt[:, j * 128:(j + 1) * 128], in_=xi[:, c * 128:(c + 1) * 128], identity=S.ident[:]),
                            [xi, S.ident], [pt])
                    en = "dve" if g % 2 == 0 else "act"
                    if en == "dve":
                        P.op("dve", lambda e, pt=pt, so=so, g=g: e.tensor_copy(
                            out=so[:, g * 4:(g + 1) * 4, :], in_=pt[:].rearrange("p (a b) -> p a b", a=4)), [pt], [so])
                    else:
                        P.op("act", lambda e, pt=pt, so=so, g=g: e.activation(
                            out=so[:, g * 4:(g + 1) * 4, :], in_=pt[:].rearrange("p (a b) -> p a b", a=4), func=AF.Copy), [pt], [so])
                P.dma("act", S.xT[b, :, tt * 128:(tt + 1) * 128].rearrange("(c p) t -> p c t", p=128), so[:], reads=[so], writes=[])
        craw = Tl(nc, es, [128, 16, 3], F32, name="craw")
        P.op("dve", lambda e: e.memset(craw[:], 0.0), [], [craw])
        for b in range(S.NB):
            P.dma("sp", craw[:, :, b], c_in[b, :].rearrange("(c p) -> p c", p=128), reads=[], writes=[craw], allow_slow_non_contiguous=True)
        P.dma("sp", craw[:, :, 2], cctx_in[0, :].rearrange("(c p) -> p c", p=128), reads=[], writes=[craw], allow_slow_non_contiguous=True)
        P.op("act", lambda e: e.activation(out=S.cact[:], in_=craw[:], func=AF.Silu), [craw], [S.cact])


def phase_mod(S, l):
    nc, P, W = S.nc, S.P, S.W
    with ExitStack() as es:
        wt = [Tl(nc, es, [128, 16, 512], F32, name="wm") for _ in range(2)]
        ps = [Tl(nc, es, [128, 4, 3], F32, psum=True, name="psm") for _ in range(2)]
        bm = Tl(nc, es, [128, 96], F32, name="bm")
        prm = Tl(nc, es, [128, 2, 16], F32, name="prm")
        P.dma("sp", bm[:], W["b_mod"][l, :].rearrange("(c p) -> p c", p=128), writes=[bm], allow_slow_non_contiguous=True)
        P.dma("sp", prm[:, 0, :], W["norm1_g"][l, :].rearrange("(c p) -> p c", p=128), writes=[prm], allow_slow_non_contiguous=True)
        P.dma("sp", prm[:, 1, :], W["norm2_g"][l, :].rearrange("(c p) -> p c", p=128), writes=[prm], allow_slow_non_contiguous=True)
        for blk in range(24):
            w = wt[blk % 2]
            pt = ps[blk % 2]
            P.dma("sp" if blk % 2 == 0 else "act", w[:], W["w_mod"][l, :, blk * 512:(blk + 1) * 512].rearrange("(kc p) n -> p kc n", p=128), writes=[w])
            for j in range(4):
                for kc in range(16):
                    P.op("pe", lambda e, w=w, pt=pt, j=j, kc=kc: e.matmul(
                        out=pt[:, j, :], lhsT=w[:, kc, j * 128:(j + 1) * 128], rhs=S.cact[:, kc, :], start=(kc == 0), stop=(kc == 15)),
                        [w, S.cact], [pt])
            P.op("dve", lambda e, pt=pt, blk=blk: e.tensor_tensor(
                out=S.mod[:, blk * 4:(blk + 1) * 4, :], in0=pt[:], in1=bc(bm[:, blk * 4:(blk + 1) * 4].unsqueeze(2), [128, 4, 3]), op=ALU.add),
                [pt, bm], [S.mod])
        for i in range(2):
            sc = S.mod[:, (3 * i + 1) * 16:(3 * i + 2) * 16, :]
            P.op("dve", lambda e, i=i, sc=sc: e.scalar_tensor_tensor(
                out=S.gs[i][:], in0=sc, scalar=1.0, in1=bc(prm[:, i, :].unsqueeze(2), [128, 16, 3]), op0=ALU.add, op1=ALU.mult),
                [S.mod, prm], [S.gs[i]])


def phase_norm(S, l, b, which, hT):
    nc, P = S.nc, S.P
    with ExitStack() as es:
        xc = [Tl(nc, es, [128, NT], F32, name="xc") for _ in range(3)]
        sq = [Tl(nc, es, [128, NT], F32, name="sq") for _ in range(2)]
        ps = [Tl(nc, es, [128, 512], F32, psum=True, name="psn") for _ in range(5)]
        rstd = Tl(nc, es, [128, NT], F32, name="rstd")
        epsb = Tl(nc, es, [128, 1], F32, name="epsb")
        P.op("dve", lambda e: e.memset(epsb[:], 1e-6), [], [epsb])
        for c in range(16):
            xi, si = xc[c % 3], sq[c % 2]
            P.dma("sp" if c % 2 == 0 else "act", xi[:], S.xT[b, c * 128:(c + 1) * 128, :], writes=[xi])
            if c % 2 == 0:
                P.op("act", lambda e, xi=xi, si=si: e.activation(out=si[:], in_=xi[:], func=AF.Square), [xi], [si])
            else:
                P.op("dve", lambda e, xi=xi, si=si: e.tensor_tensor(out=si[:], in0=xi[:], in1=xi[:], op=ALU.mult), [xi], [si])
            for ti, (t0, tn) in enumerate(TT):
                P.op("pe", lambda e, si=si, ti=ti, t0=t0, tn=tn, c=c: e.matmul(
                    out=ps[ti][:, :tn], lhsT=S.ones[:], rhs=si[:, t0:t0 + tn], start=(c == 0), stop=(c == 15)),
                    [si, S.ones], [ps[ti]])
        for ti, (t0, tn) in enumerate(TT):
            P.op("act", lambda e, ti=ti, t0=t0, tn=tn: e.activation(
                out=rstd[:, t0:t0 + tn], in_=ps[ti][:, :tn], func=AF.Sqrt, bias=epsb[:], scale=1.0 / DM), [ps[ti], epsb], [rstd])
        P.op("dve", lambda e: e.reciprocal(out=rstd[:], in_=rstd[:]), [rstd], [rstd])
        shi = 3 * which
        for c in range(16):
            xi, si = xc[c % 3], sq[c % 2]
            P.dma("sp" if c % 2 == 0 else "act", xi[:], S.xT[b, c * 128:(c + 1) * 128, :], writes=[xi])
            for (s0, s1), v in zip(SEGS, (2, b)):
                P.op("dve", lambda e, xi=xi, si=si, s0=s0, s1=s1, v=v, c=c: e.scalar_tensor_tensor(
                    out=si[:, s0:s1], in0=xi[:, s0:s1], scalar=S.gs[which][:, c, v:v + 1], in1=rstd[:, s0:s1], op0=ALU.mult, op1=ALU.mult),
                    [xi, rstd, S.gs[which]], [si])
                P.op("act", lambda e, si=si, s0=s0, s1=s1, v=v, c=c: e.activation(
                    out=hT[:, c, s0:s1], in_=si[:, s0:s1], func=AF.Identity, bias=S.mod[:, shi * 16 + c, v:v + 1], scale=1.0),
                    [si, S.mod], [hT])


def proj(S, es, w_ap, ncols, rhs, KC, evac, wq="pool", blk=512, nps=7):
    nc, P = S.nc, S.P
    wt = [Tl(nc, es, [128, KC, blk], BF16, name="wp") for _ in range(2)]
    ps = [Tl(nc, es, [128, 512], F32, psum=True, name="psp") for _ in range(nps)]
    k = 0
    nblk = (ncols + blk - 1) // blk
    for bi in range(nblk):
        w = wt[bi % 2]
        c0 = bi * blk
        cn = min(blk, ncols - c0)
        P.dma(wq, w[:, :, :cn], w_ap[:, c0:c0 + cn].rearrange("(kc p) n -> p kc n", p=128), writes=[w])
        for j in range((cn + 127) // 128):
            n = min(128, cn - j * 128)
            tiles = []
            for (t0, tn) in TT:
                pt = ps[k % nps]
                k += 1
                for kc in range(KC):
                    P.op("pe", lambda e, pt=pt, w=w, j=j, n=n, kc=kc, t0=t0, tn=tn: e.matmul(
                        out=pt[:n, :tn], lhsT=w[:, kc, j * 128:j * 128 + n], rhs=rhs[:, kc, t0:t0 + tn], start=(kc == 0), stop=(kc == KC - 1)),
                        [w, rhs], [pt])
                tiles.append((pt, t0, tn))
            evac(bi * (blk // 128) + j, n, tiles)


def phase_inproj(S, l, hT):
    nc, P = S.nc, S.P
    with ExitStack() as es:
        st = [Tl(nc, es, [128, NT], F32, name="ipst") for _ in range(3)]
        cnt = [0]

        def evac(ci, n, tiles, key):
            so = st[cnt[0] % 3]
            cnt[0] += 1
            for i, (pt, t0, tn) in enumerate(tiles):
                if i % 2 == 0:
                    P.op("dve", lambda e, pt=pt, t0=t0, tn=tn, so=so, n=n: e.tensor_copy(out=so[:n, t0:t0 + tn], in_=pt[:n, :tn]), [pt], [so])
                else:
                    P.op("act", lambda e, pt=pt, t0=t0, tn=tn, so=so, n=n: e.activation(out=so[:n, t0:t0 + tn], in_=pt[:n, :tn], func=AF.Copy), [pt], [so])
            P.dma("sp" if ci % 2 == 0 else "act", S.PF[key][ci * 128:ci * 128 + n, :], so[:n, :], reads=[so], writes=[])

        for key, c0, cn in (("g", 0, 6176), ("s", 6176, 10368), ("r", 16544, 6784), ("t", 23328, 6144)):
            with ExitStack() as es2:
                proj(S, es2, S.W["w_in"][l][:, c0:c0 + cn], cn, hT, 16, lambda ci, n, tiles, key=key: evac(ci, n, tiles, key))
            P.barrier()


def chunk_order(rev):
    return list(range(36)) if not rev else [3, 2, 1, 0] + list(range(35, 3, -1))


def seg_cumsum(S, out, src, tot, tmp, rev):
    P = S.P
    P.op("dve", lambda e: e.tensor_tensor_scan(out=out[:], data0=S.segmask[:out.t.shape[0], :], data1=src[:], initial=0.0, op0=ALU.mult, op1=ALU.add),
         [src, S.segmask], [out])
    P.op("dve", lambda e: e.tensor_copy(out=tot[:], in_=out[:].rearrange("p (c q) -> p c q", q=64)[:, :, 63]), [out], [tot])
    if rev:
        P.op("dve", lambda e: e.tensor_tensor(out=tmp[:], in0=src[:], in1=out[:], op=ALU.subtract), [src, out], [tmp])
        P.op("dve", lambda e: e.tensor_tensor(out=out[:].rearrange("p (c q) -> p c q", q=64), in0=tmp[:].rearrange("p (c q) -> p c q", q=64),
                                              in1=bc(tot[:].unsqueeze(2), [out.t.shape[0], 36, 64]), op=ALU.add), [tmp, tot], [out])


def phase_gla(S, l, b):
    nc, P, W = S.nc, S.P, S.W
    PFg = S.PF["g"]
    with ExitStack() as es:
        adT = [Tl(nc, es, [17, NT], F32, name="adT") for _ in range(2)]
        waA = [Tl(nc, es, [17, 1024], F32, name="waA") for _ in range(2)]
        ng = Tl(nc, es, [128, 4], F32, name="ng")
        vT = Tl(nc, es, [64, 36, 512], BF16, name="vT")
        sg = Tl(nc, es, [128, 4, NT], BF16, name="sg")
        ogh = Tl(nc, es, [128, 4, NT], BF16, name="ogh")
        qd = Tl(nc, es, [128, 2, NT], BF16, name="qd")
        ki = Tl(nc, es, [128, 2, NT], BF16, name="ki")
        kdT = Tl(nc, es, [128, 2, NT], BF16, name="kdT")
        etot = Tl(nc, es, [128, 2, 36], F32, name="etot")
        Sf = Tl(nc, es, [128, 2, 512], F32, name="Sf")
        Sb = Tl(nc, es, [128, 2, 512], BF16, name="Sb")
        eps = Tl(nc, es, [128, 1], F32, name="epsg")
        P.op("dve", lambda e: e.memset(eps[:], 1e-6), [], [eps])
        P.dma("sp", ng[:], W["gla_norm_g"][l, :].rearrange("(c p) -> p c", p=128), writes=[ng], allow_slow_non_contiguous=True)
        for d in range(2):
            P.op("dve", lambda e, d=d: e.memset(adT[d][:], 1.0), [], [adT[d]])
            P.dma("sp", adT[d][0:16, :], PFg[O_AD + d * 16:O_AD + d * 16 + 16, :], writes=[adT[d]])
            P.dma("sp", waA[d][0:16, :], W["gla_wa_up"][l, d], writes=[waA[d]])
            P.dma("sp", waA[d][16:17, :], W["gla_ba"][l, d:d + 1, :], writes=[waA[d]])
        for h in range(4):
            with ExitStack() as es1:
                ld = [Tl(nc, es1, [128, NT], F32, name="gld") for _ in range(2)]
                pv = [Tl(nc, es1, [64, 4, 128], F32, psum=True, name="pv") for _ in range(2)]
                for j in range(4):
                    P.dma("sp", ld[j % 2][:], PFg[O_G + h * 512 + j * 128:O_G + h * 512 + (j + 1) * 128, :], writes=[ld[j % 2]])
                    P.op("act", lambda e, j=j: e.activation(out=sg[:, j, :], in_=ld[j % 2][:], func=AF.Silu), [ld[j % 2]], [sg])
                for j in range(4):
                    vl = ld[j % 2]
                    P.dma("act", vl[:], PFg[O_V + h * 512 + j * 128:O_V + h * 512 + (j + 1) * 128, :], writes=[vl])
                    for cg in range(9):
                        pt = pv[cg % 2]
                        for a in range(4):
                            ck = cg * 4 + a
                            P.op("pe", lambda e, pt=pt, a=a, ck=ck, vl=vl: e.transpose(out=pt[:, a, :], in_=vl[:, ck * 64:(ck + 1) * 64], identity=S.ident[:]),
                                 [vl, S.ident], [pt])
                        if cg % 2 == 0:
                            P.op("dve", lambda e, pt=pt, cg=cg, j=j: e.tensor_copy(out=vT[:, cg * 4:cg * 4 + 4, j * 128:(j + 1) * 128], in_=pt[:]), [pt], [vT])
                        else:
                            P.op("act", lambda e, pt=pt, cg=cg, j=j: e.activation(out=vT[:, cg * 4:cg * 4 + 4, j * 128:(j + 1) * 128], in_=pt[:], func=AF.Copy), [pt], [vT])
            P.barrier()
            for d in range(2):
                rev = d == 1
                with ExitStack() as es1:
                    pl = [Tl(nc, es1, [128, 512], F32, psum=True, name="pl") for _ in range(5)]
                    nl = Tl(nc, es1, [128, NT], F32, name="nl")
                    Bc = Tl(nc, es1, [128, NT], F32, name="Bc")
                    tmp = Tl(nc, es1, [128, NT], F32, name="tmp")
                    ex = Tl(nc, es1, [128, NT], F32, name="ex")
                    qf = Tl(nc, es1, [128, NT], F32, name="qf")
                    kf = Tl(nc, es1, [128, NT], F32, name="kf")
                    tot = Tl(nc, es1, [128, 36], F32, name="tot")
                    for dc in range(2):
                        dg = h * 256 + dc * 128
                        P.dma("sp", qf[:], PFg[O_Q + dg:O_Q + dg + 128, :], writes=[qf])
                        P.dma("act", kf[:], PFg[O_K + dg:O_K + dg + 128, :], writes=[kf])
                        for ti, (t0, tn) in enumerate(TT):
                            P.op("pe", lambda e, ti=ti, t0=t0, tn=tn, dg=dg: e.matmul(
                                out=pl[ti][:, :tn], lhsT=waA[d][:, dg:dg + 128], rhs=adT[d][:, t0:t0 + tn], start=True, stop=True),
                                [waA[d], adT[d]], [pl[ti]])
                            P.op("act", lambda e, ti=ti, t0=t0, tn=tn: e.activation(out=nl[:, t0:t0 + tn], in_=pl[ti][:, :tn], func=AF.Exp, scale=-1.0),
                                 [pl[ti]], [nl])
                        P.op("act", lambda e: e.activation(out=nl[:], in_=nl[:], func=AF.Ln, bias=S.ones[:, 0:1], scale=1.0), [nl, S.ones], [nl])
                        seg_cumsum(S, Bc, nl, tot, tmp, rev)
                        P.op("act", lambda e, dc=dc: e.activation(out=etot[:, dc, :], in_=tot[:], func=AF.Exp, scale=-1.0 / 16), [tot], [etot])
                        P.op("act", lambda e: e.activation(out=ex[:], in_=Bc[:], func=AF.Exp, scale=-1.0 / 16), [Bc], [ex])
                        P.op("dve", lambda e, dc=dc: e.scalar_tensor_tensor(out=qd[:, dc, :], in0=qf[:], scalar=1.0 / 16, in1=ex[:], op0=ALU.mult, op1=ALU.mult),
                             [qf, ex], [qd])
                        P.op("act", lambda e: e.activation(out=ex[:], in_=Bc[:], func=AF.Exp, scale=1.0 / 16), [Bc], [ex])
                        P.op("dve", lambda e, dc=dc: e.tensor_tensor(out=ki[:, dc, :], in0=kf[:], in1=ex[:], op=ALU.mult), [kf, ex], [ki])
                        P.op("dve", lambda e: e.tensor_tensor(out=tmp[:].rearrange("p (c q) -> p c q", q=64), in0=Bc[:].rearrange("p (c q) -> p c q", q=64),
                                                              in1=bc(tot[:].unsqueeze(2), [128, 36, 64]), op=ALU.subtract), [Bc, tot], [tmp])
                        P.op("act", lambda e: e.activation(out=ex[:], in_=tmp[:], func=AF.Exp, scale=1.0 / 16), [tmp], [ex])
                        P.op("dve", lambda e, dc=dc: e.tensor_tensor(out=kdT[:, dc, :], in0=kf[:], in1=ex[:], op=ALU.mult), [kf, ex], [kdT])
                P.barrier()
                with ExitStack() as es1:
                    pa = Tl(nc, es1, [64, 64], F32, psum=True, name="pa")
                    pk = Tl(nc, es1, [64, 2, 128], BF16, psum=True, name="pk")
                    po = [Tl(nc, es1, [64, 512], F32, psum=True, name="po") for _ in range(2)]
                    pu = [Tl(nc, es1, [128, 512], F32, psum=True, name="pu") for _ in range(2)]
                    ptr = Tl(nc, es1, [128, 4, 64], F32, psum=True, name="ptr")
                    att = [Tl(nc, es1, [64, 64], BF16, name="att") for _ in range(2)]
                    kd = [Tl(nc, es1, [64, 256], BF16, name="kd") for _ in range(2)]
                    ost = [Tl(nc, es1, [64, 512], F32, name="ost") for _ in range(2)]
                    oft = [Tl(nc, es1, [64, 512], F32, name="oft") for _ in range(2)]
                    sqt = Tl(nc, es1, [64, 512], F32, name="sqt")
                    ss = Tl(nc, es1, [64, 1], F32, name="ss")
                    P.op("dve", lambda e: e.memset(Sf[:], 0.0), [], [Sf])
                    P.op("dve", lambda e: e.memset(Sb[:], 0.0), [], [Sb])
                    deferred = []
                    for i, ck in enumerate(chunk_order(rev)):
                        ts = slice(ck * 64, ck * 64 + 64)
                        a_, k_, o_, f_, p_ = att[i % 2], kd[i % 2], ost[i % 2], oft[i % 2], po[i % 2]
                        for dc in range(2):
                            P.op("pe", lambda e, dc=dc, ts=ts: e.matmul(out=pa[:], lhsT=ki[:, dc, ts], rhs=qd[:, dc, ts], start=(dc == 0), stop=(dc == 1)),
                                 [ki, qd], [pa])
                        P.op("dve", lambda e, a_=a_: e.tensor_tensor(out=a_[:], in0=pa[:], in1=S.tri[d][:], op=ALU.mult), [pa, S.tri[d]], [a_])
                        for dc in range(2):
                            P.op("pe", lambda e, dc=dc, ts=ts: e.transpose(out=pk[:, dc, :], in_=kdT[:, dc, ts], identity=S.identb[:]), [kdT, S.identb], [pk])
                        P.op("act", lambda e, k_=k_: e.activation(out=k_[:], in_=pk[:].rearrange("p a b -> p (a b)"), func=AF.Copy), [pk], [k_])
                        for dc in range(2):
                            P.op("pe", lambda e, dc=dc, ts=ts, p_=p_: e.matmul(out=p_[:], lhsT=qd[:, dc, ts], rhs=Sb[:, dc, :], start=(dc == 0), stop=False),
                                 [qd, Sb], [p_])
                        P.op("pe", lambda e, a_=a_, ck=ck, p_=p_: e.matmul(out=p_[:], lhsT=a_[:], rhs=vT[:, ck, :], start=False, stop=True), [a_, vT], [p_])
                        for dc in range(2):
                            P.op("pe", lambda e, dc=dc, k_=k_, ck=ck: e.matmul(out=pu[dc][:], lhsT=k_[:, dc * 128:(dc + 1) * 128], rhs=vT[:, ck, :], start=True, stop=True),
                                 [k_, vT], [pu[dc]])
                            P.op("dve", lambda e, dc=dc, ck=ck: e.scalar_tensor_tensor(out=Sf[:, dc, :], in0=Sf[:, dc, :], scalar=etot[:, dc, ck:ck + 1], in1=pu[dc][:],
                                                                                       op0=ALU.mult, op1=ALU.add), [Sf, etot, pu[dc]], [Sf])
                            P.op("act", lambda e, dc=dc: e.activation(out=Sb[:, dc, :], in_=Sf[:, dc, :], func=AF.Copy), [Sf], [Sb])
                        while len(deferred) > 0:
                            deferred.pop(0)()
                        gof = S.GOF[ts, h * 512:(h + 1) * 512]
                        if not rev:
                            P.op("act", lambda e, o_=o_, p_=p_: e.activation(out=o_[:], in_=p_[:], func=AF.Copy), [p_], [o_])
                            P.dma("sp", gof, o_[:], reads=[o_], writes=[])
                        else:
                            P.dma("sp", f_[:], gof, writes=[f_])
                            P.op("dve", lambda e, o_=o_, p_=p_, f_=f_: e.tensor_tensor(out=o_[:], in0=p_[:], in1=f_[:], op=ALU.add), [p_, f_], [o_])
                            P.op("act", lambda e, o_=o_: e.activation(out=sqt[:], in_=o_[:], func=AF.Square), [o_], [sqt])
                            P.op("dve", lambda e: e.reduce_sum(out=ss[:], in_=sqt[:], axis=AX.X), [sqt], [ss])
                            P.op("act", lambda e: e.activation(out=ss[:], in_=ss[:], func=AF.Sqrt, bias=eps[:64, :], scale=1.0 / 512), [ss, eps], [ss])
                            P.op("dve", lambda e: e.reciprocal(out=ss[:], in_=ss[:]), [ss], [ss])
                            P.op("dve", lambda e, o_=o_: e.tensor_scalar(out=o_[:], in0=o_[:], scalar1=ss[:, 0:1], scalar2=None, op0=ALU.mult), [o_, ss], [o_])
                            def post(o_=o_, ts=ts):
                                for j in range(4):
                                    P.op("pe", lambda e: e.transpose(out=ptr[:, j, :], in_=o_[:, j * 128:(j + 1) * 128], identity=S.ident[:64, :64]), [o_, S.ident], [ptr])
                                for j in range(4):
                                    P.op("dve", lambda e: e.scalar_tensor_tensor(out=ogh[:, j, ts], in0=ptr[:, j, :], scalar=ng[:, j:j + 1], in1=sg[:, j, ts],
                                                                                 op0=ALU.mult, op1=ALU.mult), [ptr, ng, sg], [ogh])
                            deferred.append(post)
                    while len(deferred) > 0:
                        deferred.pop(0)()
                P.barrier()
            for j in range(4):
                P.dma("sp", S.OG[h * 512 + j * 128:h * 512 + (j + 1) * 128, :], ogh[:, j, :], reads=[ogh], writes=[])
            P.barrier()


def vec_pp(S, es, src_ap, nchunk, name):
    t = Tl(S.nc, es, [128, nchunk], F32, name=name)
    S.P.dma("sp", t[:], src_ap.rearrange("(c p) -> p c", p=128), writes=[t], allow_slow_non_contiguous=True)
    return t


def phase_ssd(S, l, b):
    nc, P, W = S.nc, S.P, S.W
    PFs = S.PF["s"]
    OX, OB, OC, ODT = 4096, 8192, 9216, 10240
    with ExitStack() as es:
        dtT = Tl(nc, es, [64, 36, 128], F32, name="dtT")
        cumT = Tl(nc, es, [64, 36, 128], F32, name="cumT")
        laT = Tl(nc, es, [64, 36, 128], F32, name="laT")
        Dbc = Tl(nc, es, [64, 64], F32, name="Dbc")
        P.dma("sp", Dbc[:], bc(W["ssm_d"][l:l + 1, :], [64, 64]), writes=[Dbc])
        with ExitStack() as es1:
            raw = Tl(nc, es1, [128, NT], F32, name="raw")
            la = Tl(nc, es1, [128, NT], F32, name="la")
            cum = Tl(nc, es1, [128, NT], F32, name="cum")
            tmp = Tl(nc, es1, [128, NT], F32, name="tmp")
            tot = Tl(nc, es1, [128, 36], F32, name="tot")
            pp = Tl(nc, es1, [128, 2], F32, name="pp")
            pt = [Tl(nc, es1, [64, 4, 128], F32, psum=True, name="pts") for _ in range(2)]
            P.dma("sp", raw[:], PFs[ODT:ODT + 128, :], writes=[raw])
            P.dma("sp", pp[:, 0:1], W["ssm_dt_bias"][l].rearrange("d (h o) -> (d h) o", o=1), writes=[pp])
            P.dma("sp", pp[:, 1:2], W["ssm_a_log"][l].rearrange("d (h o) -> (d h) o", o=1), writes=[pp])
            P.op("act", lambda e: e.activation(out=pp[:, 1:2], in_=pp[:, 1:2], func=AF.Exp), [pp], [pp])
            P.op("dve", lambda e: e.tensor_scalar(out=pp[:, 1:2], in0=pp[:, 1:2], scalar1=-1.0, scalar2=None, op0=ALU.mult), [pp], [pp])
            P.op("act", lambda e: e.activation(out=raw[:], in_=raw[:], func=AF.Exp, bias=pp[:, 0:1], scale=1.0), [raw, pp], [raw])
            P.op("act", lambda e: e.activation(out=raw[:], in_=raw[:], func=AF.Ln, bias=S.ones[:, 0:1], scale=1.0), [raw, S.ones], [raw])
            P.op("dve", lambda e: e.tensor_scalar(out=la[:], in0=raw[:], scalar1=pp[:, 1:2], scalar2=None, op0=ALU.mult), [raw, pp], [la])
            P.op("dve", lambda e: e.tensor_tensor_scan(out=cum[:], data0=S.segmask[:], data1=la[:], initial=0.0, op0=ALU.mult, op1=ALU.add), [la, S.segmask], [cum])
            P.op("dve", lambda e: e.tensor_copy(out=tot[:], in_=cum[:].rearrange("p (c q) -> p c q", q=64)[:, :, 63]), [cum], [tot])
            P.op("dve", lambda e: e.tensor_tensor(out=tmp[64:128, :], in0=la[64:128, :], in1=cum[64:128, :], op=ALU.subtract), [la, cum], [tmp])
            P.op("dve", lambda e: e.tensor_tensor(out=cum[64:128, :].rearrange("p (c q) -> p c q", q=64), in0=tmp[64:128, :].rearrange("p (c q) -> p c q", q=64),
                                                  in1=bc(tot[64:128, :].unsqueeze(2), [64, 36, 64]), op=ALU.add), [tmp, tot], [cum])
            P.dma("sp", S.CUM, cum[:], reads=[cum], writes=[])
            k = 0
            for src, dst in ((raw, dtT), (cum, cumT), (la, laT)):
                for cg in range(9):
                    p_ = pt[k % 2]
                    k += 1
                    for a in range(4):
                        ck = cg * 4 + a
                        P.op("pe", lambda e: e.transpose(out=p_[:, a, :], in_=src[:, ck * 64:(ck + 1) * 64], identity=S.ident[:]), [src, S.ident], [p_])
                    P.op("act", lambda e: e.activation(out=dst[:, cg * 4:cg * 4 + 4, :], in_=p_[:], func=AF.Copy), [p_], [dst])
        P.barrier()
        with ExitStack() as es1:
            cw = [vec_pp(S, es1, W["ssm_conv_w"][l, j, :], 48, "cw") for j in range(3)]
            cb = vec_pp(S, es1, W["ssm_conv_b"][l, :], 48, "cb")
            xin = [Tl(nc, es1, [128, NT], F32, name="xin") for _ in range(2)]
            yc = [Tl(nc, es1, [128, NT], F32, name="yc") for _ in range(2)]
            yb = [Tl(nc, es1, [128, NT], BF16, name="yb") for _ in range(2)]
            stg = [Tl(nc, es1, [128, 18, 128], BF16, name="stg") for _ in range(2)]
            pt = [Tl(nc, es1, [128, 4, 128], BF16, psum=True, name="ptc") for _ in range(2)]
            k = 0
            for c in range(48):
                xi, yi, ybi, sg_ = xin[c % 2], yc[c % 2], yb[c % 2], stg[c % 2]
                P.dma("sp" if c % 2 == 0 else "act", xi[:], PFs[OX + c * 128:OX + (c + 1) * 128, :], writes=[xi])
                P.op("dve", lambda e: e.tensor_scalar(out=yi[:], in0=xi[:], scalar1=cw[1][:, c:c + 1], scalar2=None, op0=ALU.mult), [xi, cw[1]], [yi])
                for (s0, s1) in SEGS:
                    P.op("dve", lambda e: e.scalar_tensor_tensor(out=yi[:, s0 + 1:s1], in0=xi[:, s0:s1 - 1], scalar=cw[0][:, c:c + 1], in1=yi[:, s0 + 1:s1],
                                                                 op0=ALU.mult, op1=ALU.add), [xi, cw[0], yi], [yi])
                    P.op("dve", lambda e: e.scalar_tensor_tensor(out=yi[:, s0:s1 - 1], in0=xi[:, s0 + 1:s1], scalar=cw[2][:, c:c + 1], in1=yi[:, s0:s1 - 1],
                                                                 op0=ALU.mult, op1=ALU.add), [xi, cw[2], yi], [yi])
                P.op("act", lambda e: e.activation(out=ybi[:], in_=yi[:], func=AF.Silu, bias=cb[:, c:c + 1], scale=1.0), [yi, cb], [ybi])
                if c >= 32:
                    P.dma("sp", S.BCF[(c - 32) * 128:(c - 31) * 128, :], ybi[:], reads=[ybi], writes=[])
                if c < 40:
                    for tg in range(5):
                        p_ = pt[k % 2]
                        k += 1
                        na = 4 if tg < 4 else 2
                        for a in range(na):
                            tt = tg * 4 + a
                            P.op("pe", lambda e: e.transpose(out=p_[:, a, :], in_=ybi[:, tt * 128:(tt + 1) * 128], identity=S.identb[:]), [ybi, S.identb], [p_])
                        if tg % 2 == 0:
                            P.op("dve", lambda e: e.tensor_copy(out=sg_[:, tg * 4:tg * 4 + na, :], in_=p_[:, :na, :]), [p_], [sg_])
                        else:
                            P.op("act", lambda e: e.activation(out=sg_[:, tg * 4:tg * 4 + na, :], in_=p_[:, :na, :], func=AF.Copy), [p_], [sg_])
                    dst = S.XT[:, c * 128:(c + 1) * 128] if c < 32 else S.BT[:, (c - 32) * 128:(c - 31) * 128]
                    P.dma("act", dst.rearrange("(tt p) c -> p tt c", p=128), sg_[:], reads=[sg_], writes=[])
        P.barrier()
        for g in range(8):
            with ExitStack() as es1:
                BF = Tl(nc, es1, [128, NT], BF16, name="BF")
                CF = Tl(nc, es1, [128, NT], BF16, name="CF")
                P.dma("sp", BF[:], S.BCF[g * 128:(g + 1) * 128, :], writes=[BF])
                P.dma("sp", CF[:], S.BCF[1024 + g * 128:1024 + (g + 1) * 128, :], writes=[CF])
                def two(shape, dt, name, psum=False):
                    return [[Tl(nc, es1, shape, dt, psum=psum, name=name) for _ in range(2)] for _ in range(2)]
                hf = [Tl(nc, es1, [128, 512], F32, name="hf") for _ in range(2)]
                hb = [Tl(nc, es1, [128, 512], BF16, name="hb") for _ in range(2)]
                xc, bt, cbc, wd = two([64, 512], BF16, "xc"), two([64, 128], BF16, "bt"), two([64, 8, 64], F32, "cbc"), two([64, 8, 64], BF16, "wd")
                cbm, xd, ys, yf = two([64, 64], F32, "cbm"), two([64, 512], BF16, "xd"), two([64, 512], F32, "ys"), two([64, 512], F32, "yf")
                ec, sd, etb = two([64, 8], F32, "ec"), two([64, 8], F32, "sd"), two([128, 8], F32, "etb")
                pcb = Tl(nc, es1, [64, 64], F32, psum=True, name="pcb")
                py1 = [Tl(nc, es1, [64, 512], F32, psum=True, name="py1") for _ in range(2)]
                py2 = [Tl(nc, es1, [64, 512], F32, psum=True, name="py2") for _ in range(2)]
                pst = Tl(nc, es1, [128, 512], F32, psum=True, name="pst")
                ptb = Tl(nc, es1, [128, 8], F32, psum=True, name="ptb")
                orders = [chunk_order(False), chunk_order(True)]
                pos = [{c: i for i, c in enumerate(o)} for o in orders]
                for d in range(2):
                    P.op("dve", lambda e: e.memset(hf[d][:], 0.0), [], [hf[d]])
                    P.op("dve", lambda e: e.memset(hb[d][:], 0.0), [], [hb[d]])

                def A1(d, i):
                    ck, r = orders[d][i], i % 2
                    ts = slice(ck * 64, ck * 64 + 64)
                    hs = slice(d * 64 + g * 8, d * 64 + g * 8 + 8)
                    P.dma("sp", xc[d][r][:], S.XT[ts, g * 512:(g + 1) * 512], writes=[xc[d][r]])
                    P.dma("sp", bt[d][r][:], S.BT[ts, g * 128:(g + 1) * 128], writes=[bt[d][r]])
                    P.dma("act", cbc[d][r][:], bc(S.CUM[d * 64 + g * 8:d * 64 + g * 8 + 8, ts].unsqueeze(0), [64, 8, 64]), writes=[cbc[d][r]])
                    P.op("pe", lambda e: e.matmul(out=pcb[:], lhsT=BF[:, ts], rhs=CF[:, ts], start=True, stop=True), [BF, CF], [pcb])
                    P.op("dve", lambda e: e.tensor_tensor(out=cbm[d][r][:], in0=pcb[:], in1=S.tri[d][:], op=ALU.mult), [pcb, S.tri[d]], [cbm[d][r]])
                    P.op("dve", lambda e: e.tensor_tensor(out=cbc[d][r][:], in0=cbc[d][r][:], in1=bc(cumT[:, ck, hs].unsqueeze(2), [64, 8, 64]), op=ALU.subtract),
                         [cbc[d][r], cumT], [cbc[d][r]])
                    P.op("pool", lambda e: e.tensor_scalar_min(out=cbc[d][r][:], in0=cbc[d][r][:], scalar1=0.0), [cbc[d][r]], [cbc[d][r]])
                    P.op("act", lambda e: e.activation(out=cbc[d][r][:], in_=cbc[d][r][:], func=AF.Exp), [cbc[d][r]], [cbc[d][r]])
                    P.op("pool", lambda e: e.tensor_tensor(out=cbc[d][r][:], in0=cbc[d][r][:], in1=bc(dtT[:, ck, hs].unsqueeze(2), [64, 8, 64]), op=ALU.mult),
                         [cbc[d][r], dtT], [cbc[d][r]])
                    P.op("act", lambda e: e.activation(out=ec[d][r][:], in_=cumT[:, ck, hs], func=AF.Exp), [cumT], [ec[d][r]])

                def A2(d, i):
                    ck, r = orders[d][i], i % 2
                    hs = slice(d * 64 + g * 8, d * 64 + g * 8 + 8)
                    P.op("dve", lambda e: e.tensor_tensor(out=wd[d][r][:], in0=cbc[d][r][:], in1=bc(cbm[d][r][:].unsqueeze(1), [64, 8, 64]), op=ALU.mult),
                         [cbc[d][r], cbm[d][r]], [wd[d][r]])
                    P.op("pe", lambda e: e.matmul(out=ptb[:], lhsT=S.ones[:64, :], rhs=laT[:, ck, hs], start=True, stop=True), [S.ones, laT], [ptb])
                    P.op("dve", lambda e: e.tensor_tensor(out=sd[d][r][:], in0=ptb[:64, :], in1=cumT[:, ck, hs], op=ALU.subtract), [ptb, cumT], [sd[d][r]])
                    P.op("act", lambda e: e.activation(out=etb[d][r][:], in_=ptb[:], func=AF.Exp), [ptb], [etb[d][r]])
                    P.op("act", lambda e: e.activation(out=sd[d][r][:], in_=sd[d][r][:], func=AF.Exp), [sd[d][r]], [sd[d][r]])
                    P.op("pool", lambda e: e.tensor_tensor(out=sd[d][r][:], in0=sd[d][r][:], in1=dtT[:, ck, hs], op=ALU.mult), [sd[d][r], dtT], [sd[d][r]])
                    P.op("pool", lambda e: e.tensor_tensor(out=xd[d][r][:].rearrange("p (a b) -> p a b", a=8), in0=xc[d][r][:].rearrange("p (a b) -> p a b", a=8),
                                                           in1=bc(sd[d][r][:].unsqueeze(2), [64, 8, 64]), op=ALU.mult), [xc[d][r], sd[d][r]], [xd[d][r]])

                def Bst(d, i):
                    ck, r = orders[d][i], i % 2
                    ts = slice(ck * 64, ck * 64 + 64)
                    for e_ in range(8):
                        P.op("pe", lambda e: e.matmul(out=py1[d][:, e_ * 64:(e_ + 1) * 64], lhsT=wd[d][r][:, e_, :], rhs=xc[d][r][:, e_ * 64:(e_ + 1) * 64], start=True, stop=True),
                             [wd[d][r], xc[d][r]], [py1[d]])
                    P.op("pe", lambda e: e.matmul(out=py2[d][:], lhsT=CF[:, ts], rhs=hb[d][:], start=True, stop=True), [CF, hb[d]], [py2[d]])
                    P.op("pe", lambda e: e.matmul(out=pst[:], lhsT=bt[d][r][:], rhs=xd[d][r][:], start=True, stop=True), [bt[d][r], xd[d][r]], [pst])
                    y_, f_ = ys[d][r], yf[d][r]
                    P.op("dve", lambda e: e.tensor_tensor(out=y_[:].rearrange("p (a b) -> p a b", a=8), in0=py2[d][:].rearrange("p (a b) -> p a b", a=8),
                                                          in1=bc(ec[d][r][:].unsqueeze(2), [64, 8, 64]), op=ALU.mult), [py2[d], ec[d][r]], [y_])
                    P.op("dve", lambda e: e.tensor_tensor(out=y_[:], in0=y_[:], in1=py1[d][:], op=ALU.add), [y_, py1[d]], [y_])
                    P.op("dve", lambda e: e.tensor_tensor(out=hf[d][:].rearrange("p (a b) -> p a b", a=8), in0=hf[d][:].rearrange("p (a b) -> p a b", a=8),
                                                          in1=bc(etb[d][r][:].unsqueeze(2), [128, 8, 64]), op=ALU.mult), [hf[d], etb[d][r]], [hf[d]])
                    P.op("dve", lambda e: e.tensor_tensor(out=hf[d][:], in0=hf[d][:], in1=pst[:], op=ALU.add), [hf[d], pst], [hf[d]])
                    P.op("act", lambda e: e.activation(out=hb[d][:], in_=hf[d][:], func=AF.Copy), [hf[d]], [hb[d]])
                    second = pos[d][ck] > pos[1 - d][ck]
                    ydst = S.YF[ts, g * 512:(g + 1) * 512]
                    key = P.B(("ssdyf", g, ck))
                    if not second:
                        P.dma("sp", ydst, y_[:], reads=[y_], writes=[key])
                    else:
                        P.dma("sp", f_[:], ydst, reads=[key], writes=[f_])
                        P.op("pool", lambda e: e.tensor_tensor(out=y_[:], in0=y_[:], in1=f_[:], op=ALU.add), [y_, f_], [y_])
                        P.op("pool", lambda e: e.tensor_tensor(out=f_[:].rearrange("p (a b) -> p a b", a=8), in0=xc[d][r][:].rearrange("p (a b) -> p a b", a=8),
                                                               in1=bc(Dbc[:, g * 8:g * 8 + 8].unsqueeze(2), [64, 8, 64]), op=ALU.mult), [xc[d][r], Dbc], [f_])
                        P.op("pool", lambda e: e.tensor_tensor(out=y_[:], in0=y_[:], in1=f_[:], op=ALU.add), [y_, f_], [y_])
                        P.dma("sp", S.YT[ts, g * 512:(g + 1) * 512], y_[:], reads=[y_], writes=[])

                for d in range(2):
                    A1(d, 0)
                    A2(d, 0)
                for i in range(36):
                    for d in range(2):
                        if i + 1 < 36:
                            A1(d, i + 1)
                        Bst(d, i)
                        if i + 1 < 36:
                            A2(d, i + 1)
                P.barrier()
        with ExitStack() as es1:
            ngv = vec_pp(S, es1, W["ssm_norm_g"][l, :], 32, "ngv")
            eps = Tl(nc, es1, [128, 1], F32, name="epss")
            P.op("dve", lambda e: e.memset(eps[:], 1e-6), [], [eps])
            yt = [Tl(nc, es1, [128, 4096], F32, name="yt") for _ in range(2)]
            zt = [Tl(nc, es1, [128, 32, 128], F32, name="zt") for _ in range(2)]
            yg = Tl(nc, es1, [128, 32, 128], F32, name="yg")
            sq = Tl(nc, es1, [128, 32, 128], F32, name="sq")
            rs = Tl(nc, es1, [128, 8, 128], F32, name="rs")
            ob = [Tl(nc, es1, [128, 32, 128], BF16, name="ob") for _ in range(2)]
            ptr = [Tl(nc, es1, [128, 4, 128], F32, psum=True, name="ptr4") for _ in range(2)]
            pss = [Tl(nc, es1, [128, 4, 128], F32, psum=True, name="pss") for _ in range(2)]
            k = 0
            for tt in range(18):
                r = tt % 2
                tsl = slice(tt * 128, (tt + 1) * 128)
                P.dma("sp", yt[r][:], S.YT[tsl, :], writes=[yt[r]])
                P.dma("act", zt[r][:], PFs[0:4096, tsl].rearrange("(c p) t -> p c t", p=128), writes=[zt[r]])
                P.op("act", lambda e: e.activation(out=zt[r][:], in_=zt[r][:], func=AF.Silu), [zt[r]], [zt[r]])
                for cg in range(8):
                    p_ = ptr[k % 2]
                    k += 1
                    for a in range(4):
                        c = cg * 4 + a
                        P.op("pe", lambda e: e.transpose(out=p_[:, a, :], in_=yt[r][:, c * 128:(c + 1) * 128], identity=S.ident[:]), [yt[r], S.ident], [p_])
                    P.op("dve", lambda e: e.tensor_tensor(out=yg[:, cg * 4:cg * 4 + 4, :], in0=p_[:], in1=zt[r][:, cg * 4:cg * 4 + 4, :], op=ALU.mult), [p_, zt[r]], [yg])
                P.op("pool", lambda e: e.tensor_tensor(out=sq[:], in0=yg[:], in1=yg[:], op=ALU.mult), [yg], [sq])
                for gg in range(8):
                    p_ = pss[gg // 4]
                    for a in range(4):
                        P.op("pe", lambda e: e.matmul(out=p_[:, gg % 4, :], lhsT=S.ones[:], rhs=sq[:, gg * 4 + a, :], start=(a == 0), stop=(a == 3)), [sq, S.ones], [p_])
                for hh in range(2):
                    P.op("act", lambda e: e.activation(out=rs[:, hh * 4:hh * 4 + 4, :], in_=pss[hh][:], func=AF.Sqrt, bias=eps[:], scale=1.0 / 512), [pss[hh], eps], [rs])
                P.op("dve", lambda e: e.reciprocal(out=rs[:], in_=rs[:]), [rs], [rs])
                P.op("dve", lambda e: e.tensor_tensor(out=yg[:].rearrange("p (g a) t -> p g a t", a=4), in0=yg[:].rearrange("p (g a) t -> p g a t", a=4),
                                                      in1=bc(rs[:].unsqueeze(2), [128, 8, 4, 128]), op=ALU.mult), [yg, rs], [yg])
                P.op("pool", lambda e: e.tensor_tensor(out=ob[r][:], in0=yg[:], in1=bc(ngv[:].unsqueeze(2), [128, 32, 128]), op=ALU.mult), [yg, ngv], [ob[r]])
                P.dma("sp", S.OS[:, tsl].rearrange("(c p) t -> p c t", p=128), ob[r][:], reads=[ob[r]], writes=[])


CEXP = 0.6065306597126334


def mix_shift(S, out, x, omu, hmu, n):
    P = S.P
    P.op("dve", lambda e: e.tensor_scalar(out=out[:n, :], in0=x[:n, :], scalar1=omu, scalar2=None, op0=ALU.mult), [x], [out])
    for (s0, s1) in SEGS:
        P.op("dve", lambda e: e.scalar_tensor_tensor(out=out[:n, s0 + 1:s1], in0=x[:n, s0:s1 - 1], scalar=hmu, in1=out[:n, s0 + 1:s1], op0=ALU.mult, op1=ALU.add), [x, out], [out])
        P.op("dve", lambda e: e.scalar_tensor_tensor(out=out[:n, s0:s1 - 1], in0=x[:n, s0 + 1:s1], scalar=hmu, in1=out[:n, s0:s1 - 1], op0=ALU.mult, op1=ALU.add), [x, out], [out])


def ckview(d, h):
    return d[:, :, h, :].rearrange("c k t -> k c t")


def phase_rwkv(S, l, b):
    nc, P, W = S.nc, S.P, S.W
    PFr = S.PF["r"]
    OWD, OAD, OGD = 6144, 6336, 6528
    mu = W["rwkv_mu"][l]
    with ExitStack() as es:
        with ExitStack() as es1:
            def pp64(src, name):
                t = Tl(nc, es1, [128, 16], F32, name=name)
                P.dma("sp", t[:], src.rearrange("(c p) -> p c", p=128), writes=[t], allow_slow_non_contiguous=True)
                return t
            mu3 = Tl(nc, es1, [128, 3, 16], F32, name="mu3")
            P.dma("sp", mu3[:], mu[0:6144].rearrange("(c j p) -> p c j", c=3, p=128), writes=[mu3], allow_slow_non_contiguous=True)
            omu3 = Tl(nc, es1, [128, 3, 16], F32, name="omu3")
            P.op("dve", lambda e: e.tensor_scalar(out=omu3[:], in0=mu3[:], scalar1=-1.0, scalar2=1.0, op0=ALU.mult, op1=ALU.add), [mu3], [omu3])
            P.op("dve", lambda e: e.tensor_scalar(out=mu3[:], in0=mu3[:], scalar1=0.5, scalar2=None, op0=ALU.mult), [mu3], [mu3])
            kkk = pp64(W["rwkv_k_k"][l], "kkk")
            kka_ = pp64(W["rwkv_k_a"][l], "kka")
            rk = pp64(W["rwkv_r_k"][l].rearrange("h k -> (h k)"), "rk")
            w0 = [pp64(W["rwkv_w0"][l, d], "w0") for d in range(2)]
            a0 = [pp64(W["rwkv_a0"][l, d], "a0") for d in range(2)]
            blk = Tl(nc, es1, [128, 128], F32, name="blk1")
            P.op("dve", lambda e: e.memset(blk[:], 0.0), [], [blk])
            P.op("dve", lambda e: e.memset(blk[0:64, 0:64], 1.0), [], [blk])
            P.op("dve", lambda e: e.memset(blk[64:128, 64:128], 1.0), [], [blk])
            mus = Tl(nc, es1, [128, 3], F32, name="mus")
            twd = [Tl(nc, es1, [96, NT], F32, name="twd") for _ in range(2)]
            adm = [Tl(nc, es1, [96, NT], F32, name="adm") for _ in range(2)]
            sgd = Tl(nc, es1, [128, 2, NT], F32, name="sgd")
            es_l = ExitStack()
            lin = Tl(nc, es_l, [128, NT], F32, name="lin")
            mixt = Tl(nc, es_l, [128, NT], F32, name="mixt")
            jobs = [(OWD, 96, twd[0], AF.Tanh), (OWD + 96, 96, twd[1], AF.Tanh), (OAD, 96, adm[0], None), (OAD + 96, 96, adm[1], None),
                    (OGD, 128, None, AF.Sigmoid), (OGD + 128, 128, None, AF.Sigmoid)]
            for ji, (r0, n, dst, fn) in enumerate(jobs):
                P.dma("sp", lin[:n, :], PFr[r0:r0 + n, :], writes=[lin])
                P.dma("sp", mus[:n, 0:1], mu[r0:r0 + n].rearrange("(p o) -> p o", o=1), writes=[mus])
                P.op("dve", lambda e: e.tensor_scalar(out=mus[:n, 1:2], in0=mus[:n, 0:1], scalar1=-1.0, scalar2=1.0, op0=ALU.mult, op1=ALU.add), [mus], [mus])
                P.op("dve", lambda e: e.tensor_scalar(out=mus[:n, 2:3], in0=mus[:n, 0:1], scalar1=0.5, scalar2=None, op0=ALU.mult), [mus], [mus])
                if dst is None:
                    dsl = sgd[:, ji - 4, :]
                    mix_shift(S, mixt, lin, mus[:n, 1:2], mus[:n, 2:3], n)
                    P.op("act", lambda e: e.activation(out=dsl, in_=mixt[:], func=fn), [mixt], [sgd])
                else:
                    mix_shift(S, dst, lin, mus[:n, 1:2], mus[:n, 2:3], n)
                    if fn is not None:
                        P.op("act", lambda e: e.activation(out=dst[:], in_=dst[:], func=fn), [dst], [dst])
            P.barrier()
            es_l.close()
            def T64(name, dt=F32):
                return Tl(nc, es1, [128, NT], dt, name=name)
            xin = [T64("rxin") for _ in range(1)]
            rm, km, vm, kkn, tA, tB, tC, ki, kkat, bon, Cc, av = (T64(n_) for n_ in ("rm", "km", "vm", "kkn", "tA", "tB", "tC", "ki", "kkat", "bon", "Cc", "av"))
            ob = [T64("rob", BF16) for _ in range(3)]
            tot = Tl(nc, es1, [128, 36], F32, name="rtot")
            gam = Tl(nc, es1, [128, 36], F32, name="rgam")
            wl = [Tl(nc, es1, [96, 128], F32, name="wl") for _ in range(4)]
            gl = Tl(nc, es1, [128, 2, 128], F32, name="gl")
            ps = [Tl(nc, es1, [128, 512], F32, psum=True, name="rps") for _ in range(7)]
            pk = [0]
            obk = [0]

            def mm_tiles(fn_lhs_rhs, evac):
                for (t0, tn) in TT:
                    pt = ps[pk[0] % 7]
                    pk[0] += 1
                    lst = fn_lhs_rhs(t0, tn)
                    for i, (lh, rh, deps) in enumerate(lst):
                        P.op("pe", lambda e: e.matmul(out=pt[:, :tn], lhsT=lh, rhs=rh, start=(i == 0), stop=(i == len(lst) - 1)), deps, [pt])
                    evac(pt, t0, tn)

            def dma_ck(dst, j, t_, q="act"):
                for hh in range(2):
                    P.dma(q, ckview(dst, 2 * j + hh), t_[hh * 64:(hh + 1) * 64, :].rearrange("k (c t) -> k c t", t=64), reads=[t_], writes=[])

            def store_ck(dst, j, fn):
                o_ = ob[obk[0] % 3]
                obk[0] += 1
                fn(o_)
                dma_ck(dst, j, o_)

            for j in range(16):
                hs = slice(j * 128, (j + 1) * 128)
                for ci, dstt in enumerate((rm, km, vm)):
                    xi = xin[0]
                    P.dma("sp", xi[:], PFr[ci * 2048 + j * 128:ci * 2048 + (j + 1) * 128, :], writes=[xi])
                    mix_shift(S, dstt, xi, omu3[:, ci, j:j + 1], mu3[:, ci, j:j + 1], 128)
                store_ck(S.RW["V"], j, lambda o_: P.op("act", lambda e: e.activation(out=o_[:], in_=vm[:], func=AF.Copy), [vm], [o_]))
                P.op("dve", lambda e: e.tensor_scalar(out=kkn[:], in0=km[:], scalar1=kkk[:, j:j + 1], scalar2=None, op0=ALU.mult), [km, kkk], [kkn])
                P.op("pool", lambda e: e.tensor_tensor(out=tA[:], in0=kkn[:], in1=kkn[:], op=ALU.mult), [kkn], [tA])
                mm_tiles(lambda t0, tn: [(blk[:], tA[:, t0:t0 + tn], [blk, tA])],
                         lambda pt, t0, tn: P.op("dve", lambda e: e.tensor_scalar_max(out=tB[:, t0:t0 + tn], in0=pt[:, :tn], scalar1=1e-24), [pt], [tB]))
                P.op("act", lambda e: e.activation(out=tB[:], in_=tB[:], func=AF.Sqrt), [tB], [tB])
                P.op("dve", lambda e: e.reciprocal(out=tB[:], in_=tB[:]), [tB], [tB])
                P.op("pool", lambda e: e.tensor_tensor(out=kkn[:], in0=kkn[:], in1=tB[:], op=ALU.mult), [kkn, tB], [kkn])
                P.dma("sp", gl[:], W["rwkv_g_up"][l][:, hs].rearrange("(kc p) n -> p kc n", p=128), writes=[gl])
                mm_tiles(lambda t0, tn: [(gl[:, kc, :], sgd[:, kc, t0:t0 + tn], [gl, sgd]) for kc in range(2)],
                         lambda pt, t0, tn: P.op("act", lambda e: e.activation(out=tC[:, t0:t0 + tn], in_=pt[:, :tn], func=AF.Copy), [pt], [tC]))
                dma_ck(S.RW["G"], j, tC)
                for d in range(2):
                    rev = d == 1
                    P.dma("sp", wl[d][:], W["rwkv_w_up"][l, d][:, hs], writes=[wl[d]])
                    P.dma("sp", wl[2 + d][:], W["rwkv_a_up"][l, d][:, hs], writes=[wl[2 + d]])
                    mm_tiles(lambda t0, tn: [(wl[d][:], twd[d][:, t0:t0 + tn], [wl[d], twd[d]])],
                             lambda pt, t0, tn: P.op("act", lambda e: e.activation(out=tA[:, t0:t0 + tn], in_=pt[:, :tn], func=AF.Sigmoid, bias=w0[d][:, j:j + 1], scale=1.0), [pt, w0[d]], [tA]))
                    mm_tiles(lambda t0, tn: [(wl[2 + d][:], adm[d][:, t0:t0 + tn], [wl[2 + d], adm[d]])],
                             lambda pt, t0, tn: P.op("act", lambda e: e.activation(out=av[:, t0:t0 + tn], in_=pt[:, :tn], func=AF.Sigmoid, bias=a0[d][:, j:j + 1], scale=1.0), [pt, a0[d]], [av]))
                    P.op("dve", lambda e: e.tensor_scalar(out=ki[:], in0=av[:], scalar1=-1.0, scalar2=kka_[:, j:j + 1], op0=ALU.add, op1=ALU.mult), [av, kka_], [ki])
                    P.op("dve", lambda e: e.scalar_tensor_tensor(out=ki[:], in0=ki[:], scalar=1.0, in1=km[:], op0=ALU.add, op1=ALU.mult), [ki, km], [ki])
                    P.op("pool", lambda e: e.tensor_tensor(out=kkat[:], in0=kkn[:], in1=av[:], op=ALU.mult), [kkn, av], [kkat])
                    P.op("dve", lambda e: e.scalar_tensor_tensor(out=tB[:], in0=rm[:], scalar=rk[:, j:j + 1], in1=ki[:], op0=ALU.mult, op1=ALU.mult), [rm, rk, ki], [tB])
                    if d == 0:
                        mm_tiles(lambda t0, tn: [(blk[:], tB[:, t0:t0 + tn], [blk, tB])],
                                 lambda pt, t0, tn: P.op("dve", lambda e: e.tensor_tensor(out=bon[:, t0:t0 + tn], in0=pt[:, :tn], in1=vm[:, t0:t0 + tn], op=ALU.mult), [pt, vm], [bon]))
                    else:
                        mm_tiles(lambda t0, tn: [(blk[:], tB[:, t0:t0 + tn], [blk, tB])],
                                 lambda pt, t0, tn: P.op("dve", lambda e: e.tensor_tensor(out=tC[:, t0:t0 + tn], in0=pt[:, :tn], in1=vm[:, t0:t0 + tn], op=ALU.mult), [pt, vm], [tC]))
                        P.op("pool", lambda e: e.tensor_tensor(out=bon[:], in0=bon[:], in1=tC[:], op=ALU.add), [bon, tC], [bon])
                        dma_ck(S.RW["BON"], j, bon)
                    seg_cumsum(S, Cc, tA, tot, tB, rev)
                    P.op("act", lambda e: e.activation(out=gam[:], in_=tot[:], func=AF.Exp, scale=-CEXP), [tot], [gam])
                    for hh in range(2):
                        P.dma("sp", S.RW["GAM"][d][2 * j + hh], gam[hh * 64:(hh + 1) * 64, :], reads=[gam], writes=[])
                    P.op("pool", lambda e: e.tensor_tensor(out=tB[:], in0=Cc[:], in1=tA[:], op=ALU.subtract), [Cc, tA], [tB])
                    P.op("act", lambda e: e.activation(out=tB[:], in_=tB[:], func=AF.Exp, scale=-CEXP), [tB], [tB])
                    store_ck(S.RW["A"][d], j, lambda o_: P.op("dve", lambda e: e.tensor_tensor(out=o_[:], in0=kkn[:], in1=tB[:], op=ALU.mult), [kkn, tB], [o_]))
                    P.op("act", lambda e: e.activation(out=tB[:], in_=Cc[:], func=AF.Exp, scale=-CEXP), [Cc], [tB])
                    store_ck(S.RW["R"][d], j, lambda o_: P.op("pool", lambda e: e.tensor_tensor(out=o_[:], in0=rm[:], in1=tB[:], op=ALU.mult), [rm, tB], [o_]))
                    P.op("act", lambda e: e.activation(out=tB[:], in_=Cc[:], func=AF.Exp, scale=CEXP), [Cc], [tB])
                    store_ck(S.RW["K"][d], j, lambda o_: P.op("dve", lambda e: e.tensor_tensor(out=o_[:], in0=ki[:], in1=tB[:], op=ALU.mult), [ki, tB], [o_]))
                    store_ck(S.RW["B"][d], j, lambda o_: P.op("pool", lambda e: e.tensor_tensor(out=o_[:], in0=kkat[:], in1=tB[:], op=ALU.mult), [kkat, tB], [o_]))
                    P.op("dve", lambda e: e.tensor_tensor(out=tB[:].rearrange("p (c q) -> p c q", q=64), in0=Cc[:].rearrange("p (c q) -> p c q", q=64),
                                                          in1=bc(tot[:].unsqueeze(2), [128, 36, 64]), op=ALU.subtract), [Cc, tot], [tB])
                    P.op("act", lambda e: e.activation(out=tB[:], in_=tB[:], func=AF.Exp, scale=CEXP), [tB], [tB])
                    store_ck(S.RW["KT"][d], j, lambda o_: P.op("dve", lambda e: e.tensor_tensor(out=o_[:], in0=ki[:], in1=tB[:], op=ALU.mult), [ki, tB], [o_]))
                    store_ck(S.RW["BT"][d], j, lambda o_: P.op("pool", lambda e: e.tensor_tensor(out=o_[:], in0=kkat[:], in1=tB[:], op=ALU.mult), [kkat, tB], [o_]))
        P.barrier()
        with ExitStack() as es1:
            def pp64b(src, name):
                t = Tl(nc, es1, [64, 32], F32, name=name)
                P.dma("sp", t[:], src.rearrange("(h k) -> k h", k=64), writes=[t], allow_slow_non_contiguous=True)
                return t
            lnw = pp64b(W["rwkv_ln_w"][l], "lnw")
            lnb = pp64b(W["rwkv_ln_b"][l], "lnb")
            eps = Tl(nc, es1, [64, 1], F32, name="repsl")
            P.op("dve", lambda e: e.memset(eps[:], 64e-5), [], [eps])
            SH = [64, 8, 64]
            pf = [Tl(nc, es1, SH, F32, psum=True, name="rpf") for _ in range(7)]
            pb = Tl(nc, es1, SH, BF16, psum=True, name="rpb")
            pfk = [0]

            def nps():
                t = pf[pfk[0] % 7]
                pfk[0] += 1
                return t
            def make_chain():
                C = K()
                C.ldb = {n_: [Tl(nc, es1, SH, BF16, name="l" + n_) for _ in range(2)] for n_ in ("A", "R", "K", "B", "KT", "BT", "V")}
                C.gamt = Tl(nc, es1, [64, 8, 36], F32, name="gamt")
                C.Tf = Tl(nc, es1, SH, F32, name="Tf")
                C.Tb = Tl(nc, es1, SH, BF16, name="Tb")
                C.PTf = [[Tl(nc, es1, SH, F32, name="PTf") for _ in range(6)] for _ in range(2)]
                C.Pb = [Tl(nc, es1, SH, BF16, name="Pb") for _ in range(2)]
                C.PTb = [Tl(nc, es1, SH, BF16, name="PTb") for _ in range(2)]
                dbl = lambda n_: [Tl(nc, es1, SH, BF16, name=n_) for _ in range(2)]
                C.LkT, C.MKT, C.MBT, C.KtT, C.BtT, C.VT = dbl("LkT"), dbl("MKT"), dbl("MBT"), dbl("KtT"), dbl("BtT"), dbl("VT")
                C.X = Tl(nc, es1, SH, F32, name="X")
                C.Ub = Tl(nc, es1, SH, BF16, name="Ub")
                C.ys = [Tl(nc, es1, SH, F32, name="rys") for _ in range(2)]
                C.yfl = [Tl(nc, es1, SH, F32, name="ryf") for _ in range(2)]
                C.bnl = [Tl(nc, es1, SH, F32, name="rbn") for _ in range(2)]
                C.ggl = [Tl(nc, es1, SH, F32, name="rgg") for _ in range(2)]
                C.sqv = Tl(nc, es1, SH, F32, name="rsq")
                C.st8 = Tl(nc, es1, [64, 8], F32, name="st8")
                C.st9 = Tl(nc, es1, [64, 8], F32, name="st9")
                C.obf = [Tl(nc, es1, SH, BF16, name="robf") for _ in range(2)]
                return C
            chains = [make_chain() for _ in range(2)]

            def mm8(pt, *pairs):
                for e_ in range(8):
                    for i_, (lh, rh) in enumerate(pairs):
                        P.op("pe", lambda e: e.matmul(out=pt[:, e_, :], lhsT=lh[:, e_, :], rhs=rh[:, e_, :], start=(i_ == 0), stop=(i_ == len(pairs) - 1)), [lh, rh], [pt])

            def masked(pt, dst, mask):
                P.op("dve", lambda e: e.tensor_tensor(out=dst[:], in0=pt[:], in1=bc(mask[:].unsqueeze(1), SH), op=ALU.mult), [pt, mask], [dst])

            for hgp in range(2):
                for d in range(2):
                    rev = d == 1
                    order = chunk_order(rev)
                    for c_ in range(2):
                        C = chains[c_]
                        hg = hgp * 2 + c_
                        P.dma("sp", C.gamt[:], S.RW["GAM"][d][hg * 8:hg * 8 + 8].rearrange("h k c -> k h c"), writes=[C.gamt])
                        P.op("dve", lambda e: e.memset(C.Tf[:], 0.0), [], [C.Tf])
                        P.op("dve", lambda e: e.memset(C.Tb[:], 0.0), [], [C.Tb])

                    def A_slices(C, hg, i):
                        ck, r_ = order[i], i % 2
                        h8 = slice(hg * 8, hg * 8 + 8)
                        L = {n_: C.ldb[n_][r_] for n_ in C.ldb}

                        def s0():
                            for qi, n_ in enumerate(("A", "R", "K", "B", "KT", "BT", "V")):
                                src = S.RW[n_] if n_ == "V" else S.RW[n_][d]
                                P.dma("sp" if qi % 2 == 0 else "act", L[n_][:], src[ck, :, h8, :], writes=[L[n_]])
                            p1 = nps(); mm8(p1, (L["B"], L["A"])); masked(p1, C.PTf[r_][0], S.nstr[d])
                            P.op("pool", lambda e: e.tensor_copy(out=C.PTb[0][:], in_=C.PTf[r_][0][:]), [C.PTf[r_][0]], [C.PTb[0]])
                            p2 = nps(); mm8(p2, (L["A"], L["B"])); masked(p2, C.Pb[0], S.nstr[1 - d])

                        def s1():
                            p3 = nps(); mm8(p3, (L["K"], L["A"])); masked(p3, C.LkT[r_], S.str_[d])
                            p4 = nps(); mm8(p4, (L["K"], L["R"])); masked(p4, C.MKT[r_], S.tri[d])
                            p5 = nps(); mm8(p5, (L["B"], L["R"])); masked(p5, C.MBT[r_], S.ntri[d])

                        def s2():
                            for src_, dst_, sc_ in ((L["KT"], C.KtT[r_], 1.0), (L["BT"], C.BtT[r_], -1.0), (L["V"], C.VT[r_], 1.0)):
                                for e_ in range(8):
                                    P.op("pe", lambda e: e.transpose(out=pb[:, e_, :], in_=src_[:, e_, :], identity=S.identb[:64, :64]), [src_, S.identb], [pb])
                                P.op("act", lambda e: e.activation(out=dst_[:], in_=pb[:], func=AF.Copy, scale=sc_), [pb], [dst_])

                        def sq(j):
                            def f():
                                cur, nxt = j % 2, (j + 1) % 2
                                pr = nps(); mm8(pr, (C.Pb[cur], C.PTb[cur]))
                                P.op("act", lambda e: e.activation(out=C.PTf[r_][j + 1][:], in_=pr[:], func=AF.Copy), [pr], [C.PTf[r_][j + 1]])
                                if j < 4:
                                    pq = nps(); mm8(pq, (C.PTb[cur], C.Pb[cur]))
                                    P.op("act", lambda e: e.activation(out=C.Pb[nxt][:], in_=pq[:], func=AF.Copy), [pq], [C.Pb[nxt]])
                                    P.op("pool", lambda e: e.tensor_copy(out=C.PTb[nxt][:], in_=C.PTf[r_][j + 1][:]), [C.PTf[r_][j + 1]], [C.PTb[nxt]])
                            return f
                        return [s0, s1, s2, sq(0), sq(1), sq(2), sq(3), sq(4)]

                    def B_steps(C, hg, i):
                        ck, r_ = order[i], i % 2
                        h8 = slice(hg * 8, hg * 8 + 8)
                        L = {n_: C.ldb[n_][r_] for n_ in C.ldb}
                        tsl = slice(ck * 64, ck * 64 + 64)

                        def b0():
                            px = nps()
                            mm8(px, (L["A"], C.Tb), (C.LkT[r_], C.VT[r_]))
                            P.op("act", lambda e: e.activation(out=C.X[:], in_=px[:], func=AF.Copy), [px], [C.X])

                        def ap(j):
                            def f():
                                pa = nps()
                                mm8(pa, (C.PTf[r_][j], C.X))
                                P.op("dve", lambda e: e.tensor_tensor(out=C.X[:], in0=C.X[:], in1=pa[:], op=ALU.add), [C.X, pa], [C.X])
                            return f

                        def fin():
                            P.op("act", lambda e: e.activation(out=C.Ub[:], in_=C.X[:], func=AF.Copy), [C.X], [C.Ub])
                            py = nps()
                            mm8(py, (L["R"], C.Tb), (C.MKT[r_], C.VT[r_]), (C.MBT[r_], C.Ub))
                            pS = nps()
                            mm8(pS, (C.KtT[r_], C.VT[r_]), (C.BtT[r_], C.Ub))
                            P.op("dve", lambda e: e.tensor_tensor(out=C.Tf[:], in0=C.Tf[:], in1=bc(C.gamt[:, :, ck].unsqueeze(2), SH), op=ALU.mult), [C.Tf, C.gamt], [C.Tf])
                            P.op("dve", lambda e: e.tensor_tensor(out=C.Tf[:], in0=C.Tf[:], in1=pS[:], op=ALU.add), [C.Tf, pS], [C.Tf])
                            P.op("act", lambda e: e.activation(out=C.Tb[:], in_=C.Tf[:], func=AF.Copy), [C.Tf], [C.Tb])
                            ydst = S.RW["YF"][tsl, hg * 512:(hg + 1) * 512]
                            y_ = C.ys[r_]
                            if not rev:
                                P.op("act", lambda e: e.activation(out=y_[:], in_=py[:], func=AF.Copy), [py], [y_])
                                P.dma("sp", ydst, y_[:].rearrange("p a b -> p (a b)"), reads=[y_], writes=[])
                            else:
                                f_, bn_, gg_, o_ = C.yfl[r_], C.bnl[r_], C.ggl[r_], C.obf[r_]
                                P.dma("sp", f_[:].rearrange("p a b -> p (a b)"), ydst, writes=[f_])
                                P.dma("act", bn_[:], S.RW["BON"][ck, :, h8, :], writes=[bn_])
                                P.dma("act", gg_[:], S.RW["G"][ck, :, h8, :], writes=[gg_])
                                P.op("dve", lambda e: e.tensor_tensor(out=y_[:], in0=py[:], in1=f_[:], op=ALU.add), [py, f_], [y_])
                                post1()

                        def post1():
                            if not rev:
                                return
                            y_ = C.ys[r_]
                            f_, bn_, gg_, o_ = C.yfl[r_], C.bnl[r_], C.ggl[r_], C.obf[r_]
                            P.op("dve", lambda e: e.reduce_sum(out=C.st8[:], in_=y_[:], axis=AX.X), [y_], [C.st8])
                            P.op("pool", lambda e: e.tensor_scalar(out=C.st8[:], in0=C.st8[:], scalar1=1.0 / 64, scalar2=None, op0=ALU.mult), [C.st8], [C.st8])
                            P.op("pool", lambda e: e.tensor_tensor(out=y_[:], in0=y_[:], in1=bc(C.st8[:].unsqueeze(2), SH), op=ALU.subtract), [y_, C.st8], [y_])
                            P.op("pool", lambda e: e.tensor_tensor(out=C.sqv[:], in0=y_[:], in1=y_[:], op=ALU.mult), [y_], [C.sqv])
                            P.op("dve", lambda e: e.reduce_sum(out=C.st9[:], in_=C.sqv[:], axis=AX.X), [C.sqv], [C.st9])
                            P.op("act", lambda e: e.activation(out=C.st9[:], in_=C.st9[:], func=AF.Sqrt, bias=eps[:], scale=1.0 / 64), [C.st9, eps], [C.st9])
                            P.op("dve", lambda e: e.reciprocal(out=C.st9[:], in_=C.st9[:]), [C.st9], [C.st9])
                            P.op("pool", lambda e: e.tensor_tensor(out=y_[:], in0=y_[:], in1=bc(C.st9[:].unsqueeze(2), SH), op=ALU.mult), [y_, C.st9], [y_])

                        def post():
                            if not rev:
                                return
                            y_ = C.ys[r_]
                            f_, bn_, gg_, o_ = C.yfl[r_], C.bnl[r_], C.ggl[r_], C.obf[r_]
                            pt_ = nps()
                            for e_ in range(8):
                                P.op("pe", lambda e: e.transpose(out=pt_[:, e_, :], in_=y_[:, e_, :], identity=S.ident[:64, :64]), [y_, S.ident], [pt_])
                            P.op("dve", lambda e: e.tensor_tensor(out=C.sqv[:], in0=pt_[:], in1=bc(lnw[:, h8].unsqueeze(2), SH), op=ALU.mult), [pt_, lnw], [C.sqv])
                            P.op("pool", lambda e: e.tensor_tensor(out=C.sqv[:], in0=C.sqv[:], in1=bc(lnb[:, h8].unsqueeze(2), SH), op=ALU.add), [C.sqv, lnb], [C.sqv])
                            P.op("pool", lambda e: e.tensor_tensor(out=C.sqv[:], in0=C.sqv[:], in1=bn_[:], op=ALU.add), [C.sqv, bn_], [C.sqv])
                            P.op("pool", lambda e: e.tensor_tensor(out=o_[:], in0=C.sqv[:], in1=gg_[:], op=ALU.mult), [C.sqv, gg_], [o_])
                            P.dma("sp", S.ORW[hg * 512:(hg + 1) * 512, tsl].rearrange("(h v) t -> v h t", v=64), o_[:], reads=[o_], writes=[])
                        return [b0, ap(0), ap(1), ap(2), ap(3), ap(4), ap(5), fin, post]

                    for c_ in range(2):
                        for f in A_slices(chains[c_], hgp * 2 + c_, 0):
                            f()
                    prev_post = [None, None]
                    for i in range(36):
                        a_ = [A_slices(chains[c_], hgp * 2 + c_, i + 1) if i + 1 < 36 else [] for c_ in range(2)]
                        b_ = [B_steps(chains[c_], hgp * 2 + c_, i) for c_ in range(2)]
                        for k_ in range(8):
                            for c_ in range(2):
                                b_[c_][k_]()
                                if k_ < len(a_[c_]):
                                    a_[c_][k_]()
                                if k_ == 3 and prev_post[c_] is not None:
                                    prev_post[c_]()
                        prev_post = [b_[c_][8] for c_ in range(2)]
                    for c_ in range(2):
                        prev_post[c_]()
                    P.barrier()


TT2 = [(0, 256), (256, 512), (768, 512), (1280, 512), (1792, 512)]


def resid_update(S, b, ci, pt_ap, t0, tn, gidx, xo, k, pt):
    nc, P = S.nc, S.P
    v = 2 if t0 < 256 else b
    xi = xo[k % len(xo)]
    dst = S.xT[b, ci * 128:(ci + 1) * 128, t0:t0 + tn]
    P.dma("sp", xi[:, :tn], dst, writes=[xi])
    P.op("dve", lambda e: e.scalar_tensor_tensor(out=xi[:, :tn], in0=pt_ap, scalar=S.mod[:, gidx * 16 + ci, v:v + 1], in1=xi[:, :tn], op0=ALU.mult, op1=ALU.add),
         [pt, S.mod, xi], [xi])
    P.dma("act", dst, xi[:, :tn], reads=[xi], writes=[])


def resid_evac(S, b, gidx, xo, cnt):
    def evac(ci, n, tiles):
        for (pt, t0, tn) in tiles:
            if t0 == 0:
                for (a0, a1) in ((0, 256), (256, tn)):
                    resid_update(S, b, ci, pt[:, a0:a1], a0, a1 - a0, gidx, xo, cnt[0], pt)
                    cnt[0] += 1
            else:
                resid_update(S, b, ci, pt[:, :tn], t0, tn, gidx, xo, cnt[0], pt)
                cnt[0] += 1
    return evac


def phase_merge(S, l, b):
    nc, P, W = S.nc, S.P, S.W
    passes = [(S.OG, 0, W["w_br_gla"][l], 0), (S.OS, 0, W["w_br_ssm"][l][0:2048, :], 2048), (S.OS, 16, W["w_br_ssm"][l][2048:4096, :], 2048),
              (S.ORW, 0, W["w_br_rwkv"][l], 4096)]
    with ExitStack() as es:
        mT = Tl(nc, es, [128, 16, NT], BF16, name="mT")
        with ExitStack() as es1:
            rhs = Tl(nc, es1, [128, 16, NT], BF16, name="mrhs")
            gt = [Tl(nc, es1, [128, NT], F32, name="mgt") for _ in range(2)]
            tmp = [Tl(nc, es1, [128, 512], F32, name="mtmp") for _ in range(3)]
            tk = [0]
            for pi, (src, k0, w_ap, go) in enumerate(passes):
                P.dma("sp", rhs[:], src[k0 * 128:(k0 + 16) * 128, :].rearrange("(kc p) t -> p kc t", p=128), writes=[rhs])

                def evac(ci, n, tiles, pi=pi, go=go):
                    g_ = gt[ci % 2]
                    P.dma("act", g_[:], S.PF["t"][go + ci * 128:go + (ci + 1) * 128, :], writes=[g_])
                    P.op("act", lambda e: e.activation(out=g_[:], in_=g_[:], func=AF.Sigmoid), [g_], [g_])
                    for (pt, t0, tn) in tiles:
                        if pi == 0:
                            P.op("dve", lambda e: e.tensor_tensor(out=mT[:, ci, t0:t0 + tn], in0=pt[:, :tn], in1=g_[:, t0:t0 + tn], op=ALU.mult), [pt, g_], [mT])
                        else:
                            t_ = tmp[tk[0] % 3]
                            tk[0] += 1
                            P.op("dve", lambda e: e.tensor_tensor(out=t_[:, :tn], in0=pt[:, :tn], in1=g_[:, t0:t0 + tn], op=ALU.mult), [pt, g_], [t_])
                            P.op("pool", lambda e: e.tensor_tensor(out=mT[:, ci, t0:t0 + tn], in0=mT[:, ci, t0:t0 + tn], in1=t_[:, :tn], op=ALU.add), [mT, t_], [mT])

                with ExitStack() as es2:
                    proj(S, es2, w_ap, 2048, rhs, 16, evac, blk=256)
                P.barrier()
        with ExitStack() as es1:
            xo = [Tl(nc, es1, [128, 512], F32, name="mxo") for _ in range(4)]
            proj(S, es1, W["w_out"][l], 2048, mT, 16, resid_evac(S, b, 2, xo, [0]), blk=256)
        P.barrier()


def phase_ffn_up(S, l, hT):
    nc, P, W = S.nc, S.P, S.W
    with ExitStack() as es:
        cw = [[vec_pp(S, es, W["ffn_conv_w"][l, i, j, :], 88, "fcw") for j in range(3)] for i in range(3)]
        cb = vec_pp(S, es, W["ffn_conv_b"][l, :], 88, "fcb")
        u = [Tl(nc, es, [128, NT], F32, name="fu") for _ in range(2)]
        y = [Tl(nc, es, [128, NT], F32, name="fy") for _ in range(2)]
        gb = [Tl(nc, es, [128, NT], BF16, name="fgb") for _ in range(2)]
        ab = [Tl(nc, es, [128, NT], BF16, name="fab") for _ in range(2)]
        cnt = [0]

        def evac(ci, n, tiles):
            k = cnt[0]
            cnt[0] += 1
            ui, yi = u[k % 2], y[k % 2]
            eng = "dve"
            for i, (pt, t0, tn) in enumerate(tiles):
                if i % 2 == 0 or True:
                    P.op("act", lambda e: e.activation(out=ui[:, t0:t0 + tn], in_=pt[:, :tn], func=AF.Copy), [pt], [ui])
                else:
                    P.op("dve", lambda e: e.tensor_copy(out=ui[:, t0:t0 + tn], in_=pt[:, :tn]), [pt], [ui])
            P.op(eng, lambda e: e.tensor_scalar(out=yi[:], in0=ui[:], scalar1=cw[1][1][:, ci:ci + 1], scalar2=cb[:, ci:ci + 1], op0=ALU.mult, op1=ALU.add), [ui, cw[1][1], cb], [yi])
            for dc in (-1, 1):
                o0, o1 = max(0, -dc), 256 - max(0, dc)
                P.op(eng, lambda e: e.scalar_tensor_tensor(out=yi[:, o0:o1], in0=ui[:, o0 + dc:o1 + dc], scalar=cw[1][1 + dc][:, ci:ci + 1], in1=yi[:, o0:o1], op0=ALU.mult, op1=ALU.add),
                     [ui, yi, cw[1][1 + dc]], [yi])
            uv = ui[:, 256:].rearrange("p (r c) -> p r c", c=64)
            yv = yi[:, 256:].rearrange("p (r c) -> p r c", c=64)
            for dr in (-1, 0, 1):
                for dc in (-1, 0, 1):
                    if dr == 0 and dc == 0:
                        continue
                    r0, r1 = max(0, -dr), 32 - max(0, dr)
                    c0, c1 = max(0, -dc), 64 - max(0, dc)
                    P.op(eng, lambda e: e.scalar_tensor_tensor(out=yv[:, r0:r1, c0:c1], in0=uv[:, r0 + dr:r1 + dr, c0 + dc:c1 + dc], scalar=cw[1 + dr][1 + dc][:, ci:ci + 1],
                                                               in1=yv[:, r0:r1, c0:c1], op0=ALU.mult, op1=ALU.add), [ui, yi, cw[1 + dr][1 + dc]], [yi])
            if ci < 44:
                g_ = gb[k % 2]
                P.op("act", lambda e: e.activation(out=g_[:], in_=yi[:], func=AF.Silu), [yi], [g_])
                P.dma("sp", S.ACTG[ci * 128:(ci + 1) * 128, :], g_[:], reads=[g_], writes=[S.P.B(("actg", ci))])
            else:
                g_, a_ = gb[k % 2], ab[k % 2]
                P.dma("sp", g_[:], S.ACTG[(ci - 44) * 128:(ci - 43) * 128, :], reads=[S.P.B(("actg", ci - 44))], writes=[g_])
                P.op("pool", lambda e: e.tensor_tensor(out=a_[:], in0=yi[:], in1=g_[:], op=ALU.mult), [yi, g_], [a_])
                P.dma("act", S.ACT[(ci - 44) * 128:(ci - 43) * 128, :], a_[:], reads=[a_], writes=[])

        proj(S, es, W["ffn_w_up"][l], 2 * FH, hT, 16, evac, blk=512, nps=7)


def phase_ffn_down(S, l, b):
    nc, P, W = S.nc, S.P, S.W
    with ExitStack() as es:
        rhs = Tl(nc, es, [128, 22, NT], BF16, name="drhs")
        xo = [Tl(nc, es, [128, 512], F32, name="dxo") for _ in range(4)]
        cnt = [0]
        for half in range(2):
            r0 = half * 22 * 128
            P.dma("sp", rhs[:], S.ACT[r0:r0 + 22 * 128, :].rearrange("(kc p) t -> p kc t", p=128), writes=[rhs])
            with ExitStack() as es2:
                proj(S, es2, W["ffn_w_down"][l][r0:r0 + 22 * 128, :], 2048, rhs, 22, resid_evac(S, b, 5, xo, cnt), blk=256)
            P.barrier()


def phase_final(S, b, out):
    nc, P, W = S.nc, S.P, S.W
    with ExitStack() as es:
        gbc = Tl(nc, es, [128, DM], F32, name="gbc")
        P.dma("sp", gbc[:], bc(W["final_norm_g"].unsqueeze(0), [128, DM]), writes=[gbc])
        eps = Tl(nc, es, [128, 1], F32, name="feps")
        P.op("dve", lambda e: e.memset(eps[:], 1e-6), [], [eps])
        xt = [Tl(nc, es, [128, 16, 128], F32, name="fxt") for _ in range(2)]
        sq = [Tl(nc, es, [128, 16, 128], F32, name="fsq") for _ in range(2)]
        ot = [Tl(nc, es, [128, DM], F32, name="fot") for _ in range(2)]
        rs = [Tl(nc, es, [128, 1], F32, name="frs") for _ in range(2)]
        pss = [Tl(nc, es, [128, 1], F32, psum=True, name="fpss") for _ in range(2)]
        ptr = [Tl(nc, es, [128, 512], F32, psum=True, name="fptr") for _ in range(4)]
        k = 0
        for tt in range(16):
            r = tt % 2
            t0 = 256 + tt * 128
            P.dma("sp" if r == 0 else "act", xt[r][:], S.xT[b, :, t0:t0 + 128].rearrange("(c p) t -> p c t", p=128), writes=[xt[r]])
            P.op("act", lambda e: e.activation(out=sq[r][:], in_=xt[r][:], func=AF.Square), [xt[r]], [sq[r]])
            for c in range(16):
                P.op("pe", lambda e: e.matmul(out=pss[r][:], lhsT=sq[r][:, c, :], rhs=S.ones[:, 0:1], start=(c == 0), stop=(c == 15)), [sq[r], S.ones], [pss[r]])
            P.op("act", lambda e: e.activation(out=rs[r][:], in_=pss[r][:], func=AF.Sqrt, bias=eps[:], scale=1.0 / DM), [pss[r], eps], [rs[r]])
            P.op("dve", lambda e: e.reciprocal(out=rs[r][:], in_=rs[r][:]), [rs[r]], [rs[r]])
            for cg in range(4):
                p_ = ptr[k % 4]
                k += 1
                for a in range(4):
                    c = cg * 4 + a
                    P.op("pe", lambda e: e.transpose(out=p_[:, a * 128:(a + 1) * 128], in_=xt[r][:, c, :], identity=S.ident[:]), [xt[r], S.ident], [p_])
                P.op("dve", lambda e: e.scalar_tensor_tensor(out=ot[r][:, cg * 512:(cg + 1) * 512], in0=p_[:], scalar=rs[r][:, 0:1], in1=gbc[:, cg * 512:(cg + 1) * 512],
                                                             op0=ALU.mult, op1=ALU.mult), [p_, rs[r], gbc], [ot[r]])
            P.dma("sp", out[b, tt * 128:(tt + 1) * 128, :], ot[r][:], reads=[ot[r]], writes=[])


_NC_CACHE = {}


def kernel(**inputs):
    n = 8
    NB = 2
    if "nc" not in _NC_CACHE:
        _NC_CACHE["nc"] = build(NB=NB, L=2)
    nc = _NC_CACHE["nc"]
    cst = consts()
    shared = {k: np.ascontiguousarray(np.asarray(inputs[k], dtype=np.float32)) for k in WSHAPES}
    x = np.asarray(inputs["x"], dtype=np.float32)
    ctx = np.asarray(inputs["ctx"], dtype=np.float32)
    c = np.asarray(inputs["c"], dtype=np.float32)
    c_ctx = np.ascontiguousarray(np.asarray(inputs["c_ctx"], dtype=np.float32)[None, :])
    in_maps = []
    for i in range(n):
        m = {"x": np.ascontiguousarray(x[i * NB:(i + 1) * NB]), "ctx": np.ascontiguousarray(ctx[i * NB:(i + 1) * NB]),
             "c": np.ascontiguousarray(c[i * NB:(i + 1) * NB]), "c_ctx": c_ctx}
        m.update(shared)
        m.update(cst)
        in_maps.append(m)
    res = run_bass_kernel_spmd(nc, in_maps, core_ids=list(range(n)))
    return np.concatenate([r["out"] for r in res.results], axis=0)
```

```python
import numpy as np
import concourse.bass as bass
import concourse.mybir as mybir
from concourse.bass_utils import run_bass_kernel_spmd
from contextlib import ExitStack

F32 = mybir.dt.float32
BF16 = mybir.dt.bfloat16
ALU = mybir.AluOpType
AF = mybir.ActivationFunctionType
AX = mybir.AxisListType
NDS = 12

NT = 2304
TC = 256
DM = 2048
NIN = 29472
NPAD = 29568
TT = [(0, 512), (512, 512), (1024, 512), (1536, 512), (2048, 256)]
SEGS = [(0, 256), (256, 2304)]
FH = 5632

O_Q, O_K, O_V, O_G, O_AD, O_Z, O_XBC, O_DT, O_RW, O_GT = 0, 1024, 2048, 4096, 6144, 6176, 10272, 16416, 16544, 23328

WSHAPES = {
    "w_mod": [2, 2048, 12288], "b_mod": [2, 12288], "norm1_g": [2, 2048], "norm2_g": [2, 2048],
    "w_in": [2, 2048, 29472], "gla_wa_up": [2, 2, 16, 1024], "gla_ba": [2, 2, 1024], "gla_norm_g": [2, 512],
    "ssm_conv_w": [2, 3, 6144], "ssm_conv_b": [2, 6144], "ssm_a_log": [2, 2, 64], "ssm_dt_bias": [2, 2, 64],
    "ssm_d": [2, 64], "ssm_norm_g": [2, 4096], "rwkv_mu": [2, 6784], "rwkv_w0": [2, 2, 2048],
    "rwkv_w_up": [2, 2, 96, 2048], "rwkv_a0": [2, 2, 2048], "rwkv_a_up": [2, 2, 96, 2048],
    "rwkv_g_up": [2, 256, 2048], "rwkv_k_k": [2, 2048], "rwkv_k_a": [2, 2048], "rwkv_r_k": [2, 32, 64],
    "rwkv_ln_w": [2, 2048], "rwkv_ln_b": [2, 2048], "w_br_gla": [2, 2048, 2048], "w_br_ssm": [2, 4096, 2048],
    "w_br_rwkv": [2, 2048, 2048], "w_out": [2, 2048, 2048], "ffn_w_up": [2, 2048, 11264],
    "ffn_conv_w": [2, 3, 3, 11264], "ffn_conv_b": [2, 11264], "ffn_w_down": [2, 5632, 2048],
    "final_norm_g": [2048],
}


class Buf:
    __slots__ = ("w", "r")

    def __init__(self):
        self.w = None
        self.r = {}


class Rec:
    def __getattr__(self, name):
        def f(*a, **k):
            self.call = (name, a, k)
            return self
        return f


class Eng:
    def __init__(self, name, sem):
        self.name, self.sem, self.n, self.ops, self.waited = name, sem, 0, [], {}


class Prog:
    def __init__(self, nc, es):
        self.nc = nc
        self.E = {}
        for name in ("pe", "act", "dve", "pool", "sp"):
            self.E[name] = Eng(name, es.enter_context(nc.semaphore(name + "_sem")))
        self.dq = {}
        self.dqi = {}
        for q in ("sp", "pool", "act"):
            self.dq[q] = [[es.enter_context(nc.semaphore(f"d{q}{i}")), 0] for i in range(NDS)]
            self.dqi[q] = 0
        self.bufs = {}

    def B(self, key):
        b = self.bufs.get(key)
        if b is None:
            b = self.bufs[key] = Buf()
        return b

    def _wait(self, eng, tok):
        sem, val = tok
        k = id(sem)
        if eng.waited.get(k, 0) >= val:
            return
        eng.waited[k] = val
        eng.ops.append(("w", sem, val))

    def _deps(self, eng, reads, writes):
        own = id(eng.sem)
        skip_own = eng.name in ("pe", "sp")
        for b in reads:
            if b.w is not None and not (skip_own and id(b.w[0]) == own):
                self._wait(eng, b.w)
        for b in writes:
            if b.w is not None and not (skip_own and id(b.w[0]) == own):
                self._wait(eng, b.w)
            for t in b.r.values():
                if not (skip_own and id(t[0]) == own):
                    self._wait(eng, t)

    def _commit(self, tok, reads, writes):
        k = id(tok[0])
        for b in reads:
            b.r[k] = tok
        for b in writes:
            b.w = tok
            b.r = {}

    def op(self, en, fn, reads=(), writes=()):
        eng = self.E[en]
        writes = [getattr(r, "b", r) for r in writes] + [r.b for r in reads if getattr(r, "psum", False)]
        reads = [getattr(r, "b", r) for r in reads if not getattr(r, "psum", False)]
        self._deps(eng, reads, writes)
        eng.n += 1
        tok = (eng.sem, eng.n)
        rec = Rec()
        fn(rec)
        eng.ops.append(("i", rec.call, eng.sem, 1))
        self._commit(tok, reads, writes)

    def dma(self, q, out, in_, reads=(), writes=(), **kw):
        eng = self.E[q]
        reads = [getattr(r, "b", r) for r in reads]
        writes = [getattr(r, "b", r) for r in writes]
        self._deps(eng, reads, writes)
        slot = self.dq[q][self.dqi[q] % NDS]
        self.dqi[q] += 1
        if slot[1] > 0:
            self._wait(eng, (slot[0], slot[1]))
        slot[1] += 16
        tok = (slot[0], slot[1])
        eng.ops.append(("i", ("dma_start", (), dict(out=out, in_=in_, **kw)), slot[0], 16))
        self._commit(tok, reads, writes)

    def barrier(self):
        toks = [(e.sem, e.n) for e in self.E.values() if e.n > 0]
        for q in self.dq:
            for s in self.dq[q]:
                if s[1] > 0:
                    toks.append((s[0], s[1]))
        for e in self.E.values():
            for t in toks:
                if id(t[0]) == id(e.sem):
                    continue
                self._wait(e, t)
        self.bufs = {}

    def emit(self):
        nc = self.nc
        self.barrier()
        handles = {"pe": "tensor", "act": "scalar", "dve": "vector", "pool": "gpsimd", "sp": "sync"}
        with nc.Block() as block:
            for en, attr in handles.items():
                eng = self.E[en]

                def body(h, eng=eng):
                    for o in eng.ops:
                        if o[0] == "w":
                            h.wait_ge(o[1], o[2])
                        else:
                            getattr(h, o[1][0])(*o[1][1], **o[1][2]).then_inc(o[2], o[3])

                getattr(block, attr)(body)


class Tl:
    _n = 0

    def __init__(self, nc, es, shape, dtype, psum=False, name=None):
        Tl._n += 1
        name = (name or "t") + str(Tl._n)
        self.psum = psum
        self.b = Buf()
        if not psum:
            self.t = es.enter_context(nc.sbuf_tensor(name, list(shape), dtype))
            return
        esz = 2 if dtype == BF16 else 4
        n = 1
        for v in shape[1:]:
            n *= v
        assert n * esz <= 2048
        raw = es.enter_context(nc.psum_tensor(name, [shape[0], 2048 // esz], dtype))
        v = raw[:, :n]
        if len(shape) == 3:
            v = v.rearrange("p (a b) -> p a b", a=shape[1])
        self.t = v

    def __getitem__(self, k):
        return self.t[k]


class K:
    pass


def bc(ap, shape):
    return ap.to_broadcast(list(shape))


def build(NB=2, L=2, stop_after=None, dbg=(), skip=()):
    nc = bass.Bass("TRN2", target_bir_lowering=False)

    def din(name, shape, dt=F32):
        return nc.dram_tensor(name, list(shape), dt, kind="ExternalInput").ap()

    def dscr(name, shape, dt=F32):
        return nc.dram_tensor(name, list(shape), dt, kind="Internal").ap()

    x_in = din("x", [NB, 2048, DM])
    ctx_in = din("ctx", [NB, TC, DM])
    c_in = din("c", [NB, DM])
    cctx_in = din("c_ctx", [1, DM])
    W = {n: din(n, ([L] + s[1:]) if len(s) > 1 else s) for n, s in WSHAPES.items()}
    ident_in = din("ident", [128, 128])
    segmask_in = din("segmask", [128, NT])
    tri_in = din("tri", [2, 64, 64])
    out = nc.dram_tensor("out", [NB, 2048, DM], F32, kind="ExternalOutput").ap()
    dbg_out = {n: nc.dram_tensor("dbg_" + n, list(s), F32, kind="ExternalOutput").ap() for n, s in dbg}

    xT = dscr("xT", [NB, DM, NT])
    PF = {"g": dscr("PFg", [6272, NT]), "s": dscr("PFs", [10368, NT]), "r": dscr("PFr", [6784, NT]), "t": dscr("PFt", [6144, NT])}

    with ExitStack() as es0:
        P = Prog(nc, es0)
        S = K()
        S.nc, S.P, S.W, S.NB = nc, P, W, NB
        S.xT, S.PF = xT, PF
        S.dbg = dbg_out
        S.GOF = dscr("GOF", [NT, 2048])
        S.OG = dscr("OG", [2048, NT], BF16)
        S.CUM = dscr("CUM", [128, NT])
        S.XT = dscr("XT", [NT, 4096], BF16)
        S.BT = dscr("BT", [NT, 1024], BF16)
        S.BCF = dscr("BCF", [2048, NT], BF16)
        S.YF = dscr("YF", [NT, 4096])
        S.YT = dscr("YT", [NT, 4096])
        S.OS = dscr("OS", [4096, NT], BF16)
        CK = [36, 64, 32, 64]
        S.RW = {n_: [dscr(f"RW{n_}{d_}", CK, BF16) for d_ in range(2)] for n_ in ("A", "R", "K", "B", "KT", "BT")}
        S.RW["V"] = dscr("RWV", CK, BF16)
        S.RW["G"] = dscr("RWG", CK)
        S.RW["BON"] = dscr("RWBON", CK)
        S.RW["GAM"] = [dscr(f"RWGAM{d_}", [32, 64, 36]) for d_ in range(2)]
        S.RW["YF"] = dscr("RWYF", [NT, 2048])
        S.ORW = dscr("ORW", [2048, NT], BF16)
        mk_in = din("masks", [6, 64, 64])
        mks = [Tl(nc, es0, [64, 64], F32, name="mk") for _ in range(6)]
        for i_ in range(6):
            P.dma("sp", mks[i_][:], mk_in[i_], writes=[mks[i_]])
        S.str_, S.nstr, S.ntri = mks[0:2], mks[2:4], mks[4:6]
        S.ACTG = dscr("ACTG", [FH, NT], BF16)
        S.ACT = dscr("ACT", [FH, NT], BF16)
        S.segmask = Tl(nc, es0, [128, NT], F32, name="segmask")
        P.dma("sp", S.segmask[:], segmask_in, writes=[S.segmask])
        S.tri = [Tl(nc, es0, [64, 64], F32, name="tri") for _ in range(2)]
        for d_ in range(2):
            P.dma("sp", S.tri[d_][:], tri_in[d_], writes=[S.tri[d_]])
        S.ident = Tl(nc, es0, [128, 128], F32, name="ident")
        S.identb = Tl(nc, es0, [128, 128], BF16, name="identb")
        S.ones = Tl(nc, es0, [128, 128], F32, name="ones")
        P.dma("sp", S.ident[:], ident_in, writes=[S.ident])
        P.op("dve", lambda e: e.tensor_copy(out=S.identb[:], in_=S.ident[:]), [S.ident], [S.identb])
        P.op("dve", lambda e: e.memset(S.ones[:], 1.0), [], [S.ones])
        S.mod = Tl(nc, es0, [128, 96, 3], F32, name="mod")
        S.gs = [Tl(nc, es0, [128, 16, 3], F32, name="gs") for _ in range(2)]
        S.cact = Tl(nc, es0, [128, 16, 3], F32, name="cact")

        phase_init(S, x_in, ctx_in, c_in, cctx_in)
        P.barrier()
        done = False
        for l in range(L):
            if done:
                break
            phase_mod(S, l)
            P.barrier()
            for b in range(NB):
                with ExitStack() as es:
                    hT = Tl(nc, es, [128, 16, NT], BF16, name="hT")
                    phase_norm(S, l, b, 0, hT)
                    P.barrier()
                    if stop_after == "norm1":
                        dump_bf(S, es, hT, "hT")
                        done = True
                        break
                    phase_inproj(S, l, hT)
                    P.barrier()
                if stop_after == "ip":
                    done = True
                    break
                if "gla" not in skip:
                    phase_gla(S, l, b)
                    P.barrier()
                if stop_after == "gla":
                    done = True
                    break
                if "ssd" not in skip:
                    phase_ssd(S, l, b)
                    P.barrier()
                if stop_after == "ssd":
                    done = True
                    break
                if "rwkv" not in skip:
                    phase_rwkv(S, l, b)
                    P.barrier()
                if stop_after == "rwkv":
                    done = True
                    break
                phase_merge(S, l, b)
                P.barrier()
                if stop_after == "merge":
                    done = True
                    break
                with ExitStack() as es:
                    hT = Tl(nc, es, [128, 16, NT], BF16, name="hT2")
                    phase_norm(S, l, b, 1, hT)
                    P.barrier()
                    phase_ffn_up(S, l, hT)
                    P.barrier()
                phase_ffn_down(S, l, b)
                P.barrier()
        if not done:
            for b in range(NB):
                phase_final(S, b, out)
            P.barrier()
        if "xT" in dbg_out:
            P.dma("sp", dbg_out["xT"], S.xT[0], reads=[], writes=[])
        if "RWYF" in dbg_out:
            P.dma("sp", dbg_out["RWYF"], S.RW["YF"], reads=[], writes=[])
        if "ORW" in dbg_out:
            with ExitStack() as es:
                dump_bf2(S, es, S.ORW, "ORW", 16)
        if "YT" in dbg_out:
            P.dma("sp", dbg_out["YT"], S.YT, reads=[], writes=[])
        if "YF" in dbg_out:
            P.dma("sp", dbg_out["YF"], S.YF, reads=[], writes=[])
        if "OS" in dbg_out:
            with ExitStack() as es:
                dump_bf2(S, es, S.OS, "OS", 32)
        if "GOF" in dbg_out:
            P.dma("sp", dbg_out["GOF"], S.GOF, reads=[], writes=[])
        if "OG" in dbg_out:
            with ExitStack() as es:
                dump_bf2(S, es, S.OG, "OG", 16)
        for kk_ in ("g", "s", "r", "t"):
            if "PF" + kk_ in dbg_out:
                P.dma("sp", dbg_out["PF" + kk_], PF[kk_], reads=[], writes=[])
        P.emit()
    return nc


def dump_bf(S, es, t, name):
    nc, P = S.nc, S.P
    st = Tl(nc, es, [128, NT], F32, name="dump")
    for c in range(16):
        P.op("dve", lambda e, c=c: e.tensor_copy(out=st[:], in_=t[:, c, :]), [t], [st])
        P.dma("sp", S.dbg[name][c * 128:(c + 1) * 128, :], st[:], reads=[st], writes=[])


def dump_bf2(S, es, src, name, nch):
    nc, P = S.nc, S.P
    P.barrier()
    sb = Tl(nc, es, [128, NT], BF16, name="dumpb")
    st = Tl(nc, es, [128, NT], F32, name="dump")
    for c in range(nch):
        P.dma("sp", sb[:], src[c * 128:(c + 1) * 128, :], writes=[sb])
        P.op("dve", lambda e: e.tensor_copy(out=st[:], in_=sb[:]), [sb], [st])
        P.dma("sp", S.dbg[name][c * 128:(c + 1) * 128, :], st[:], reads=[st], writes=[])


def consts():
    seg = np.ones((128, NT), np.float32)
    seg[:, ::64] = 0.0
    s = np.arange(64)[:, None]
    q = np.arange(64)[None, :]
    tri = np.stack([(s <= q), (s >= q)]).astype(np.float32)
    eye = np.eye(64, dtype=np.float32)
    strict = tri - eye
    masks = np.concatenate([strict, -strict, -tri], 0)
    return {"ident": np.eye(128, dtype=np.float32), "segmask": seg, "tri": tri, "masks": masks}


def phase_init(S, x_in, ctx_in, c_in, cctx_in):
    nc, P = S.nc, S.P
    with ExitStack() as es:
        xt = [Tl(nc, es, [128, DM], F32, name="xt") for _ in range(2)]
        st = [Tl(nc, es, [128, 16, 128], F32, name="st") for _ in range(2)]
        ps = [Tl(nc, es, [128, 512], F32, psum=True, name="ps") for _ in range(4)]
        k = 0
        n = 0
        for b in range(S.NB):
            for tt in range(18):
                src = ctx_in[b, tt * 128:(tt + 1) * 128, :] if tt < 2 else x_in[b, (tt - 2) * 128:(tt - 1) * 128, :]
                xi = xt[n % 2]
                so = st[n % 2]
                n += 1
                P.dma("sp", xi[:], src, writes=[xi])
                for g in range(4):
                    pt = ps[k % 4]
                    k += 1
                    for j in range(4):
                        c = g * 4 + j
                        P.op("pe", lambda e, pt=pt, xi=xi, c=c, j=j: e.transpose(
                            out=pt[:, j * 128:(j + 1) * 128], in_=xi[:, c * 128:(c + 1) * 128], identity=S.ident[:]),
                            [xi, S.ident], [pt])
                    en = "dve" if g % 2 == 0 else "act"
                    if en == "dve":
                        P.op("dve", lambda e, pt=pt, so=so, g=g: e.tensor_copy(
                            out=so[:, g * 4:(g + 1) * 4, :], in_=pt[:].rearrange("p (a b) -> p a b", a=4)), [pt], [so])
                    else:
                        P.op("act", lambda e, pt=pt, so=so, g=g: e.activation(
                            out=so[:, g * 4:(g + 1) * 4, :], in_=pt[:].rearrange("p (a b) -> p a b", a=4), func=AF.Copy), [pt], [so])
                P.dma("act", S.xT[b, :, tt * 128:(tt + 1) * 128].rearrange("(c p) t -> p c t", p=128), so[:], reads=[so], writes=[])
        craw = Tl(nc, es, [128, 16, 3], F32, name="craw")
        P.op("dve", lambda e: e.memset(craw[:], 0.0), [], [craw])
        for b in range(S.NB):
            P.dma("sp", craw[:, :, b], c_in[b, :].rearrange("(c p) -> p c", p=128), reads=[], writes=[craw], allow_slow_non_contiguous=True)
        P.dma("sp", craw[:, :, 2], cctx_in[0, :].rearrange("(c p) -> p c", p=128), reads=[], writes=[craw], allow_slow_non_contiguous=True)
        P.op("act", lambda e: e.activation(out=S.cact[:], in_=craw[:], func=AF.Silu), [craw], [S.cact])


def phase_mod(S, l):
    nc, P, W = S.nc, S.P, S.W
    with ExitStack() as es:
        wt = [Tl(nc, es, [128, 16, 512], F32, name="wm") for _ in range(2)]
        ps = [Tl(nc, es, [128, 4, 3], F32, psum=True, name="psm") for _ in range(2)]
        bm = Tl(nc, es, [128, 96], F32, name="bm")
        prm = Tl(nc, es, [128, 2, 16], F32, name="prm")
        P.dma("sp", bm[:], W["b_mod"][l, :].rearrange("(c p) -> p c", p=128), writes=[bm], allow_slow_non_contiguous=True)
        P.dma("sp", prm[:, 0, :], W["norm1_g"][l, :].rearrange("(c p) -> p c", p=128), writes=[prm], allow_slow_non_contiguous=True)
        P.dma("sp", prm[:, 1, :], W["norm2_g"][l, :].rearrange("(c p) -> p c", p=128), writes=[prm], allow_slow_non_contiguous=True)
        for blk in range(24):
            w = wt[blk % 2]
            pt = ps[blk % 2]
            P.dma("sp" if blk % 2 == 0 else "act", w[:], W["w_mod"][l, :, blk * 512:(blk + 1) * 512].rearrange("(kc p) n -> p kc n", p=128), writes=[w])
            for j in range(4):
                for kc in range(16):
                    P.op("pe", lambda e, w=w, pt=pt, j=j, kc=kc: e.matmul(
                        out=pt[:, j, :], lhsT=w[:, kc, j * 128:(j + 1) * 128], rhs=S.cact[:, kc, :], start=(kc == 0), stop=(kc == 15)),
                        [w, S.cact], [pt])
            P.op("dve", lambda e, pt=pt, blk=blk: e.tensor_tensor(
                out=S.mod[:, blk * 4:(blk + 1) * 4, :], in0=pt[:], in1=bc(bm[:, blk * 4:(blk + 1) * 4].unsqueeze(2), [128, 4, 3]), op=ALU.add),
                [pt, bm], [S.mod])
        for i in range(2):
            sc = S.mod[:, (3 * i + 1) * 16:(3 * i + 2) * 16, :]
            P.op("dve", lambda e, i=i, sc=sc: e.scalar_tensor_tensor(
                out=S.gs[i][:], in0=sc, scalar=1.0, in1=bc(prm[:, i, :].unsqueeze(2), [128, 16, 3]), op0=ALU.add, op1=ALU.mult),
                [S.mod, prm], [S.gs[i]])


def phase_norm(S, l, b, which, hT):
    nc, P = S.nc, S.P
    with ExitStack() as es:
        xc = [Tl(nc, es, [128, NT], F32, name="xc") for _ in range(3)]
        sq = [Tl(nc, es, [128, NT], F32, name="sq") for _ in range(2)]
        ps = [Tl(nc, es, [128, 512], F32, psum=True, name="psn") for _ in range(5)]
        rstd = Tl(nc, es, [128, NT], F32, name="rstd")
        epsb = Tl(nc, es, [128, 1], F32, name="epsb")
        P.op("dve", lambda e: e.memset(epsb[:], 1e-6), [], [epsb])
        for c in range(16):
            xi, si = xc[c % 3], sq[c % 2]
            P.dma("sp" if c % 2 == 0 else "act", xi[:], S.xT[b, c * 128:(c + 1) * 128, :], writes=[xi])
            if c % 2 == 0:
                P.op("act", lambda e, xi=xi, si=si: e.activation(out=si[:], in_=xi[:], func=AF.Square), [xi], [si])
            else:
                P.op("dve", lambda e, xi=xi, si=si: e.tensor_tensor(out=si[:], in0=xi[:], in1=xi[:], op=ALU.mult), [xi], [si])
            for ti, (t0, tn) in enumerate(TT):
                P.op("pe", lambda e, si=si, ti=ti, t0=t0, tn=tn, c=c: e.matmul(
                    out=ps[ti][:, :tn], lhsT=S.ones[:], rhs=si[:, t0:t0 + tn], start=(c == 0), stop=(c == 15)),
                    [si, S.ones], [ps[ti]])
        for ti, (t0, tn) in enumerate(TT):
            P.op("act", lambda e, ti=ti, t0=t0, tn=tn: e.activation(
                out=rstd[:, t0:t0 + tn], in_=ps[ti][:, :tn], func=AF.Sqrt, bias=epsb[:], scale=1.0 / DM), [ps[ti], epsb], [rstd])
        P.op("dve", lambda e: e.reciprocal(out=rstd[:], in_=rstd[:]), [rstd], [rstd])
        shi = 3 * which
        for c in range(16):
            xi, si = xc[c % 3], sq[c % 2]
            P.dma("sp" if c % 2 == 0 else "act", xi[:], S.xT[b, c * 128:(c + 1) * 128, :], writes=[xi])
            for (s0, s1), v in zip(SEGS, (2, b)):
                P.op("dve", lambda e, xi=xi, si=si, s0=s0, s1=s1, v=v, c=c: e.scalar_tensor_tensor(
                    out=si[:, s0:s1], in0=xi[:, s0:s1], scalar=S.gs[which][:, c, v:v + 1], in1=rstd[:, s0:s1], op0=ALU.mult, op1=ALU.mult),
                    [xi, rstd, S.gs[which]], [si])
                P.op("act", lambda e, si=si, s0=s0, s1=s1, v=v, c=c: e.activation(
                    out=hT[:, c, s0:s1], in_=si[:, s0:s1], func=AF.Identity, bias=S.mod[:, shi * 16 + c, v:v + 1], scale=1.0),
                    [si, S.mod], [hT])


def proj(S, es, w_ap, ncols, rhs, KC, evac, wq="pool", blk=512, nps=7):
    nc, P = S.nc, S.P
    wt = [Tl(nc, es, [128, KC, blk], BF16, name="wp") for _ in range(2)]
    ps = [Tl(nc, es, [128, 512], F32, psum=True, name="psp") for _ in range(nps)]
    k = 0
    nblk = (ncols + blk - 1) // blk
    for bi in range(nblk):
        w = wt[bi % 2]
        c0 = bi * blk
        cn = min(blk, ncols - c0)
        P.dma(wq, w[:, :, :cn], w_ap[:, c0:c0 + cn].rearrange("(kc p) n -> p kc n", p=128), writes=[w])
        for j in range((cn + 127) // 128):
            n = min(128, cn - j * 128)
            tiles = []
            for (t0, tn) in TT:
                pt = ps[k % nps]
                k += 1
                for kc in range(KC):
                    P.op("pe", lambda e, pt=pt, w=w, j=j, n=n, kc=kc, t0=t0, tn=tn: e.matmul(
                        out=pt[:n, :tn], lhsT=w[:, kc, j * 128:j * 128 + n], rhs=rhs[:, kc, t0:t0 + tn], start=(kc == 0), stop=(kc == KC - 1)),
                        [w, rhs], [pt])
                tiles.append((pt, t0, tn))
            evac(bi * (blk // 128) + j, n, tiles)


def phase_inproj(S, l, hT):
    nc, P = S.nc, S.P
    with ExitStack() as es:
        st = [Tl(nc, es, [128, NT], F32, name="ipst") for _ in range(3)]
        cnt = [0]

        def evac(ci, n, tiles, key):
            so = st[cnt[0] % 3]
            cnt[0] += 1
            for i, (pt, t0, tn) in enumerate(tiles):
                if i % 2 == 0:
                    P.op("dve", lambda e, pt=pt, t0=t0, tn=tn, so=so, n=n: e.tensor_copy(out=so[:n, t0:t0 + tn], in_=pt[:n, :tn]), [pt], [so])
                else:
                    P.op("act", lambda e, pt=pt, t0=t0, tn=tn, so=so, n=n: e.activation(out=so[:n, t0:t0 + tn], in_=pt[:n, :tn], func=AF.Copy), [pt], [so])
            P.dma("sp" if ci % 2 == 0 else "act", S.PF[key][ci * 128:ci * 128 + n, :], so[:n, :], reads=[so], writes=[])

        for key, c0, cn in (("g", 0, 6176), ("s", 6176, 10368), ("r", 16544, 6784), ("t", 23328, 6144)):
            with ExitStack() as es2:
                proj(S, es2, S.W["w_in"][l][:, c0:c0 + cn], cn, hT, 16, lambda ci, n, tiles, key=key: evac(ci, n, tiles, key))
            P.barrier()


def chunk_order(rev):
    return list(range(36)) if not rev else [3, 2, 1, 0] + list(range(35, 3, -1))


def seg_cumsum(S, out, src, tot, tmp, rev):
    P = S.P
    P.op("dve", lambda e: e.tensor_tensor_scan(out=out[:], data0=S.segmask[:out.t.shape[0], :], data1=src[:], initial=0.0, op0=ALU.mult, op1=ALU.add),
         [src, S.segmask], [out])
    P.op("dve", lambda e: e.tensor_copy(out=tot[:], in_=out[:].rearrange("p (c q) -> p c q", q=64)[:, :, 63]), [out], [tot])
    if rev:
        P.op("dve", lambda e: e.tensor_tensor(out=tmp[:], in0=src[:], in1=out[:], op=ALU.subtract), [src, out], [tmp])
        P.op("dve", lambda e: e.tensor_tensor(out=out[:].rearrange("p (c q) -> p c q", q=64), in0=tmp[:].rearrange("p (c q) -> p c q", q=64),
                                              in1=bc(tot[:].unsqueeze(2), [out.t.shape[0], 36, 64]), op=ALU.add), [tmp, tot], [out])


def phase_gla(S, l, b):
    nc, P, W = S.nc, S.P, S.W
    PFg = S.PF["g"]
    with ExitStack() as es:
        adT = [Tl(nc, es, [17, NT], F32, name="adT") for _ in range(2)]
        waA = [Tl(nc, es, [17, 1024], F32, name="waA") for _ in range(2)]
        ng = Tl(nc, es, [128, 4], F32, name="ng")
        vT = Tl(nc, es, [64, 36, 512], BF16, name="vT")
        sg = Tl(nc, es, [128, 4, NT], BF16, name="sg")
        ogh = Tl(nc, es, [128, 4, NT], BF16, name="ogh")
        qd = Tl(nc, es, [128, 2, NT], BF16, name="qd")
        ki = Tl(nc, es, [128, 2, NT], BF16, name="ki")
        kdT = Tl(nc, es, [128, 2, NT], BF16, name="kdT")
        etot = Tl(nc, es, [128, 2, 36], F32, name="etot")
        Sf = Tl(nc, es, [128, 2, 512], F32, name="Sf")
        Sb = Tl(nc, es, [128, 2, 512], BF16, name="Sb")
        eps = Tl(nc, es, [128, 1], F32, name="epsg")
        P.op("dve", lambda e: e.memset(eps[:], 1e-6), [], [eps])
        P.dma("sp", ng[:], W["gla_norm_g"][l, :].rearrange("(c p) -> p c", p=128), writes=[ng], allow_slow_non_contiguous=True)
        for d in range(2):
            P.op("dve", lambda e, d=d: e.memset(adT[d][:], 1.0), [], [adT[d]])
            P.dma("sp", adT[d][0:16, :], PFg[O_AD + d * 16:O_AD + d * 16 + 16, :], writes=[adT[d]])
            P.dma("sp", waA[d][0:16, :], W["gla_wa_up"][l, d], writes=[waA[d]])
            P.dma("sp", waA[d][16:17, :], W["gla_ba"][l, d:d + 1, :], writes=[waA[d]])
        for h in range(4):
            with ExitStack() as es1:
                ld = [Tl(nc, es1, [128, NT], F32, name="gld") for _ in range(2)]
                pv = [Tl(nc, es1, [64, 4, 128], F32, psum=True, name="pv") for _ in range(2)]
                for j in range(4):
                    P.dma("sp", ld[j % 2][:], PFg[O_G + h * 512 + j * 128:O_G + h * 512 + (j + 1) * 128, :], writes=[ld[j % 2]])
                    P.op("act", lambda e, j=j: e.activation(out=sg[:, j, :], in_=ld[j % 2][:], func=AF.Silu), [ld[j % 2]], [sg])
                for j in range(4):
                    vl = ld[j % 2]
                    P.dma("act", vl[:], PFg[O_V + h * 512 + j * 128:O_V + h * 512 + (j + 1) * 128, :], writes=[vl])
                    for cg in range(9):
                        pt = pv[cg % 2]
                        for a in range(4):
                            ck = cg * 4 + a
                            P.op("pe", lambda e, pt=pt, a=a, ck=ck, vl=vl: e.transpose(out=pt[:, a, :], in_=vl[:, ck * 64:(ck + 1) * 64], identity=S.ident[:]),
                                 [vl, S.ident], [pt])
                        if cg % 2 == 0:
                            P.op("dve", lambda e, pt=pt, cg=cg, j=j: e.tensor_copy(out=vT[:, cg * 4:cg * 4 + 4, j * 128:(j + 1) * 128], in_=pt[:]), [pt], [vT])
                        else:
                            P.op("act", lambda e, pt=pt, cg=cg, j=j: e.activation(out=vT[:, cg * 4:cg * 4 + 4, j * 128:(j + 1) * 128], in_=pt[:], func=AF.Copy), [pt], [vT])
            P.barrier()
            for d in range(2):
                rev = d == 1
                with ExitStack() as es1:
                    pl = [Tl(nc, es1, [128, 512], F32, psum=True, name="pl") for _ in range(5)]
                    nl = Tl(nc, es1, [128, NT], F32, name="nl")
                    Bc = Tl(nc, es1, [128, NT], F32, name="Bc")
                    tmp = Tl(nc, es1, [128, NT], F32, name="tmp")
                    ex = Tl(nc, es1, [128, NT], F32, name="ex")
                    qf = Tl(nc, es1, [128, NT], F32, name="qf")
                    kf = Tl(nc, es1, [128, NT], F32, name="kf")
                    tot = Tl(nc, es1, [128, 36], F32, name="tot")
                    for dc in range(2):
                        dg = h * 256 + dc * 128
                        P.dma("sp", qf[:], PFg[O_Q + dg:O_Q + dg + 128, :], writes=[qf])
                        P.dma("act", kf[:], PFg[O_K + dg:O_K + dg + 128, :], writes=[kf])
                        for ti, (t0, tn) in enumerate(TT):
                            P.op("pe", lambda e, ti=ti, t0=t0, tn=tn, dg=dg: e.matmul(
                                out=pl[ti][:, :tn], lhsT=waA[d][:, dg:dg + 128], rhs=adT[d][:, t0:t0 + tn], start=True, stop=True),
                                [waA[d], adT[d]], [pl[ti]])
                            P.op("act", lambda e, ti=ti, t0=t0, tn=tn: e.activation(out=nl[:, t0:t0 + tn], in_=pl[ti][:, :tn], func=AF.Exp, scale=-1.0),
                                 [pl[ti]], [nl])
                        P.op("act", lambda e: e.activation(out=nl[:], in_=nl[:], func=AF.Ln, bias=S.ones[:, 0:1], scale=1.0), [nl, S.ones], [nl])
                        seg_cumsum(S, Bc, nl, tot, tmp, rev)
                        P.op("act", lambda e, dc=dc: e.activation(out=etot[:, dc, :], in_=tot[:], func=AF.Exp, scale=-1.0 / 16), [tot], [etot])
                        P.op("act", lambda e: e.activation(out=ex[:], in_=Bc[:], func=AF.Exp, scale=-1.0 / 16), [Bc], [ex])
                        P.op("dve", lambda e, dc=dc: e.scalar_tensor_tensor(out=qd[:, dc, :], in0=qf[:], scalar=1.0 / 16, in1=ex[:], op0=ALU.mult, op1=ALU.mult),
                             [qf, ex], [qd])
                        P.op("act", lambda e: e.activation(out=ex[:], in_=Bc[:], func=AF.Exp, scale=1.0 / 16), [Bc], [ex])
                        P.op("dve", lambda e, dc=dc: e.tensor_tensor(out=ki[:, dc, :], in0=kf[:], in1=ex[:], op=ALU.mult), [kf, ex], [ki])
                        P.op("dve", lambda e: e.tensor_tensor(out=tmp[:].rearrange("p (c q) -> p c q", q=64), in0=Bc[:].rearrange("p (c q) -> p c q", q=64),
                                                              in1=bc(tot[:].unsqueeze(2), [128, 36, 64]), op=ALU.subtract), [Bc, tot], [tmp])
                        P.op("act", lambda e: e.activation(out=ex[:], in_=tmp[:], func=AF.Exp, scale=1.0 / 16), [tmp], [ex])
                        P.op("dve", lambda e, dc=dc: e.tensor_tensor(out=kdT[:, dc, :], in0=kf[:], in1=ex[:], op=ALU.mult), [kf, ex], [kdT])
                P.barrier()
                with ExitStack() as es1:
                    pa = Tl(nc, es1, [64, 64], F32, psum=True, name="pa")
                    pk = Tl(nc, es1, [64, 2, 128], BF16, psum=True, name="pk")
                    po = [Tl(nc, es1, [64, 512], F32, psum=True, name="po") for _ in range(2)]
                    pu = [Tl(nc, es1, [128, 512], F32, psum=True, name="pu") for _ in range(2)]
                    ptr = Tl(nc, es1, [128, 4, 64], F32, psum=True, name="ptr")
                    att = [Tl(nc, es1, [64, 64], BF16, name="att") for _ in range(2)]
                    kd = [Tl(nc, es1, [64, 256], BF16, name="kd") for _ in range(2)]
                    ost = [Tl(nc, es1, [64, 512], F32, name="ost") for _ in range(2)]
                    oft = [Tl(nc, es1, [64, 512], F32, name="oft") for _ in range(2)]
                    sqt = Tl(nc, es1, [64, 512], F32, name="sqt")
                    ss = Tl(nc, es1, [64, 1], F32, name="ss")
                    P.op("dve", lambda e: e.memset(Sf[:], 0.0), [], [Sf])
                    P.op("dve", lambda e: e.memset(Sb[:], 0.0), [], [Sb])
                    deferred = []
                    for i, ck in enumerate(chunk_order(rev)):
                        ts = slice(ck * 64, ck * 64 + 64)
                        a_, k_, o_, f_, p_ = att[i % 2], kd[i % 2], ost[i % 2], oft[i % 2], po[i % 2]
                        for dc in range(2):
                            P.op("pe", lambda e, dc=dc, ts=ts: e.matmul(out=pa[:], lhsT=ki[:, dc, ts], rhs=qd[:, dc, ts], start=(dc == 0), stop=(dc == 1)),
                                 [ki, qd], [pa])
                        P.op("dve", lambda e, a_=a_: e.tensor_tensor(out=a_[:], in0=pa[:], in1=S.tri[d][:], op=ALU.mult), [pa, S.tri[d]], [a_])
                        for dc in range(2):
                            P.op("pe", lambda e, dc=dc, ts=ts: e.transpose(out=pk[:, dc, :], in_=kdT[:, dc, ts], identity=S.identb[:]), [kdT, S.identb], [pk])
                        P.op("act", lambda e, k_=k_: e.activation(out=k_[:], in_=pk[:].rearrange("p a b -> p (a b)"), func=AF.Copy), [pk], [k_])
                        for dc in range(2):
                            P.op("pe", lambda e, dc=dc, ts=ts, p_=p_: e.matmul(out=p_[:], lhsT=qd[:, dc, ts], rhs=Sb[:, dc, :], start=(dc == 0), stop=False),
                                 [qd, Sb], [p_])
                        P.op("pe", lambda e, a_=a_, ck=ck, p_=p_: e.matmul(out=p_[:], lhsT=a_[:], rhs=vT[:, ck, :], start=False, stop=True), [a_, vT], [p_])
                        for dc in range(2):
                            P.op("pe", lambda e, dc=dc, k_=k_, ck=ck: e.matmul(out=pu[dc][:], lhsT=k_[:, dc * 128:(dc + 1) * 128], rhs=vT[:, ck, :], start=True, stop=True),
                                 [k_, vT], [pu[dc]])
                            P.op("dve", lambda e, dc=dc, ck=ck: e.scalar_tensor_tensor(out=Sf[:, dc, :], in0=Sf[:, dc, :], scalar=etot[:, dc, ck:ck + 1], in1=pu[dc][:],
                                                                                       op0=ALU.mult, op1=ALU.add), [Sf, etot, pu[dc]], [Sf])
                            P.op("act", lambda e, dc=dc: e.activation(out=Sb[:, dc, :], in_=Sf[:, dc, :], func=AF.Copy), [Sf], [Sb])
                        while len(deferred) > 0:
                            deferred.pop(0)()
                        gof = S.GOF[ts, h * 512:(h + 1) * 512]
                        if not rev:
                            P.op("act", lambda e, o_=o_, p_=p_: e.activation(out=o_[:], in_=p_[:], func=AF.Copy), [p_], [o_])
                            P.dma("sp", gof, o_[:], reads=[o_], writes=[])
                        else:
                            P.dma("sp", f_[:], gof, writes=[f_])
                            P.op("dve", lambda e, o_=o_, p_=p_, f_=f_: e.tensor_tensor(out=o_[:], in0=p_[:], in1=f_[:], op=ALU.add), [p_, f_], [o_])
                            P.op("act", lambda e, o_=o_: e.activation(out=sqt[:], in_=o_[:], func=AF.Square), [o_], [sqt])
                            P.op("dve", lambda e: e.reduce_sum(out=ss[:], in_=sqt[:], axis=AX.X), [sqt], [ss])
                            P.op("act", lambda e: e.activation(out=ss[:], in_=ss[:], func=AF.Sqrt, bias=eps[:64, :], scale=1.0 / 512), [ss, eps], [ss])
                            P.op("dve", lambda e: e.reciprocal(out=ss[:], in_=ss[:]), [ss], [ss])
                            P.op("dve", lambda e, o_=o_: e.tensor_scalar(out=o_[:], in0=o_[:], scalar1=ss[:, 0:1], scalar2=None, op0=ALU.mult), [o_, ss], [o_])
                            def post(o_=o_, ts=ts):
                                for j in range(4):
                                    P.op("pe", lambda e: e.transpose(out=ptr[:, j, :], in_=o_[:, j * 128:(j + 1) * 128], identity=S.ident[:64, :64]), [o_, S.ident], [ptr])
                                for j in range(4):
                                    P.op("dve", lambda e: e.scalar_tensor_tensor(out=ogh[:, j, ts], in0=ptr[:, j, :], scalar=ng[:, j:j + 1], in1=sg[:, j, ts],
                                                                                 op0=ALU.mult, op1=ALU.mult), [ptr, ng, sg], [ogh])
                            deferred.append(post)
                    while len(deferred) > 0:
                        deferred.pop(0)()
                P.barrier()
            for j in range(4):
                P.dma("sp", S.OG[h * 512 + j * 128:h * 512 + (j + 1) * 128, :], ogh[:, j, :], reads=[ogh], writes=[])
            P.barrier()


def vec_pp(S, es, src_ap, nchunk, name):
    t = Tl(S.nc, es, [128, nchunk], F32, name=name)
    S.P.dma("sp", t[:], src_ap.rearrange("(c p) -> p c", p=128), writes=[t], allow_slow_non_contiguous=True)
    return t


def phase_ssd(S, l, b):
    nc, P, W = S.nc, S.P, S.W
    PFs = S.PF["s"]
    OX, OB, OC, ODT = 4096, 8192, 9216, 10240
    with ExitStack() as es:
        dtT = Tl(nc, es, [64, 36, 128], F32, name="dtT")
        cumT = Tl(nc, es, [64, 36, 128], F32, name="cumT")
        laT = Tl(nc, es, [64, 36, 128], F32, name="laT")
        cml = Tl(nc, es, [64, 36, 128], F32, name="cml")
        five = Tl(nc, es, [64, 1], F32, name="five")
        P.op("dve", lambda e: e.memset(five[:], 5.0), [], [five])
        Dbc = Tl(nc, es, [64, 64], F32, name="Dbc")
        P.dma("sp", Dbc[:], bc(W["ssm_d"][l:l + 1, :], [64, 64]), writes=[Dbc])
        with ExitStack() as es1:
            raw = Tl(nc, es1, [128, NT], F32, name="raw")
            la = Tl(nc, es1, [128, NT], F32, name="la")
            cum = Tl(nc, es1, [128, NT], F32, name="cum")
            tmp = Tl(nc, es1, [128, NT], F32, name="tmp")
            tot = Tl(nc, es1, [128, 36], F32, name="tot")
            pp = Tl(nc, es1, [128, 2], F32, name="pp")
            pt = [Tl(nc, es1, [64, 4, 128], F32, psum=True, name="pts") for _ in range(2)]
            P.dma("sp", raw[:], PFs[ODT:ODT + 128, :], writes=[raw])
            P.dma("sp", pp[:, 0:1], W["ssm_dt_bias"][l].rearrange("d (h o) -> (d h) o", o=1), writes=[pp])
            P.dma("sp", pp[:, 1:2], W["ssm_a_log"][l].rearrange("d (h o) -> (d h) o", o=1), writes=[pp])
            P.op("act", lambda e: e.activation(out=pp[:, 1:2], in_=pp[:, 1:2], func=AF.Exp), [pp], [pp])
            P.op("dve", lambda e: e.tensor_scalar(out=pp[:, 1:2], in0=pp[:, 1:2], scalar1=-1.0, scalar2=None, op0=ALU.mult), [pp], [pp])
            P.op("act", lambda e: e.activation(out=raw[:], in_=raw[:], func=AF.Exp, bias=pp[:, 0:1], scale=1.0), [raw, pp], [raw])
            P.op("act", lambda e: e.activation(out=raw[:], in_=raw[:], func=AF.Ln, bias=S.ones[:, 0:1], scale=1.0), [raw, S.ones], [raw])
            P.op("dve", lambda e: e.tensor_scalar(out=la[:], in0=raw[:], scalar1=pp[:, 1:2], scalar2=None, op0=ALU.mult), [raw, pp], [la])
            P.op("dve", lambda e: e.tensor_tensor_scan(out=cum[:], data0=S.segmask[:], data1=la[:], initial=0.0, op0=ALU.mult, op1=ALU.add), [la, S.segmask], [cum])
            P.op("dve", lambda e: e.tensor_copy(out=tot[:], in_=cum[:].rearrange("p (c q) -> p c q", q=64)[:, :, 63]), [cum], [tot])
            P.op("dve", lambda e: e.tensor_tensor(out=tmp[64:128, :], in0=la[64:128, :], in1=cum[64:128, :], op=ALU.subtract), [la, cum], [tmp])
            P.op("dve", lambda e: e.tensor_tensor(out=cum[64:128, :].rearrange("p (c q) -> p c q", q=64), in0=tmp[64:128, :].rearrange("p (c q) -> p c q", q=64),
                                                  in1=bc(tot[64:128, :].unsqueeze(2), [64, 36, 64]), op=ALU.add), [tmp, tot], [cum])
            P.dma("sp", S.CUM, cum[:], reads=[cum], writes=[])
            k = 0
            for src, dst in ((raw, dtT), (cum, cumT), (la, laT)):
                for cg in range(9):
                    p_ = pt[k % 2]
                    k += 1
                    for a in range(4):
                        ck = cg * 4 + a
                        P.op("pe", lambda e: e.transpose(out=p_[:, a, :], in_=src[:, ck * 64:(ck + 1) * 64], identity=S.ident[:]), [src, S.ident], [p_])
                    P.op("act", lambda e: e.activation(out=dst[:, cg * 4:cg * 4 + 4, :], in_=p_[:], func=AF.Copy), [p_], [dst])
            P.op("act", lambda e: e.activation(out=cml[:], in_=dtT[:], func=AF.Ln), [dtT], [cml])
            P.op("dve", lambda e: e.tensor_tensor(out=cml[:], in0=cumT[:], in1=cml[:], op=ALU.subtract), [cumT, cml], [cml])
        P.barrier()
        with ExitStack() as es1:
            cw = [vec_pp(S, es1, W["ssm_conv_w"][l, j, :], 48, "cw") for j in range(3)]
            cb = vec_pp(S, es1, W["ssm_conv_b"][l, :], 48, "cb")
            xin = [Tl(nc, es1, [128, NT], F32, name="xin") for _ in range(2)]
            yc = [Tl(nc, es1, [128, NT], F32, name="yc") for _ in range(2)]
            yb = [Tl(nc, es1, [128, NT], BF16, name="yb") for _ in range(2)]
            stg = [Tl(nc, es1, [128, 18, 128], BF16, name="stg") for _ in range(2)]
            pt = [Tl(nc, es1, [128, 4, 128], BF16, psum=True, name="ptc") for _ in range(2)]
            k = 0
            for c in range(48):
                xi, yi, ybi, sg_ = xin[c % 2], yc[c % 2], yb[c % 2], stg[c % 2]
                P.dma("sp" if c % 2 == 0 else "act", xi[:], PFs[OX + c * 128:OX + (c + 1) * 128, :], writes=[xi])
                P.op("dve", lambda e: e.tensor_scalar(out=yi[:], in0=xi[:], scalar1=cw[1][:, c:c + 1], scalar2=None, op0=ALU.mult), [xi, cw[1]], [yi])
                for (s0, s1) in SEGS:
                    P.op("dve", lambda e: e.scalar_tensor_tensor(out=yi[:, s0 + 1:s1], in0=xi[:, s0:s1 - 1], scalar=cw[0][:, c:c + 1], in1=yi[:, s0 + 1:s1],
                                                                 op0=ALU.mult, op1=ALU.add), [xi, cw[0], yi], [yi])
                    P.op("dve", lambda e: e.scalar_tensor_tensor(out=yi[:, s0:s1 - 1], in0=xi[:, s0 + 1:s1], scalar=cw[2][:, c:c + 1], in1=yi[:, s0:s1 - 1],
                                                                 op0=ALU.mult, op1=ALU.add), [xi, cw[2], yi], [yi])
                P.op("act", lambda e: e.activation(out=ybi[:], in_=yi[:], func=AF.Silu, bias=cb[:, c:c + 1], scale=1.0), [yi, cb], [ybi])
                if c >= 32:
                    P.dma("sp", S.BCF[(c - 32) * 128:(c - 31) * 128, :], ybi[:], reads=[ybi], writes=[])
                if c < 40:
                    for tg in range(5):
                        p_ = pt[k % 2]
                        k += 1
                        na = 4 if tg < 4 else 2
                        for a in range(na):
                            tt = tg * 4 + a
                            P.op("pe", lambda e: e.transpose(out=p_[:, a, :], in_=ybi[:, tt * 128:(tt + 1) * 128], identity=S.identb[:]), [ybi, S.identb], [p_])
                        if tg % 2 == 0:
                            P.op("dve", lambda e: e.tensor_copy(out=sg_[:, tg * 4:tg * 4 + na, :], in_=p_[:, :na, :]), [p_], [sg_])
                        else:
                            P.op("act", lambda e: e.activation(out=sg_[:, tg * 4:tg * 4 + na, :], in_=p_[:, :na, :], func=AF.Copy), [p_], [sg_])
                    dst = S.XT[:, c * 128:(c + 1) * 128] if c < 32 else S.BT[:, (c - 32) * 128:(c - 31) * 128]
                    P.dma("act", dst.rearrange("(tt p) c -> p tt c", p=128), sg_[:], reads=[sg_], writes=[])
        P.barrier()
        for g in range(8):
            with ExitStack() as es1:
                BF = Tl(nc, es1, [128, NT], BF16, name="BF")
                CF = Tl(nc, es1, [128, NT], BF16, name="CF")
                P.dma("sp", BF[:], S.BCF[g * 128:(g + 1) * 128, :], writes=[BF])
                P.dma("sp", CF[:], S.BCF[1024 + g * 128:1024 + (g + 1) * 128, :], writes=[CF])
                def two(shape, dt, name, psum=False):
                    return [[Tl(nc, es1, shape, dt, psum=psum, name=name) for _ in range(2)] for _ in range(2)]
                hf = [Tl(nc, es1, [128, 512], F32, name="hf") for _ in range(2)]
                hb = [Tl(nc, es1, [128, 512], BF16, name="hb") for _ in range(2)]
                xc, bt, cbc, wd = two([64, 512], BF16, "xc"), two([64, 128], BF16, "bt"), two([64, 8, 64], F32, "cbc"), two([64, 8, 64], BF16, "wd")
                cbm, xd, ys, yf = two([64, 64], F32, "cbm"), two([64, 512], BF16, "xd"), two([64, 512], F32, "ys"), two([64, 512], F32, "yf")
                ec, sd, etb = two([64, 8], F32, "ec"), two([64, 8], F32, "sd"), two([128, 8], F32, "etb")
                pcb = Tl(nc, es1, [64, 64], F32, psum=True, name="pcb")
                py1 = [Tl(nc, es1, [64, 512], F32, psum=True, name="py1") for _ in range(2)]
                py2 = [Tl(nc, es1, [64, 512], F32, psum=True, name="py2") for _ in range(2)]
                pst = Tl(nc, es1, [128, 512], F32, psum=True, name="pst")
                ptb = Tl(nc, es1, [128, 8], F32, psum=True, name="ptb")
                orders = [chunk_order(False), chunk_order(True)]
                pos = [{c: i for i, c in enumerate(o)} for o in orders]
                for d in range(2):
                    P.op("dve", lambda e: e.memset(hf[d][:], 0.0), [], [hf[d]])
                    P.op("dve", lambda e: e.memset(hb[d][:], 0.0), [], [hb[d]])

                def A1(d, i):
                    ck, r = orders[d][i], i % 2
                    ts = slice(ck * 64, ck * 64 + 64)
                    hs = slice(d * 64 + g * 8, d * 64 + g * 8 + 8)
                    P.dma("sp", xc[d][r][:], S.XT[ts, g * 512:(g + 1) * 512], writes=[xc[d][r]])
                    P.dma("sp", bt[d][r][:], S.BT[ts, g * 128:(g + 1) * 128], writes=[bt[d][r]])
                    P.dma("act", cbc[d][r][:], bc(S.CUM[d * 64 + g * 8:d * 64 + g * 8 + 8, ts].unsqueeze(0), [64, 8, 64]), writes=[cbc[d][r]])
                    P.op("pe", lambda e: e.matmul(out=pcb[:], lhsT=BF[:, ts], rhs=CF[:, ts], start=True, stop=True), [BF, CF], [pcb])
                    P.op("dve", lambda e: e.tensor_tensor(out=cbm[d][r][:], in0=pcb[:], in1=S.tri[d][:], op=ALU.mult), [pcb, S.tri[d]], [cbm[d][r]])
                    P.op("dve", lambda e: e.tensor_tensor(out=cbc[d][r][:], in0=cbc[d][r][:], in1=bc(cml[:, ck, hs].unsqueeze(2), [64, 8, 64]), op=ALU.subtract),
                         [cbc[d][r], cml], [cbc[d][r]])
                    P.op("act", lambda e: e.activation(out=cbc[d][r][:], in_=cbc[d][r][:], func=AF.Relu, bias=five[:], scale=-1.0), [cbc[d][r], five], [cbc[d][r]])
                    P.op("act", lambda e: e.activation(out=cbc[d][r][:], in_=cbc[d][r][:], func=AF.Exp, bias=five[:], scale=-1.0), [cbc[d][r], five], [cbc[d][r]])
                    P.op("act", lambda e: e.activation(out=ec[d][r][:], in_=cumT[:, ck, hs], func=AF.Exp), [cumT], [ec[d][r]])

                def A2(d, i):
                    ck, r = orders[d][i], i % 2
                    hs = slice(d * 64 + g * 8, d * 64 + g * 8 + 8)
                    P.op("dve", lambda e: e.tensor_tensor(out=wd[d][r][:], in0=cbc[d][r][:], in1=bc(cbm[d][r][:].unsqueeze(1), [64, 8, 64]), op=ALU.mult),
                         [cbc[d][r], cbm[d][r]], [wd[d][r]])
                    P.op("pe", lambda e: e.matmul(out=ptb[:], lhsT=S.ones[:64, :], rhs=laT[:, ck, hs], start=True, stop=True), [S.ones, laT], [ptb])
                    P.op("dve", lambda e: e.tensor_tensor(out=sd[d][r][:], in0=ptb[:64, :], in1=cumT[:, ck, hs], op=ALU.subtract), [ptb, cumT], [sd[d][r]])
                    P.op("act", lambda e: e.activation(out=etb[d][r][:], in_=ptb[:], func=AF.Exp), [ptb], [etb[d][r]])
                    P.op("act", lambda e: e.activation(out=sd[d][r][:], in_=sd[d][r][:], func=AF.Exp), [sd[d][r]], [sd[d][r]])
                    P.op("pool", lambda e: e.tensor_tensor(out=sd[d][r][:], in0=sd[d][r][:], in1=dtT[:, ck, hs], op=ALU.mult), [sd[d][r], dtT], [sd[d][r]])
                    P.op("pool", lambda e: e.tensor_tensor(out=xd[d][r][:].rearrange("p (a b) -> p a b", a=8), in0=xc[d][r][:].rearrange("p (a b) -> p a b", a=8),
                                                           in1=bc(sd[d][r][:].unsqueeze(2), [64, 8, 64]), op=ALU.mult), [xc[d][r], sd[d][r]], [xd[d][r]])

                def Bst(d, i):
                    ck, r = orders[d][i], i % 2
                    ts = slice(ck * 64, ck * 64 + 64)
                    for e_ in range(8):
                        P.op("pe", lambda e: e.matmul(out=py1[d][:, e_ * 64:(e_ + 1) * 64], lhsT=wd[d][r][:, e_, :], rhs=xc[d][r][:, e_ * 64:(e_ + 1) * 64], start=True, stop=True),
                             [wd[d][r], xc[d][r]], [py1[d]])
                    P.op("pe", lambda e: e.matmul(out=py2[d][:], lhsT=CF[:, ts], rhs=hb[d][:], start=True, stop=True), [CF, hb[d]], [py2[d]])
                    P.op("pe", lambda e: e.matmul(out=pst[:], lhsT=bt[d][r][:], rhs=xd[d][r][:], start=True, stop=True), [bt[d][r], xd[d][r]], [pst])
                    y_, f_ = ys[d][r], yf[d][r]
                    P.op("dve", lambda e: e.tensor_tensor(out=y_[:].rearrange("p (a b) -> p a b", a=8), in0=py2[d][:].rearrange("p (a b) -> p a b", a=8),
                                                          in1=bc(ec[d][r][:].unsqueeze(2), [64, 8, 64]), op=ALU.mult), [py2[d], ec[d][r]], [y_])
                    P.op("dve", lambda e: e.tensor_tensor(out=y_[:], in0=y_[:], in1=py1[d][:], op=ALU.add), [y_, py1[d]], [y_])
                    P.op("dve", lambda e: e.tensor_tensor(out=hf[d][:].rearrange("p (a b) -> p a b", a=8), in0=hf[d][:].rearrange("p (a b) -> p a b", a=8),
                                                          in1=bc(etb[d][r][:].unsqueeze(2), [128, 8, 64]), op=ALU.mult), [hf[d], etb[d][r]], [hf[d]])
                    P.op("dve", lambda e: e.tensor_tensor(out=hf[d][:], in0=hf[d][:], in1=pst[:], op=ALU.add), [hf[d], pst], [hf[d]])
                    P.op("act", lambda e: e.activation(out=hb[d][:], in_=hf[d][:], func=AF.Copy), [hf[d]], [hb[d]])
                    second = pos[d][ck] > pos[1 - d][ck]
                    ydst = S.YF[ts, g * 512:(g + 1) * 512]
                    key = P.B(("ssdyf", g, ck))
                    if not second:
                        P.dma("sp", ydst, y_[:], reads=[y_], writes=[key])
                    else:
                        P.dma("sp", f_[:], ydst, reads=[key], writes=[f_])
                        P.op("pool", lambda e: e.tensor_tensor(out=y_[:], in0=y_[:], in1=f_[:], op=ALU.add), [y_, f_], [y_])
                        P.op("pool", lambda e: e.tensor_tensor(out=f_[:].rearrange("p (a b) -> p a b", a=8), in0=xc[d][r][:].rearrange("p (a b) -> p a b", a=8),
                                                               in1=bc(Dbc[:, g * 8:g * 8 + 8].unsqueeze(2), [64, 8, 64]), op=ALU.mult), [xc[d][r], Dbc], [f_])
                        P.op("pool", lambda e: e.tensor_tensor(out=y_[:], in0=y_[:], in1=f_[:], op=ALU.add), [y_, f_], [y_])
                        P.dma("sp", S.YT[ts, g * 512:(g + 1) * 512], y_[:], reads=[y_], writes=[])

                for d in range(2):
                    A1(d, 0)
                    A2(d, 0)
                for i in range(36):
                    for d in range(2):
                        if i + 1 < 36:
                            A1(d, i + 1)
                        Bst(d, i)
                        if i + 1 < 36:
                            A2(d, i + 1)
                P.barrier()
        with ExitStack() as es1:
            ngv = vec_pp(S, es1, W["ssm_norm_g"][l, :], 32, "ngv")
            eps = Tl(nc, es1, [128, 1], F32, name="epss")
            P.op("dve", lambda e: e.memset(eps[:], 1e-6), [], [eps])
            yt = [Tl(nc, es1, [128, 4096], F32, name="yt") for _ in range(2)]
            zt = [Tl(nc, es1, [128, 32, 128], F32, name="zt") for _ in range(2)]
            yg = Tl(nc, es1, [128, 32, 128], F32, name="yg")
            sq = Tl(nc, es1, [128, 32, 128], F32, name="sq")
            rs = Tl(nc, es1, [128, 8, 128], F32, name="rs")
            ob = [Tl(nc, es1, [128, 32, 128], BF16, name="ob") for _ in range(2)]
            ptr = [Tl(nc, es1, [128, 4, 128], F32, psum=True, name="ptr4") for _ in range(2)]
            pss = [Tl(nc, es1, [128, 4, 128], F32, psum=True, name="pss") for _ in range(2)]
            k = 0
            for tt in range(18):
                r = tt % 2
                tsl = slice(tt * 128, (tt + 1) * 128)
                P.dma("sp", yt[r][:], S.YT[tsl, :], writes=[yt[r]])
                P.dma("act", zt[r][:], PFs[0:4096, tsl].rearrange("(c p) t -> p c t", p=128), writes=[zt[r]])
                P.op("act", lambda e: e.activation(out=zt[r][:], in_=zt[r][:], func=AF.Silu), [zt[r]], [zt[r]])
                for cg in range(8):
                    p_ = ptr[k % 2]
                    k += 1
                    for a in range(4):
                        c = cg * 4 + a
                        P.op("pe", lambda e: e.transpose(out=p_[:, a, :], in_=yt[r][:, c * 128:(c + 1) * 128], identity=S.ident[:]), [yt[r], S.ident], [p_])
                    P.op("dve", lambda e: e.tensor_tensor(out=yg[:, cg * 4:cg * 4 + 4, :], in0=p_[:], in1=zt[r][:, cg * 4:cg * 4 + 4, :], op=ALU.mult), [p_, zt[r]], [yg])
                P.op("pool", lambda e: e.tensor_tensor(out=sq[:], in0=yg[:], in1=yg[:], op=ALU.mult), [yg], [sq])
                for gg in range(8):
                    p_ = pss[gg // 4]
                    for a in range(4):
                        P.op("pe", lambda e: e.matmul(out=p_[:, gg % 4, :], lhsT=S.ones[:], rhs=sq[:, gg * 4 + a, :], start=(a == 0), stop=(a == 3)), [sq, S.ones], [p_])
                for hh in range(2):
                    P.op("act", lambda e: e.activation(out=rs[:, hh * 4:hh * 4 + 4, :], in_=pss[hh][:], func=AF.Sqrt, bias=eps[:], scale=1.0 / 512), [pss[hh], eps], [rs])
                P.op("dve", lambda e: e.reciprocal(out=rs[:], in_=rs[:]), [rs], [rs])
                P.op("dve", lambda e: e.tensor_tensor(out=yg[:].rearrange("p (g a) t -> p g a t", a=4), in0=yg[:].rearrange("p (g a) t -> p g a t", a=4),
                                                      in1=bc(rs[:].unsqueeze(2), [128, 8, 4, 128]), op=ALU.mult), [yg, rs], [yg])
                P.op("pool", lambda e: e.tensor_tensor(out=ob[r][:], in0=yg[:], in1=bc(ngv[:].unsqueeze(2), [128, 32, 128]), op=ALU.mult), [yg, ngv], [ob[r]])
                P.dma("sp", S.OS[:, tsl].rearrange("(c p) t -> p c t", p=128), ob[r][:], reads=[ob[r]], writes=[])


CEXP = 0.6065306597126334


def mix_shift(S, out, x, omu, hmu, n):
    P = S.P
    P.op("dve", lambda e: e.tensor_scalar(out=out[:n, :], in0=x[:n, :], scalar1=omu, scalar2=None, op0=ALU.mult), [x], [out])
    for (s0, s1) in SEGS:
        P.op("dve", lambda e: e.scalar_tensor_tensor(out=out[:n, s0 + 1:s1], in0=x[:n, s0:s1 - 1], scalar=hmu, in1=out[:n, s0 + 1:s1], op0=ALU.mult, op1=ALU.add), [x, out], [out])
        P.op("dve", lambda e: e.scalar_tensor_tensor(out=out[:n, s0:s1 - 1], in0=x[:n, s0 + 1:s1], scalar=hmu, in1=out[:n, s0:s1 - 1], op0=ALU.mult, op1=ALU.add), [x, out], [out])


def ckview(d, h):
    return d[:, :, h, :].rearrange("c k t -> k c t")


def phase_rwkv(S, l, b):
    nc, P, W = S.nc, S.P, S.W
    PFr = S.PF["r"]
    OWD, OAD, OGD = 6144, 6336, 6528
    mu = W["rwkv_mu"][l]
    with ExitStack() as es:
        with ExitStack() as es1:
            def pp64(src, name):
                t = Tl(nc, es1, [128, 16], F32, name=name)
                P.dma("sp", t[:], src.rearrange("(c p) -> p c", p=128), writes=[t], allow_slow_non_contiguous=True)
                return t
            mu3 = Tl(nc, es1, [128, 3, 16], F32, name="mu3")
            P.dma("sp", mu3[:], mu[0:6144].rearrange("(c j p) -> p c j", c=3, p=128), writes=[mu3], allow_slow_non_contiguous=True)
            omu3 = Tl(nc, es1, [128, 3, 16], F32, name="omu3")
            P.op("dve", lambda e: e.tensor_scalar(out=omu3[:], in0=mu3[:], scalar1=-1.0, scalar2=1.0, op0=ALU.mult, op1=ALU.add), [mu3], [omu3])
            P.op("dve", lambda e: e.tensor_scalar(out=mu3[:], in0=mu3[:], scalar1=0.5, scalar2=None, op0=ALU.mult), [mu3], [mu3])
            kkk = pp64(W["rwkv_k_k"][l], "kkk")
            kka_ = pp64(W["rwkv_k_a"][l], "kka")
            rk = pp64(W["rwkv_r_k"][l].rearrange("h k -> (h k)"), "rk")
            w0 = [pp64(W["rwkv_w0"][l, d], "w0") for d in range(2)]
            a0 = [pp64(W["rwkv_a0"][l, d], "a0") for d in range(2)]
            blk = Tl(nc, es1, [128, 128], F32, name="blk1")
            P.op("dve", lambda e: e.memset(blk[:], 0.0), [], [blk])
            P.op("dve", lambda e: e.memset(blk[0:64, 0:64], 1.0), [], [blk])
            P.op("dve", lambda e: e.memset(blk[64:128, 64:128], 1.0), [], [blk])
            mus = Tl(nc, es1, [128, 3], F32, name="mus")
            twd = [Tl(nc, es1, [96, NT], F32, name="twd") for _ in range(2)]
            adm = [Tl(nc, es1, [96, NT], F32, name="adm") for _ in range(2)]
            sgd = Tl(nc, es1, [128, 2, NT], F32, name="sgd")
            es_l = ExitStack()
            lin = Tl(nc, es_l, [128, NT], F32, name="lin")
            mixt = Tl(nc, es_l, [128, NT], F32, name="mixt")
            jobs = [(OWD, 96, twd[0], AF.Tanh), (OWD + 96, 96, twd[1], AF.Tanh), (OAD, 96, adm[0], None), (OAD + 96, 96, adm[1], None),
                    (OGD, 128, None, AF.Sigmoid), (OGD + 128, 128, None, AF.Sigmoid)]
            for ji, (r0, n, dst, fn) in enumerate(jobs):
                P.dma("sp", lin[:n, :], PFr[r0:r0 + n, :], writes=[lin])
                P.dma("sp", mus[:n, 0:1], mu[r0:r0 + n].rearrange("(p o) -> p o", o=1), writes=[mus])
                P.op("dve", lambda e: e.tensor_scalar(out=mus[:n, 1:2], in0=mus[:n, 0:1], scalar1=-1.0, scalar2=1.0, op0=ALU.mult, op1=ALU.add), [mus], [mus])
                P.op("dve", lambda e: e.tensor_scalar(out=mus[:n, 2:3], in0=mus[:n, 0:1], scalar1=0.5, scalar2=None, op0=ALU.mult), [mus], [mus])
                if dst is None:
                    dsl = sgd[:, ji - 4, :]
                    mix_shift(S, mixt, lin, mus[:n, 1:2], mus[:n, 2:3], n)
                    P.op("act", lambda e: e.activation(out=dsl, in_=mixt[:], func=fn), [mixt], [sgd])
                else:
                    mix_shift(S, dst, lin, mus[:n, 1:2], mus[:n, 2:3], n)
                    if fn is not None:
                        P.op("act", lambda e: e.activation(out=dst[:], in_=dst[:], func=fn), [dst], [dst])
            P.barrier()
            es_l.close()
            def T64(name, dt=F32):
                return Tl(nc, es1, [128, NT], dt, name=name)
            xin = [T64("rxin") for _ in range(1)]
            rm, km, vm, kkn, tA, tB, tC, ki, kkat, bon, Cc, av = (T64(n_) for n_ in ("rm", "km", "vm", "kkn", "tA", "tB", "tC", "ki", "kkat", "bon", "Cc", "av"))
            ob = [T64("rob", BF16) for _ in range(3)]
            tot = Tl(nc, es1, [128, 36], F32, name="rtot")
            gam = Tl(nc, es1, [128, 36], F32, name="rgam")
            wl = [Tl(nc, es1, [96, 128], F32, name="wl") for _ in range(4)]
            gl = Tl(nc, es1, [128, 2, 128], F32, name="gl")
            ps = [Tl(nc, es1, [128, 512], F32, psum=True, name="rps") for _ in range(7)]
            pk = [0]
            obk = [0]

            def mm_tiles(fn_lhs_rhs, evac):
                for (t0, tn) in TT:
                    pt = ps[pk[0] % 7]
                    pk[0] += 1
                    lst = fn_lhs_rhs(t0, tn)
                    for i, (lh, rh, deps) in enumerate(lst):
                        P.op("pe", lambda e: e.matmul(out=pt[:, :tn], lhsT=lh, rhs=rh, start=(i == 0), stop=(i == len(lst) - 1)), deps, [pt])
                    evac(pt, t0, tn)

            def dma_ck(dst, j, t_, q="act"):
                for hh in range(2):
                    P.dma(q, ckview(dst, 2 * j + hh), t_[hh * 64:(hh + 1) * 64, :].rearrange("k (c t) -> k c t", t=64), reads=[t_], writes=[])

            def store_ck(dst, j, fn):
                o_ = ob[obk[0] % 3]
                obk[0] += 1
                fn(o_)
                dma_ck(dst, j, o_)

            for j in range(16):
                hs = slice(j * 128, (j + 1) * 128)
                for ci, dstt in enumerate((rm, km, vm)):
                    xi = xin[0]
                    P.dma("sp", xi[:], PFr[ci * 2048 + j * 128:ci * 2048 + (j + 1) * 128, :], writes=[xi])
                    mix_shift(S, dstt, xi, omu3[:, ci, j:j + 1], mu3[:, ci, j:j + 1], 128)
                store_ck(S.RW["V"], j, lambda o_: P.op("act", lambda e: e.activation(out=o_[:], in_=vm[:], func=AF.Copy), [vm], [o_]))
                P.op("dve", lambda e: e.tensor_scalar(out=kkn[:], in0=km[:], scalar1=kkk[:, j:j + 1], scalar2=None, op0=ALU.mult), [km, kkk], [kkn])
                P.op("pool", lambda e: e.tensor_tensor(out=tA[:], in0=kkn[:], in1=kkn[:], op=ALU.mult), [kkn], [tA])
                mm_tiles(lambda t0, tn: [(blk[:], tA[:, t0:t0 + tn], [blk, tA])],
                         lambda pt, t0, tn: P.op("dve", lambda e: e.tensor_scalar_max(out=tB[:, t0:t0 + tn], in0=pt[:, :tn], scalar1=1e-24), [pt], [tB]))
                P.op("act", lambda e: e.activation(out=tB[:], in_=tB[:], func=AF.Sqrt), [tB], [tB])
                P.op("dve", lambda e: e.reciprocal(out=tB[:], in_=tB[:]), [tB], [tB])
                P.op("pool", lambda e: e.tensor_tensor(out=kkn[:], in0=kkn[:], in1=tB[:], op=ALU.mult), [kkn, tB], [kkn])
                P.dma("sp", gl[:], W["rwkv_g_up"][l][:, hs].rearrange("(kc p) n -> p kc n", p=128), writes=[gl])
                mm_tiles(lambda t0, tn: [(gl[:, kc, :], sgd[:, kc, t0:t0 + tn], [gl, sgd]) for kc in range(2)],
                         lambda pt, t0, tn: P.op("act", lambda e: e.activation(out=tC[:, t0:t0 + tn], in_=pt[:, :tn], func=AF.Copy), [pt], [tC]))
                dma_ck(S.RW["G"], j, tC)
                for d in range(2):
                    rev = d == 1
                    P.dma("sp", wl[d][:], W["rwkv_w_up"][l, d][:, hs], writes=[wl[d]])
                    P.dma("sp", wl[2 + d][:], W["rwkv_a_up"][l, d][:, hs], writes=[wl[2 + d]])
                    mm_tiles(lambda t0, tn: [(wl[d][:], twd[d][:, t0:t0 + tn], [wl[d], twd[d]])],
                             lambda pt, t0, tn: P.op("act", lambda e: e.activation(out=tA[:, t0:t0 + tn], in_=pt[:, :tn], func=AF.Sigmoid, bias=w0[d][:, j:j + 1], scale=1.0), [pt, w0[d]], [tA]))
                    mm_tiles(lambda t0, tn: [(wl[2 + d][:], adm[d][:, t0:t0 + tn], [wl[2 + d], adm[d]])],
                             lambda pt, t0, tn: P.op("act", lambda e: e.activation(out=av[:, t0:t0 + tn], in_=pt[:, :tn], func=AF.Sigmoid, bias=a0[d][:, j:j + 1], scale=1.0), [pt, a0[d]], [av]))
                    P.op("dve", lambda e: e.tensor_scalar(out=ki[:], in0=av[:], scalar1=-1.0, scalar2=kka_[:, j:j + 1], op0=ALU.add, op1=ALU.mult), [av, kka_], [ki])
                    P.op("dve", lambda e: e.scalar_tensor_tensor(out=ki[:], in0=ki[:], scalar=1.0, in1=km[:], op0=ALU.add, op1=ALU.mult), [ki, km], [ki])
                    P.op("pool", lambda e: e.tensor_tensor(out=kkat[:], in0=kkn[:], in1=av[:], op=ALU.mult), [kkn, av], [kkat])
                    P.op("dve", lambda e: e.scalar_tensor_tensor(out=tB[:], in0=rm[:], scalar=rk[:, j:j + 1], in1=ki[:], op0=ALU.mult, op1=ALU.mult), [rm, rk, ki], [tB])
                    if d == 0:
                        mm_tiles(lambda t0, tn: [(blk[:], tB[:, t0:t0 + tn], [blk, tB])],
                                 lambda pt, t0, tn: P.op("dve", lambda e: e.tensor_tensor(out=bon[:, t0:t0 + tn], in0=pt[:, :tn], in1=vm[:, t0:t0 + tn], op=ALU.mult), [pt, vm], [bon]))
                    else:
                        mm_tiles(lambda t0, tn: [(blk[:], tB[:, t0:t0 + tn], [blk, tB])],
                                 lambda pt, t0, tn: P.op("dve", lambda e: e.tensor_tensor(out=tC[:, t0:t0 + tn], in0=pt[:, :tn], in1=vm[:, t0:t0 + tn], op=ALU.mult), [pt, vm], [tC]))
                        P.op("pool", lambda e: e.tensor_tensor(out=bon[:], in0=bon[:], in1=tC[:], op=ALU.add), [bon, tC], [bon])
                        dma_ck(S.RW["BON"], j, bon)
                    seg_cumsum(S, Cc, tA, tot, tB, rev)
                    P.op("act", lambda e: e.activation(out=gam[:], in_=tot[:], func=AF.Exp, scale=-CEXP), [tot], [gam])
                    for hh in range(2):
                        P.dma("sp", S.RW["GAM"][d][2 * j + hh], gam[hh * 64:(hh + 1) * 64, :], reads=[gam], writes=[])
                    P.op("pool", lambda e: e.tensor_tensor(out=tB[:], in0=Cc[:], in1=tA[:], op=ALU.subtract), [Cc, tA], [tB])
                    P.op("act", lambda e: e.activation(out=tB[:], in_=tB[:], func=AF.Exp, scale=-CEXP), [tB], [tB])
                    store_ck(S.RW["A"][d], j, lambda o_: P.op("dve", lambda e: e.tensor_tensor(out=o_[:], in0=kkn[:], in1=tB[:], op=ALU.mult), [kkn, tB], [o_]))
                    P.op("act", lambda e: e.activation(out=tB[:], in_=Cc[:], func=AF.Exp, scale=-CEXP), [Cc], [tB])
                    store_ck(S.RW["R"][d], j, lambda o_: P.op("pool", lambda e: e.tensor_tensor(out=o_[:], in0=rm[:], in1=tB[:], op=ALU.mult), [rm, tB], [o_]))
                    P.op("act", lambda e: e.activation(out=tB[:], in_=Cc[:], func=AF.Exp, scale=CEXP), [Cc], [tB])
                    store_ck(S.RW["K"][d], j, lambda o_: P.op("dve", lambda e: e.tensor_tensor(out=o_[:], in0=ki[:], in1=tB[:], op=ALU.mult), [ki, tB], [o_]))
                    store_ck(S.RW["B"][d], j, lambda o_: P.op("pool", lambda e: e.tensor_tensor(out=o_[:], in0=kkat[:], in1=tB[:], op=ALU.mult), [kkat, tB], [o_]))
                    P.op("dve", lambda e: e.tensor_tensor(out=tB[:].rearrange("p (c q) -> p c q", q=64), in0=Cc[:].rearrange("p (c q) -> p c q", q=64),
                                                          in1=bc(tot[:].unsqueeze(2), [128, 36, 64]), op=ALU.subtract), [Cc, tot], [tB])
                    P.op("act", lambda e: e.activation(out=tB[:], in_=tB[:], func=AF.Exp, scale=CEXP), [tB], [tB])
                    store_ck(S.RW["KT"][d], j, lambda o_: P.op("dve", lambda e: e.tensor_tensor(out=o_[:], in0=ki[:], in1=tB[:], op=ALU.mult), [ki, tB], [o_]))
                    store_ck(S.RW["BT"][d], j, lambda o_: P.op("pool", lambda e: e.tensor_tensor(out=o_[:], in0=kkat[:], in1=tB[:], op=ALU.mult), [kkat, tB], [o_]))
        P.barrier()
        with ExitStack() as es1:
            def pp64b(src, name):
                t = Tl(nc, es1, [64, 32], F32, name=name)
                P.dma("sp", t[:], src.rearrange("(h k) -> k h", k=64), writes=[t], allow_slow_non_contiguous=True)
                return t
            lnw = pp64b(W["rwkv_ln_w"][l], "lnw")
            lnb = pp64b(W["rwkv_ln_b"][l], "lnb")
            eps = Tl(nc, es1, [64, 1], F32, name="repsl")
            P.op("dve", lambda e: e.memset(eps[:], 64e-5), [], [eps])
            SH = [64, 8, 64]
            pf = [Tl(nc, es1, SH, F32, psum=True, name="rpf") for _ in range(7)]
            pb = Tl(nc, es1, SH, BF16, psum=True, name="rpb")
            pfk = [0]

            def nps():
                t = pf[pfk[0] % 7]
                pfk[0] += 1
                return t
            def make_chain():
                C = K()
                C.ldb = {n_: [Tl(nc, es1, SH, BF16, name="l" + n_) for _ in range(2)] for n_ in ("A", "R", "K", "B", "KT", "BT", "V")}
                C.gamt = Tl(nc, es1, [64, 8, 36], F32, name="gamt")
                C.Tf = Tl(nc, es1, SH, F32, name="Tf")
                C.Tb = Tl(nc, es1, SH, BF16, name="Tb")
                C.PTf = [[Tl(nc, es1, SH, F32, name="PTf") for _ in range(6)] for _ in range(2)]
                C.Pb = [Tl(nc, es1, SH, BF16, name="Pb") for _ in range(2)]
                C.PTb = [Tl(nc, es1, SH, BF16, name="PTb") for _ in range(2)]
                dbl = lambda n_: [Tl(nc, es1, SH, BF16, name=n_) for _ in range(2)]
                C.LkT, C.MKT, C.MBT, C.KtT, C.BtT, C.VT = dbl("LkT"), dbl("MKT"), dbl("MBT"), dbl("KtT"), dbl("BtT"), dbl("VT")
                C.X = Tl(nc, es1, SH, F32, name="X")
                C.Ub = Tl(nc, es1, SH, BF16, name="Ub")
                C.ys = [Tl(nc, es1, SH, F32, name="rys") for _ in range(2)]
                C.yfl = [Tl(nc, es1, SH, F32, name="ryf") for _ in range(2)]
                C.bnl = [Tl(nc, es1, SH, F32, name="rbn") for _ in range(2)]
                C.ggl = [Tl(nc, es1, SH, F32, name="rgg") for _ in range(2)]
                C.sqv = Tl(nc, es1, SH, F32, name="rsq")
                C.st8 = Tl(nc, es1, [64, 8], F32, name="st8")
                C.st9 = Tl(nc, es1, [64, 8], F32, name="st9")
                C.obf = [Tl(nc, es1, SH, BF16, name="robf") for _ in range(2)]
                return C
            chains = [make_chain() for _ in range(2)]

            def mm8(pt, *pairs):
                for e_ in range(8):
                    for i_, (lh, rh) in enumerate(pairs):
                        P.op("pe", lambda e: e.matmul(out=pt[:, e_, :], lhsT=lh[:, e_, :], rhs=rh[:, e_, :], start=(i_ == 0), stop=(i_ == len(pairs) - 1)), [lh, rh], [pt])

            def masked(pt, dst, mask):
                P.op("dve", lambda e: e.tensor_tensor(out=dst[:], in0=pt[:], in1=bc(mask[:].unsqueeze(1), SH), op=ALU.mult), [pt, mask], [dst])

            for hgp in range(2):
                for d in range(2):
                    rev = d == 1
                    order = chunk_order(rev)
                    for c_ in range(2):
                        C = chains[c_]
                        hg = hgp * 2 + c_
                        P.dma("sp", C.gamt[:], S.RW["GAM"][d][hg * 8:hg * 8 + 8].rearrange("h k c -> k h c"), writes=[C.gamt])
                        P.op("dve", lambda e: e.memset(C.Tf[:], 0.0), [], [C.Tf])
                        P.op("dve", lambda e: e.memset(C.Tb[:], 0.0), [], [C.Tb])

                    def A_slices(C, hg, i):
                        ck, r_ = order[i], i % 2
                        h8 = slice(hg * 8, hg * 8 + 8)
                        L = {n_: C.ldb[n_][r_] for n_ in C.ldb}

                        def s0():
                            for qi, n_ in enumerate(("A", "R", "K", "B", "KT", "BT", "V")):
                                src = S.RW[n_] if n_ == "V" else S.RW[n_][d]
                                P.dma("sp" if qi % 2 == 0 else "act", L[n_][:], src[ck, :, h8, :], writes=[L[n_]])
                            p1 = nps(); mm8(p1, (L["B"], L["A"])); masked(p1, C.PTf[r_][0], S.nstr[d])
                            P.op("pool", lambda e: e.tensor_copy(out=C.PTb[0][:], in_=C.PTf[r_][0][:]), [C.PTf[r_][0]], [C.PTb[0]])
                            p2 = nps(); mm8(p2, (L["A"], L["B"])); masked(p2, C.Pb[0], S.nstr[1 - d])

                        def s1():
                            p3 = nps(); mm8(p3, (L["K"], L["A"])); masked(p3, C.LkT[r_], S.str_[d])
                            p4 = nps(); mm8(p4, (L["K"], L["R"])); masked(p4, C.MKT[r_], S.tri[d])
                            p5 = nps(); mm8(p5, (L["B"], L["R"])); masked(p5, C.MBT[r_], S.ntri[d])

                        def s2():
                            for src_, dst_, sc_ in ((L["KT"], C.KtT[r_], 1.0), (L["BT"], C.BtT[r_], -1.0), (L["V"], C.VT[r_], 1.0)):
                                for e_ in range(8):
                                    P.op("pe", lambda e: e.transpose(out=pb[:, e_, :], in_=src_[:, e_, :], identity=S.identb[:64, :64]), [src_, S.identb], [pb])
                                P.op("act", lambda e: e.activation(out=dst_[:], in_=pb[:], func=AF.Copy, scale=sc_), [pb], [dst_])

                        def sq(j):
                            def f():
                                cur, nxt = j % 2, (j + 1) % 2
                                pr = nps(); mm8(pr, (C.Pb[cur], C.PTb[cur]))
                                P.op("act", lambda e: e.activation(out=C.PTf[r_][j + 1][:], in_=pr[:], func=AF.Copy), [pr], [C.PTf[r_][j + 1]])
                                if j < 4:
                                    pq = nps(); mm8(pq, (C.PTb[cur], C.Pb[cur]))
                                    P.op("act", lambda e: e.activation(out=C.Pb[nxt][:], in_=pq[:], func=AF.Copy), [pq], [C.Pb[nxt]])
                                    P.op("pool", lambda e: e.tensor_copy(out=C.PTb[nxt][:], in_=C.PTf[r_][j + 1][:]), [C.PTf[r_][j + 1]], [C.PTb[nxt]])
                            return f
                        return [s0, s1, s2, sq(0), sq(1), sq(2), sq(3), sq(4)]

                    def B_steps(C, hg, i):
                        ck, r_ = order[i], i % 2
                        h8 = slice(hg * 8, hg * 8 + 8)
                        L = {n_: C.ldb[n_][r_] for n_ in C.ldb}
                        tsl = slice(ck * 64, ck * 64 + 64)

                        def b0():
                            px = nps()
                            mm8(px, (L["A"], C.Tb), (C.LkT[r_], C.VT[r_]))
                            P.op("act", lambda e: e.activation(out=C.X[:], in_=px[:], func=AF.Copy), [px], [C.X])

                        def ap(j):
                            def f():
                                pa = nps()
                                mm8(pa, (C.PTf[r_][j], C.X))
                                P.op("dve", lambda e: e.tensor_tensor(out=C.X[:], in0=C.X[:], in1=pa[:], op=ALU.add), [C.X, pa], [C.X])
                            return f

                        def fin():
                            P.op("act", lambda e: e.activation(out=C.Ub[:], in_=C.X[:], func=AF.Copy), [C.X], [C.Ub])
                            py = nps()
                            mm8(py, (L["R"], C.Tb), (C.MKT[r_], C.VT[r_]), (C.MBT[r_], C.Ub))
                            pS = nps()
                            mm8(pS, (C.KtT[r_], C.VT[r_]), (C.BtT[r_], C.Ub))
                            P.op("dve", lambda e: e.tensor_tensor(out=C.Tf[:], in0=C.Tf[:], in1=bc(C.gamt[:, :, ck].unsqueeze(2), SH), op=ALU.mult), [C.Tf, C.gamt], [C.Tf])
                            P.op("dve", lambda e: e.tensor_tensor(out=C.Tf[:], in0=C.Tf[:], in1=pS[:], op=ALU.add), [C.Tf, pS], [C.Tf])
                            P.op("act", lambda e: e.activation(out=C.Tb[:], in_=C.Tf[:], func=AF.Copy), [C.Tf], [C.Tb])
                            ydst = S.RW["YF"][tsl, hg * 512:(hg + 1) * 512]
                            y_ = C.ys[r_]
                            if not rev:
                                P.op("act", lambda e: e.activation(out=y_[:], in_=py[:], func=AF.Copy), [py], [y_])
                                P.dma("sp", ydst, y_[:].rearrange("p a b -> p (a b)"), reads=[y_], writes=[])
                            else:
                                f_, bn_, gg_, o_ = C.yfl[r_], C.bnl[r_], C.ggl[r_], C.obf[r_]
                                P.dma("sp", f_[:].rearrange("p a b -> p (a b)"), ydst, writes=[f_])
                                P.dma("act", bn_[:], S.RW["BON"][ck, :, h8, :], writes=[bn_])
                                P.dma("act", gg_[:], S.RW["G"][ck, :, h8, :], writes=[gg_])
                                P.op("dve", lambda e: e.tensor_tensor(out=y_[:], in0=py[:], in1=f_[:], op=ALU.add), [py, f_], [y_])
                                post1()

                        def post1():
                            if not rev:
                                return
                            y_ = C.ys[r_]
                            f_, bn_, gg_, o_ = C.yfl[r_], C.bnl[r_], C.ggl[r_], C.obf[r_]
                            P.op("dve", lambda e: e.reduce_sum(out=C.st8[:], in_=y_[:], axis=AX.X), [y_], [C.st8])
                            P.op("pool", lambda e: e.tensor_scalar(out=C.st8[:], in0=C.st8[:], scalar1=1.0 / 64, scalar2=None, op0=ALU.mult), [C.st8], [C.st8])
                            P.op("pool", lambda e: e.tensor_tensor(out=y_[:], in0=y_[:], in1=bc(C.st8[:].unsqueeze(2), SH), op=ALU.subtract), [y_, C.st8], [y_])
                            P.op("pool", lambda e: e.tensor_tensor(out=C.sqv[:], in0=y_[:], in1=y_[:], op=ALU.mult), [y_], [C.sqv])
                            P.op("dve", lambda e: e.reduce_sum(out=C.st9[:], in_=C.sqv[:], axis=AX.X), [C.sqv], [C.st9])
                            P.op("act", lambda e: e.activation(out=C.st9[:], in_=C.st9[:], func=AF.Sqrt, bias=eps[:], scale=1.0 / 64), [C.st9, eps], [C.st9])
                            P.op("dve", lambda e: e.reciprocal(out=C.st9[:], in_=C.st9[:]), [C.st9], [C.st9])
                            P.op("pool", lambda e: e.tensor_tensor(out=y_[:], in0=y_[:], in1=bc(C.st9[:].unsqueeze(2), SH), op=ALU.mult), [y_, C.st9], [y_])

                        def post():
                            if not rev:
                                return
                            y_ = C.ys[r_]
                            f_, bn_, gg_, o_ = C.yfl[r_], C.bnl[r_], C.ggl[r_], C.obf[r_]
                            pt_ = nps()
                            for e_ in range(8):
                                P.op("pe", lambda e: e.transpose(out=pt_[:, e_, :], in_=y_[:, e_, :], identity=S.ident[:64, :64]), [y_, S.ident], [pt_])
                            P.op("dve", lambda e: e.tensor_tensor(out=C.sqv[:], in0=pt_[:], in1=bc(lnw[:, h8].unsqueeze(2), SH), op=ALU.mult), [pt_, lnw], [C.sqv])
                            P.op("pool", lambda e: e.tensor_tensor(out=C.sqv[:], in0=C.sqv[:], in1=bc(lnb[:, h8].unsqueeze(2), SH), op=ALU.add), [C.sqv, lnb], [C.sqv])
                            P.op("pool", lambda e: e.tensor_tensor(out=C.sqv[:], in0=C.sqv[:], in1=bn_[:], op=ALU.add), [C.sqv, bn_], [C.sqv])
                            P.op("pool", lambda e: e.tensor_tensor(out=o_[:], in0=C.sqv[:], in1=gg_[:], op=ALU.mult), [C.sqv, gg_], [o_])
                            P.dma("sp", S.ORW[hg * 512:(hg + 1) * 512, tsl].rearrange("(h v) t -> v h t", v=64), o_[:], reads=[o_], writes=[])
                        return [b0, ap(0), ap(1), ap(2), ap(3), ap(4), ap(5), fin, post]

                    for c_ in range(2):
                        for f in A_slices(chains[c_], hgp * 2 + c_, 0):
                            f()
                    prev_post = [None, None]
                    for i in range(36):
                        a_ = [A_slices(chains[c_], hgp * 2 + c_, i + 1) if i + 1 < 36 else [] for c_ in range(2)]
                        b_ = [B_steps(chains[c_], hgp * 2 + c_, i) for c_ in range(2)]
                        for k_ in range(8):
                            for c_ in range(2):
                                b_[c_][k_]()
                                if k_ < len(a_[c_]):
                                    a_[c_][k_]()
                                if k_ == 3 and prev_post[c_] is not None:
                                    prev_post[c_]()
                        prev_post = [b_[c_][8] for c_ in range(2)]
                    for c_ in range(2):
                        prev_post[c_]()
                    P.barrier()


TT2 = [(0, 256), (256, 512), (768, 512), (1280, 512), (1792, 512)]


def resid_update(S, b, ci, pt_ap, t0, tn, gidx, xo, k, pt):
    nc, P = S.nc, S.P
    v = 2 if t0 < 256 else b
    xi = xo[k % len(xo)]
    dst = S.xT[b, ci * 128:(ci + 1) * 128, t0:t0 + tn]
    P.dma("sp", xi[:, :tn], dst, writes=[xi])
    P.op("dve", lambda e: e.scalar_tensor_tensor(out=xi[:, :tn], in0=pt_ap, scalar=S.mod[:, gidx * 16 + ci, v:v + 1], in1=xi[:, :tn], op0=ALU.mult, op1=ALU.add),
         [pt, S.mod, xi], [xi])
    P.dma("act", dst, xi[:, :tn], reads=[xi], writes=[])


def resid_evac(S, b, gidx, xo, cnt):
    def evac(ci, n, tiles):
        for (pt, t0, tn) in tiles:
            if t0 == 0:
                for (a0, a1) in ((0, 256), (256, tn)):
                    resid_update(S, b, ci, pt[:, a0:a1], a0, a1 - a0, gidx, xo, cnt[0], pt)
                    cnt[0] += 1
            else:
                resid_update(S, b, ci, pt[:, :tn], t0, tn, gidx, xo, cnt[0], pt)
                cnt[0] += 1
    return evac


def phase_merge(S, l, b):
    nc, P, W = S.nc, S.P, S.W
    passes = [(S.OG, 0, W["w_br_gla"][l], 0), (S.OS, 0, W["w_br_ssm"][l][0:2048, :], 2048), (S.OS, 16, W["w_br_ssm"][l][2048:4096, :], 2048),
              (S.ORW, 0, W["w_br_rwkv"][l], 4096)]
    with ExitStack() as es:
        mT = Tl(nc, es, [128, 16, NT], BF16, name="mT")
        with ExitStack() as es1:
            rhs = Tl(nc, es1, [128, 16, NT], BF16, name="mrhs")
            gt = [Tl(nc, es1, [128, NT], F32, name="mgt") for _ in range(2)]
            tmp = [Tl(nc, es1, [128, 512], F32, name="mtmp") for _ in range(3)]
            tk = [0]
            for pi, (src, k0, w_ap, go) in enumerate(passes):
                P.dma("sp", rhs[:], src[k0 * 128:(k0 + 16) * 128, :].rearrange("(kc p) t -> p kc t", p=128), writes=[rhs])

                def evac(ci, n, tiles, pi=pi, go=go):
                    g_ = gt[ci % 2]
                    P.dma("act", g_[:], S.PF["t"][go + ci * 128:go + (ci + 1) * 128, :], writes=[g_])
                    P.op("act", lambda e: e.activation(out=g_[:], in_=g_[:], func=AF.Sigmoid), [g_], [g_])
                    for (pt, t0, tn) in tiles:
                        if pi == 0:
                            P.op("dve", lambda e: e.tensor_tensor(out=mT[:, ci, t0:t0 + tn], in0=pt[:, :tn], in1=g_[:, t0:t0 + tn], op=ALU.mult), [pt, g_], [mT])
                        else:
                            t_ = tmp[tk[0] % 3]
                            tk[0] += 1
                            P.op("dve", lambda e: e.tensor_tensor(out=t_[:, :tn], in0=pt[:, :tn], in1=g_[:, t0:t0 + tn], op=ALU.mult), [pt, g_], [t_])
                            P.op("pool", lambda e: e.tensor_tensor(out=mT[:, ci, t0:t0 + tn], in0=mT[:, ci, t0:t0 + tn], in1=t_[:, :tn], op=ALU.add), [mT, t_], [mT])

                with ExitStack() as es2:
                    proj(S, es2, w_ap, 2048, rhs, 16, evac, blk=256)
                P.barrier()
        with ExitStack() as es1:
            xo = [Tl(nc, es1, [128, 512], F32, name="mxo") for _ in range(4)]
            proj(S, es1, W["w_out"][l], 2048, mT, 16, resid_evac(S, b, 2, xo, [0]), blk=256)
        P.barrier()


def phase_ffn_up(S, l, hT):
    nc, P, W = S.nc, S.P, S.W
    with ExitStack() as es:
        cw = [[vec_pp(S, es, W["ffn_conv_w"][l, i, j, :], 88, "fcw") for j in range(3)] for i in range(3)]
        cb = vec_pp(S, es, W["ffn_conv_b"][l, :], 88, "fcb")
        u = [Tl(nc, es, [128, NT], F32, name="fu") for _ in range(2)]
        y = [Tl(nc, es, [128, NT], F32, name="fy") for _ in range(2)]
        gb = [Tl(nc, es, [128, NT], BF16, name="fgb") for _ in range(2)]
        ab = [Tl(nc, es, [128, NT], BF16, name="fab") for _ in range(2)]
        cnt = [0]

        def evac(ci, n, tiles):
            k = cnt[0]
            cnt[0] += 1
            ui, yi = u[k % 2], y[k % 2]
            eng = "dve"
            for i, (pt, t0, tn) in enumerate(tiles):
                if i % 2 == 0 or True:
                    P.op("act", lambda e: e.activation(out=ui[:, t0:t0 + tn], in_=pt[:, :tn], func=AF.Copy), [pt], [ui])
                else:
                    P.op("dve", lambda e: e.tensor_copy(out=ui[:, t0:t0 + tn], in_=pt[:, :tn]), [pt], [ui])
            P.op(eng, lambda e: e.tensor_scalar(out=yi[:], in0=ui[:], scalar1=cw[1][1][:, ci:ci + 1], scalar2=cb[:, ci:ci + 1], op0=ALU.mult, op1=ALU.add), [ui, cw[1][1], cb], [yi])
            for dc in (-1, 1):
                o0, o1 = max(0, -dc), 256 - max(0, dc)
                P.op(eng, lambda e: e.scalar_tensor_tensor(out=yi[:, o0:o1], in0=ui[:, o0 + dc:o1 + dc], scalar=cw[1][1 + dc][:, ci:ci + 1], in1=yi[:, o0:o1], op0=ALU.mult, op1=ALU.add),
                     [ui, yi, cw[1][1 + dc]], [yi])
            uv = ui[:, 256:].rearrange("p (r c) -> p r c", c=64)
            yv = yi[:, 256:].rearrange("p (r c) -> p r c", c=64)
            for dr in (-1, 0, 1):
                for dc in (-1, 0, 1):
                    if dr == 0 and dc == 0:
                        continue
                    r0, r1 = max(0, -dr), 32 - max(0, dr)
                    c0, c1 = max(0, -dc), 64 - max(0, dc)
                    P.op(eng, lambda e: e.scalar_tensor_tensor(out=yv[:, r0:r1, c0:c1], in0=uv[:, r0 + dr:r1 + dr, c0 + dc:c1 + dc], scalar=cw[1 + dr][1 + dc][:, ci:ci + 1],
                                                               in1=yv[:, r0:r1, c0:c1], op0=ALU.mult, op1=ALU.add), [ui, yi, cw[1 + dr][1 + dc]], [yi])
            if ci < 44:
                g_ = gb[k % 2]
                P.op("act", lambda e: e.activation(out=g_[:], in_=yi[:], func=AF.Silu), [yi], [g_])
                P.dma("sp", S.ACTG[ci * 128:(ci + 1) * 128, :], g_[:], reads=[g_], writes=[S.P.B(("actg", ci))])
            else:
                g_, a_ = gb[k % 2], ab[k % 2]
                P.dma("sp", g_[:], S.ACTG[(ci - 44) * 128:(ci - 43) * 128, :], reads=[S.P.B(("actg", ci - 44))], writes=[g_])
                P.op("pool", lambda e: e.tensor_tensor(out=a_[:], in0=yi[:], in1=g_[:], op=ALU.mult), [yi, g_], [a_])
                P.dma("act", S.ACT[(ci - 44) * 128:(ci - 43) * 128, :], a_[:], reads=[a_], writes=[])

        proj(S, es, W["ffn_w_up"][l], 2 * FH, hT, 16, evac, blk=512, nps=7)


def phase_ffn_down(S, l, b):
    nc, P, W = S.nc, S.P, S.W
    with ExitStack() as es:
        rhs = Tl(nc, es, [128, 22, NT], BF16, name="drhs")
        xo = [Tl(nc, es, [128, 512], F32, name="dxo") for _ in range(4)]
        cnt = [0]
        for half in range(2):
            r0 = half * 22 * 128
            P.dma("sp", rhs[:], S.ACT[r0:r0 + 22 * 128, :].rearrange("(kc p) t -> p kc t", p=128), writes=[rhs])
            with ExitStack() as es2:
                proj(S, es2, W["ffn_w_down"][l][r0:r0 + 22 * 128, :], 2048, rhs, 22, resid_evac(S, b, 5, xo, cnt), blk=256)
            P.barrier()


def phase_final(S, b, out):
    nc, P, W = S.nc, S.P, S.W
    with ExitStack() as es:
        gbc = Tl(nc, es, [128, DM], F32, name="gbc")
        P.dma("sp", gbc[:], bc(W["final_norm_g"].unsqueeze(0), [128, DM]), writes=[gbc])
        eps = Tl(nc, es, [128, 1], F32, name="feps")
        P.op("dve", lambda e: e.memset(eps[:], 1e-6), [], [eps])
        xt = [Tl(nc, es, [128, 16, 128], F32, name="fxt") for _ in range(2)]
        sq = [Tl(nc, es, [128, 16, 128], F32, name="fsq") for _ in range(2)]
        ot = [Tl(nc, es, [128, DM], F32, name="fot") for _ in range(2)]
        rs = [Tl(nc, es, [128, 1], F32, name="frs") for _ in range(2)]
        pss = [Tl(nc, es, [128, 1], F32, psum=True, name="fpss") for _ in range(2)]
        ptr = [Tl(nc, es, [128, 512], F32, psum=True, name="fptr") for _ in range(4)]
        k = 0
        for tt in range(16):
            r = tt % 2
            t0 = 256 + tt * 128
            P.dma("sp" if r == 0 else "act", xt[r][:], S.xT[b, :, t0:t0 + 128].rearrange("(c p) t -> p c t", p=128), writes=[xt[r]])
            P.op("act", lambda e: e.activation(out=sq[r][:], in_=xt[r][:], func=AF.Square), [xt[r]], [sq[r]])
            for c in range(16):
                P.op("pe", lambda e: e.matmul(out=pss[r][:], lhsT=sq[r][:, c, :], rhs=S.ones[:, 0:1], start=(c == 0), stop=(c == 15)), [sq[r], S.ones], [pss[r]])
            P.op("act", lambda e: e.activation(out=rs[r][:], in_=pss[r][:], func=AF.Sqrt, bias=eps[:], scale=1.0 / DM), [pss[r], eps], [rs[r]])
            P.op("dve", lambda e: e.reciprocal(out=rs[r][:], in_=rs[r][:]), [rs[r]], [rs[r]])
            for cg in range(4):
                p_ = ptr[k % 4]
                k += 1
                for a in range(4):
                    c = cg * 4 + a
                    P.op("pe", lambda e: e.transpose(out=p_[:, a * 128:(a + 1) * 128], in_=xt[r][:, c, :], identity=S.ident[:]), [xt[r], S.ident], [p_])
                P.op("dve", lambda e: e.scalar_tensor_tensor(out=ot[r][:, cg * 512:(cg + 1) * 512], in0=p_[:], scalar=rs[r][:, 0:1], in1=gbc[:, cg * 512:(cg + 1) * 512],
                                                             op0=ALU.mult, op1=ALU.mult), [p_, rs[r], gbc], [ot[r]])
            P.dma("sp", out[b, tt * 128:(tt + 1) * 128, :], ot[r][:], reads=[ot[r]], writes=[])


_NC_CACHE = {}


def kernel(**inputs):
    n = 8
    NB = 2
    if "nc" not in _NC_CACHE:
        _NC_CACHE["nc"] = build(NB=NB, L=2)
    nc = _NC_CACHE["nc"]
    cst = consts()
    shared = {k: np.ascontiguousarray(np.asarray(inputs[k], dtype=np.float32)) for k in WSHAPES}
    x = np.asarray(inputs["x"], dtype=np.float32)
    ctx = np.asarray(inputs["ctx"], dtype=np.float32)
    c = np.asarray(inputs["c"], dtype=np.float32)
    c_ctx = np.ascontiguousarray(np.asarray(inputs["c_ctx"], dtype=np.float32)[None, :])
    in_maps = []
    for i in range(n):
        m = {"x": np.ascontiguousarray(x[i * NB:(i + 1) * NB]), "ctx": np.ascontiguousarray(ctx[i * NB:(i + 1) * NB]),
             "c": np.ascontiguousarray(c[i * NB:(i + 1) * NB]), "c_ctx": c_ctx}
        m.update(shared)
        m.update(cst)
        in_maps.append(m)
    res = run_bass_kernel_spmd(nc, in_maps, core_ids=list(range(n)))
    return np.concatenate([r["out"] for r in res.results], axis=0)
```

```python
import numpy as np
import concourse.bass as bass
import concourse.mybir as mybir
from concourse.bass_utils import run_bass_kernel_spmd
from contextlib import ExitStack

F32 = mybir.dt.float32
BF16 = mybir.dt.bfloat16
ALU = mybir.AluOpType
AF = mybir.ActivationFunctionType
AX = mybir.AxisListType
NDS = 12

NT = 2304
TC = 256
DM = 2048
NIN = 29472
NPAD = 29568
TT = [(0, 512), (512, 512), (1024, 512), (1536, 512), (2048, 256)]
SEGS = [(0, 256), (256, 2304)]
FH = 5632

O_Q, O_K, O_V, O_G, O_AD, O_Z, O_XBC, O_DT, O_RW, O_GT = 0, 1024, 2048, 4096, 6144, 6176, 10272, 16416, 16544, 23328

WSHAPES = {
    "w_mod": [2, 2048, 12288], "b_mod": [2, 12288], "norm1_g": [2, 2048], "norm2_g": [2, 2048],
    "w_in": [2, 2048, 29472], "gla_wa_up": [2, 2, 16, 1024], "gla_ba": [2, 2, 1024], "gla_norm_g": [2, 512],
    "ssm_conv_w": [2, 3, 6144], "ssm_conv_b": [2, 6144], "ssm_a_log": [2, 2, 64], "ssm_dt_bias": [2, 2, 64],
    "ssm_d": [2, 64], "ssm_norm_g": [2, 4096], "rwkv_mu": [2, 6784], "rwkv_w0": [2, 2, 2048],
    "rwkv_w_up": [2, 2, 96, 2048], "rwkv_a0": [2, 2, 2048], "rwkv_a_up": [2, 2, 96, 2048],
    "rwkv_g_up": [2, 256, 2048], "rwkv_k_k": [2, 2048], "rwkv_k_a": [2, 2048], "rwkv_r_k": [2, 32, 64],
    "rwkv_ln_w": [2, 2048], "rwkv_ln_b": [2, 2048], "w_br_gla": [2, 2048, 2048], "w_br_ssm": [2, 4096, 2048],
    "w_br_rwkv": [2, 2048, 2048], "w_out": [2, 2048, 2048], "ffn_w_up": [2, 2048, 11264],
    "ffn_conv_w": [2, 3, 3, 11264], "ffn_conv_b": [2, 11264], "ffn_w_down": [2, 5632, 2048],
    "final_norm_g": [2048],
}


class Buf:
    __slots__ = ("w", "r")

    def __init__(self):
        self.w = None
        self.r = {}


class Rec:
    def __getattr__(self, name):
        def f(*a, **k):
            self.call = (name, a, k)
            return self
        return f


class Eng:
    def __init__(self, name, sem):
        self.name, self.sem, self.n, self.ops, self.waited = name, sem, 0, [], {}


class Prog:
    def __init__(self, nc, es):
        self.nc = nc
        self.E = {}
        for name in ("pe", "act", "dve", "pool", "sp"):
            self.E[name] = Eng(name, es.enter_context(nc.semaphore(name + "_sem")))
        self.dq = {}
        self.dqi = {}
        for q in ("sp", "pool", "act"):
            self.dq[q] = [[es.enter_context(nc.semaphore(f"d{q}{i}")), 0] for i in range(NDS)]
            self.dqi[q] = 0
        self.bufs = {}

    def B(self, key):
        b = self.bufs.get(key)
        if b is None:
            b = self.bufs[key] = Buf()
        return b

    def _wait(self, eng, tok):
        sem, val = tok
        k = id(sem)
        if eng.waited.get(k, 0) >= val:
            return
        eng.waited[k] = val
        eng.ops.append(("w", sem, val))

    def _deps(self, eng, reads, writes):
        own = id(eng.sem)
        skip_own = eng.name in ("pe", "sp")
        for b in reads:
            if b.w is not None and not (skip_own and id(b.w[0]) == own):
                self._wait(eng, b.w)
        for b in writes:
            if b.w is not None and not (skip_own and id(b.w[0]) == own):
                self._wait(eng, b.w)
            for t in b.r.values():
                if not (skip_own and id(t[0]) == own):
                    self._wait(eng, t)

    def _commit(self, tok, reads, writes):
        k = id(tok[0])
        for b in reads:
            b.r[k] = tok
        for b in writes:
            b.w = tok
            b.r = {}

    def op(self, en, fn, reads=(), writes=()):
        eng = self.E[en]
        writes = [getattr(r, "b", r) for r in writes] + [r.b for r in reads if getattr(r, "psum", False)]
        reads = [getattr(r, "b", r) for r in reads if not getattr(r, "psum", False)]
        self._deps(eng, reads, writes)
        eng.n += 1
        tok = (eng.sem, eng.n)
        rec = Rec()
        fn(rec)
        eng.ops.append(("i", rec.call, eng.sem, 1))
        self._commit(tok, reads, writes)

    def dma(self, q, out, in_, reads=(), writes=(), **kw):
        eng = self.E[q]
        reads = [getattr(r, "b", r) for r in reads]
        writes = [getattr(r, "b", r) for r in writes]
        self._deps(eng, reads, writes)
        slot = self.dq[q][self.dqi[q] % NDS]
        self.dqi[q] += 1
        if slot[1] > 0:
            self._wait(eng, (slot[0], slot[1]))
        slot[1] += 16
        tok = (slot[0], slot[1])
        eng.ops.append(("i", ("dma_start", (), dict(out=out, in_=in_, **kw)), slot[0], 16))
        self._commit(tok, reads, writes)

    def barrier(self):
        toks = [(e.sem, e.n) for e in self.E.values() if e.n > 0]
        for q in self.dq:
            for s in self.dq[q]:
                if s[1] > 0:
                    toks.append((s[0], s[1]))
        for e in self.E.values():
            for t in toks:
                if id(t[0]) == id(e.sem):
                    continue
                self._wait(e, t)
        self.bufs = {}

    def emit(self):
        nc = self.nc
        self.barrier()
        handles = {"pe": "tensor", "act": "scalar", "dve": "vector", "pool": "gpsimd", "sp": "sync"}
        with nc.Block() as block:
            for en, attr in handles.items():
                eng = self.E[en]

                def body(h, eng=eng):
                    for o in eng.ops:
                        if o[0] == "w":
                            h.wait_ge(o[1], o[2])
                        else:
                            getattr(h, o[1][0])(*o[1][1], **o[1][2]).then_inc(o[2], o[3])

                getattr(block, attr)(body)


class Tl:
    _n = 0

    def __init__(self, nc, es, shape, dtype, psum=False, name=None):
        Tl._n += 1
        name = (name or "t") + str(Tl._n)
        self.psum = psum
        self.b = Buf()
        if not psum:
            self.t = es.enter_context(nc.sbuf_tensor(name, list(shape), dtype))
            return
        esz = 2 if dtype == BF16 else 4
        n = 1
        for v in shape[1:]:
            n *= v
        assert n * esz <= 2048
        raw = es.enter_context(nc.psum_tensor(name, [shape[0], 2048 // esz], dtype))
        v = raw[:, :n]
        if len(shape) == 3:
            v = v.rearrange("p (a b) -> p a b", a=shape[1])
        self.t = v

    def __getitem__(self, k):
        return self.t[k]


class K:
    pass


def bc(ap, shape):
    return ap.to_broadcast(list(shape))


def build(NB=2, L=2, stop_after=None, dbg=(), skip=()):
    nc = bass.Bass("TRN2", target_bir_lowering=False)

    def din(name, shape, dt=F32):
        return nc.dram_tensor(name, list(shape), dt, kind="ExternalInput").ap()

    def dscr(name, shape, dt=F32):
        return nc.dram_tensor(name, list(shape), dt, kind="Internal").ap()

    x_in = din("x", [NB, 2048, DM])
    ctx_in = din("ctx", [NB, TC, DM])
    c_in = din("c", [NB, DM])
    cctx_in = din("c_ctx", [1, DM])
    W = {n: din(n, ([L] + s[1:]) if len(s) > 1 else s) for n, s in WSHAPES.items()}
    ident_in = din("ident", [128, 128])
    segmask_in = din("segmask", [128, NT])
    tri_in = din("tri", [2, 64, 64])
    out = nc.dram_tensor("out", [NB, 2048, DM], F32, kind="ExternalOutput").ap()
    dbg_out = {n: nc.dram_tensor("dbg_" + n, list(s), F32, kind="ExternalOutput").ap() for n, s in dbg}

    xT = dscr("xT", [NB, DM, NT])
    PF = {"g": dscr("PFg", [6272, NT]), "s": dscr("PFs", [10368, NT]), "r": dscr("PFr", [6784, NT]), "t": dscr("PFt", [6144, NT])}

    with ExitStack() as es0:
        P = Prog(nc, es0)
        S = K()
        S.nc, S.P, S.W, S.NB = nc, P, W, NB
        S.xT, S.PF = xT, PF
        S.dbg = dbg_out
        S.GOF = dscr("GOF", [NT, 2048])
        S.OG = dscr("OG", [2048, NT], BF16)
        S.CUM = dscr("CUM", [128, NT])
        S.XT = dscr("XT", [NT, 4096], BF16)
        S.BT = dscr("BT", [NT, 1024], BF16)
        S.BCF = dscr("BCF", [2048, NT], BF16)
        S.YF = dscr("YF", [NT, 4096])
        S.YT = dscr("YT", [NT, 4096])
        S.OS = dscr("OS", [4096, NT], BF16)
        CK = [36, 64, 32, 64]
        S.RW = {n_: [dscr(f"RW{n_}{d_}", CK, BF16) for d_ in range(2)] for n_ in ("A", "R", "K", "B", "KT", "BT")}
        S.RW["V"] = dscr("RWV", CK, BF16)
        S.RW["G"] = dscr("RWG", CK)
        S.RW["BON"] = dscr("RWBON", CK)
        S.RW["GAM"] = [dscr(f"RWGAM{d_}", [32, 64, 36]) for d_ in range(2)]
        S.RW["YF"] = dscr("RWYF", [NT, 2048])
        S.ORW = dscr("ORW", [2048, NT], BF16)
        mk_in = din("masks", [6, 64, 64])
        mks = [Tl(nc, es0, [64, 64], F32, name="mk") for _ in range(6)]
        for i_ in range(6):
            P.dma("sp", mks[i_][:], mk_in[i_], writes=[mks[i_]])
        S.str_, S.nstr, S.ntri = mks[0:2], mks[2:4], mks[4:6]
        S.ACTG = dscr("ACTG", [FH, NT], BF16)
        S.ACT = dscr("ACT", [FH, NT], BF16)
        S.segmask = Tl(nc, es0, [128, NT], F32, name="segmask")
        P.dma("sp", S.segmask[:], segmask_in, writes=[S.segmask])
        S.tri = [Tl(nc, es0, [64, 64], F32, name="tri") for _ in range(2)]
        for d_ in range(2):
            P.dma("sp", S.tri[d_][:], tri_in[d_], writes=[S.tri[d_]])
        S.ident = Tl(nc, es0, [128, 128], F32, name="ident")
        S.identb = Tl(nc, es0, [128, 128], BF16, name="identb")
        S.ones = Tl(nc, es0, [128, 128], F32, name="ones")
        P.dma("sp", S.ident[:], ident_in, writes=[S.ident])
        P.op("dve", lambda e: e.tensor_copy(out=S.identb[:], in_=S.ident[:]), [S.ident], [S.identb])
        P.op("dve", lambda e: e.memset(S.ones[:], 1.0), [], [S.ones])
        S.mod = Tl(nc, es0, [128, 96, 3], F32, name="mod")
        S.gs = [Tl(nc, es0, [128, 16, 3], F32, name="gs") for _ in range(2)]
        S.cact = Tl(nc, es0, [128, 16, 3], F32, name="cact")

        phase_init(S, x_in, ctx_in, c_in, cctx_in)
        P.barrier()
        done = False
        for l in range(L):
            if done:
                break
            phase_mod(S, l)
            P.barrier()
            for b in range(NB):
                with ExitStack() as es:
                    hT = Tl(nc, es, [128, 16, NT], BF16, name="hT")
                    phase_norm(S, l, b, 0, hT)
                    P.barrier()
                    if stop_after == "norm1":
                        dump_bf(S, es, hT, "hT")
                        done = True
                        break
                    phase_inproj(S, l, hT)
                    P.barrier()
                if stop_after == "ip":
                    done = True
                    break
                if "gla" not in skip:
                    phase_gla(S, l, b)
                    P.barrier()
                if stop_after == "gla":
                    done = True
                    break
                if "ssd" not in skip:
                    phase_ssd(S, l, b)
                    P.barrier()
                if stop_after == "ssd":
                    done = True
                    break
                if "rwkv" not in skip:
                    phase_rwkv(S, l, b)
                    P.barrier()
                if stop_after == "rwkv":
                    done = True
                    break
                phase_merge(S, l, b)
                P.barrier()
                if stop_after == "merge":
                    done = True
                    break
                with ExitStack() as es:
                    hT = Tl(nc, es, [128, 16, NT], BF16, name="hT2")
                    phase_norm(S, l, b, 1, hT)
                    P.barrier()
                    phase_ffn_up(S, l, hT)
                    P.barrier()
                phase_ffn_down(S, l, b)
                P.barrier()
        if not done:
            for b in range(NB):
                phase_final(S, b, out)
            P.barrier()
        if "xT" in dbg_out:
            P.dma("sp", dbg_out["xT"], S.xT[0], reads=[], writes=[])
        if "RWYF" in dbg_out:
            P.dma("sp", dbg_out["RWYF"], S.RW["YF"], reads=[], writes=[])
        if "ORW" in dbg_out:
            with ExitStack() as es:
                dump_bf2(S, es, S.ORW, "ORW", 16)
        if "YT" in dbg_out:
            P.dma("sp", dbg_out["YT"], S.YT, reads=[], writes=[])
        if "YF" in dbg_out:
            P.dma("sp", dbg_out["YF"], S.YF, reads=[], writes=[])
        if "OS" in dbg_out:
            with ExitStack() as es:
                dump_bf2(S, es, S.OS, "OS", 32)
        if "GOF" in dbg_out:
            P.dma("sp", dbg_out["GOF"], S.GOF, reads=[], writes=[])
        if "OG" in dbg_out:
            with ExitStack() as es:
                dump_bf2(S, es, S.OG, "OG", 16)
        for kk_ in ("g", "s", "r", "t"):
            if "PF" + kk_ in dbg_out:
                P.dma("sp", dbg_out["PF" + kk_], PF[kk_], reads=[], writes=[])
        P.emit()
    return nc


def dump_bf(S, es, t, name):
    nc, P = S.nc, S.P
    st = Tl(nc, es, [128, NT], F32, name="dump")
    for c in range(16):
        P.op("dve", lambda e, c=c: e.tensor_copy(out=st[:], in_=t[:, c, :]), [t], [st])
        P.dma("sp", S.dbg[name][c * 128:(c + 1) * 128, :], st[:], reads=[st], writes=[])


def dump_bf2(S, es, src, name, nch):
    nc, P = S.nc, S.P
    P.barrier()
    sb = Tl(nc, es, [128, NT], BF16, name="dumpb")
    st = Tl(nc, es, [128, NT], F32, name="dump")
    for c in range(nch):
        P.dma("sp", sb[:], src[c * 128:(c + 1) * 128, :], writes=[sb])
        P.op("dve", lambda e: e.tensor_copy(out=st[:], in_=sb[:]), [sb], [st])
        P.dma("sp", S.dbg[name][c * 128:(c + 1) * 128, :], st[:], reads=[st], writes=[])


def consts():
    seg = np.ones((128, NT), np.float32)
    seg[:, ::64] = 0.0
    s = np.arange(64)[:, None]
    q = np.arange(64)[None, :]
    tri = np.stack([(s <= q), (s >= q)]).astype(np.float32)
    eye = np.eye(64, dtype=np.float32)
    strict = tri - eye
    masks = np.concatenate([strict, -strict, -tri], 0)
    return {"ident": np.eye(128, dtype=np.float32), "segmask": seg, "tri": tri, "masks": masks}


def phase_init(S, x_in, ctx_in, c_in, cctx_in):
    nc, P = S.nc, S.P
    with ExitStack() as es:
        xt = [Tl(nc, es, [128, DM], F32, name="xt") for _ in range(2)]
        st = [Tl(nc, es, [128, 16, 128], F32, name="st") for _ in range(2)]
        ps = [Tl(nc, es, [128, 512], F32, psum=True, name="ps") for _ in range(4)]
        k = 0
        n = 0
        for b in range(S.NB):
            for tt in range(18):
                src = ctx_in[b, tt * 128:(tt + 1) * 128, :] if tt < 2 else x_in[b, (tt - 2) * 128:(tt - 1) * 128, :]
                xi = xt[n % 2]
                so = st[n % 2]
                n += 1
                P.dma("sp", xi[:], src, writes=[xi])
                for g in range(4):
                    pt = ps[k % 4]
                    k += 1
                    for j in range(4):
                        c = g * 4 + j
                        P.op("pe", lambda e, pt=pt, xi=xi, c=c, j=j: e.transpose(
                            out=pt[:, j * 128:(j + 1) * 128], in_=xi[:, c * 128:(c + 1) * 128], identity=S.ident[:]),
                            [xi, S.ident], [pt])
                    en = "dve" if g % 2 == 0 else "act"
                    if en == "dve":
                        P.op("dve", lambda e, pt=pt, so=so, g=g: e.tensor_copy(
                            out=so[:, g * 4:(g + 1) * 4, :], in_=pt[:].rearrange("p (a b) -> p a b", a=4)), [pt], [so])
                    else:
                        P.op("act", lambda e, pt=pt, so=so, g=g: e.activation(
                            out=so[:, g * 4:(g + 1) * 4, :], in_=pt[:].rearrange("p (a b) -> p a b", a=4), func=AF.Copy), [pt], [so])
                P.dma("act", S.xT[b, :, tt * 128:(tt + 1) * 128].rearrange("(c p) t -> p c t", p=128), so[:], reads=[so], writes=[])
        craw = Tl(nc, es, [128, 16, 3], F32, name="craw")
        P.op("dve", lambda e: e.memset(craw[:], 0.0), [], [craw])
        for b in range(S.NB):
            P.dma("sp", craw[:, :, b], c_in[b, :].rearrange("(c p) -> p c", p=128), reads=[], writes=[craw], allow_slow_non_contiguous=True)
        P.dma("sp", craw[:, :, 2], cctx_in[0, :].rearrange("(c p) -> p c", p=128), reads=[], writes=[craw], allow_slow_non_contiguous=True)
        P.op("act", lambda e: e.activation(out=S.cact[:], in_=craw[:], func=AF.Silu), [craw], [S.cact])


def phase_mod(S, l):
    nc, P, W = S.nc, S.P, S.W
    with ExitStack() as es:
        wt = [Tl(nc, es, [128, 16, 512], F32, name="wm") for _ in range(2)]
        ps = [Tl(nc, es, [128, 4, 3], F32, psum=True, name="psm") for _ in range(2)]
        bm = Tl(nc, es, [128, 96], F32, name="bm")
        prm = Tl(nc, es, [128, 2, 16], F32, name="prm")
        P.dma("sp", bm[:], W["b_mod"][l, :].rearrange("(c p) -> p c", p=128), writes=[bm], allow_slow_non_contiguous=True)
        P.dma("sp", prm[:, 0, :], W["norm1_g"][l, :].rearrange("(c p) -> p c", p=128), writes=[prm], allow_slow_non_contiguous=True)
        P.dma("sp", prm[:, 1, :], W["norm2_g"][l, :].rearrange("(c p) -> p c", p=128), writes=[prm], allow_slow_non_contiguous=True)
        for blk in range(24):
            w = wt[blk % 2]
            pt = ps[blk % 2]
            P.dma("sp" if blk % 2 == 0 else "act", w[:], W["w_mod"][l, :, blk * 512:(blk + 1) * 512].rearrange("(kc p) n -> p kc n", p=128), writes=[w])
            for j in range(4):
                for kc in range(16):
                    P.op("pe", lambda e, w=w, pt=pt, j=j, kc=kc: e.matmul(
                        out=pt[:, j, :], lhsT=w[:, kc, j * 128:(j + 1) * 128], rhs=S.cact[:, kc, :], start=(kc == 0), stop=(kc == 15)),
                        [w, S.cact], [pt])
            P.op("dve", lambda e, pt=pt, blk=blk: e.tensor_tensor(
                out=S.mod[:, blk * 4:(blk + 1) * 4, :], in0=pt[:], in1=bc(bm[:, blk * 4:(blk + 1) * 4].unsqueeze(2), [128, 4, 3]), op=ALU.add),
                [pt, bm], [S.mod])
        for i in range(2):
            sc = S.mod[:, (3 * i + 1) * 16:(3 * i + 2) * 16, :]
            P.op("dve", lambda e, i=i, sc=sc: e.scalar_tensor_tensor(
                out=S.gs[i][:], in0=sc, scalar=1.0, in1=bc(prm[:, i, :].unsqueeze(2), [128, 16, 3]), op0=ALU.add, op1=ALU.mult),
                [S.mod, prm], [S.gs[i]])


def phase_norm(S, l, b, which, hT):
    nc, P = S.nc, S.P
    with ExitStack() as es:
        xc = [Tl(nc, es, [128, NT], F32, name="xc") for _ in range(3)]
        sq = [Tl(nc, es, [128, NT], F32, name="sq") for _ in range(2)]
        ps = [Tl(nc, es, [128, 512], F32, psum=True, name="psn") for _ in range(5)]
        rstd = Tl(nc, es, [128, NT], F32, name="rstd")
        epsb = Tl(nc, es, [128, 1], F32, name="epsb")
        P.op("dve", lambda e: e.memset(epsb[:], 1e-6), [], [epsb])
        for c in range(16):
            xi, si = xc[c % 3], sq[c % 2]
            P.dma("sp" if c % 2 == 0 else "act", xi[:], S.xT[b, c * 128:(c + 1) * 128, :], writes=[xi])
            if c % 2 == 0:
                P.op("act", lambda e, xi=xi, si=si: e.activation(out=si[:], in_=xi[:], func=AF.Square), [xi], [si])
            else:
                P.op("dve", lambda e, xi=xi, si=si: e.tensor_tensor(out=si[:], in0=xi[:], in1=xi[:], op=ALU.mult), [xi], [si])
            for ti, (t0, tn) in enumerate(TT):
                P.op("pe", lambda e, si=si, ti=ti, t0=t0, tn=tn, c=c: e.matmul(
                    out=ps[ti][:, :tn], lhsT=S.ones[:], rhs=si[:, t0:t0 + tn], start=(c == 0), stop=(c == 15)),
                    [si, S.ones], [ps[ti]])
        for ti, (t0, tn) in enumerate(TT):
            P.op("act", lambda e, ti=ti, t0=t0, tn=tn: e.activation(
                out=rstd[:, t0:t0 + tn], in_=ps[ti][:, :tn], func=AF.Sqrt, bias=epsb[:], scale=1.0 / DM), [ps[ti], epsb], [rstd])
        P.op("dve", lambda e: e.reciprocal(out=rstd[:], in_=rstd[:]), [rstd], [rstd])
        shi = 3 * which
        for c in range(16):
            xi, si = xc[c % 3], sq[c % 2]
            P.dma("sp" if c % 2 == 0 else "act", xi[:], S.xT[b, c * 128:(c + 1) * 128, :], writes=[xi])
            for (s0, s1), v in zip(SEGS, (2, b)):
                P.op("dve", lambda e, xi=xi, si=si, s0=s0, s1=s1, v=v, c=c: e.scalar_tensor_tensor(
                    out=si[:, s0:s1], in0=xi[:, s0:s1], scalar=S.gs[which][:, c, v:v + 1], in1=rstd[:, s0:s1], op0=ALU.mult, op1=ALU.mult),
                    [xi, rstd, S.gs[which]], [si])
                P.op("act", lambda e, si=si, s0=s0, s1=s1, v=v, c=c: e.activation(
                    out=hT[:, c, s0:s1], in_=si[:, s0:s1], func=AF.Identity, bias=S.mod[:, shi * 16 + c, v:v + 1], scale=1.0),
                    [si, S.mod], [hT])


def proj(S, es, w_ap, ncols, rhs, KC, evac, wq="pool", blk=512, nps=7):
    nc, P = S.nc, S.P
    wt = [Tl(nc, es, [128, KC, blk], BF16, name="wp") for _ in range(2)]
    ps = [Tl(nc, es, [128, 512], F32, psum=True, name="psp") for _ in range(nps)]
    k = 0
    nblk = (ncols + blk - 1) // blk
    for bi in range(nblk):
        w = wt[bi % 2]
        c0 = bi * blk
        cn = min(blk, ncols - c0)
        P.dma(wq, w[:, :, :cn], w_ap[:, c0:c0 + cn].rearrange("(kc p) n -> p kc n", p=128), writes=[w])
        for j in range((cn + 127) // 128):
            n = min(128, cn - j * 128)
            tiles = []
            for (t0, tn) in TT:
                pt = ps[k % nps]
                k += 1
                for kc in range(KC):
                    P.op("pe", lambda e, pt=pt, w=w, j=j, n=n, kc=kc, t0=t0, tn=tn: e.matmul(
                        out=pt[:n, :tn], lhsT=w[:, kc, j * 128:j * 128 + n], rhs=rhs[:, kc, t0:t0 + tn], start=(kc == 0), stop=(kc == KC - 1)),
                        [w, rhs], [pt])
                tiles.append((pt, t0, tn))
            evac(bi * (blk // 128) + j, n, tiles)


def phase_inproj(S, l, hT):
    nc, P = S.nc, S.P
    with ExitStack() as es:
        st = [Tl(nc, es, [128, NT], F32, name="ipst") for _ in range(3)]
        cnt = [0]

        def evac(ci, n, tiles, key):
            so = st[cnt[0] % 3]
            cnt[0] += 1
            for i, (pt, t0, tn) in enumerate(tiles):
                if i % 2 == 0:
                    P.op("dve", lambda e, pt=pt, t0=t0, tn=tn, so=so, n=n: e.tensor_copy(out=so[:n, t0:t0 + tn], in_=pt[:n, :tn]), [pt], [so])
                else:
                    P.op("act", lambda e, pt=pt, t0=t0, tn=tn, so=so, n=n: e.activation(out=so[:n, t0:t0 + tn], in_=pt[:n, :tn], func=AF.Copy), [pt], [so])
            P.dma("sp" if ci % 2 == 0 else "act", S.PF[key][ci * 128:ci * 128 + n, :], so[:n, :], reads=[so], writes=[])

        for key, c0, cn in (("g", 0, 6176), ("s", 6176, 10368), ("r", 16544, 6784), ("t", 23328, 6144)):
            with ExitStack() as es2:
                proj(S, es2, S.W["w_in"][l][:, c0:c0 + cn], cn, hT, 16, lambda ci, n, tiles, key=key: evac(ci, n, tiles, key))
            P.barrier()


def chunk_order(rev):
    return list(range(36)) if not rev else [3, 2, 1, 0] + list(range(35, 3, -1))


def seg_cumsum(S, out, src, tot, tmp, rev):
    P = S.P
    P.op("dve", lambda e: e.tensor_tensor_scan(out=out[:], data0=S.segmask[:out.t.shape[0], :], data1=src[:], initial=0.0, op0=ALU.mult, op1=ALU.add),
         [src, S.segmask], [out])
    P.op("dve", lambda e: e.tensor_copy(out=tot[:], in_=out[:].rearrange("p (c q) -> p c q", q=64)[:, :, 63]), [out], [tot])
    if rev:
        P.op("dve", lambda e: e.tensor_tensor(out=tmp[:], in0=src[:], in1=out[:], op=ALU.subtract), [src, out], [tmp])
        P.op("dve", lambda e: e.tensor_tensor(out=out[:].rearrange("p (c q) -> p c q", q=64), in0=tmp[:].rearrange("p (c q) -> p c q", q=64),
                                              in1=bc(tot[:].unsqueeze(2), [out.t.shape[0], 36, 64]), op=ALU.add), [tmp, tot], [out])


def phase_gla(S, l, b):
    nc, P, W = S.nc, S.P, S.W
    PFg = S.PF["g"]
    with ExitStack() as es:
        adT = [Tl(nc, es, [17, NT], F32, name="adT") for _ in range(2)]
        waA = [Tl(nc, es, [17, 1024], F32, name="waA") for _ in range(2)]
        ng = Tl(nc, es, [128, 4], F32, name="ng")
        vT = Tl(nc, es, [64, 36, 512], BF16, name="vT")
        sg = Tl(nc, es, [128, 4, NT], BF16, name="sg")
        ogh = Tl(nc, es, [128, 4, NT], BF16, name="ogh")
        qd = Tl(nc, es, [128, 2, NT], BF16, name="qd")
        ki = Tl(nc, es, [128, 2, NT], BF16, name="ki")
        kdT = Tl(nc, es, [128, 2, NT], BF16, name="kdT")
        etot = Tl(nc, es, [128, 2, 36], F32, name="etot")
        Sf = Tl(nc, es, [128, 2, 512], F32, name="Sf")
        Sb = Tl(nc, es, [128, 2, 512], BF16, name="Sb")
        eps = Tl(nc, es, [128, 1], F32, name="epsg")
        P.op("dve", lambda e: e.memset(eps[:], 1e-6), [], [eps])
        P.dma("sp", ng[:], W["gla_norm_g"][l, :].rearrange("(c p) -> p c", p=128), writes=[ng], allow_slow_non_contiguous=True)
        for d in range(2):
            P.op("dve", lambda e, d=d: e.memset(adT[d][:], 1.0), [], [adT[d]])
            P.dma("sp", adT[d][0:16, :], PFg[O_AD + d * 16:O_AD + d * 16 + 16, :], writes=[adT[d]])
            P.dma("sp", waA[d][0:16, :], W["gla_wa_up"][l, d], writes=[waA[d]])
            P.dma("sp", waA[d][16:17, :], W["gla_ba"][l, d:d + 1, :], writes=[waA[d]])
        for h in range(4):
            with ExitStack() as es1:
                ld = [Tl(nc, es1, [128, NT], F32, name="gld") for _ in range(2)]
                pv = [Tl(nc, es1, [64, 4, 128], F32, psum=True, name="pv") for _ in range(2)]
                for j in range(4):
                    P.dma("sp", ld[j % 2][:], PFg[O_G + h * 512 + j * 128:O_G + h * 512 + (j + 1) * 128, :], writes=[ld[j % 2]])
                    P.op("act", lambda e, j=j: e.activation(out=sg[:, j, :], in_=ld[j % 2][:], func=AF.Silu), [ld[j % 2]], [sg])
                for j in range(4):
                    vl = ld[j % 2]
                    P.dma("act", vl[:], PFg[O_V + h * 512 + j * 128:O_V + h * 512 + (j + 1) * 128, :], writes=[vl])
                    for cg in range(9):
                        pt = pv[cg % 2]
                        for a in range(4):
                            ck = cg * 4 + a
                            P.op("pe", lambda e, pt=pt, a=a, ck=ck, vl=vl: e.transpose(out=pt[:, a, :], in_=vl[:, ck * 64:(ck + 1) * 64], identity=S.ident[:]),
                                 [vl, S.ident], [pt])
                        if cg % 2 == 0:
                            P.op("dve", lambda e, pt=pt, cg=cg, j=j: e.tensor_copy(out=vT[:, cg * 4:cg * 4 + 4, j * 128:(j + 1) * 128], in_=pt[:]), [pt], [vT])
                        else:
                            P.op("act", lambda e, pt=pt, cg=cg, j=j: e.activation(out=vT[:, cg * 4:cg * 4 + 4, j * 128:(j + 1) * 128], in_=pt[:], func=AF.Copy), [pt], [vT])
            P.barrier()
            for d in range(2):
                rev = d == 1
                with ExitStack() as es1:
                    pl = [Tl(nc, es1, [128, 512], F32, psum=True, name="pl") for _ in range(5)]
                    nl = Tl(nc, es1, [128, NT], F32, name="nl")
                    Bc = Tl(nc, es1, [128, NT], F32, name="Bc")
                    tmp = Tl(nc, es1, [128, NT], F32, name="tmp")
                    ex = Tl(nc, es1, [128, NT], F32, name="ex")
                    qf = Tl(nc, es1, [128, NT], F32, name="qf")
                    kf = Tl(nc, es1, [128, NT], F32, name="kf")
                    tot = Tl(nc, es1, [128, 36], F32, name="tot")
                    for dc in range(2):
                        dg = h * 256 + dc * 128
                        P.dma("sp", qf[:], PFg[O_Q + dg:O_Q + dg + 128, :], writes=[qf])
                        P.dma("act", kf[:], PFg[O_K + dg:O_K + dg + 128, :], writes=[kf])
                        for ti, (t0, tn) in enumerate(TT):
                            P.op("pe", lambda e, ti=ti, t0=t0, tn=tn, dg=dg: e.matmul(
                                out=pl[ti][:, :tn], lhsT=waA[d][:, dg:dg + 128], rhs=adT[d][:, t0:t0 + tn], start=True, stop=True),
                                [waA[d], adT[d]], [pl[ti]])
                            P.op("act", lambda e, ti=ti, t0=t0, tn=tn: e.activation(out=nl[:, t0:t0 + tn], in_=pl[ti][:, :tn], func=AF.Exp, scale=-1.0),
                                 [pl[ti]], [nl])
                        P.op("act", lambda e: e.activation(out=nl[:], in_=nl[:], func=AF.Ln, bias=S.ones[:, 0:1], scale=1.0), [nl, S.ones], [nl])
                        seg_cumsum(S, Bc, nl, tot, tmp, rev)
                        P.op("act", lambda e, dc=dc: e.activation(out=etot[:, dc, :], in_=tot[:], func=AF.Exp, scale=-1.0 / 16), [tot], [etot])
                        P.op("act", lambda e: e.activation(out=ex[:], in_=Bc[:], func=AF.Exp, scale=-1.0 / 16), [Bc], [ex])
                        P.op("dve", lambda e, dc=dc: e.scalar_tensor_tensor(out=qd[:, dc, :], in0=qf[:], scalar=1.0 / 16, in1=ex[:], op0=ALU.mult, op1=ALU.mult),
                             [qf, ex], [qd])
                        P.op("act", lambda e: e.activation(out=ex[:], in_=Bc[:], func=AF.Exp, scale=1.0 / 16), [Bc], [ex])
                        P.op("dve", lambda e, dc=dc: e.tensor_tensor(out=ki[:, dc, :], in0=kf[:], in1=ex[:], op=ALU.mult), [kf, ex], [ki])
                        P.op("dve", lambda e: e.tensor_tensor(out=tmp[:].rearrange("p (c q) -> p c q", q=64), in0=Bc[:].rearrange("p (c q) -> p c q", q=64),
                                                              in1=bc(tot[:].unsqueeze(2), [128, 36, 64]), op=ALU.subtract), [Bc, tot], [tmp])
                        P.op("act", lambda e: e.activation(out=ex[:], in_=tmp[:], func=AF.Exp, scale=1.0 / 16), [tmp], [ex])
                        P.op("dve", lambda e, dc=dc: e.tensor_tensor(out=kdT[:, dc, :], in0=kf[:], in1=ex[:], op=ALU.mult), [kf, ex], [kdT])
                P.barrier()
                with ExitStack() as es1:
                    pa = Tl(nc, es1, [64, 64], F32, psum=True, name="pa")
                    pk = Tl(nc, es1, [64, 2, 128], BF16, psum=True, name="pk")
                    po = [Tl(nc, es1, [64, 512], F32, psum=True, name="po") for _ in range(2)]
                    pu = [Tl(nc, es1, [128, 512], F32, psum=True, name="pu") for _ in range(2)]
                    ptr = Tl(nc, es1, [128, 4, 64], F32, psum=True, name="ptr")
                    att = [Tl(nc, es1, [64, 64], BF16, name="att") for _ in range(2)]
                    kd = [Tl(nc, es1, [64, 256], BF16, name="kd") for _ in range(2)]
                    ost = [Tl(nc, es1, [64, 512], F32, name="ost") for _ in range(2)]
                    oft = [Tl(nc, es1, [64, 512], F32, name="oft") for _ in range(2)]
                    sqt = Tl(nc, es1, [64, 512], F32, name="sqt")
                    ss = Tl(nc, es1, [64, 1], F32, name="ss")
                    P.op("dve", lambda e: e.memset(Sf[:], 0.0), [], [Sf])
                    P.op("dve", lambda e: e.memset(Sb[:], 0.0), [], [Sb])
                    deferred = []
                    for i, ck in enumerate(chunk_order(rev)):
                        ts = slice(ck * 64, ck * 64 + 64)
                        a_, k_, o_, f_, p_ = att[i % 2], kd[i % 2], ost[i % 2], oft[i % 2], po[i % 2]
                        for dc in range(2):
                            P.op("pe", lambda e, dc=dc, ts=ts: e.matmul(out=pa[:], lhsT=ki[:, dc, ts], rhs=qd[:, dc, ts], start=(dc == 0), stop=(dc == 1)),
                                 [ki, qd], [pa])
                        P.op("dve", lambda e, a_=a_: e.tensor_tensor(out=a_[:], in0=pa[:], in1=S.tri[d][:], op=ALU.mult), [pa, S.tri[d]], [a_])
                        for dc in range(2):
                            P.op("pe", lambda e, dc=dc, ts=ts: e.transpose(out=pk[:, dc, :], in_=kdT[:, dc, ts], identity=S.identb[:]), [kdT, S.identb], [pk])
                        P.op("act", lambda e, k_=k_: e.activation(out=k_[:], in_=pk[:].rearrange("p a b -> p (a b)"), func=AF.Copy), [pk], [k_])
                        for dc in range(2):
                            P.op("pe", lambda e, dc=dc, ts=ts, p_=p_: e.matmul(out=p_[:], lhsT=qd[:, dc, ts], rhs=Sb[:, dc, :], start=(dc == 0), stop=False),
                                 [qd, Sb], [p_])
                        P.op("pe", lambda e, a_=a_, ck=ck, p_=p_: e.matmul(out=p_[:], lhsT=a_[:], rhs=vT[:, ck, :], start=False, stop=True), [a_, vT], [p_])
                        for dc in range(2):
                            P.op("pe", lambda e, dc=dc, k_=k_, ck=ck: e.matmul(out=pu[dc][:], lhsT=k_[:, dc * 128:(dc + 1) * 128], rhs=vT[:, ck, :], start=True, stop=True),
                                 [k_, vT], [pu[dc]])
                            P.op("dve", lambda e, dc=dc, ck=ck: e.scalar_tensor_tensor(out=Sf[:, dc, :], in0=Sf[:, dc, :], scalar=etot[:, dc, ck:ck + 1], in1=pu[dc][:],
                                                                                       op0=ALU.mult, op1=ALU.add), [Sf, etot, pu[dc]], [Sf])
                            P.op("act", lambda e, dc=dc: e.activation(out=Sb[:, dc, :], in_=Sf[:, dc, :], func=AF.Copy), [Sf], [Sb])
                        while len(deferred) > 0:
                            deferred.pop(0)()
                        gof = S.GOF[ts, h * 512:(h + 1) * 512]
                        if not rev:
                            P.op("act", lambda e, o_=o_, p_=p_: e.activation(out=o_[:], in_=p_[:], func=AF.Copy), [p_], [o_])
                            P.dma("sp", gof, o_[:], reads=[o_], writes=[])
                        else:
                            P.dma("sp", f_[:], gof, writes=[f_])
                            P.op("dve", lambda e, o_=o_, p_=p_, f_=f_: e.tensor_tensor(out=o_[:], in0=p_[:], in1=f_[:], op=ALU.add), [p_, f_], [o_])
                            P.op("act", lambda e, o_=o_: e.activation(out=sqt[:], in_=o_[:], func=AF.Square), [o_], [sqt])
                            P.op("dve", lambda e: e.reduce_sum(out=ss[:], in_=sqt[:], axis=AX.X), [sqt], [ss])
                            P.op("act", lambda e: e.activation(out=ss[:], in_=ss[:], func=AF.Sqrt, bias=eps[:64, :], scale=1.0 / 512), [ss, eps], [ss])
                            P.op("dve", lambda e: e.reciprocal(out=ss[:], in_=ss[:]), [ss], [ss])
                            P.op("dve", lambda e, o_=o_: e.tensor_scalar(out=o_[:], in0=o_[:], scalar1=ss[:, 0:1], scalar2=None, op0=ALU.mult), [o_, ss], [o_])
                            def post(o_=o_, ts=ts):
                                for j in range(4):
                                    P.op("pe", lambda e: e.transpose(out=ptr[:, j, :], in_=o_[:, j * 128:(j + 1) * 128], identity=S.ident[:64, :64]), [o_, S.ident], [ptr])
                                for j in range(4):
                                    P.op("dve", lambda e: e.scalar_tensor_tensor(out=ogh[:, j, ts], in0=ptr[:, j, :], scalar=ng[:, j:j + 1], in1=sg[:, j, ts],
                                                                                 op0=ALU.mult, op1=ALU.mult), [ptr, ng, sg], [ogh])
                            deferred.append(post)
                    while len(deferred) > 0:
                        deferred.pop(0)()
                P.barrier()
            for j in range(4):
                P.dma("sp", S.OG[h * 512 + j * 128:h * 512 + (j + 1) * 128, :], ogh[:, j, :], reads=[ogh], writes=[])
            P.barrier()


def vec_pp(S, es, src_ap, nchunk, name):
    t = Tl(S.nc, es, [128, nchunk], F32, name=name)
    S.P.dma("sp", t[:], src_ap.rearrange("(c p) -> p c", p=128), writes=[t], allow_slow_non_contiguous=True)
    return t


def phase_ssd(S, l, b):
    nc, P, W = S.nc, S.P, S.W
    PFs = S.PF["s"]
    OX, OB, OC, ODT = 4096, 8192, 9216, 10240
    with ExitStack() as es:
        dtT = Tl(nc, es, [64, 36, 128], F32, name="dtT")
        cumT = Tl(nc, es, [64, 36, 128], F32, name="cumT")
        laT = Tl(nc, es, [64, 36, 128], F32, name="laT")
        cml = Tl(nc, es, [64, 36, 128], F32, name="cml")
        five = Tl(nc, es, [64, 1], F32, name="five")
        P.op("dve", lambda e: e.memset(five[:], 5.0), [], [five])
        Dbc = Tl(nc, es, [64, 64], F32, name="Dbc")
        P.dma("sp", Dbc[:], bc(W["ssm_d"][l:l + 1, :], [64, 64]), writes=[Dbc])
        with ExitStack() as es1:
            raw = Tl(nc, es1, [128, NT], F32, name="raw")
            la = Tl(nc, es1, [128, NT], F32, name="la")
            cum = Tl(nc, es1, [128, NT], F32, name="cum")
            tmp = Tl(nc, es1, [128, NT], F32, name="tmp")
            tot = Tl(nc, es1, [128, 36], F32, name="tot")
            pp = Tl(nc, es1, [128, 2], F32, name="pp")
            pt = [Tl(nc, es1, [64, 4, 128], F32, psum=True, name="pts") for _ in range(2)]
            P.dma("sp", raw[:], PFs[ODT:ODT + 128, :], writes=[raw])
            P.dma("sp", pp[:, 0:1], W["ssm_dt_bias"][l].rearrange("d (h o) -> (d h) o", o=1), writes=[pp])
            P.dma("sp", pp[:, 1:2], W["ssm_a_log"][l].rearrange("d (h o) -> (d h) o", o=1), writes=[pp])
            P.op("act", lambda e: e.activation(out=pp[:, 1:2], in_=pp[:, 1:2], func=AF.Exp), [pp], [pp])
            P.op("dve", lambda e: e.tensor_scalar(out=pp[:, 1:2], in0=pp[:, 1:2], scalar1=-1.0, scalar2=None, op0=ALU.mult), [pp], [pp])
            P.op("act", lambda e: e.activation(out=raw[:], in_=raw[:], func=AF.Exp, bias=pp[:, 0:1], scale=1.0), [raw, pp], [raw])
            P.op("act", lambda e: e.activation(out=raw[:], in_=raw[:], func=AF.Ln, bias=S.ones[:, 0:1], scale=1.0), [raw, S.ones], [raw])
            P.op("dve", lambda e: e.tensor_scalar(out=la[:], in0=raw[:], scalar1=pp[:, 1:2], scalar2=None, op0=ALU.mult), [raw, pp], [la])
            P.op("dve", lambda e: e.tensor_tensor_scan(out=cum[:], data0=S.segmask[:], data1=la[:], initial=0.0, op0=ALU.mult, op1=ALU.add), [la, S.segmask], [cum])
            P.op("dve", lambda e: e.tensor_copy(out=tot[:], in_=cum[:].rearrange("p (c q) -> p c q", q=64)[:, :, 63]), [cum], [tot])
            P.op("dve", lambda e: e.tensor_tensor(out=tmp[64:128, :], in0=la[64:128, :], in1=cum[64:128, :], op=ALU.subtract), [la, cum], [tmp])
            P.op("dve", lambda e: e.tensor_tensor(out=cum[64:128, :].rearrange("p (c q) -> p c q", q=64), in0=tmp[64:128, :].rearrange("p (c q) -> p c q", q=64),
                                                  in1=bc(tot[64:128, :].unsqueeze(2), [64, 36, 64]), op=ALU.add), [tmp, tot], [cum])
            P.dma("sp", S.CUM, cum[:], reads=[cum], writes=[])
            k = 0
            for src, dst in ((raw, dtT), (cum, cumT), (la, laT)):
                for cg in range(9):
                    p_ = pt[k % 2]
                    k += 1
                    for a in range(4):
                        ck = cg * 4 + a
                        P.op("pe", lambda e: e.transpose(out=p_[:, a, :], in_=src[:, ck * 64:(ck + 1) * 64], identity=S.ident[:]), [src, S.ident], [p_])
                    P.op("act", lambda e: e.activation(out=dst[:, cg * 4:cg * 4 + 4, :], in_=p_[:], func=AF.Copy), [p_], [dst])
            P.op("act", lambda e: e.activation(out=cml[:], in_=dtT[:], func=AF.Ln), [dtT], [cml])
            P.op("dve", lambda e: e.tensor_tensor(out=cml[:], in0=cumT[:], in1=cml[:], op=ALU.subtract), [cumT, cml], [cml])
            P.op("dve", lambda e: e.tensor_scalar(out=cml[:], in0=cml[:], scalar1=5.0, scalar2=None, op0=ALU.add), [cml], [cml])
        P.barrier()
        with ExitStack() as es1:
            cw = [vec_pp(S, es1, W["ssm_conv_w"][l, j, :], 48, "cw") for j in range(3)]
            cb = vec_pp(S, es1, W["ssm_conv_b"][l, :], 48, "cb")
            xin = [Tl(nc, es1, [128, NT], F32, name="xin") for _ in range(2)]
            yc = [Tl(nc, es1, [128, NT], F32, name="yc") for _ in range(2)]
            yb = [Tl(nc, es1, [128, NT], BF16, name="yb") for _ in range(2)]
            stg = [Tl(nc, es1, [128, 18, 128], BF16, name="stg") for _ in range(2)]
            pt = [Tl(nc, es1, [128, 4, 128], BF16, psum=True, name="ptc") for _ in range(2)]
            k = 0
            for c in range(48):
                xi, yi, ybi, sg_ = xin[c % 2], yc[c % 2], yb[c % 2], stg[c % 2]
                P.dma("sp" if c % 2 == 0 else "act", xi[:], PFs[OX + c * 128:OX + (c + 1) * 128, :], writes=[xi])
                P.op("dve", lambda e: e.tensor_scalar(out=yi[:], in0=xi[:], scalar1=cw[1][:, c:c + 1], scalar2=None, op0=ALU.mult), [xi, cw[1]], [yi])
                for (s0, s1) in SEGS:
                    P.op("dve", lambda e: e.scalar_tensor_tensor(out=yi[:, s0 + 1:s1], in0=xi[:, s0:s1 - 1], scalar=cw[0][:, c:c + 1], in1=yi[:, s0 + 1:s1],
                                                                 op0=ALU.mult, op1=ALU.add), [xi, cw[0], yi], [yi])
                    P.op("dve", lambda e: e.scalar_tensor_tensor(out=yi[:, s0:s1 - 1], in0=xi[:, s0 + 1:s1], scalar=cw[2][:, c:c + 1], in1=yi[:, s0:s1 - 1],
                                                                 op0=ALU.mult, op1=ALU.add), [xi, cw[2], yi], [yi])
                P.op("act", lambda e: e.activation(out=ybi[:], in_=yi[:], func=AF.Silu, bias=cb[:, c:c + 1], scale=1.0), [yi, cb], [ybi])
                if c >= 32:
                    P.dma("sp", S.BCF[(c - 32) * 128:(c - 31) * 128, :], ybi[:], reads=[ybi], writes=[])
                if c < 40:
                    for tg in range(5):
                        p_ = pt[k % 2]
                        k += 1
                        na = 4 if tg < 4 else 2
                        for a in range(na):
                            tt = tg * 4 + a
                            P.op("pe", lambda e: e.transpose(out=p_[:, a, :], in_=ybi[:, tt * 128:(tt + 1) * 128], identity=S.identb[:]), [ybi, S.identb], [p_])
                        if tg % 2 == 0:
                            P.op("dve", lambda e: e.tensor_copy(out=sg_[:, tg * 4:tg * 4 + na, :], in_=p_[:, :na, :]), [p_], [sg_])
                        else:
                            P.op("act", lambda e: e.activation(out=sg_[:, tg * 4:tg * 4 + na, :], in_=p_[:, :na, :], func=AF.Copy), [p_], [sg_])
                    dst = S.XT[:, c * 128:(c + 1) * 128] if c < 32 else S.BT[:, (c - 32) * 128:(c - 31) * 128]
                    P.dma("act", dst.rearrange("(tt p) c -> p tt c", p=128), sg_[:], reads=[sg_], writes=[])
        P.barrier()
        for g in range(8):
            with ExitStack() as es1:
                BF = Tl(nc, es1, [128, NT], BF16, name="BF")
                CF = Tl(nc, es1, [128, NT], BF16, name="CF")
                P.dma("sp", BF[:], S.BCF[g * 128:(g + 1) * 128, :], writes=[BF])
                P.dma("sp", CF[:], S.BCF[1024 + g * 128:1024 + (g + 1) * 128, :], writes=[CF])
                def two(shape, dt, name, psum=False):
                    return [[Tl(nc, es1, shape, dt, psum=psum, name=name) for _ in range(2)] for _ in range(2)]
                hf = [Tl(nc, es1, [128, 512], F32, name="hf") for _ in range(2)]
                hb = [Tl(nc, es1, [128, 512], BF16, name="hb") for _ in range(2)]
                xc, bt, cbc, wd = two([64, 512], BF16, "xc"), two([64, 128], BF16, "bt"), two([64, 8, 64], F32, "cbc"), two([64, 8, 64], BF16, "wd")
                cbm, xd, ys, yf = two([64, 64], F32, "cbm"), two([64, 512], BF16, "xd"), two([64, 512], F32, "ys"), two([64, 512], F32, "yf")
                ec, sd, etb = two([64, 8], F32, "ec"), two([64, 8], F32, "sd"), two([128, 8], F32, "etb")
                pcb = Tl(nc, es1, [64, 64], F32, psum=True, name="pcb")
                py1 = [Tl(nc, es1, [64, 512], F32, psum=True, name="py1") for _ in range(2)]
                py2 = [Tl(nc, es1, [64, 512], F32, psum=True, name="py2") for _ in range(2)]
                pst = Tl(nc, es1, [128, 512], F32, psum=True, name="pst")
                ptb = Tl(nc, es1, [128, 8], F32, psum=True, name="ptb")
                orders = [chunk_order(False), chunk_order(True)]
                pos = [{c: i for i, c in enumerate(o)} for o in orders]
                for d in range(2):
                    P.op("dve", lambda e: e.memset(hf[d][:], 0.0), [], [hf[d]])
                    P.op("dve", lambda e: e.memset(hb[d][:], 0.0), [], [hb[d]])

                def A1(d, i):
                    ck, r = orders[d][i], i % 2
                    ts = slice(ck * 64, ck * 64 + 64)
                    hs = slice(d * 64 + g * 8, d * 64 + g * 8 + 8)
                    P.dma("sp", xc[d][r][:], S.XT[ts, g * 512:(g + 1) * 512], writes=[xc[d][r]])
                    P.dma("sp", bt[d][r][:], S.BT[ts, g * 128:(g + 1) * 128], writes=[bt[d][r]])
                    P.dma("act", cbc[d][r][:], bc(S.CUM[d * 64 + g * 8:d * 64 + g * 8 + 8, ts].unsqueeze(0), [64, 8, 64]), writes=[cbc[d][r]])
                    P.op("pe", lambda e: e.matmul(out=pcb[:], lhsT=BF[:, ts], rhs=CF[:, ts], start=True, stop=True), [BF, CF], [pcb])
                    P.op("dve", lambda e: e.tensor_tensor(out=cbm[d][r][:], in0=pcb[:], in1=S.tri[d][:], op=ALU.mult), [pcb, S.tri[d]], [cbm[d][r]])
                    for e_ in range(8):
                        P.op("act", lambda e: e.activation(out=cbc[d][r][:, e_, :], in_=cbc[d][r][:, e_, :], func=AF.Relu,
                                                           bias=cml[:, ck, hs.start + e_:hs.start + e_ + 1], scale=-1.0), [cbc[d][r], cml], [cbc[d][r]])
                    P.op("act", lambda e: e.activation(out=cbc[d][r][:], in_=cbc[d][r][:], func=AF.Exp, bias=five[:], scale=-1.0), [cbc[d][r], five], [cbc[d][r]])
                    P.op("act", lambda e: e.activation(out=ec[d][r][:], in_=cumT[:, ck, hs], func=AF.Exp), [cumT], [ec[d][r]])

                def A2(d, i):
                    ck, r = orders[d][i], i % 2
                    hs = slice(d * 64 + g * 8, d * 64 + g * 8 + 8)
                    P.op("dve", lambda e: e.tensor_tensor(out=wd[d][r][:], in0=cbc[d][r][:], in1=bc(cbm[d][r][:].unsqueeze(1), [64, 8, 64]), op=ALU.mult),
                         [cbc[d][r], cbm[d][r]], [wd[d][r]])
                    P.op("pe", lambda e: e.matmul(out=ptb[:], lhsT=S.ones[:64, :], rhs=laT[:, ck, hs], start=True, stop=True), [S.ones, laT], [ptb])
                    P.op("dve", lambda e: e.tensor_tensor(out=sd[d][r][:], in0=ptb[:64, :], in1=cumT[:, ck, hs], op=ALU.subtract), [ptb, cumT], [sd[d][r]])
                    P.op("act", lambda e: e.activation(out=etb[d][r][:], in_=ptb[:], func=AF.Exp), [ptb], [etb[d][r]])
                    P.op("act", lambda e: e.activation(out=sd[d][r][:], in_=sd[d][r][:], func=AF.Exp), [sd[d][r]], [sd[d][r]])
                    P.op("pool", lambda e: e.tensor_tensor(out=sd[d][r][:], in0=sd[d][r][:], in1=dtT[:, ck, hs], op=ALU.mult), [sd[d][r], dtT], [sd[d][r]])
                    P.op("pool", lambda e: e.tensor_tensor(out=xd[d][r][:].rearrange("p (a b) -> p a b", a=8), in0=xc[d][r][:].rearrange("p (a b) -> p a b", a=8),
                                                           in1=bc(sd[d][r][:].unsqueeze(2), [64, 8, 64]), op=ALU.mult), [xc[d][r], sd[d][r]], [xd[d][r]])

                def Bst(d, i):
                    ck, r = orders[d][i], i % 2
                    ts = slice(ck * 64, ck * 64 + 64)
                    for e_ in range(8):
                        P.op("pe", lambda e: e.matmul(out=py1[d][:, e_ * 64:(e_ + 1) * 64], lhsT=wd[d][r][:, e_, :], rhs=xc[d][r][:, e_ * 64:(e_ + 1) * 64], start=True, stop=True),
                             [wd[d][r], xc[d][r]], [py1[d]])
                    P.op("pe", lambda e: e.matmul(out=py2[d][:], lhsT=CF[:, ts], rhs=hb[d][:], start=True, stop=True), [CF, hb[d]], [py2[d]])
                    P.op("pe", lambda e: e.matmul(out=pst[:], lhsT=bt[d][r][:], rhs=xd[d][r][:], start=True, stop=True), [bt[d][r], xd[d][r]], [pst])
                    y_, f_ = ys[d][r], yf[d][r]
                    P.op("dve", lambda e: e.tensor_tensor(out=y_[:].rearrange("p (a b) -> p a b", a=8), in0=py2[d][:].rearrange("p (a b) -> p a b", a=8),
                                                          in1=bc(ec[d][r][:].unsqueeze(2), [64, 8, 64]), op=ALU.mult), [py2[d], ec[d][r]], [y_])
                    P.op("dve", lambda e: e.tensor_tensor(out=y_[:], in0=y_[:], in1=py1[d][:], op=ALU.add), [y_, py1[d]], [y_])
                    P.op("dve", lambda e: e.tensor_tensor(out=hf[d][:].rearrange("p (a b) -> p a b", a=8), in0=hf[d][:].rearrange("p (a b) -> p a b", a=8),
                                                          in1=bc(etb[d][r][:].unsqueeze(2), [128, 8, 64]), op=ALU.mult), [hf[d], etb[d][r]], [hf[d]])
                    P.op("dve", lambda e: e.tensor_tensor(out=hf[d][:], in0=hf[d][:], in1=pst[:], op=ALU.add), [hf[d], pst], [hf[d]])
                    P.op("act", lambda e: e.activation(out=hb[d][:], in_=hf[d][:], func=AF.Copy), [hf[d]], [hb[d]])
                    second = pos[d][ck] > pos[1 - d][ck]
                    ydst = S.YF[ts, g * 512:(g + 1) * 512]
                    key = P.B(("ssdyf", g, ck))
                    if not second:
                        P.dma("sp", ydst, y_[:], reads=[y_], writes=[key])
                    else:
                        P.dma("sp", f_[:], ydst, reads=[key], writes=[f_])
                        P.op("pool", lambda e: e.tensor_tensor(out=y_[:], in0=y_[:], in1=f_[:], op=ALU.add), [y_, f_], [y_])
                        P.op("pool", lambda e: e.tensor_tensor(out=f_[:].rearrange("p (a b) -> p a b", a=8), in0=xc[d][r][:].rearrange("p (a b) -> p a b", a=8),
                                                               in1=bc(Dbc[:, g * 8:g * 8 + 8].unsqueeze(2), [64, 8, 64]), op=ALU.mult), [xc[d][r], Dbc], [f_])
                        P.op("pool", lambda e: e.tensor_tensor(out=y_[:], in0=y_[:], in1=f_[:], op=ALU.add), [y_, f_], [y_])
                        P.dma("sp", S.YT[ts, g * 512:(g + 1) * 512], y_[:], reads=[y_], writes=[])

                for d in range(2):
                    A1(d, 0)
                    A2(d, 0)
                for i in range(36):
                    for d in range(2):
                        if i + 1 < 36:
                            A1(d, i + 1)
                        Bst(d, i)
                        if i + 1 < 36:
                            A2(d, i + 1)
                P.barrier()
        with ExitStack() as es1:
            ngv = vec_pp(S, es1, W["ssm_norm_g"][l, :], 32, "ngv")
            eps = Tl(nc, es1, [128, 1], F32, name="epss")
            P.op("dve", lambda e: e.memset(eps[:], 1e-6), [], [eps])
            yt = [Tl(nc, es1, [128, 4096], F32, name="yt") for _ in range(2)]
            zt = [Tl(nc, es1, [128, 32, 128], F32, name="zt") for _ in range(2)]
            yg = Tl(nc, es1, [128, 32, 128], F32, name="yg")
            sq = Tl(nc, es1, [128, 32, 128], F32, name="sq")
            rs = Tl(nc, es1, [128, 8, 128], F32, name="rs")
            ob = [Tl(nc, es1, [128, 32, 128], BF16, name="ob") for _ in range(2)]
            ptr = [Tl(nc, es1, [128, 4, 128], F32, psum=True, name="ptr4") for _ in range(2)]
            pss = [Tl(nc, es1, [128, 4, 128], F32, psum=True, name="pss") for _ in range(2)]
            k = 0
            for tt in range(18):
                r = tt % 2
                tsl = slice(tt * 128, (tt + 1) * 128)
                P.dma("sp", yt[r][:], S.YT[tsl, :], writes=[yt[r]])
                P.dma("act", zt[r][:], PFs[0:4096, tsl].rearrange("(c p) t -> p c t", p=128), writes=[zt[r]])
                P.op("act", lambda e: e.activation(out=zt[r][:], in_=zt[r][:], func=AF.Silu), [zt[r]], [zt[r]])
                for cg in range(8):
                    p_ = ptr[k % 2]
                    k += 1
                    for a in range(4):
                        c = cg * 4 + a
                        P.op("pe", lambda e: e.transpose(out=p_[:, a, :], in_=yt[r][:, c * 128:(c + 1) * 128], identity=S.ident[:]), [yt[r], S.ident], [p_])
                    P.op("dve", lambda e: e.tensor_tensor(out=yg[:, cg * 4:cg * 4 + 4, :], in0=p_[:], in1=zt[r][:, cg * 4:cg * 4 + 4, :], op=ALU.mult), [p_, zt[r]], [yg])
                P.op("pool", lambda e: e.tensor_tensor(out=sq[:], in0=yg[:], in1=yg[:], op=ALU.mult), [yg], [sq])
                for gg in range(8):
                    p_ = pss[gg // 4]
                    for a in range(4):
                        P.op("pe", lambda e: e.matmul(out=p_[:, gg % 4, :], lhsT=S.ones[:], rhs=sq[:, gg * 4 + a, :], start=(a == 0), stop=(a == 3)), [sq, S.ones], [p_])
                for hh in range(2):
                    P.op("act", lambda e: e.activation(out=rs[:, hh * 4:hh * 4 + 4, :], in_=pss[hh][:], func=AF.Sqrt, bias=eps[:], scale=1.0 / 512), [pss[hh], eps], [rs])
                P.op("dve", lambda e: e.reciprocal(out=rs[:], in_=rs[:]), [rs], [rs])
                P.op("dve", lambda e: e.tensor_tensor(out=yg[:].rearrange("p (g a) t -> p g a t", a=4), in0=yg[:].rearrange("p (g a) t -> p g a t", a=4),
                                                      in1=bc(rs[:].unsqueeze(2), [128, 8, 4, 128]), op=ALU.mult), [yg, rs], [yg])
                P.op("pool", lambda e: e.tensor_tensor(out=ob[r][:], in0=yg[:], in1=bc(ngv[:].unsqueeze(2), [128, 32, 128]), op=ALU.mult), [yg, ngv], [ob[r]])
                P.dma("sp", S.OS[:, tsl].rearrange("(c p) t -> p c t", p=128), ob[r][:], reads=[ob[r]], writes=[])


CEXP = 0.6065306597126334


def mix_shift(S, out, x, omu, hmu, n):
    P = S.P
    P.op("dve", lambda e: e.tensor_scalar(out=out[:n, :], in0=x[:n, :], scalar1=omu, scalar2=None, op0=ALU.mult), [x], [out])
    for (s0, s1) in SEGS:
        P.op("dve", lambda e: e.scalar_tensor_tensor(out=out[:n, s0 + 1:s1], in0=x[:n, s0:s1 - 1], scalar=hmu, in1=out[:n, s0 + 1:s1], op0=ALU.mult, op1=ALU.add), [x, out], [out])
        P.op("dve", lambda e: e.scalar_tensor_tensor(out=out[:n, s0:s1 - 1], in0=x[:n, s0 + 1:s1], scalar=hmu, in1=out[:n, s0:s1 - 1], op0=ALU.mult, op1=ALU.add), [x, out], [out])


def ckview(d, h):
    return d[:, :, h, :].rearrange("c k t -> k c t")


def phase_rwkv(S, l, b):
    nc, P, W = S.nc, S.P, S.W
    PFr = S.PF["r"]
    OWD, OAD, OGD = 6144, 6336, 6528
    mu = W["rwkv_mu"][l]
    with ExitStack() as es:
        with ExitStack() as es1:
            def pp64(src, name):
                t = Tl(nc, es1, [128, 16], F32, name=name)
                P.dma("sp", t[:], src.rearrange("(c p) -> p c", p=128), writes=[t], allow_slow_non_contiguous=True)
                return t
            mu3 = Tl(nc, es1, [128, 3, 16], F32, name="mu3")
            P.dma("sp", mu3[:], mu[0:6144].rearrange("(c j p) -> p c j", c=3, p=128), writes=[mu3], allow_slow_non_contiguous=True)
            omu3 = Tl(nc, es1, [128, 3, 16], F32, name="omu3")
            P.op("dve", lambda e: e.tensor_scalar(out=omu3[:], in0=mu3[:], scalar1=-1.0, scalar2=1.0, op0=ALU.mult, op1=ALU.add), [mu3], [omu3])
            P.op("dve", lambda e: e.tensor_scalar(out=mu3[:], in0=mu3[:], scalar1=0.5, scalar2=None, op0=ALU.mult), [mu3], [mu3])
            kkk = pp64(W["rwkv_k_k"][l], "kkk")
            kka_ = pp64(W["rwkv_k_a"][l], "kka")
            rk = pp64(W["rwkv_r_k"][l].rearrange("h k -> (h k)"), "rk")
            w0 = [pp64(W["rwkv_w0"][l, d], "w0") for d in range(2)]
            a0 = [pp64(W["rwkv_a0"][l, d], "a0") for d in range(2)]
            blk = Tl(nc, es1, [128, 128], F32, name="blk1")
            P.op("dve", lambda e: e.memset(blk[:], 0.0), [], [blk])
            P.op("dve", lambda e: e.memset(blk[0:64, 0:64], 1.0), [], [blk])
            P.op("dve", lambda e: e.memset(blk[64:128, 64:128], 1.0), [], [blk])
            mus = Tl(nc, es1, [128, 3], F32, name="mus")
            twd = [Tl(nc, es1, [96, NT], F32, name="twd") for _ in range(2)]
            adm = [Tl(nc, es1, [96, NT], F32, name="adm") for _ in range(2)]
            sgd = Tl(nc, es1, [128, 2, NT], F32, name="sgd")
            es_l = ExitStack()
            lin = Tl(nc, es_l, [128, NT], F32, name="lin")
            mixt = Tl(nc, es_l, [128, NT], F32, name="mixt")
            jobs = [(OWD, 96, twd[0], AF.Tanh), (OWD + 96, 96, twd[1], AF.Tanh), (OAD, 96, adm[0], None), (OAD + 96, 96, adm[1], None),
                    (OGD, 128, None, AF.Sigmoid), (OGD + 128, 128, None, AF.Sigmoid)]
            for ji, (r0, n, dst, fn) in enumerate(jobs):
                P.dma("sp", lin[:n, :], PFr[r0:r0 + n, :], writes=[lin])
                P.dma("sp", mus[:n, 0:1], mu[r0:r0 + n].rearrange("(p o) -> p o", o=1), writes=[mus])
                P.op("dve", lambda e: e.tensor_scalar(out=mus[:n, 1:2], in0=mus[:n, 0:1], scalar1=-1.0, scalar2=1.0, op0=ALU.mult, op1=ALU.add), [mus], [mus])
                P.op("dve", lambda e: e.tensor_scalar(out=mus[:n, 2:3], in0=mus[:n, 0:1], scalar1=0.5, scalar2=None, op0=ALU.mult), [mus], [mus])
                if dst is None:
                    dsl = sgd[:, ji - 4, :]
                    mix_shift(S, mixt, lin, mus[:n, 1:2], mus[:n, 2:3], n)
                    P.op("act", lambda e: e.activation(out=dsl, in_=mixt[:], func=fn), [mixt], [sgd])
                else:
                    mix_shift(S, dst, lin, mus[:n, 1:2], mus[:n, 2:3], n)
                    if fn is not None:
                        P.op("act", lambda e: e.activation(out=dst[:], in_=dst[:], func=fn), [dst], [dst])
            P.barrier()
            es_l.close()
            def T64(name, dt=F32):
                return Tl(nc, es1, [128, NT], dt, name=name)
            xin = [T64("rxin") for _ in range(1)]
            rm, km, vm, kkn, tA, tB, tC, ki, kkat, bon, Cc, av = (T64(n_) for n_ in ("rm", "km", "vm", "kkn", "tA", "tB", "tC", "ki", "kkat", "bon", "Cc", "av"))
            ob = [T64("rob", BF16) for _ in range(3)]
            tot = Tl(nc, es1, [128, 36], F32, name="rtot")
            gam = Tl(nc, es1, [128, 36], F32, name="rgam")
            wl = [Tl(nc, es1, [96, 128], F32, name="wl") for _ in range(4)]
            gl = Tl(nc, es1, [128, 2, 128], F32, name="gl")
            ps = [Tl(nc, es1, [128, 512], F32, psum=True, name="rps") for _ in range(7)]
            pk = [0]
            obk = [0]

            def mm_tiles(fn_lhs_rhs, evac):
                for (t0, tn) in TT:
                    pt = ps[pk[0] % 7]
                    pk[0] += 1
                    lst = fn_lhs_rhs(t0, tn)
                    for i, (lh, rh, deps) in enumerate(lst):
                        P.op("pe", lambda e: e.matmul(out=pt[:, :tn], lhsT=lh, rhs=rh, start=(i == 0), stop=(i == len(lst) - 1)), deps, [pt])
                    evac(pt, t0, tn)

            def dma_ck(dst, j, t_, q="act"):
                for hh in range(2):
                    P.dma(q, ckview(dst, 2 * j + hh), t_[hh * 64:(hh + 1) * 64, :].rearrange("k (c t) -> k c t", t=64), reads=[t_], writes=[])

            def store_ck(dst, j, fn):
                o_ = ob[obk[0] % 3]
                obk[0] += 1
                fn(o_)
                dma_ck(dst, j, o_)

            for j in range(16):
                hs = slice(j * 128, (j + 1) * 128)
                for ci, dstt in enumerate((rm, km, vm)):
                    xi = xin[0]
                    P.dma("sp", xi[:], PFr[ci * 2048 + j * 128:ci * 2048 + (j + 1) * 128, :], writes=[xi])
                    mix_shift(S, dstt, xi, omu3[:, ci, j:j + 1], mu3[:, ci, j:j + 1], 128)
                store_ck(S.RW["V"], j, lambda o_: P.op("act", lambda e: e.activation(out=o_[:], in_=vm[:], func=AF.Copy), [vm], [o_]))
                P.op("dve", lambda e: e.tensor_scalar(out=kkn[:], in0=km[:], scalar1=kkk[:, j:j + 1], scalar2=None, op0=ALU.mult), [km, kkk], [kkn])
                P.op("pool", lambda e: e.tensor_tensor(out=tA[:], in0=kkn[:], in1=kkn[:], op=ALU.mult), [kkn], [tA])
                mm_tiles(lambda t0, tn: [(blk[:], tA[:, t0:t0 + tn], [blk, tA])],
                         lambda pt, t0, tn: P.op("dve", lambda e: e.tensor_scalar_max(out=tB[:, t0:t0 + tn], in0=pt[:, :tn], scalar1=1e-24), [pt], [tB]))
                P.op("act", lambda e: e.activation(out=tB[:], in_=tB[:], func=AF.Sqrt), [tB], [tB])
                P.op("dve", lambda e: e.reciprocal(out=tB[:], in_=tB[:]), [tB], [tB])
                P.op("pool", lambda e: e.tensor_tensor(out=kkn[:], in0=kkn[:], in1=tB[:], op=ALU.mult), [kkn, tB], [kkn])
                P.dma("sp", gl[:], W["rwkv_g_up"][l][:, hs].rearrange("(kc p) n -> p kc n", p=128), writes=[gl])
                mm_tiles(lambda t0, tn: [(gl[:, kc, :], sgd[:, kc, t0:t0 + tn], [gl, sgd]) for kc in range(2)],
                         lambda pt, t0, tn: P.op("act", lambda e: e.activation(out=tC[:, t0:t0 + tn], in_=pt[:, :tn], func=AF.Copy), [pt], [tC]))
                dma_ck(S.RW["G"], j, tC)
                for d in range(2):
                    rev = d == 1
                    P.dma("sp", wl[d][:], W["rwkv_w_up"][l, d][:, hs], writes=[wl[d]])
                    P.dma("sp", wl[2 + d][:], W["rwkv_a_up"][l, d][:, hs], writes=[wl[2 + d]])
                    mm_tiles(lambda t0, tn: [(wl[d][:], twd[d][:, t0:t0 + tn], [wl[d], twd[d]])],
                             lambda pt, t0, tn: P.op("act", lambda e: e.activation(out=tA[:, t0:t0 + tn], in_=pt[:, :tn], func=AF.Sigmoid, bias=w0[d][:, j:j + 1], scale=1.0), [pt, w0[d]], [tA]))
                    mm_tiles(lambda t0, tn: [(wl[2 + d][:], adm[d][:, t0:t0 + tn], [wl[2 + d], adm[d]])],
                             lambda pt, t0, tn: P.op("act", lambda e: e.activation(out=av[:, t0:t0 + tn], in_=pt[:, :tn], func=AF.Sigmoid, bias=a0[d][:, j:j + 1], scale=1.0), [pt, a0[d]], [av]))
                    P.op("dve", lambda e: e.tensor_scalar(out=ki[:], in0=av[:], scalar1=-1.0, scalar2=kka_[:, j:j + 1], op0=ALU.add, op1=ALU.mult), [av, kka_], [ki])
                    P.op("dve", lambda e: e.scalar_tensor_tensor(out=ki[:], in0=ki[:], scalar=1.0, in1=km[:], op0=ALU.add, op1=ALU.mult), [ki, km], [ki])
                    P.op("pool", lambda e: e.tensor_tensor(out=kkat[:], in0=kkn[:], in1=av[:], op=ALU.mult), [kkn, av], [kkat])
                    P.op("dve", lambda e: e.scalar_tensor_tensor(out=tB[:], in0=rm[:], scalar=rk[:, j:j + 1], in1=ki[:], op0=ALU.mult, op1=ALU.mult), [rm, rk, ki], [tB])
                    if d == 0:
                        mm_tiles(lambda t0, tn: [(blk[:], tB[:, t0:t0 + tn], [blk, tB])],
                                 lambda pt, t0, tn: P.op("dve", lambda e: e.tensor_tensor(out=bon[:, t0:t0 + tn], in0=pt[:, :tn], in1=vm[:, t0:t0 + tn], op=ALU.mult), [pt, vm], [bon]))
                    else:
                        mm_tiles(lambda t0, tn: [(blk[:], tB[:, t0:t0 + tn], [blk, tB])],
                                 lambda pt, t0, tn: P.op("dve", lambda e: e.tensor_tensor(out=tC[:, t0:t0 + tn], in0=pt[:, :tn], in1=vm[:, t0:t0 + tn], op=ALU.mult), [pt, vm], [tC]))
                        P.op("pool", lambda e: e.tensor_tensor(out=bon[:], in0=bon[:], in1=tC[:], op=ALU.add), [bon, tC], [bon])
                        dma_ck(S.RW["BON"], j, bon)
                    seg_cumsum(S, Cc, tA, tot, tB, rev)
                    P.op("act", lambda e: e.activation(out=gam[:], in_=tot[:], func=AF.Exp, scale=-CEXP), [tot], [gam])
                    for hh in range(2):
                        P.dma("sp", S.RW["GAM"][d][2 * j + hh], gam[hh * 64:(hh + 1) * 64, :], reads=[gam], writes=[])
                    P.op("pool", lambda e: e.tensor_tensor(out=tB[:], in0=Cc[:], in1=tA[:], op=ALU.subtract), [Cc, tA], [tB])
                    P.op("act", lambda e: e.activation(out=tB[:], in_=tB[:], func=AF.Exp, scale=-CEXP), [tB], [tB])
                    store_ck(S.RW["A"][d], j, lambda o_: P.op("dve", lambda e: e.tensor_tensor(out=o_[:], in0=kkn[:], in1=tB[:], op=ALU.mult), [kkn, tB], [o_]))
                    P.op("act", lambda e: e.activation(out=tB[:], in_=Cc[:], func=AF.Exp, scale=-CEXP), [Cc], [tB])
                    store_ck(S.RW["R"][d], j, lambda o_: P.op("pool", lambda e: e.tensor_tensor(out=o_[:], in0=rm[:], in1=tB[:], op=ALU.mult), [rm, tB], [o_]))
                    P.op("act", lambda e: e.activation(out=tB[:], in_=Cc[:], func=AF.Exp, scale=CEXP), [Cc], [tB])
                    store_ck(S.RW["K"][d], j, lambda o_: P.op("dve", lambda e: e.tensor_tensor(out=o_[:], in0=ki[:], in1=tB[:], op=ALU.mult), [ki, tB], [o_]))
                    store_ck(S.RW["B"][d], j, lambda o_: P.op("pool", lambda e: e.tensor_tensor(out=o_[:], in0=kkat[:], in1=tB[:], op=ALU.mult), [kkat, tB], [o_]))
                    P.op("dve", lambda e: e.tensor_tensor(out=tB[:].rearrange("p (c q) -> p c q", q=64), in0=Cc[:].rearrange("p (c q) -> p c q", q=64),
                                                          in1=bc(tot[:].unsqueeze(2), [128, 36, 64]), op=ALU.subtract), [Cc, tot], [tB])
                    P.op("act", lambda e: e.activation(out=tB[:], in_=tB[:], func=AF.Exp, scale=CEXP), [tB], [tB])
                    store_ck(S.RW["KT"][d], j, lambda o_: P.op("dve", lambda e: e.tensor_tensor(out=o_[:], in0=ki[:], in1=tB[:], op=ALU.mult), [ki, tB], [o_]))
                    store_ck(S.RW["BT"][d], j, lambda o_: P.op("pool", lambda e: e.tensor_tensor(out=o_[:], in0=kkat[:], in1=tB[:], op=ALU.mult), [kkat, tB], [o_]))
        P.barrier()
        with ExitStack() as es1:
            def pp64b(src, name):
                t = Tl(nc, es1, [64, 32], F32, name=name)
                P.dma("sp", t[:], src.rearrange("(h k) -> k h", k=64), writes=[t], allow_slow_non_contiguous=True)
                return t
            lnw = pp64b(W["rwkv_ln_w"][l], "lnw")
            lnb = pp64b(W["rwkv_ln_b"][l], "lnb")
            eps = Tl(nc, es1, [64, 1], F32, name="repsl")
            P.op("dve", lambda e: e.memset(eps[:], 64e-5), [], [eps])
            SH = [64, 8, 64]
            pf = [Tl(nc, es1, SH, F32, psum=True, name="rpf") for _ in range(7)]
            pb = Tl(nc, es1, SH, BF16, psum=True, name="rpb")
            pfk = [0]

            def nps():
                t = pf[pfk[0] % 7]
                pfk[0] += 1
                return t
            def make_chain():
                C = K()
                C.ldb = {n_: [Tl(nc, es1, SH, BF16, name="l" + n_) for _ in range(2)] for n_ in ("A", "R", "K", "B", "KT", "BT", "V")}
                C.gamt = Tl(nc, es1, [64, 8, 36], F32, name="gamt")
                C.Tf = Tl(nc, es1, SH, F32, name="Tf")
                C.Tb = Tl(nc, es1, SH, BF16, name="Tb")
                C.PTf = [[Tl(nc, es1, SH, F32, name="PTf") for _ in range(6)] for _ in range(2)]
                C.Pb = [Tl(nc, es1, SH, BF16, name="Pb") for _ in range(2)]
                C.PTb = [Tl(nc, es1, SH, BF16, name="PTb") for _ in range(2)]
                dbl = lambda n_: [Tl(nc, es1, SH, BF16, name=n_) for _ in range(2)]
                C.LkT, C.MKT, C.MBT, C.KtT, C.BtT, C.VT = dbl("LkT"), dbl("MKT"), dbl("MBT"), dbl("KtT"), dbl("BtT"), dbl("VT")
                C.X = Tl(nc, es1, SH, F32, name="X")
                C.Ub = Tl(nc, es1, SH, BF16, name="Ub")
                C.ys = [Tl(nc, es1, SH, F32, name="rys") for _ in range(2)]
                C.yfl = [Tl(nc, es1, SH, F32, name="ryf") for _ in range(2)]
                C.bnl = [Tl(nc, es1, SH, F32, name="rbn") for _ in range(2)]
                C.ggl = [Tl(nc, es1, SH, F32, name="rgg") for _ in range(2)]
                C.sqv = Tl(nc, es1, SH, F32, name="rsq")
                C.st8 = Tl(nc, es1, [64, 8], F32, name="st8")
                C.st9 = Tl(nc, es1, [64, 8], F32, name="st9")
                C.obf = [Tl(nc, es1, SH, BF16, name="robf") for _ in range(2)]
                return C
            chains = [make_chain() for _ in range(2)]

            def mm8(pt, *pairs):
                for e_ in range(8):
                    for i_, (lh, rh) in enumerate(pairs):
                        P.op("pe", lambda e: e.matmul(out=pt[:, e_, :], lhsT=lh[:, e_, :], rhs=rh[:, e_, :], start=(i_ == 0), stop=(i_ == len(pairs) - 1)), [lh, rh], [pt])

            def masked(pt, dst, mask):
                P.op("dve", lambda e: e.tensor_tensor(out=dst[:], in0=pt[:], in1=bc(mask[:].unsqueeze(1), SH), op=ALU.mult), [pt, mask], [dst])

            for hgp in range(2):
                for d in range(2):
                    rev = d == 1
                    order = chunk_order(rev)
                    for c_ in range(2):
                        C = chains[c_]
                        hg = hgp * 2 + c_
                        P.dma("sp", C.gamt[:], S.RW["GAM"][d][hg * 8:hg * 8 + 8].rearrange("h k c -> k h c"), writes=[C.gamt])
                        P.op("dve", lambda e: e.memset(C.Tf[:], 0.0), [], [C.Tf])
                        P.op("dve", lambda e: e.memset(C.Tb[:], 0.0), [], [C.Tb])

                    def A_slices(C, hg, i):
                        ck, r_ = order[i], i % 2
                        h8 = slice(hg * 8, hg * 8 + 8)
                        L = {n_: C.ldb[n_][r_] for n_ in C.ldb}

                        def s0():
                            for qi, n_ in enumerate(("A", "R", "K", "B", "KT", "BT", "V")):
                                src = S.RW[n_] if n_ == "V" else S.RW[n_][d]
                                P.dma("sp" if qi % 2 == 0 else "act", L[n_][:], src[ck, :, h8, :], writes=[L[n_]])
                            p1 = nps(); mm8(p1, (L["B"], L["A"])); masked(p1, C.PTf[r_][0], S.nstr[d])
                            P.op("pool", lambda e: e.tensor_copy(out=C.PTb[0][:], in_=C.PTf[r_][0][:]), [C.PTf[r_][0]], [C.PTb[0]])
                            p2 = nps(); mm8(p2, (L["A"], L["B"])); masked(p2, C.Pb[0], S.nstr[1 - d])

                        def s1():
                            p3 = nps(); mm8(p3, (L["K"], L["A"])); masked(p3, C.LkT[r_], S.str_[d])
                            p4 = nps(); mm8(p4, (L["K"], L["R"])); masked(p4, C.MKT[r_], S.tri[d])
                            p5 = nps(); mm8(p5, (L["B"], L["R"])); masked(p5, C.MBT[r_], S.ntri[d])

                        def s2():
                            for src_, dst_, sc_ in ((L["KT"], C.KtT[r_], 1.0), (L["BT"], C.BtT[r_], -1.0), (L["V"], C.VT[r_], 1.0)):
                                for e_ in range(8):
                                    P.op("pe", lambda e: e.transpose(out=pb[:, e_, :], in_=src_[:, e_, :], identity=S.identb[:64, :64]), [src_, S.identb], [pb])
                                P.op("act", lambda e: e.activation(out=dst_[:], in_=pb[:], func=AF.Copy, scale=sc_), [pb], [dst_])

                        def sq(j):
                            def f():
                                cur, nxt = j % 2, (j + 1) % 2
                                pr = nps(); mm8(pr, (C.Pb[cur], C.PTb[cur]))
                                P.op("act", lambda e: e.activation(out=C.PTf[r_][j + 1][:], in_=pr[:], func=AF.Copy), [pr], [C.PTf[r_][j + 1]])
                                if j < 4:
                                    pq = nps(); mm8(pq, (C.PTb[cur], C.Pb[cur]))
                                    P.op("act", lambda e: e.activation(out=C.Pb[nxt][:], in_=pq[:], func=AF.Copy), [pq], [C.Pb[nxt]])
                                    P.op("pool", lambda e: e.tensor_copy(out=C.PTb[nxt][:], in_=C.PTf[r_][j + 1][:]), [C.PTf[r_][j + 1]], [C.PTb[nxt]])
                            return f
                        return [s0, s1, s2, sq(0), sq(1), sq(2), sq(3), sq(4)]

                    def B_steps(C, hg, i):
                        ck, r_ = order[i], i % 2
                        h8 = slice(hg * 8, hg * 8 + 8)
                        L = {n_: C.ldb[n_][r_] for n_ in C.ldb}
                        tsl = slice(ck * 64, ck * 64 + 64)

                        def b0():
                            px = nps()
                            mm8(px, (L["A"], C.Tb), (C.LkT[r_], C.VT[r_]))
                            P.op("act", lambda e: e.activation(out=C.X[:], in_=px[:], func=AF.Copy), [px], [C.X])

                        def ap(j):
                            def f():
                                pa = nps()
                                mm8(pa, (C.PTf[r_][j], C.X))
                                P.op("dve", lambda e: e.tensor_tensor(out=C.X[:], in0=C.X[:], in1=pa[:], op=ALU.add), [C.X, pa], [C.X])
                            return f

                        def fin():
                            P.op("act", lambda e: e.activation(out=C.Ub[:], in_=C.X[:], func=AF.Copy), [C.X], [C.Ub])
                            py = nps()
                            mm8(py, (L["R"], C.Tb), (C.MKT[r_], C.VT[r_]), (C.MBT[r_], C.Ub))
                            pS = nps()
                            mm8(pS, (C.KtT[r_], C.VT[r_]), (C.BtT[r_], C.Ub))
                            P.op("dve", lambda e: e.tensor_tensor(out=C.Tf[:], in0=C.Tf[:], in1=bc(C.gamt[:, :, ck].unsqueeze(2), SH), op=ALU.mult), [C.Tf, C.gamt], [C.Tf])
                            P.op("dve", lambda e: e.tensor_tensor(out=C.Tf[:], in0=C.Tf[:], in1=pS[:], op=ALU.add), [C.Tf, pS], [C.Tf])
                            P.op("act", lambda e: e.activation(out=C.Tb[:], in_=C.Tf[:], func=AF.Copy), [C.Tf], [C.Tb])
                            ydst = S.RW["YF"][tsl, hg * 512:(hg + 1) * 512]
                            y_ = C.ys[r_]
                            if not rev:
                                P.op("act", lambda e: e.activation(out=y_[:], in_=py[:], func=AF.Copy), [py], [y_])
                                P.dma("sp", ydst, y_[:].rearrange("p a b -> p (a b)"), reads=[y_], writes=[])
                            else:
                                f_, bn_, gg_, o_ = C.yfl[r_], C.bnl[r_], C.ggl[r_], C.obf[r_]
                                P.dma("sp", f_[:].rearrange("p a b -> p (a b)"), ydst, writes=[f_])
                                P.dma("act", bn_[:], S.RW["BON"][ck, :, h8, :], writes=[bn_])
                                P.dma("act", gg_[:], S.RW["G"][ck, :, h8, :], writes=[gg_])
                                P.op("dve", lambda e: e.tensor_tensor(out=y_[:], in0=py[:], in1=f_[:], op=ALU.add), [py, f_], [y_])
                                post1()

                        def post1():
                            if not rev:
                                return
                            y_ = C.ys[r_]
                            f_, bn_, gg_, o_ = C.yfl[r_], C.bnl[r_], C.ggl[r_], C.obf[r_]
                            P.op("dve", lambda e: e.reduce_sum(out=C.st8[:], in_=y_[:], axis=AX.X), [y_], [C.st8])
                            P.op("pool", lambda e: e.tensor_scalar(out=C.st8[:], in0=C.st8[:], scalar1=1.0 / 64, scalar2=None, op0=ALU.mult), [C.st8], [C.st8])
                            P.op("pool", lambda e: e.tensor_tensor(out=y_[:], in0=y_[:], in1=bc(C.st8[:].unsqueeze(2), SH), op=ALU.subtract), [y_, C.st8], [y_])
                            P.op("pool", lambda e: e.tensor_tensor(out=C.sqv[:], in0=y_[:], in1=y_[:], op=ALU.mult), [y_], [C.sqv])
                            P.op("dve", lambda e: e.reduce_sum(out=C.st9[:], in_=C.sqv[:], axis=AX.X), [C.sqv], [C.st9])
                            P.op("act", lambda e: e.activation(out=C.st9[:], in_=C.st9[:], func=AF.Sqrt, bias=eps[:], scale=1.0 / 64), [C.st9, eps], [C.st9])
                            P.op("dve", lambda e: e.reciprocal(out=C.st9[:], in_=C.st9[:]), [C.st9], [C.st9])
                            P.op("pool", lambda e: e.tensor_tensor(out=y_[:], in0=y_[:], in1=bc(C.st9[:].unsqueeze(2), SH), op=ALU.mult), [y_, C.st9], [y_])

                        def post():
                            if not rev:
                                return
                            y_ = C.ys[r_]
                            f_, bn_, gg_, o_ = C.yfl[r_], C.bnl[r_], C.ggl[r_], C.obf[r_]
                            pt_ = nps()
                            for e_ in range(8):
                                P.op("pe", lambda e: e.transpose(out=pt_[:, e_, :], in_=y_[:, e_, :], identity=S.ident[:64, :64]), [y_, S.ident], [pt_])
                            P.op("dve", lambda e: e.tensor_tensor(out=C.sqv[:], in0=pt_[:], in1=bc(lnw[:, h8].unsqueeze(2), SH), op=ALU.mult), [pt_, lnw], [C.sqv])
                            P.op("pool", lambda e: e.tensor_tensor(out=C.sqv[:], in0=C.sqv[:], in1=bc(lnb[:, h8].unsqueeze(2), SH), op=ALU.add), [C.sqv, lnb], [C.sqv])
                            P.op("pool", lambda e: e.tensor_tensor(out=C.sqv[:], in0=C.sqv[:], in1=bn_[:], op=ALU.add), [C.sqv, bn_], [C.sqv])
                            P.op("pool", lambda e: e.tensor_tensor(out=o_[:], in0=C.sqv[:], in1=gg_[:], op=ALU.mult), [C.sqv, gg_], [o_])
                            P.dma("sp", S.ORW[hg * 512:(hg + 1) * 512, tsl].rearrange("(h v) t -> v h t", v=64), o_[:], reads=[o_], writes=[])
                        return [b0, ap(0), ap(1), ap(2), ap(3), ap(4), ap(5), fin, post]

                    for c_ in range(2):
                        for f in A_slices(chains[c_], hgp * 2 + c_, 0):
                            f()
                    prev_post = [None, None]
                    for i in range(36):
                        a_ = [A_slices(chains[c_], hgp * 2 + c_, i + 1) if i + 1 < 36 else [] for c_ in range(2)]
                        b_ = [B_steps(chains[c_], hgp * 2 + c_, i) for c_ in range(2)]
                        for k_ in range(8):
                            for c_ in range(2):
                                b_[c_][k_]()
                                if k_ < len(a_[c_]):
                                    a_[c_][k_]()
                                if k_ == 3 and prev_post[c_] is not None:
                                    prev_post[c_]()
                        prev_post = [b_[c_][8] for c_ in range(2)]
                    for c_ in range(2):
                        prev_post[c_]()
                    P.barrier()


TT2 = [(0, 256), (256, 512), (768, 512), (1280, 512), (1792, 512)]


def resid_update(S, b, ci, pt_ap, t0, tn, gidx, xo, k, pt):
    nc, P = S.nc, S.P
    v = 2 if t0 < 256 else b
    xi = xo[k % len(xo)]
    dst = S.xT[b, ci * 128:(ci + 1) * 128, t0:t0 + tn]
    P.dma("sp", xi[:, :tn], dst, writes=[xi])
    P.op("dve", lambda e: e.scalar_tensor_tensor(out=xi[:, :tn], in0=pt_ap, scalar=S.mod[:, gidx * 16 + ci, v:v + 1], in1=xi[:, :tn], op0=ALU.mult, op1=ALU.add),
         [pt, S.mod, xi], [xi])
    P.dma("act", dst, xi[:, :tn], reads=[xi], writes=[])


def resid_evac(S, b, gidx, xo, cnt):
    def evac(ci, n, tiles):
        for (pt, t0, tn) in tiles:
            if t0 == 0:
                for (a0, a1) in ((0, 256), (256, tn)):
                    resid_update(S, b, ci, pt[:, a0:a1], a0, a1 - a0, gidx, xo, cnt[0], pt)
                    cnt[0] += 1
            else:
                resid_update(S, b, ci, pt[:, :tn], t0, tn, gidx, xo, cnt[0], pt)
                cnt[0] += 1
    return evac


def phase_merge(S, l, b):
    nc, P, W = S.nc, S.P, S.W
    passes = [(S.OG, 0, W["w_br_gla"][l], 0), (S.OS, 0, W["w_br_ssm"][l][0:2048, :], 2048), (S.OS, 16, W["w_br_ssm"][l][2048:4096, :], 2048),
              (S.ORW, 0, W["w_br_rwkv"][l], 4096)]
    with ExitStack() as es:
        mT = Tl(nc, es, [128, 16, NT], BF16, name="mT")
        with ExitStack() as es1:
            rhs = Tl(nc, es1, [128, 16, NT], BF16, name="mrhs")
            gt = [Tl(nc, es1, [128, NT], F32, name="mgt") for _ in range(2)]
            tmp = [Tl(nc, es1, [128, 512], F32, name="mtmp") for _ in range(3)]
            tk = [0]
            for pi, (src, k0, w_ap, go) in enumerate(passes):
                P.dma("sp", rhs[:], src[k0 * 128:(k0 + 16) * 128, :].rearrange("(kc p) t -> p kc t", p=128), writes=[rhs])

                def evac(ci, n, tiles, pi=pi, go=go):
                    g_ = gt[ci % 2]
                    P.dma("act", g_[:], S.PF["t"][go + ci * 128:go + (ci + 1) * 128, :], writes=[g_])
                    P.op("act", lambda e: e.activation(out=g_[:], in_=g_[:], func=AF.Sigmoid), [g_], [g_])
                    for (pt, t0, tn) in tiles:
                        if pi == 0:
                            P.op("dve", lambda e: e.tensor_tensor(out=mT[:, ci, t0:t0 + tn], in0=pt[:, :tn], in1=g_[:, t0:t0 + tn], op=ALU.mult), [pt, g_], [mT])
                        else:
                            t_ = tmp[tk[0] % 3]
                            tk[0] += 1
                            P.op("dve", lambda e: e.tensor_tensor(out=t_[:, :tn], in0=pt[:, :tn], in1=g_[:, t0:t0 + tn], op=ALU.mult), [pt, g_], [t_])
                            P.op("pool", lambda e: e.tensor_tensor(out=mT[:, ci, t0:t0 + tn], in0=mT[:, ci, t0:t0 + tn], in1=t_[:, :tn], op=ALU.add), [mT, t_], [mT])

                with ExitStack() as es2:
                    proj(S, es2, w_ap, 2048, rhs, 16, evac, blk=256)
                P.barrier()
        with ExitStack() as es1:
            xo = [Tl(nc, es1, [128, 512], F32, name="mxo") for _ in range(4)]
            proj(S, es1, W["w_out"][l], 2048, mT, 16, resid_evac(S, b, 2, xo, [0]), blk=256)
        P.barrier()


def phase_ffn_up(S, l, hT):
    nc, P, W = S.nc, S.P, S.W
    with ExitStack() as es:
        cw = [[vec_pp(S, es, W["ffn_conv_w"][l, i, j, :], 88, "fcw") for j in range(3)] for i in range(3)]
        cb = vec_pp(S, es, W["ffn_conv_b"][l, :], 88, "fcb")
        u = [Tl(nc, es, [128, NT], F32, name="fu") for _ in range(2)]
        y = [Tl(nc, es, [128, NT], F32, name="fy") for _ in range(2)]
        gb = [Tl(nc, es, [128, NT], BF16, name="fgb") for _ in range(2)]
        ab = [Tl(nc, es, [128, NT], BF16, name="fab") for _ in range(2)]
        cnt = [0]

        def evac(ci, n, tiles):
            k = cnt[0]
            cnt[0] += 1
            ui, yi = u[k % 2], y[k % 2]
            eng = "dve"
            for i, (pt, t0, tn) in enumerate(tiles):
                if i % 2 == 0 or True:
                    P.op("act", lambda e: e.activation(out=ui[:, t0:t0 + tn], in_=pt[:, :tn], func=AF.Copy), [pt], [ui])
                else:
                    P.op("dve", lambda e: e.tensor_copy(out=ui[:, t0:t0 + tn], in_=pt[:, :tn]), [pt], [ui])
            P.op(eng, lambda e: e.tensor_scalar(out=yi[:], in0=ui[:], scalar1=cw[1][1][:, ci:ci + 1], scalar2=cb[:, ci:ci + 1], op0=ALU.mult, op1=ALU.add), [ui, cw[1][1], cb], [yi])
            for dc in (-1, 1):
                o0, o1 = max(0, -dc), 256 - max(0, dc)
                P.op(eng, lambda e: e.scalar_tensor_tensor(out=yi[:, o0:o1], in0=ui[:, o0 + dc:o1 + dc], scalar=cw[1][1 + dc][:, ci:ci + 1], in1=yi[:, o0:o1], op0=ALU.mult, op1=ALU.add),
                     [ui, yi, cw[1][1 + dc]], [yi])
            uv = ui[:, 256:].rearrange("p (r c) -> p r c", c=64)
            yv = yi[:, 256:].rearrange("p (r c) -> p r c", c=64)
            for dr in (-1, 0, 1):
                for dc in (-1, 0, 1):
                    if dr == 0 and dc == 0:
                        continue
                    r0, r1 = max(0, -dr), 32 - max(0, dr)
                    c0, c1 = max(0, -dc), 64 - max(0, dc)
                    P.op(eng, lambda e: e.scalar_tensor_tensor(out=yv[:, r0:r1, c0:c1], in0=uv[:, r0 + dr:r1 + dr, c0 + dc:c1 + dc], scalar=cw[1 + dr][1 + dc][:, ci:ci + 1],
                                                               in1=yv[:, r0:r1, c0:c1], op0=ALU.mult, op1=ALU.add), [ui, yi, cw[1 + dr][1 + dc]], [yi])
            if ci < 44:
                g_ = gb[k % 2]
                P.op("act", lambda e: e.activation(out=g_[:], in_=yi[:], func=AF.Silu), [yi], [g_])
                P.dma("sp", S.ACTG[ci * 128:(ci + 1) * 128, :], g_[:], reads=[g_], writes=[S.P.B(("actg", ci))])
            else:
                g_, a_ = gb[k % 2], ab[k % 2]
                P.dma("sp", g_[:], S.ACTG[(ci - 44) * 128:(ci - 43) * 128, :], reads=[S.P.B(("actg", ci - 44))], writes=[g_])
                P.op("pool", lambda e: e.tensor_tensor(out=a_[:], in0=yi[:], in1=g_[:], op=ALU.mult), [yi, g_], [a_])
                P.dma("act", S.ACT[(ci - 44) * 128:(ci - 43) * 128, :], a_[:], reads=[a_], writes=[])

        proj(S, es, W["ffn_w_up"][l], 2 * FH, hT, 16, evac, blk=512, nps=7)


def phase_ffn_down(S, l, b):
    nc, P, W = S.nc, S.P, S.W
    with ExitStack() as es:
        rhs = Tl(nc, es, [128, 22, NT], BF16, name="drhs")
        xo = [Tl(nc, es, [128, 512], F32, name="dxo") for _ in range(4)]
        cnt = [0]
        for half in range(2):
            r0 = half * 22 * 128
            P.dma("sp", rhs[:], S.ACT[r0:r0 + 22 * 128, :].rearrange("(kc p) t -> p kc t", p=128), writes=[rhs])
            with ExitStack() as es2:
                proj(S, es2, W["ffn_w_down"][l][r0:r0 + 22 * 128, :], 2048, rhs, 22, resid_evac(S, b, 5, xo, cnt), blk=256)
            P.barrier()


def phase_final(S, b, out):
    nc, P, W = S.nc, S.P, S.W
    with ExitStack() as es:
        gbc = Tl(nc, es, [128, DM], F32, name="gbc")
        P.dma("sp", gbc[:], bc(W["final_norm_g"].unsqueeze(0), [128, DM]), writes=[gbc])
        eps = Tl(nc, es, [128, 1], F32, name="feps")
        P.op("dve", lambda e: e.memset(eps[:], 1e-6), [], [eps])
        xt = [Tl(nc, es, [128, 16, 128], F32, name="fxt") for _ in range(2)]
        sq = [Tl(nc, es, [128, 16, 128], F32, name="fsq") for _ in range(2)]
        ot = [Tl(nc, es, [128, DM], F32, name="fot") for _ in range(2)]
        rs = [Tl(nc, es, [128, 1], F32, name="frs") for _ in range(2)]
        pss = [Tl(nc, es, [128, 1], F32, psum=True, name="fpss") for _ in range(2)]
        ptr = [Tl(nc, es, [128, 512], F32, psum=True, name="fptr") for _ in range(4)]
        k = 0
        for tt in range(16):
            r = tt % 2
            t0 = 256 + tt * 128
            P.dma("sp" if r == 0 else "act", xt[r][:], S.xT[b, :, t0:t0 + 128].rearrange("(c p) t -> p c t", p=128), writes=[xt[r]])
            P.op("act", lambda e: e.activation(out=sq[r][:], in_=xt[r][:], func=AF.Square), [xt[r]], [sq[r]])
            for c in range(16):
                P.op("pe", lambda e: e.matmul(out=pss[r][:], lhsT=sq[r][:, c, :], rhs=S.ones[:, 0:1], start=(c == 0), stop=(c == 15)), [sq[r], S.ones], [pss[r]])
            P.op("act", lambda e: e.activation(out=rs[r][:], in_=pss[r][:], func=AF.Sqrt, bias=eps[:], scale=1.0 / DM), [pss[r], eps], [rs[r]])
            P.op("dve", lambda e: e.reciprocal(out=rs[r][:], in_=rs[r][:]), [rs[r]], [rs[r]])
            for cg in range(4):
                p_ = ptr[k % 4]
                k += 1
                for a in range(4):
                    c = cg * 4 + a
                    P.op("pe", lambda e: e.transpose(out=p_[:, a * 128:(a + 1) * 128], in_=xt[r][:, c, :], identity=S.ident[:]), [xt[r], S.ident], [p_])
                P.op("dve", lambda e: e.scalar_tensor_tensor(out=ot[r][:, cg * 512:(cg + 1) * 512], in0=p_[:], scalar=rs[r][:, 0:1], in1=gbc[:, cg * 512:(cg + 1) * 512],
                                                             op0=ALU.mult, op1=ALU.mult), [p_, rs[r], gbc], [ot[r]])
            P.dma("sp", out[b, tt * 128:(tt + 1) * 128, :], ot[r][:], reads=[ot[r]], writes=[])


_NC_CACHE = {}


def kernel(**inputs):
    n = 8
    NB = 2
    if "nc" not in _NC_CACHE:
        _NC_CACHE["nc"] = build(NB=NB, L=2)
    nc = _NC_CACHE["nc"]
    cst = consts()
    shared = {k: np.ascontiguousarray(np.asarray(inputs[k], dtype=np.float32)) for k in WSHAPES}
    x = np.asarray(inputs["x"], dtype=np.float32)
    ctx = np.asarray(inputs["ctx"], dtype=np.float32)
    c = np.asarray(inputs["c"], dtype=np.float32)
    c_ctx = np.ascontiguousarray(np.asarray(inputs["c_ctx"], dtype=np.float32)[None, :])
    in_maps = []
    for i in range(n):
        m = {"x": np.ascontiguousarray(x[i * NB:(i + 1) * NB]), "ctx": np.ascontiguousarray(ctx[i * NB:(i + 1) * NB]),
             "c": np.ascontiguousarray(c[i * NB:(i + 1) * NB]), "c_ctx": c_ctx}
        m.update(shared)
        m.update(cst)
        in_maps.append(m)
    res = run_bass_kernel_spmd(nc, in_maps, core_ids=list(range(n)))
    return np.concatenate([r["out"] for r in res.results], axis=0)
```
